# Optimizing a Trainium2 kernel written in Bass

```python
import math
import jax, jax.numpy as jnp
from jax import lax
import numpy as np

D_MODEL = 2048
BATCH = 2
SEQ = 8192
DEPTH = 4

GRID_W = 64
CTX_LEN = 256
N_MIXERS = 4
D_FF = -(-8 * D_MODEL // (3 * 256)) * 256
D_RNN = -(-4 * D_MODEL // (3 * 256)) * 256
RG_BLOCKS = 16
RG_BLOCK = D_RNN // RG_BLOCKS
RG_CONV_W = 4
RG_C = 8.0
POOL_WINDOWS = (2, 4, 8, 16)
POOL_GROUPS = len(POOL_WINDOWS)
POOL_GROUP_DIM = D_MODEL // POOL_GROUPS
CONF_KERNEL = 31
FT_GROUPS = 4
FT_GROUP_DIM = D_MODEL // FT_GROUPS
ALPHA = (2 * DEPTH) ** 0.25
BETA = (8 * DEPTH) ** -0.25
LN_EPS = 1e-5
POS_THETA = 10000.0

kernel_name = "hybrid_interleaved_diffusion_trunk"


def _layer_norm(x, g, b):
    xf = x.astype(jnp.float32)
    mu = jnp.mean(xf, axis=-1, keepdims=True)
    xc = xf - mu
    var = jnp.mean(xc * xc, axis=-1, keepdims=True)
    y = xc * lax.rsqrt(var + LN_EPS) * g.astype(jnp.float32) + b.astype(jnp.float32)
    return y.astype(x.dtype)


def _pos_embed_2d(rows, cols, dim):
    quarter = dim // 4
    omega = 1.0 / (POS_THETA ** (jnp.arange(quarter, dtype=jnp.float32) / quarter))
    ar = jnp.arange(rows, dtype=jnp.float32)[:, None] * omega[None]
    ac = jnp.arange(cols, dtype=jnp.float32)[:, None] * omega[None]
    er = jnp.concatenate([jnp.sin(ar), jnp.cos(ar)], axis=-1)
    ec = jnp.concatenate([jnp.sin(ac), jnp.cos(ac)], axis=-1)
    pe = jnp.concatenate([jnp.broadcast_to(er[:, None, :], (rows, cols, dim // 2)),
                          jnp.broadcast_to(ec[None, :, :], (rows, cols, dim // 2))], axis=-1)
    return pe.reshape(rows * cols, dim)


def _depthwise_conv(x, w, b, pad_lo, pad_hi):
    ch = x.shape[-1]
    y = lax.conv_general_dilated(x, w[:, None, :].astype(x.dtype), window_strides=(1,),
                                 padding=[(pad_lo, pad_hi)],
                                 dimension_numbers=("NWC", "WIO", "NWC"),
                                 feature_group_count=ch)
    return y + b.astype(x.dtype)


def _linear_scan(a, b, h0, reverse):
    if h0 is not None:
        idx = -1 if reverse else 0
        b = b.at[:, idx].add(a[:, idx] * h0)

    def combine(l, r):
        return l[0] * r[0], r[0] * l[1] + r[1]

    _, h = lax.associative_scan(combine, (a, b), reverse=reverse, axis=1)
    return h


def _rglru_coeffs(xb, wr, br, wi, bi, lam):
    xf = xb.astype(jnp.float32)
    bsz, s, r_dim = xf.shape
    xblk = xf.reshape(bsz, s, RG_BLOCKS, RG_BLOCK)
    r = jax.nn.sigmoid(jnp.einsum("bsnk,nkj->bsnj", xblk, wr.astype(jnp.float32)).reshape(bsz, s, r_dim)
                       + br.astype(jnp.float32))
    i = jax.nn.sigmoid(jnp.einsum("bsnk,nkj->bsnj", xblk, wi.astype(jnp.float32)).reshape(bsz, s, r_dim)
                       + bi.astype(jnp.float32))
    log_a = -RG_C * r * jax.nn.softplus(-lam.astype(jnp.float32))
    a = jnp.exp(log_a)
    bcoef = jnp.sqrt(-jnp.expm1(2.0 * log_a)) * (i * xf)
    return a, bcoef


def _rglru_mixer(h, hc, w_gate, w_x, conv_w, conv_b, wr, br, wi, bi, lam, w_out, need_ctx_out):
    pad_lo, pad_hi = (RG_CONV_W - 1) // 2, RG_CONV_W // 2
    xb = _depthwise_conv(h @ w_x, conv_w, conv_b, pad_lo, pad_hi)
    xbc = _depthwise_conv(hc @ w_x, conv_w, conv_b, pad_lo, pad_hi)
    ys, ycs = [], []
    for d in range(2):
        rev = d == 1
        a_c, b_c = _rglru_coeffs(xbc, wr[d], br[d], wi[d], bi[d], lam[d])
        hs_c = _linear_scan(a_c, b_c, None, rev)
        h0 = hs_c[:, 0] if rev else hs_c[:, -1]
        a_l, b_l = _rglru_coeffs(xb, wr[d], br[d], wi[d], bi[d], lam[d])
        ys.append(_linear_scan(a_l, b_l, h0, rev))
        ycs.append(hs_c)
    y = (ys[0] + ys[1]).astype(h.dtype)
    out = (y * jax.nn.gelu(h @ w_gate)) @ w_out
    out_c = None
    if need_ctx_out:
        yc = (ycs[0] + ycs[1]).astype(hc.dtype)
        out_c = (yc * jax.nn.gelu(hc @ w_gate)) @ w_out
    return out, out_c


def _pool_mixer(h, w, b, scale):
    bsz, s, d = h.shape
    hf = h.astype(jnp.float32)
    cs = jnp.concatenate([jnp.zeros((bsz, 1, d), jnp.float32), lax.cumsum(hf, axis=1)], axis=1)
    t = jnp.arange(s)
    outs = []
    for g, win in enumerate(POOL_WINDOWS):
        sl = slice(g * POOL_GROUP_DIM, (g + 1) * POOL_GROUP_DIM)
        lo = jnp.clip(t - win // 2, 0, s)
        hi = jnp.clip(t - win // 2 + win, 0, s)
        csg = cs[:, :, sl]
        mean = (jnp.take(csg, hi, axis=1) - jnp.take(csg, lo, axis=1)) / (hi - lo).astype(jnp.float32)[:, None]
        outs.append(mean - hf[:, :, sl])
    p = jnp.stack(outs, axis=2)
    y = jnp.einsum("bsgk,gkj->bsgj", p, w.astype(jnp.float32)).reshape(bsz, s, d) + b.astype(jnp.float32)
    return (y * scale.astype(jnp.float32)).astype(h.dtype)


def _conformer_conv(h, w1, b1, dw, dwb, g, bb, w2, b2):
    d = h.shape[-1]
    u = h @ w1 + b1
    u = u[..., :d] * jax.nn.sigmoid(u[..., d:])
    pad = (CONF_KERNEL - 1) // 2
    u = _depthwise_conv(u, dw, dwb, pad, pad)
    u = jax.nn.silu(_layer_norm(u, g, bb))
    return u @ w2 + b2


def _fourier_mixer(h, w, b):
    bsz, s, d = h.shape
    f = jnp.fft.fftn(h.astype(jnp.float32).reshape(bsz, s, FT_GROUPS, FT_GROUP_DIM), axes=(1, 3), norm="ortho").real
    return f.reshape(bsz, s, d).astype(h.dtype) @ w + b


def _swiglu(h, w1, w3, w2):
    return (jax.nn.silu(h @ w1) * (h @ w3)) @ w2


def setup_inputs(seed: int = 0) -> dict:
    key = jax.random.key(seed)
    ks = iter(jax.random.split(key, 64))
    f32 = jnp.float32

    def nrm(shape, s):
        return jax.random.normal(next(ks), shape, f32) * s

    d = D_MODEL
    n_a = len(range(0, DEPTH, N_MIXERS))
    n_b = len(range(1, DEPTH, N_MIXERS))
    n_c = len(range(2, DEPTH, N_MIXERS))
    n_d = len(range(3, DEPTH, N_MIXERS))
    u = jax.random.uniform(next(ks), (n_a, 2, D_RNN), f32, minval=0.9, maxval=0.999)
    sig = u ** (1.0 / RG_C)
    rg_lam = jnp.log(sig) - jnp.log1p(-sig)
    return {
        "x": nrm((BATCH, SEQ, d), 1.0),
        "c": nrm((BATCH, d), 1.0),
        "ctx": nrm((BATCH, CTX_LEN, d), 1.0),
        "c_ctx": nrm((d,), 1.0),
        "mod_w": nrm((DEPTH, d, 6 * d), d ** -0.5),
        "mod_b": nrm((DEPTH, 6 * d), 0.02),
        "ln_g": 1.0 + nrm((DEPTH, 2, d), 0.02),
        "ln_b": nrm((DEPTH, 2, d), 0.02),
        "ffn_w1": nrm((DEPTH, d, D_FF), d ** -0.5),
        "ffn_w3": nrm((DEPTH, d, D_FF), d ** -0.5),
        "ffn_w2": nrm((DEPTH, D_FF, d), BETA * D_FF ** -0.5),
        "rg_w_gate": nrm((n_a, d, D_RNN), d ** -0.5),
        "rg_w_x": nrm((n_a, d, D_RNN), d ** -0.5),
        "rg_conv_w": nrm((n_a, RG_CONV_W, D_RNN), RG_CONV_W ** -0.5),
        "rg_conv_b": nrm((n_a, D_RNN), 0.02),
        "rg_wr": nrm((n_a, 2, RG_BLOCKS, RG_BLOCK, RG_BLOCK), RG_BLOCK ** -0.5),
        "rg_br": nrm((n_a, 2, D_RNN), 0.02),
        "rg_wi": nrm((n_a, 2, RG_BLOCKS, RG_BLOCK, RG_BLOCK), RG_BLOCK ** -0.5),
        "rg_bi": nrm((n_a, 2, D_RNN), 0.02),
        "rg_lam": rg_lam,
        "rg_w_out": nrm((n_a, D_RNN, d), BETA * D_RNN ** -0.5),
        "pool_w": nrm((n_b, POOL_GROUPS, POOL_GROUP_DIM, POOL_GROUP_DIM), BETA * POOL_GROUP_DIM ** -0.5),
        "pool_b": nrm((n_b, d), 0.02),
        "pool_scale": 1.0 + nrm((n_b, d), 0.1),
        "cv_w1": nrm((n_c, d, 2 * d), d ** -0.5),
        "cv_b1": nrm((n_c, 2 * d), 0.02),
        "cv_dw": nrm((n_c, CONF_KERNEL, d), CONF_KERNEL ** -0.5),
        "cv_dwb": nrm((n_c, d), 0.02),
        "cv_ln_g": 1.0 + nrm((n_c, d), 0.02),
        "cv_ln_b": nrm((n_c, d), 0.02),
        "cv_w2": nrm((n_c, d, d), BETA * d ** -0.5),
        "cv_b2": nrm((n_c, d), 0.02),
        "ft_w": nrm((n_d, d, d), BETA * d ** -0.5),
        "ft_b": nrm((n_d, d), 0.02),
    }


def reference(x, c, ctx, c_ctx, mod_w, mod_b, ln_g, ln_b, ffn_w1, ffn_w3, ffn_w2,
              rg_w_gate, rg_w_x, rg_conv_w, rg_conv_b, rg_wr, rg_br, rg_wi, rg_bi, rg_lam, rg_w_out,
              pool_w, pool_b, pool_scale,
              cv_w1, cv_b1, cv_dw, cv_dwb, cv_ln_g, cv_ln_b, cv_w2, cv_b2,
              ft_w, ft_b):
    bsz, s, d = x.shape
    rows = s // GRID_W
    x = x + _pos_embed_2d(rows, GRID_W, d).astype(x.dtype)[None]
    reader_layers = [i for i in range(DEPTH) if i % N_MIXERS == 0]
    last_reader = max(reader_layers) if reader_layers else -1
    xc = ctx
    cond_lat = jax.nn.silu(c)
    cond_ctx = jax.nn.silu(c_ctx)[None]
    for i in range(DEPTH):
        kind = i % N_MIXERS
        j = i // N_MIXERS
        ctx_out = i < last_reader
        m = jnp.split((cond_lat @ mod_w[i] + mod_b[i])[:, None, :], 6, axis=-1)
        h = x * (1 + m[1]) + m[0]
        hc, mc, yc = None, None, None
        if i <= last_reader:
            mc = jnp.split((cond_ctx @ mod_w[i] + mod_b[i])[:, None, :], 6, axis=-1)
            hc = xc * (1 + mc[1]) + mc[0]
        if kind == 0:
            y, yc = _rglru_mixer(h, hc, rg_w_gate[j], rg_w_x[j], rg_conv_w[j], rg_conv_b[j],
                                 rg_wr[j], rg_br[j], rg_wi[j], rg_bi[j], rg_lam[j], rg_w_out[j], ctx_out)
        else:
            if kind == 1:
                mix = lambda t: _pool_mixer(t, pool_w[j], pool_b[j], pool_scale[j])
            elif kind == 2:
                mix = lambda t: _conformer_conv(t, cv_w1[j], cv_b1[j], cv_dw[j], cv_dwb[j],
                                                cv_ln_g[j], cv_ln_b[j], cv_w2[j], cv_b2[j])
            else:
                mix = lambda t: _fourier_mixer(t, ft_w[j], ft_b[j])
            y = mix(h)
            if ctx_out:
                yc = mix(hc)
        x = _layer_norm(ALPHA * x + m[2] * y, ln_g[i, 0], ln_b[i, 0])
        h = x * (1 + m[4]) + m[3]
        x = _layer_norm(ALPHA * x + m[5] * _swiglu(h, ffn_w1[i], ffn_w3[i], ffn_w2[i]), ln_g[i, 1], ln_b[i, 1])
        if ctx_out:
            xc = _layer_norm(ALPHA * xc + mc[2] * yc, ln_g[i, 0], ln_b[i, 0])
            hc = xc * (1 + mc[4]) + mc[3]
            xc = _layer_norm(ALPHA * xc + mc[5] * _swiglu(hc, ffn_w1[i], ffn_w3[i], ffn_w2[i]), ln_g[i, 1], ln_b[i, 1])
    return x
```

```python
import math
from contextlib import ExitStack

import numpy as np
import concourse.bass as bass
import concourse.mybir as mybir
from concourse.bass_utils import run_bass_kernel_spmd

F32 = mybir.dt.float32
BF16 = mybir.dt.bfloat16
AF = mybir.ActivationFunctionType
ALU = mybir.AluOpType

D = 2048
NDC = 16
S = 8192
NCORE = 8
TOK = 2048
T = 512
NTT = TOK // T
DFF = 5632
NF = DFF // 128
GF = 4
DEPTH = 4
ALPHA = (2 * DEPTH) ** 0.25
LN_EPS = 1e-5
EPOCH = 30000


class Res:
    __slots__ = ("name", "last_write", "readers")

    def __init__(self, name=""):
        self.name = name
        self.last_write = None
        self.readers = []


class Prog:
    def __init__(self, nc, stack, n_dma_sems=8):
        self.nc = nc
        self.stack = stack
        self.eng = {"pe": nc.tensor, "act": nc.scalar, "dve": nc.vector, "pool": nc.gpsimd, "sp": nc.sync}
        self.sem = {}
        self.cnt = {}
        self.nsem = 0
        for e in ("pe", "act", "dve", "pool"):
            self._new_epoch(e)
        self.waited = {e: {} for e in self.eng}
        self.dma_sems = {}
        self.dma_rr = {}
        for q in ("sp", "pool", "act"):
            self.dma_sems[q] = [[self._alloc_sem(f"dma_{q}_{i}"), 0] for i in range(n_dma_sems)]
            self.dma_rr[q] = 0
        self.n_inst = {e: 0 for e in self.eng}

    def _alloc_sem(self, name):
        self.nsem += 1
        return self.stack.enter_context(self.nc.semaphore(f"{name}_{self.nsem}"))

    def _new_epoch(self, e):
        self.sem[e] = self._alloc_sem(f"eng_{e}")
        self.cnt[e] = 0

    def sbuf(self, name, shape, dtype):
        return self.stack.enter_context(self.nc.sbuf_tensor(name, list(shape), dtype))

    def psum(self, name, shape, dtype=F32):
        return self.stack.enter_context(self.nc.psum_tensor(name, list(shape), dtype))

    def _wait(self, e, tok):
        src, sem, val = tok
        key = id(sem)
        if self.waited[e].get(key, 0) >= val:
            return
        self.eng[e].wait_ge(sem, val)
        self.waited[e][key] = val

    def _deps(self, e, reads, writes, same_engine_ok=True):
        toks = []
        for r in reads:
            if r.last_write is not None:
                toks.append(r.last_write)
        for w in writes:
            if w.last_write is not None:
                toks.append(w.last_write)
            toks.extend(w.readers)
        for tok in toks:
            if same_engine_ok and tok[0] == e and e == "pe":
                continue
            self._wait(e, tok)

    def _commit(self, tok, reads, writes):
        for r in reads:
            r.readers.append(tok)
            if len(r.readers) > 48:
                latest = {}
                for t in r.readers:
                    k = (t[0], id(t[1]))
                    if k not in latest or latest[k][2] < t[2]:
                        latest[k] = t
                r.readers = list(latest.values())
        for w in writes:
            w.last_write = tok
            w.readers = []

    def op(self, e, fn, reads=(), writes=()):
        self._deps(e, reads, writes)
        inst = fn(self.eng[e])
        if self.cnt[e] >= EPOCH:
            self._new_epoch(e)
        self.cnt[e] += 1
        inst.then_inc(self.sem[e], 1)
        tok = (e, self.sem[e], self.cnt[e])
        self._commit(tok, reads, writes)
        self.n_inst[e] += 1
        return tok

    def dma(self, q, out, in_, reads=(), writes=(), **kw):
        self._deps(q, reads, writes, same_engine_ok=False)
        pool = self.dma_sems[q]
        slot = pool[self.dma_rr[q] % len(pool)]
        self.dma_rr[q] += 1
        sem, val = slot
        if val > 0:
            self._wait(q, ("dma", sem, val))
        inst = self.eng[q].dma_start(out=out, in_=in_, **kw)
        slot[1] = val + 16
        inst.then_inc(sem, 16)
        tok = ("dma", sem, val + 16)
        self._commit(tok, reads, writes)
        self.n_inst[q] += 1
        return tok

    def finish(self, e="sp"):
        for q, pool in self.dma_sems.items():
            for sem, val in pool:
                if val > 0:
                    self._wait(e, ("dma", sem, val))


class WStream:
    NSTG = 4
    NRING = 14

    def __init__(self, P, nstg=None, nring=None):
        self.P = P
        self.NSTG = nstg or WStream.NSTG
        self.NRING = nring or WStream.NRING
        self.stg = [P.sbuf(f"stg{i}", [128, 2048], F32) for i in range(self.NSTG)]
        self.stg_res = [Res(f"stg{i}") for i in range(self.NSTG)]
        self.ring = [P.sbuf(f"wring{i}", [128, 2048], BF16) for i in range(self.NRING)]
        self.ring_res = [Res(f"wring{i}") for i in range(self.NRING)]
        self.items = []
        self.issued = 0
        self.n_stg = 0
        self.n_ring = 0
        self.loc = {}
        self.stg_owner = [None] * self.NSTG
        self.ring_owner = [None] * self.NRING
        self.released = set()

    def add(self, src_ap, cast=True, width=2048):
        self.items.append((src_ap, cast, width))
        return len(self.items) - 1

    def issue_until(self, idx):
        P = self.P
        idx = min(idx, len(self.items) - 1)
        while self.issued <= idx:
            i = self.issued
            src, cast, wd = self.items[i]
            s = self.n_stg % self.NSTG
            if self.stg_owner[s] is not None and self.stg_owner[s] not in self.released:
                return
            if cast:
                r_ = self.n_ring % self.NRING
                if self.ring_owner[r_] is not None and self.ring_owner[r_] not in self.released:
                    return
            self.n_stg += 1
            self.stg_owner[s] = None if cast else i
            P.dma("sp", self.stg[s][:, 0:wd], src, writes=[self.stg_res[s]])
            if cast:
                r = self.n_ring % self.NRING
                self.n_ring += 1
                self.ring_owner[r] = i
                e = "pool"
                P.op(e, lambda en, s=s, r=r, wd=wd: en.tensor_copy(out=self.ring[r][:, 0:wd], in_=self.stg[s][:, 0:wd]),
                     reads=[self.stg_res[s]], writes=[self.ring_res[r]])
                self.loc[i] = (self.ring[r], self.ring_res[r])
            else:
                self.loc[i] = (self.stg[s], self.stg_res[s])
            self.issued += 1

    def get(self, idx, lookahead=6):
        self.issue_until(idx + lookahead)
        assert idx in self.loc, f"weight item {idx} could not be issued (ring full: missing release?)"
        return self.loc[idx]

    def release(self, idx):
        self.released.add(idx)


V_LNG0, V_LNB0, V_LNG1, V_LNB1, V_MX0, V_MX1, V_MX2, V_MX3 = range(8)
NV = 8


class LayerCtx:
    pass


def _bc(ap_col, n):
    return ap_col.to_broadcast([128, n])


def build_layer(kind, halo_l, halo_r, n_cond=2, extra=None):
    nc = bass.Bass("TRN2", target_bir_lowering=False)
    NT = halo_l + TOK + halo_r
    W = halo_l + T + halo_r
    dram = {}

    def din(name, shape, dt=F32):
        dram[name] = nc.dram_tensor(name, list(shape), dt, kind="ExternalInput").ap()
        return dram[name]

    xin = din("xin", [128, NDC, NT])
    vec = din("vec", [128, NV, NDC])
    modb = din("modb", [128, 96])
    cT = din("cT", [128, NDC, n_cond])
    modw = din("modw", [96, 128, NDC * 128])
    w1r = din("w1r", [NF, 128, 2048])
    w3r = din("w3r", [NF, 128, 2048])
    w2r = din("w2r", [NF, 128, 2048])
    edge = din("edge", [128, 2])
    if kind == "pool":
        poolw = din("poolw", [16, 128, 512])
        pinv = din("pinv", [4, TOK])
    NCV = 80 + 16 * 31
    if kind == "conf":
        cvw1 = din("cvw1", [32, 128, 2048])
        cvw2 = din("cvw2", [16, 128, 2048])
        cvv = din("cvv", [128, NCV])
    if kind == "lin":
        pe_in = din("pe", [128, NDC, TOK])
        prodin = din("prodin", [22, 128, TOK], BF16)
        wor = din("wor", [22, 128, 2048])
    if kind == "none":
        pe_in = din("pe", [128, NDC, TOK])
    if kind == "four":
        fin = din("fin", [128, 2, NDC, TOK], BF16)
        ccsc = din("ccsc", [4, 128, 1024])
        ftw = din("ftw", [16, 128, 2048])
    xout = nc.dram_tensor("xout", [128, NDC, TOK], F32, kind="ExternalOutput").ap()

    with ExitStack() as st:
        P = Prog(nc, st)
        ws = WStream(P)
        L = LayerCtx()
        xs = P.sbuf("xs", [128, NDC, W], F32)
        xs_res = [Res(f"xs{d}") for d in range(NDC)]
        a16 = P.sbuf("a16", [128, NDC, W], BF16)
        a16_res = [Res(f"a16_{d}") for d in range(NDC)]
        g16 = P.sbuf("g16", [128, 2, GF, T], BF16)
        g16_res = [[Res(f"g16_{i}_{j}") for j in range(GF)] for i in range(2)]
        stmp = [P.sbuf(f"stmp{i}", [128, T], F32) for i in range(2)]
        stmp_res = [Res(f"stmp{i}") for i in range(2)]
        sq = [P.sbuf(f"sq{i}", [128, T], F32) for i in range(2)]
        sq_res = [Res(f"sq{i}") for i in range(2)]
        lnt = P.sbuf("lnt", [128, 2, T], F32)
        lnt_res = Res("lnt")
        lnt2 = P.sbuf("lnt2", [128, T], F32)
        onesD = P.sbuf("onesD", [128, 128], F32)
        onesD_res = Res("onesD")
        vraw = P.sbuf("vraw", [128, NV, NDC], F32)
        vraw_res = Res("vraw")
        modb_sb = P.sbuf("modb_sb", [128, 96], F32)
        cs = P.sbuf("cs", [128, NDC, n_cond], F32)
        cs_res = Res("cs")
        msb = P.sbuf("msb", [128, 96, n_cond], F32)
        msb_res = Res("msb")
        dv = P.sbuf("dv", [128, 8, NDC], F32)
        dv_res = Res("dv")
        edge_sb = P.sbuf("edge_sb", [128, 2], F32)
        edge_res = Res("edge")
        ps = [P.psum(f"ps{i}", [128, T]) for i in range(8)]
        ps_res = [Res(f"ps{i}") for i in range(8)]

        mod_items = [ws.add(modw[oc], cast=False) for oc in range(96)]
        plan = []
        if kind == "pool":
            pw_items = [ws.add(poolw[i], width=512) for i in range(16)]
        if kind == "four":
            cc_items = [ws.add(ccsc[i], width=1024) for i in range(4)]
        for tt in range(NTT):
            d_ = {}
            if kind == "conf":
                for oc in range(NDC):
                    d_[("cv", oc)] = ws.add(cvw1[oc])
                    d_[("cg", oc)] = ws.add(cvw1[16 + oc])
                for oc in range(NDC):
                    d_[("c2", oc)] = ws.add(cvw2[oc])
            if kind == "four":
                for oc in range(NDC):
                    d_[("fw", oc)] = ws.add(ftw[oc])
            if kind == "lin":
                for kc in range(22):
                    d_[("wo", kc)] = ws.add(wor[kc])
            for f in range(NF):
                d_[("w1", f)] = ws.add(w1r[f])
                d_[("w3", f)] = ws.add(w3r[f])
                d_[("w2", f)] = ws.add(w2r[f])
            plan.append(d_)

        P.op("pool", lambda e: e.memset(onesD[:], 1.0 / D), writes=[onesD_res])
        P.dma("act", vraw[:], vec, writes=[vraw_res])
        P.dma("act", modb_sb[:], modb, writes=[vraw_res])
        P.dma("act", cs[:], cT, writes=[cs_res])
        P.dma("act", edge_sb[:], edge, writes=[edge_res])
        P.op("act", lambda e: e.activation(out=cs[:], in_=cs[:], func=AF.Silu), reads=[cs_res], writes=[cs_res])
        mps = ps[7]
        mps_res = ps_res[7]
        for oc in range(96):
            wt, wr = ws.get(mod_items[oc], lookahead=2)
            for kc in range(NDC):
                P.op("pe", lambda e, oc=oc, kc=kc, wt=wt: e.matmul(
                    mps[:, oc * n_cond:(oc + 1) * n_cond], lhsT=wt[:, kc * 128:(kc + 1) * 128], rhs=cs[:, kc, :],
                    start=(kc == 0), stop=(kc == NDC - 1)), reads=[wr, cs_res], writes=[mps_res])
            ws.release(mod_items[oc])
        for j in range(n_cond):
            P.op("dve", lambda e, j=j: e.tensor_tensor(
                out=msb[:, :, j], in0=mps[:, 0:96 * n_cond].rearrange("p (o j) -> p o j", j=n_cond)[:, :, j],
                in1=modb_sb[:], op=ALU.add), reads=[mps_res, vraw_res], writes=[msb_res])

        def mvec(k6, dc, j=0):
            return msb[:, k6 * 16 + dc, j:j + 1]

        def mrow(k6):
            return msb[:, k6 * 16:(k6 + 1) * 16, 0]
        P.op("dve", lambda e: e.tensor_scalar(out=dv[:, 0, :], in0=mrow(1), scalar1=1.0, scalar2=None, op0=ALU.add),
             reads=[msb_res], writes=[dv_res])
        if kind == "pool":
            P.op("dve", lambda e: e.tensor_tensor(out=dv[:, 1, :], in0=mrow(2), in1=vraw[:, V_MX1, :], op=ALU.mult),
                 reads=[msb_res, vraw_res], writes=[dv_res])
            P.op("dve", lambda e: e.tensor_tensor(out=dv[:, 2, :], in0=dv[:, 1, :], in1=vraw[:, V_MX0, :], op=ALU.mult),
                 reads=[vraw_res], writes=[dv_res])
        else:
            P.op("dve", lambda e: e.tensor_copy(out=dv[:, 1, :], in_=mrow(2)), reads=[msb_res], writes=[dv_res])
            P.op("dve", lambda e: e.tensor_tensor(out=dv[:, 2, :], in0=dv[:, 1, :], in1=vraw[:, V_MX0, :], op=ALU.mult),
                 reads=[vraw_res], writes=[dv_res])
        P.op("dve", lambda e: e.tensor_scalar(out=dv[:, 3, :], in0=vraw[:, V_LNG0, :], scalar1=ALPHA, scalar2=None, op0=ALU.mult),
             reads=[vraw_res], writes=[dv_res])
        P.op("dve", lambda e: e.tensor_scalar(out=dv[:, 4, :], in0=vraw[:, V_LNB0, :], scalar1=ALPHA, scalar2=None, op0=ALU.mult),
             reads=[vraw_res], writes=[dv_res])
        P.op("dve", lambda e: e.tensor_scalar(out=dv[:, 5, :], in0=mrow(4), scalar1=1.0, scalar2=1.0 / ALPHA, op0=ALU.add, op1=ALU.mult),
             reads=[msb_res], writes=[dv_res])
        VR = [vraw_res, dv_res, msb_res]

        def emit_ln(c0, gcol, bcol, buf=None, bres=None, func=AF.Identity, dst=None):
            buf = xs if buf is None else buf
            bres = xs_res if bres is None else bres
            pm, pq = ps[4], ps[5]
            pmr, pqr = ps_res[4], ps_res[5]
            for dc in range(NDC):
                k = dc % 2
                P.op("act", lambda e, dc=dc, k=k: e.activation(out=sq[k][:], in_=buf[:, dc, c0:c0 + T], func=AF.Square),
                     reads=[bres[dc]], writes=[sq_res[k]])
                P.op("pe", lambda e, dc=dc: e.matmul(pm[:], lhsT=onesD[:], rhs=buf[:, dc, c0:c0 + T],
                                                      start=(dc == 0), stop=(dc == NDC - 1)),
                     reads=[onesD_res, bres[dc]], writes=[pmr])
                P.op("pe", lambda e, dc=dc, k=k: e.matmul(pq[:], lhsT=onesD[:], rhs=sq[k][:],
                                                           start=(dc == 0), stop=(dc == NDC - 1)),
                     reads=[onesD_res, sq_res[k]], writes=[pqr])
            P.op("act", lambda e: e.activation(out=lnt[:, 0, :], in_=pm[:], func=AF.Square), reads=[pmr], writes=[lnt_res])
            P.op("dve", lambda e: e.tensor_tensor(out=lnt[:, 0, :], in0=pq[:], in1=lnt[:, 0, :], op=ALU.subtract),
                 reads=[pqr], writes=[lnt_res])
            P.op("dve", lambda e: e.tensor_scalar(out=lnt[:, 0, :], in0=lnt[:, 0, :], scalar1=LN_EPS, scalar2=None, op0=ALU.add),
                 writes=[lnt_res])
            P.op("act", lambda e: e.activation(out=lnt[:, 1, :], in_=lnt[:, 0, :], func=AF.Sqrt), reads=[lnt_res], writes=[lnt_res])
            P.op("dve", lambda e: e.reciprocal(out=lnt[:, 1, :], in_=lnt[:, 1, :]), reads=[lnt_res], writes=[lnt_res])
            for _it in range(2):
                P.op("dve", lambda e: e.tensor_tensor(out=lnt2[:], in0=lnt[:, 0, :], in1=lnt[:, 1, :], op=ALU.mult), writes=[lnt_res])
                P.op("dve", lambda e: e.tensor_tensor(out=lnt2[:], in0=lnt2[:], in1=lnt[:, 1, :], op=ALU.mult), writes=[lnt_res])
                P.op("dve", lambda e: e.tensor_scalar(out=lnt2[:], in0=lnt2[:], scalar1=-0.5, scalar2=1.5, op0=ALU.mult, op1=ALU.add), writes=[lnt_res])
                P.op("dve", lambda e: e.tensor_tensor(out=lnt[:, 1, :], in0=lnt[:, 1, :], in1=lnt2[:], op=ALU.mult), writes=[lnt_res])
            for dc in range(NDC):
                P.op("dve", lambda e, dc=dc: e.tensor_tensor(out=buf[:, dc, c0:c0 + T], in0=buf[:, dc, c0:c0 + T], in1=pm[:], op=ALU.subtract),
                     reads=[pmr], writes=[bres[dc]])
                P.op("dve", lambda e, dc=dc: e.tensor_tensor(out=buf[:, dc, c0:c0 + T], in0=buf[:, dc, c0:c0 + T], in1=lnt[:, 1, :], op=ALU.mult),
                     reads=[lnt_res], writes=[bres[dc]])
                if dst is None:
                    P.op("act", lambda e, dc=dc: e.activation(out=buf[:, dc, c0:c0 + T], in_=buf[:, dc, c0:c0 + T], func=func,
                                                              scale=gcol(dc), bias=bcol(dc)),
                         reads=VR + [bres[dc]], writes=[bres[dc]])
                else:
                    dap, dres = dst(dc)
                    P.op("act", lambda e, dc=dc, dap=dap: e.activation(out=dap, in_=buf[:, dc, c0:c0 + T], func=func,
                                                                       scale=gcol(dc), bias=bcol(dc)),
                         reads=VR + [bres[dc]], writes=[dres])

        def emit_ffn(tt, c0):
            pl = plan[tt]
            for dc in range(NDC):
                P.op("act", lambda e, dc=dc: e.activation(out=a16[:, dc, 0:T], in_=xs[:, dc, c0:c0 + T], func=AF.Identity,
                                                          scale=dv[:, 5, dc:dc + 1], bias=mvec(3, dc)),
                     reads=VR + [xs_res[dc]], writes=[a16_res[dc]])
            ngrp = NF // GF
            for grp in range(ngrp):
                gb = grp % 2
                for fi in range(GF):
                    f = grp * GF + fi
                    w1t, w1res = ws.get(pl[("w1", f)])
                    w3t, w3res = ws.get(pl[("w3", f)])
                    p1, p3 = ps[(f % 2) * 2], ps[(f % 2) * 2 + 1]
                    p1r, p3r = ps_res[(f % 2) * 2], ps_res[(f % 2) * 2 + 1]
                    for kc in range(NDC):
                        P.op("pe", lambda e, kc=kc, w1t=w1t, p1=p1: e.matmul(
                            p1[:], lhsT=w1t[:, kc * 128:(kc + 1) * 128], rhs=a16[:, kc, 0:T],
                            start=(kc == 0), stop=(kc == NDC - 1)), reads=[w1res, a16_res[kc]], writes=[p1r])
                    for kc in range(NDC):
                        P.op("pe", lambda e, kc=kc, w3t=w3t, p3=p3: e.matmul(
                            p3[:], lhsT=w3t[:, kc * 128:(kc + 1) * 128], rhs=a16[:, kc, 0:T],
                            start=(kc == 0), stop=(kc == NDC - 1)), reads=[w3res, a16_res[kc]], writes=[p3r])
                    ws.release(pl[("w1", f)])
                    ws.release(pl[("w3", f)])
                    k = f % 2
                    P.op("act", lambda e, k=k, p1=p1: e.activation(out=stmp[k][:], in_=p1[:], func=AF.Silu),
                         reads=[p1r], writes=[stmp_res[k]])
                    P.op("dve", lambda e, k=k, p3=p3, gb=gb, fi=fi: e.tensor_tensor(
                        out=g16[:, gb, fi, :], in0=p3[:], in1=stmp[k][:], op=ALU.mult),
                        reads=[p3r, stmp_res[k]], writes=[g16_res[gb][fi]])
                w2 = [ws.get(pl[("w2", grp * GF + fi)]) for fi in range(GF)]
                for dc in range(NDC):
                    po, por = ps[6 + dc % 2], ps_res[6 + dc % 2]
                    for fi in range(GF):
                        P.op("pe", lambda e, fi=fi, dc=dc, po=po: e.matmul(
                            po[:], lhsT=w2[fi][0][:, dc * 128:(dc + 1) * 128], rhs=g16[:, gb, fi, :],
                            start=(fi == 0), stop=(fi == GF - 1)), reads=[w2[fi][1], g16_res[gb][fi]], writes=[por])
                    P.op("dve", lambda e, dc=dc, po=po: e.scalar_tensor_tensor(
                        out=xs[:, dc, c0:c0 + T], in0=po[:], scalar=mvec(5, dc), in1=xs[:, dc, c0:c0 + T],
                        op0=ALU.mult, op1=ALU.add), reads=VR + [por], writes=[xs_res[dc]])
                for fi in range(GF):
                    ws.release(pl[("w2", grp * GF + fi)])

        if kind == "pool":
            pwb = P.sbuf("pwb", [128, 16, 512], BF16)
            pwb_res = Res("pwb")
            for i in range(16):
                wt, wr = ws.get(pw_items[i], lookahead=2)
                P.op("act", lambda e, i=i, wt=wt: e.activation(out=pwb[:, i, :], in_=wt[:, 0:512], func=AF.Identity), reads=[wr], writes=[pwb_res])
                ws.release(pw_items[i])
            hbuf = [P.sbuf(f"hbuf{i}", [128, W], F32) for i in range(2)]
            hbuf_res = [Res(f"hbuf{i}") for i in range(2)]
            sl = [P.sbuf(f"sl{i}", [128, W], F32) for i in range(4)]
            sl_res = [Res(f"sl{i}") for i in range(4)]
            inv_sb = P.sbuf("inv_sb", [128, 4, T], F32)
            inv_res = Res("inv")

        def emit_pool_mixer(tt):
            c0 = halo_l
            P.dma("act", inv_sb[:], pinv[:, tt * T:(tt + 1) * T].partition_broadcast(128), writes=[inv_res])
            for dc in range(NDC):
                g = dc // 4
                hb, hr = hbuf[dc % 2], hbuf_res[dc % 2]
                P.op("act", lambda e, dc=dc, hb=hb: e.activation(out=hb[:], in_=xs[:, dc, :], func=AF.Identity,
                                                               scale=dv[:, 0, dc:dc + 1], bias=mvec(0, dc)),
                     reads=VR + [xs_res[dc]], writes=[hr])
                if tt == 0:
                    P.op("dve", lambda e, hb=hb: e.tensor_scalar(out=hb[:, 0:halo_l], in0=hb[:, 0:halo_l], scalar1=edge_sb[:, 0:1],
                                                                 scalar2=None, op0=ALU.mult), reads=[edge_res], writes=[hr])
                if tt == NTT - 1:
                    P.op("dve", lambda e, hb=hb: e.tensor_scalar(out=hb[:, c0 + T:W], in0=hb[:, c0 + T:W], scalar1=edge_sb[:, 1:2],
                                                                 scalar2=None, op0=ALU.mult), reads=[edge_res], writes=[hr])
                cur, cur_r = hb, hr
                lo, hi = 0, W
                offs = [(1, 0), (1, 1), (2, 2), (4, 4)]
                for lev in range(g + 1):
                    a, b = offs[lev]
                    nlo, nhi = lo + a, hi - b
                    dst, dst_r = sl[lev], sl_res[lev]
                    P.op("pool", lambda e, cur=cur, dst=dst, a=a, b=b, nlo=nlo, nhi=nhi: e.tensor_tensor(
                        out=dst[:, nlo:nhi], in0=cur[:, nlo - a:nhi - a], in1=cur[:, nlo + b:nhi + b], op=ALU.add),
                        reads=[cur_r], writes=[dst_r])
                    cur, cur_r, lo, hi = dst, dst_r, nlo, nhi
                P.op("dve", lambda e, cur=cur, g=g: e.tensor_tensor(out=cur[:, c0:c0 + T], in0=cur[:, c0:c0 + T], in1=inv_sb[:, g, :], op=ALU.mult),
                     reads=[inv_res, cur_r], writes=[cur_r])
                P.op("dve", lambda e, cur=cur, hb=hb, dc=dc: e.tensor_tensor(out=a16[:, dc, 0:T], in0=cur[:, c0:c0 + T], in1=hb[:, c0:c0 + T], op=ALU.subtract),
                     reads=[cur_r, hr], writes=[a16_res[dc]])
            for oc in range(NDC):
                g = oc // 4
                po, por = ps[6 + oc % 2], ps_res[6 + oc % 2]
                for kc in range(4):
                    P.op("pe", lambda e, g=g, kc=kc, oc=oc, po=po: e.matmul(
                        po[:], lhsT=pwb[:, g * 4 + kc, (oc % 4) * 128:(oc % 4 + 1) * 128], rhs=a16[:, g * 4 + kc, 0:T],
                        start=(kc == 0), stop=(kc == 3)), reads=[pwb_res, a16_res[g * 4 + kc]], writes=[por])
                emit_residual(oc, po, por, c0)

        if kind == "conf":
            cvv_sb = P.sbuf("cvv_sb", [128, NCV], F32)
            P.dma("act", cvv_sb[:], cvv, writes=[vraw_res])
            ubuf = P.sbuf("ubuf", [128, NDC, W], F32)
            u_res = [Res(f"u{d}") for d in range(NDC)]
            cacc = [P.sbuf(f"cacc{i}", [128, T], F32) for i in range(2)]
            cacc_res = [Res(f"cacc{i}") for i in range(2)]
            HWD = W // 2
            sgt = [P.sbuf(f"sgt{i}", [128, HWD], F32) for i in range(2)]
            sgt_res = [Res(f"sgt{i}") for i in range(2)]

        def emit_conf_mixer(tt):
            pl = plan[tt]
            c0 = halo_l
            for dc in range(NDC):
                P.op("act", lambda e, dc=dc: e.activation(out=a16[:, dc, :], in_=xs[:, dc, :], func=AF.Identity,
                                                          scale=dv[:, 0, dc:dc + 1], bias=mvec(0, dc)),
                     reads=VR + [xs_res[dc]], writes=[a16_res[dc]])
            for oc in range(NDC):
                wv, wvr = ws.get(pl[("cv", oc)])
                wg, wgr = ws.get(pl[("cg", oc)])
                for half in range(2):
                    ca, cb = half * HWD, (half + 1) * HWD
                    pv, pvr = ps[half * 2], ps_res[half * 2]
                    pg, pgr = ps[half * 2 + 1], ps_res[half * 2 + 1]
                    for kc in range(NDC):
                        P.op("pe", lambda e, kc=kc, pv=pv, ca=ca, cb=cb: e.matmul(
                            pv[:, 0:HWD], lhsT=wv[:, kc * 128:(kc + 1) * 128], rhs=a16[:, kc, ca:cb],
                            start=(kc == 0), stop=(kc == NDC - 1)), reads=[wvr, a16_res[kc]], writes=[pvr])
                    for kc in range(NDC):
                        P.op("pe", lambda e, kc=kc, pg=pg, ca=ca, cb=cb: e.matmul(
                            pg[:, 0:HWD], lhsT=wg[:, kc * 128:(kc + 1) * 128], rhs=a16[:, kc, ca:cb],
                            start=(kc == 0), stop=(kc == NDC - 1)), reads=[wgr, a16_res[kc]], writes=[pgr])
                ws.release(pl[("cv", oc)])
                ws.release(pl[("cg", oc)])
                for half in range(2):
                    ca, cb = half * HWD, (half + 1) * HWD
                    pv, pvr = ps[half * 2], ps_res[half * 2]
                    pg, pgr = ps[half * 2 + 1], ps_res[half * 2 + 1]
                    P.op("act", lambda e, half=half, pg=pg: e.activation(out=sgt[half][:], in_=pg[:, 0:HWD], func=AF.Sigmoid,
                                                                         bias=cvv_sb[:, 16 + oc:17 + oc]),
                         reads=VR + [pgr], writes=[sgt_res[half]])
                    P.op("dve", lambda e, half=half, pv=pv, ca=ca, cb=cb: e.scalar_tensor_tensor(
                        out=ubuf[:, oc, ca:cb], in0=pv[:, 0:HWD], scalar=cvv_sb[:, oc:oc + 1], in1=sgt[half][:],
                        op0=ALU.add, op1=ALU.mult), reads=VR + [pvr, sgt_res[half]], writes=[u_res[oc]])
                if tt == 0:
                    P.op("dve", lambda e: e.tensor_scalar(out=ubuf[:, oc, 0:halo_l], in0=ubuf[:, oc, 0:halo_l], scalar1=edge_sb[:, 0:1],
                                                          scalar2=None, op0=ALU.mult), reads=[edge_res], writes=[u_res[oc]])
                if tt == NTT - 1:
                    P.op("dve", lambda e: e.tensor_scalar(out=ubuf[:, oc, c0 + T:W], in0=ubuf[:, oc, c0 + T:W], scalar1=edge_sb[:, 1:2],
                                                          scalar2=None, op0=ALU.mult), reads=[edge_res], writes=[u_res[oc]])
                ac, acr = cacc[oc % 2], cacc_res[oc % 2]
                dwc = 80 + oc * 31
                P.op("dve", lambda e, ac=ac: e.tensor_scalar(out=ac[:], in0=ubuf[:, oc, 0:T], scalar1=cvv_sb[:, dwc:dwc + 1],
                                                             scalar2=None, op0=ALU.mult), reads=VR + [u_res[oc]], writes=[acr])
                for j in range(1, 31):
                    P.op("dve", lambda e, ac=ac, j=j: e.scalar_tensor_tensor(
                        out=ac[:], in0=ubuf[:, oc, j:j + T], scalar=cvv_sb[:, dwc + j:dwc + j + 1], in1=ac[:],
                        op0=ALU.mult, op1=ALU.add), reads=[u_res[oc]], writes=[acr])
                P.op("act", lambda e, ac=ac: e.activation(out=ubuf[:, oc, 0:T], in_=ac[:], func=AF.Identity,
                                                          bias=cvv_sb[:, 32 + oc:33 + oc]),
                     reads=VR + [acr], writes=[u_res[oc]])
            emit_ln(0, lambda dc: cvv_sb[:, 48 + dc:49 + dc], lambda dc: cvv_sb[:, 64 + dc:65 + dc], buf=ubuf, bres=u_res,
                    func=AF.Silu, dst=lambda dc: (a16[:, dc, 0:T], a16_res[dc]))
            for oc in range(NDC):
                w2t, w2r_ = ws.get(pl[("c2", oc)])
                po, por = ps[6 + oc % 2], ps_res[6 + oc % 2]
                for kc in range(NDC):
                    P.op("pe", lambda e, kc=kc, po=po: e.matmul(
                        po[:], lhsT=w2t[:, kc * 128:(kc + 1) * 128], rhs=a16[:, kc, 0:T],
                        start=(kc == 0), stop=(kc == NDC - 1)), reads=[w2r_, a16_res[kc]], writes=[por])
                ws.release(pl[("c2", oc)])
                emit_residual(oc, po, por, c0)

        if kind == "four":
            ccb = P.sbuf("ccb", [128, 4, 1024], BF16)
            ccb_res = Res("ccb")
            for i in range(4):
                wt, wr = ws.get(cc_items[i], lookahead=2)
                P.op("act", lambda e, i=i, wt=wt: e.activation(out=ccb[:, i, :], in_=wt[:, 0:1024], func=AF.Identity), reads=[wr], writes=[ccb_res])
                ws.release(cc_items[i])
            fab = P.sbuf("fab", [128, 2, NDC, T], BF16)
            fab_res = [[Res(f"fab{i}_{d}") for d in range(NDC)] for i in range(2)]
            corr = P.sbuf("corr", [128, NDC, 2], F32)
            P.op("dve", lambda e: e.memset(corr[:], 0.0), writes=[dv_res])
            fl0 = P.sbuf("fl0", [128, 1], F32)
            P.op("dve", lambda e: e.tensor_scalar(out=fl0[:], in0=edge_sb[:, 0:1], scalar1=-float(S), scalar2=float(S), op0=ALU.mult, op1=ALU.add),
                 reads=[edge_res], writes=[dv_res])
            P.op("dve", lambda e: e.tensor_scalar(out=corr[:, :, 0], in0=mrow(0), scalar1=fl0[:, 0:1], scalar2=None, op0=ALU.mult),
                 reads=[msb_res], writes=[dv_res])

        def emit_four_mixer(tt):
            pl = plan[tt]
            c0 = halo_l
            for i in range(2):
                P.dma("act", fab[:, i], fin[:, i, :, tt * T:(tt + 1) * T], writes=fab_res[i])
            for i in range(2):
                for dc in range(NDC):
                    P.op("act", lambda e, i=i, dc=dc: e.activation(out=fab[:, i, dc, :], in_=fab[:, i, dc, :], func=AF.Identity,
                                                                   scale=dv[:, 0, dc:dc + 1]),
                         reads=VR + [fab_res[i][dc]], writes=[fab_res[i][dc]])
            if tt == 0:
                for dc in range(NDC):
                    P.op("dve", lambda e, dc=dc: e.tensor_tensor(out=fab[:, 0, dc, 0:2], in0=fab[:, 0, dc, 0:2], in1=corr[:, dc, :], op=ALU.add),
                         reads=VR + [fab_res[0][dc]], writes=[fab_res[0][dc]])
            for oc in range(NDC):
                g = oc // 4
                pj, pjr = ps[oc % 4], ps_res[oc % 4]
                n = 0
                for i in range(2):
                    for kc in range(4):
                        P.op("pe", lambda e, i=i, kc=kc, pj=pj, n=n: e.matmul(
                            pj[:], lhsT=ccb[:, kc, i * 512 + (oc % 4) * 128:i * 512 + (oc % 4 + 1) * 128], rhs=fab[:, i, 4 * g + kc, :],
                            start=(n == 0), stop=(n == 7)), reads=[ccb_res, fab_res[i][4 * g + kc]], writes=[pjr])
                        n += 1
                P.op("act", lambda e, pj=pj: e.activation(out=a16[:, oc, 0:T], in_=pj[:], func=AF.Identity), reads=[pjr], writes=[a16_res[oc]])
            for oc in range(NDC):
                w2t, w2r_ = ws.get(pl[("fw", oc)])
                po, por = ps[6 + oc % 2], ps_res[6 + oc % 2]
                for kc in range(NDC):
                    P.op("pe", lambda e, kc=kc, po=po: e.matmul(
                        po[:], lhsT=w2t[:, kc * 128:(kc + 1) * 128], rhs=a16[:, kc, 0:T],
                        start=(kc == 0), stop=(kc == NDC - 1)), reads=[w2r_, a16_res[kc]], writes=[por])
                ws.release(pl[("fw", oc)])
                emit_residual(oc, po, por, c0)

        if kind == "none":
            pebuf = P.sbuf("pebuf", [128, NDC, T], F32)
            pe_res = Res("pe")

        def emit_none_mixer(tt):
            P.dma("act", pebuf[:], pe_in[:, :, tt * T:(tt + 1) * T], writes=[pe_res])
            for dc in range(NDC):
                P.op("dve", lambda e, dc=dc: e.tensor_tensor(out=xs[:, dc, :], in0=xs[:, dc, :], in1=pebuf[:, dc, :], op=ALU.add),
                     reads=[pe_res], writes=[xs_res[dc]])
                P.op("act", lambda e, dc=dc: e.activation(out=xs[:, dc, :], in_=xs[:, dc, :], func=AF.Identity, scale=ALPHA),
                     reads=[xs_res[dc]], writes=[xs_res[dc]])

        if kind == "lin":
            prt = P.sbuf("prt", [128, 22, T], BF16)
            prt_res = Res("prt")
            petmp = [P.sbuf(f"petmp{i}", [128, T], F32) for i in range(2)]
            petmp_res = [Res(f"petmp{i}") for i in range(2)]

        def emit_lin_mixer(tt):
            pl = plan[tt]
            c0 = halo_l
            P.dma("act", prt[:], prodin[:, :, tt * T:(tt + 1) * T].rearrange("k p t -> p k t"), writes=[prt_res])
            for dc in range(NDC):
                k = dc % 2
                P.dma("act", petmp[k][:], pe_in[:, dc, tt * T:(tt + 1) * T], writes=[petmp_res[k]])
                P.op("dve", lambda e, dc=dc, k=k: e.tensor_tensor(out=xs[:, dc, :], in0=xs[:, dc, :], in1=petmp[k][:], op=ALU.add),
                     reads=[petmp_res[k]], writes=[xs_res[dc]])
            for half in range(2):
                wts = [ws.get(pl[("wo", half * 11 + i)]) for i in range(11)]
                for dc in range(NDC):
                    po, por = ps[6 + dc % 2], ps_res[6 + dc % 2]
                    for i in range(11):
                        P.op("pe", lambda e, i=i, dc=dc, po=po: e.matmul(
                            po[:], lhsT=wts[i][0][:, dc * 128:(dc + 1) * 128], rhs=prt[:, half * 11 + i, :],
                            start=(i == 0), stop=(i == 10)), reads=[wts[i][1], prt_res], writes=[por])
                    if half == 0:
                        emit_residual(dc, po, por, c0)
                    else:
                        P.op("dve", lambda e, dc=dc, po=po: e.scalar_tensor_tensor(
                            out=xs[:, dc, c0:c0 + T], in0=po[:], scalar=dv[:, 1, dc:dc + 1], in1=xs[:, dc, c0:c0 + T],
                            op0=ALU.mult, op1=ALU.add), reads=VR + [por], writes=[xs_res[dc]])
                for i in range(11):
                    ws.release(pl[("wo", half * 11 + i)])

        def emit_residual(dc, po, por, c0):
            P.op("act", lambda e: e.activation(out=xs[:, dc, c0:c0 + T], in_=xs[:, dc, c0:c0 + T], func=AF.Identity,
                                               scale=ALPHA, bias=dv[:, 2, dc:dc + 1]),
                 reads=VR + [xs_res[dc]], writes=[xs_res[dc]])
            P.op("dve", lambda e: e.scalar_tensor_tensor(out=xs[:, dc, c0:c0 + T], in0=po[:], scalar=dv[:, 1, dc:dc + 1],
                                                         in1=xs[:, dc, c0:c0 + T], op0=ALU.mult, op1=ALU.add),
                 reads=VR + [por], writes=[xs_res[dc]])

        for tt in range(NTT):
            c0 = halo_l
            P.dma("act", xs[:], xin[:, :, tt * T:tt * T + W], writes=xs_res)
            if kind == "pool":
                emit_pool_mixer(tt)
            elif kind == "conf":
                emit_conf_mixer(tt)
            elif kind == "four":
                emit_four_mixer(tt)
            elif kind == "none":
                emit_none_mixer(tt)
            elif kind == "lin":
                emit_lin_mixer(tt)
            emit_ln(c0, lambda dc: dv[:, 3, dc:dc + 1], lambda dc: dv[:, 4, dc:dc + 1])
            emit_ffn(tt, c0)

            def store(dc, tt=tt):
                pass
            emit_ln(c0, lambda dc: vraw[:, V_LNG1, dc:dc + 1], lambda dc: vraw[:, V_LNB1, dc:dc + 1])
            P.dma("pool", xout[:, :, tt * T:(tt + 1) * T], xs[:, :, c0:c0 + T], reads=xs_res)
        P.finish("sp")
        nc._stats = dict(P.n_inst)
    return nc


def col_table(v):
    return np.ascontiguousarray(np.asarray(v, np.float32).reshape(NDC, 128).T)


def to_xT(xb, q, halo_l, halo_r):
    t0 = q * TOK - halo_l
    t1 = (q + 1) * TOK + halo_r
    out = np.zeros((128, NDC, t1 - t0), np.float32)
    a, b = max(t0, 0), min(t1, S)
    blk = xb[a:b].reshape(b - a, NDC, 128).transpose(2, 1, 0)
    out[:, :, a - t0:b - t0] = blk
    return out


def from_xT(xt):
    return np.ascontiguousarray(xt.transpose(2, 1, 0).reshape(xt.shape[2], D))


def w_colblocks(w, nblk):
    K = w.shape[0] // 128
    return np.ascontiguousarray(w.reshape(K, 128, nblk, 128).transpose(2, 1, 0, 3).reshape(nblk, 128, K * 128))


def ffn_layouts(inputs, l):
    w1r = w_colblocks(np.asarray(inputs["ffn_w1"][l]), NF)
    w3r = w_colblocks(np.asarray(inputs["ffn_w3"][l]), NF)
    w2r = np.ascontiguousarray(np.asarray(inputs["ffn_w2"][l]).reshape(NF, 128, D))
    return w1r, w3r, w2r


def common_maps(inputs, l, mixvecs):
    vec = np.zeros((128, NV, NDC), np.float32)
    vec[:, V_LNG0] = col_table(inputs["ln_g"][l, 0])
    vec[:, V_LNB0] = col_table(inputs["ln_b"][l, 0])
    vec[:, V_LNG1] = col_table(inputs["ln_g"][l, 1])
    vec[:, V_LNB1] = col_table(inputs["ln_b"][l, 1])
    for i, v in enumerate(mixvecs):
        vec[:, V_MX0 + i] = col_table(v)
    modb = np.ascontiguousarray(np.asarray(inputs["mod_b"][l], np.float32).reshape(96, 128).T)
    modw = w_colblocks(np.asarray(inputs["mod_w"][l]), 96)
    w1r, w3r, w2r = ffn_layouts(inputs, l)
    return dict(vec=vec, modb=modb, modw=modw, w1r=w1r, w3r=w3r, w2r=w2r)


def cond_T(inputs, b):
    c = np.stack([np.asarray(inputs["c"][b], np.float32), np.asarray(inputs["c_ctx"], np.float32)], axis=-1)
    return np.ascontiguousarray(c.reshape(NDC, 128, 2).transpose(1, 0, 2))


def edge_flags(q):
    e = np.ones((128, 2), np.float32)
    if q == 0:
        e[:, 0] = 0.0
    if q == 3:
        e[:, 1] = 0.0
    return e


_NC_CACHE = {}


def run_pool_layer(inputs, l, x):
    HL = HR = 8
    key = ("pool",)
    if key not in _NC_CACHE:
        _NC_CACHE[key] = build_layer("pool", HL, HR)
    nc = _NC_CACHE[key]
    cm = common_maps(inputs, l, [inputs["pool_b"][0], inputs["pool_scale"][0]])
    pw = np.asarray(inputs["pool_w"][0], np.float32)
    poolw = np.ascontiguousarray(pw.reshape(4, 4, 128, 512).reshape(16, 128, 512))
    in_maps = []
    for core in range(NCORE):
        b, q = divmod(core, 4)
        t = np.arange(q * TOK, (q + 1) * TOK)
        pinv = np.zeros((4, TOK), np.float32)
        for g, win in enumerate((2, 4, 8, 16)):
            lo = np.clip(t - win // 2, 0, S)
            hi = np.clip(t - win // 2 + win, 0, S)
            pinv[g] = 1.0 / (hi - lo).astype(np.float32)
        m = dict(cm)
        m.update(xin=to_xT(x[b], q, HL, HR), cT=cond_T(inputs, b), edge=edge_flags(q), poolw=poolw, pinv=pinv)
        in_maps.append(m)
    res = run_bass_kernel_spmd(nc, in_maps, core_ids=list(range(NCORE)))
    out = np.empty_like(x)
    for core in range(NCORE):
        b, q = divmod(core, 4)
        out[b, q * TOK:(q + 1) * TOK] = from_xT(res.results[core]["xout"])
    return out


def run_conf_layer(inputs, l, x):
    HL = HR = 15
    key = ("conf",)
    if key not in _NC_CACHE:
        _NC_CACHE[key] = build_layer("conf", HL, HR)
    nc = _NC_CACHE[key]
    cm = common_maps(inputs, l, [inputs["cv_b2"][0]])
    cvw1 = w_colblocks(np.asarray(inputs["cv_w1"][0]), 32)
    cvw2 = w_colblocks(np.asarray(inputs["cv_w2"][0]), 16)
    b1 = np.asarray(inputs["cv_b1"][0], np.float32)
    cvv = np.concatenate([
        b1.reshape(32, 128).T,
        col_table(inputs["cv_dwb"][0]), col_table(inputs["cv_ln_g"][0]), col_table(inputs["cv_ln_b"][0]),
        np.asarray(inputs["cv_dw"][0], np.float32).reshape(31, NDC, 128).transpose(2, 1, 0).reshape(128, NDC * 31),
    ], axis=1).astype(np.float32)
    cvv = np.ascontiguousarray(cvv)
    in_maps = []
    for core in range(NCORE):
        b, q = divmod(core, 4)
        m = dict(cm)
        m.update(xin=to_xT(x[b], q, HL, HR), cT=cond_T(inputs, b), edge=edge_flags(q), cvw1=cvw1, cvw2=cvw2, cvv=cvv)
        in_maps.append(m)
    res = run_bass_kernel_spmd(nc, in_maps, core_ids=list(range(NCORE)))
    out = np.empty_like(x)
    for core in range(NCORE):
        b, q = divmod(core, 4)
        out[b, q * TOK:(q + 1) * TOK] = from_xT(res.results[core]["xout"])
    return out


def build_seqdft():
    nc = bass.Bass("TRN2", target_bir_lowering=False)
    xtok = nc.dram_tensor("xtok", [64, 128, D], F32, kind="ExternalInput").ap()
    tab = nc.dram_tensor("tab", [64, 128, 4096], BF16, kind="ExternalInput").ap()
    f12 = nc.dram_tensor("f12", [128, 2, NDC, TOK], BF16, kind="ExternalOutput").ap()
    NB = 6
    with ExitStack() as st:
        P = Prog(nc, st)
        ws = WStream(P)
        bring = [P.sbuf(f"bring{i}", [128, 2048], BF16) for i in range(NB)]
        bring_res = [Res(f"bring{i}") for i in range(NB)]
        obuf = [P.sbuf(f"obuf{i}", [128, 2, TOK], BF16) for i in range(2)]
        obuf_res = [Res(f"obuf{i}") for i in range(2)]
        ps = [P.psum(f"ps{i}", [128, 512]) for i in range(8)]
        ps_res = [Res(f"ps{i}") for i in range(8)]
        seq = [(cp, trig, sc) for cp in range(8) for trig in range(2) for sc in range(64)]
        a_items = [ws.add(xtok[sc][:, cp * 256:(cp + 1) * 256], width=256) for (cp, trig, sc) in seq]
        nb_issued = [0]

        def issue_b(upto):
            while nb_issued[0] <= min(upto, len(seq) - 1):
                i = nb_issued[0]
                cp, trig, sc = seq[i]
                P.dma("sp", bring[i % NB][:], tab[sc][:, trig * 2048:(trig + 1) * 2048], writes=[bring_res[i % NB]])
                nb_issued[0] += 1

        for i, (cp, trig, sc) in enumerate(seq):
            issue_b(i + 3)
            at, ar = ws.get(a_items[i], lookahead=4)
            bt, br = bring[i % NB], bring_res[i % NB]
            for c2 in range(2):
                for kt in range(4):
                    b_ = c2 * 4 + kt
                    P.op("pe", lambda e, c2=c2, kt=kt, b_=b_, at=at, bt=bt: e.matmul(
                        ps[b_][:], lhsT=at[:, c2 * 128:(c2 + 1) * 128], rhs=bt[:, kt * 512:(kt + 1) * 512],
                        start=(sc == 0), stop=(sc == 63)), reads=[ar, br], writes=[ps_res[b_]])
            ws.release(a_items[i])
            if sc == 63:
                ob, obr = obuf[(cp * 2 + trig) % 2], obuf_res[(cp * 2 + trig) % 2]
                for c2 in range(2):
                    for kt in range(4):
                        b_ = c2 * 4 + kt
                        eng = "act" if kt % 2 == 0 else "dve"
                        if eng == "act":
                            P.op("act", lambda e, c2=c2, kt=kt, b_=b_, ob=ob: e.activation(out=ob[:, c2, kt * 512:(kt + 1) * 512], in_=ps[b_][:], func=AF.Identity),
                                 reads=[ps_res[b_]], writes=[obr])
                        else:
                            P.op("dve", lambda e, c2=c2, kt=kt, b_=b_, ob=ob: e.tensor_copy(out=ob[:, c2, kt * 512:(kt + 1) * 512], in_=ps[b_][:]),
                                 reads=[ps_res[b_]], writes=[obr])
                P.dma("pool", f12[:, trig, cp * 2:cp * 2 + 2, :], ob[:], reads=[obr])
        P.finish("sp")
        nc._stats = dict(P.n_inst)
    return nc


def _bf16(a):
    import ml_dtypes
    return np.asarray(a, np.float32).astype(ml_dtypes.bfloat16)


def run_four_layer(inputs, l, x):
    if ("seqdft",) not in _NC_CACHE:
        _NC_CACHE[("seqdft",)] = build_seqdft()
    if ("four",) not in _NC_CACHE:
        _NC_CACHE[("four",)] = build_layer("four", 0, 0)
    s_idx = np.arange(S, dtype=np.int64)[:, None]
    in_maps = []
    for core in range(NCORE):
        b, q = divmod(core, 4)
        k_idx = np.arange(q * TOK, (q + 1) * TOK, dtype=np.int64)[None, :]
        ang = (2.0 * np.pi / S) * ((s_idx * k_idx) % S).astype(np.float64)
        tab = np.concatenate([np.cos(ang), np.sin(ang)], axis=1)
        in_maps.append(dict(xtok=np.ascontiguousarray(x[b].reshape(64, 128, D)), tab=_bf16(tab).reshape(64, 128, 4096)))
    r1 = run_bass_kernel_spmd(_NC_CACHE[("seqdft",)], in_maps, core_ids=list(range(NCORE)))
    cm = common_maps(inputs, l, [inputs["ft_b"][0]])
    c_idx = np.arange(512, dtype=np.int64)
    angc = (2.0 * np.pi / 512) * ((c_idx[:, None] * c_idx[None, :]) % 512).astype(np.float64)
    ccsc = (np.concatenate([np.cos(angc), -np.sin(angc)], axis=1) / 2048.0).astype(np.float32).reshape(4, 128, 1024)
    ftw = w_colblocks(np.asarray(inputs["ft_w"][0]), 16)
    in_maps = []
    for core in range(NCORE):
        b, q = divmod(core, 4)
        m = dict(cm)
        m.update(xin=to_xT(x[b], q, 0, 0), cT=cond_T(inputs, b), edge=edge_flags(q), fin=r1.results[core]["f12"], ccsc=ccsc, ftw=ftw)
        in_maps.append(m)
    res = run_bass_kernel_spmd(_NC_CACHE[("four",)], in_maps, core_ids=list(range(NCORE)))
    out = np.empty_like(x)
    for core in range(NCORE):
        b, q = divmod(core, 4)
        out[b, q * TOK:(q + 1) * TOK] = from_xT(res.results[core]["xout"])
    return out


def _pos_embed_table():
    rows, cols, dim = S // 64, 64, D
    quarter = dim // 4
    omega = (1.0 / (10000.0 ** (np.arange(quarter, dtype=np.float32) / np.float32(quarter)))).astype(np.float32)
    ar = np.arange(rows, dtype=np.float32)[:, None] * omega[None]
    ac = np.arange(cols, dtype=np.float32)[:, None] * omega[None]
    er = np.concatenate([np.sin(ar), np.cos(ar)], axis=-1)
    ec = np.concatenate([np.sin(ac), np.cos(ac)], axis=-1)
    pe = np.concatenate([np.broadcast_to(er[:, None, :], (rows, cols, dim // 2)),
                         np.broadcast_to(ec[None, :, :], (rows, cols, dim // 2))], axis=-1)
    return pe.reshape(rows * cols, dim).astype(np.float32)


def run_layer0_partial(inputs, x):
    key = ("none",)
    if key not in _NC_CACHE:
        _NC_CACHE[key] = build_layer("none", 0, 0)
    nc = _NC_CACHE[key]
    cm = common_maps(inputs, 0, [])
    pe = _pos_embed_table()
    in_maps = []
    for core in range(NCORE):
        b, q = divmod(core, 4)
        m = dict(cm)
        m.update(xin=to_xT(x[b], q, 0, 0), cT=cond_T(inputs, b), edge=edge_flags(q), pe=to_xT(pe, q, 0, 0))
        in_maps.append(m)
    res = run_bass_kernel_spmd(nc, in_maps, core_ids=list(range(NCORE)))
    out = np.empty_like(x)
    for core in range(NCORE):
        b, q = divmod(core, 4)
        out[b, q * TOK:(q + 1) * TOK] = from_xT(res.results[core]["xout"])
    return out


def kernel(**inputs):
    inputs = {k: np.asarray(v) for k, v in inputs.items()}
    x = np.ascontiguousarray(inputs["x"], dtype=np.float32)
    x = run_rg_layer(inputs, x)
    x = run_pool_layer(inputs, 1, x)
    x = run_conf_layer(inputs, 2, x)
    x = run_four_layer(inputs, 3, x)
    return x.astype(np.float32)


RSUB = 88
NSUB = 32
RG_HL, RG_HR = 1, 2
CTXL = 256


def build_rg(phase):
    nc = bass.Bass("TRN2", target_bir_lowering=False)
    NTW = RG_HL + TOK + RG_HR
    d = {}

    def din(name, shape, dt=F32):
        d[name] = nc.dram_tensor(name, list(shape), dt, kind="ExternalInput").ap()
        return d[name]

    xin = din("xin", [128, NDC, NTW])
    pein = din("pein", [128, NDC, NTW])
    ctxin = din("ctxin", [128, NDC, CTXL])
    modb = din("modb", [128, 96])
    cT = din("cT", [128, NDC, 2])
    modw = din("modw", [32, 128, NDC * 128])
    edge = din("edge", [128, 2])
    rv = din("rv", [128, 11, NSUB])
    wxr = din("wxr", [NSUB, 128, NDC * RSUB])
    gwr = din("gwr", [32, 128, 704])
    if phase == "B":
        wgr = din("wgr", [NSUB, 128, NDC * RSUB])
        summ = din("summ", [NCORE, 128, 4, NSUB])
        ctxs = din("ctxs", [128, 2, NSUB])
        mfb = din("mfb", [128, 2, NCORE])
        prod = nc.dram_tensor("prod", [NSUB, RSUB, TOK], BF16, kind="ExternalOutput").ap()
    else:
        sout = nc.dram_tensor("sout", [128, 6, NSUB], F32, kind="ExternalOutput").ap()

    with ExitStack() as st:
        P = Prog(nc, st)
        ws = WStream(P, nstg=3, nring=5)
        a16f = P.sbuf("a16f", [128, NDC, NTW], BF16)
        a16f_res = [Res(f"a16f{i}") for i in range(NDC)]
        xbpre = P.sbuf("xbpre", [128, NTW], F32)
        xbpre_res = Res("xbpre")
        xb = [P.sbuf(f"xb{i}", [128, TOK], F32) for i in range(2)]
        xb_res = [Res(f"xb{i}") for i in range(2)]
        xb16 = [P.sbuf(f"xb16_{i}", [128, TOK], BF16) for i in range(2)]
        xb16_res = [Res(f"xb16_{i}") for i in range(2)]
        abuf = P.sbuf("abuf", [128, NTW], F32)
        bbuf = P.sbuf("bbuf", [128, NTW], F32)
        tbuf = P.sbuf("tbuf", [128, NTW], F32)
        ab_res, bb_res, tb_res = Res("abuf"), Res("bbuf"), Res("tbuf")
        ybuf = [P.sbuf(f"ybuf{i}", [128, NTW], F32) for i in range(2)]
        yb_res = [Res(f"ybuf{i}") for i in range(2)]
        xt, xt_res = [abuf, bbuf], [ab_res, bb_res]
        pt, pt_res = [tbuf, ybuf[0]], [tb_res, yb_res[0]]
        rv_sb = P.sbuf("rv_sb", [128, 11, NSUB], F32)
        cp_sb = P.sbuf("cp_sb", [128, 2, NSUB], F32)
        rv_res = Res("rv")
        modb_sb = P.sbuf("modb_sb", [128, 96], F32)
        cs = P.sbuf("cs", [128, NDC, 2], F32)
        cs_res = Res("cs")
        msb = P.sbuf("msb", [128, 32, 2], F32)
        msb_res = Res("msb")
        a1 = P.sbuf("a1", [128, NDC, 2], F32)
        edge_sb = P.sbuf("edge_sb", [128, 2], F32)
        edge_res = Res("edge")
        rsum = P.sbuf("rsum", [128, 1], F32)
        rsum_res = Res("rsum")
        if phase == "A":
            so_sb = P.sbuf("so_sb", [128, 6, NSUB], F32)
            so_res = Res("so")
        else:
            summ_sb = P.sbuf("summ_sb", [128, NCORE, 4, NSUB], F32)
            carry = P.sbuf("carry", [128, 2, NSUB], F32)
            mfb_sb = P.sbuf("mfb_sb", [128, 2, NCORE], F32)
            ctmp = P.sbuf("ctmp", [128, NSUB], F32)
            carry_res = Res("carry")
            gl = [P.sbuf(f"gl{i}", [128, T], F32) for i in range(2)]
            gl_res = [Res(f"gl{i}") for i in range(2)]
            pr16 = [P.sbuf(f"pr16_{i}", [128, TOK], BF16) for i in range(2)]
            pr16_res = [Res(f"pr16_{i}") for i in range(2)]
        ps = [P.psum(f"ps{i}", [128, 512]) for i in range(8)]
        ps_res = [Res(f"ps{i}") for i in range(8)]

        mod_items = [ws.add(modw[oc], cast=False) for oc in range(32)]
        passes = ["ctx", "lat"] if phase == "A" else ["lat"]
        plan = {}
        for pss in passes:
            for n in range(16):
                for s in range(2):
                    plan[(pss, "wx", 2 * n + s)] = ws.add(wxr[2 * n + s], width=NDC * RSUB)
                for dd in range(2):
                    plan[(pss, "gw", dd, n)] = ws.add(gwr[dd * 16 + n], width=704)
                if phase == "B":
                    for s in range(2):
                        plan[(pss, "wg", 2 * n + s)] = ws.add(wgr[2 * n + s], width=NDC * RSUB)

        P.dma("act", rv_sb[:], rv, writes=[rv_res])
        P.dma("act", modb_sb[:], modb, writes=[rv_res])
        P.dma("act", cs[:], cT, writes=[cs_res])
        P.dma("act", edge_sb[:], edge, writes=[edge_res])
        P.op("act", lambda e: e.activation(out=cs[:], in_=cs[:], func=AF.Silu), reads=[cs_res], writes=[cs_res])
        P.op("act", lambda e: e.activation(out=cp_sb[:], in_=rv_sb[:, 9:11, :], func=AF.Sigmoid), reads=[rv_res], writes=[rv_res])
        P.op("act", lambda e: e.activation(out=cp_sb[:], in_=cp_sb[:], func=AF.Ln), reads=[rv_res], writes=[rv_res])
        P.op("dve", lambda e: e.tensor_scalar(out=cp_sb[:], in0=cp_sb[:], scalar1=8.0, scalar2=None, op0=ALU.mult), reads=[rv_res], writes=[rv_res])
        mps, mps_res = ps[7], ps_res[7]
        for oc in range(32):
            wt, wr = ws.get(mod_items[oc], lookahead=1)
            for kc in range(NDC):
                P.op("pe", lambda e, oc=oc, kc=kc, wt=wt: e.matmul(
                    mps[:, oc * 2:(oc + 1) * 2], lhsT=wt[:, kc * 128:(kc + 1) * 128], rhs=cs[:, kc, :],
                    start=(kc == 0), stop=(kc == NDC - 1)), reads=[wr, cs_res], writes=[mps_res])
            ws.release(mod_items[oc])
        for j in range(2):
            P.op("dve", lambda e, j=j: e.tensor_tensor(
                out=msb[:, :, j], in0=mps[:, 0:64].rearrange("p (o j) -> p o j", j=2)[:, :, j],
                in1=modb_sb[:, 0:32], op=ALU.add), reads=[mps_res, rv_res], writes=[msb_res])
        P.op("dve", lambda e: e.tensor_scalar(out=a1[:], in0=msb[:, 16:32, :], scalar1=1.0, scalar2=None, op0=ALU.add),
             reads=[msb_res], writes=[msb_res])
        if phase == "A":
            P.op("pool", lambda e: e.memset(so_sb[:], 0.0), writes=[so_res])
        else:
            P.dma("act", summ_sb[:], summ.rearrange("r p a s -> p r a s"), writes=[carry_res])
            P.dma("act", carry[:], ctxs, writes=[carry_res])
            P.dma("act", mfb_sb[:], mfb, writes=[carry_res])
            for dd in range(2):
                order = range(NCORE) if dd == 0 else range(NCORE - 1, -1, -1)
                for r in order:
                    A_r = summ_sb[:, r, 2 * dd, :]
                    B_r = summ_sb[:, r, 2 * dd + 1, :]
                    mk = mfb_sb[:, dd, r:r + 1]
                    P.op("dve", lambda e, A_r=A_r, mk=mk: e.tensor_scalar(out=ctmp[:], in0=A_r, scalar1=-1.0, scalar2=mk, op0=ALU.add, op1=ALU.mult),
                         reads=[carry_res], writes=[carry_res])
                    P.op("dve", lambda e: e.tensor_scalar(out=ctmp[:], in0=ctmp[:], scalar1=1.0, scalar2=None, op0=ALU.add),
                         reads=[carry_res], writes=[carry_res])
                    P.op("dve", lambda e, dd=dd: e.tensor_tensor(out=carry[:, dd, :], in0=carry[:, dd, :], in1=ctmp[:], op=ALU.mult),
                         reads=[carry_res], writes=[carry_res])
                    P.op("dve", lambda e, B_r=B_r, mk=mk: e.tensor_scalar(out=ctmp[:], in0=B_r, scalar1=mk, scalar2=None, op0=ALU.mult),
                         reads=[carry_res], writes=[carry_res])
                    P.op("dve", lambda e, dd=dd: e.tensor_tensor(out=carry[:, dd, :], in0=carry[:, dd, :], in1=ctmp[:], op=ALU.add),
                         reads=[carry_res], writes=[carry_res])

        def run_pass(pss):
            ctx = pss == "ctx"
            NTK = CTXL if ctx else TOK
            NW = RG_HL + NTK + RG_HR
            j = 1 if ctx else 0
            for dc in range(NDC):
                k = dc % 2
                if ctx:
                    P.dma("act", xt[k][:, RG_HL:RG_HL + NTK], ctxin[:, dc, :], writes=[xt_res[k]])
                    src = xt[k][:, RG_HL:RG_HL + NTK]
                    P.op("pool", lambda e, dc=dc: e.memset(a16f[:, dc, 0:NW], 0.0), writes=[a16f_res[dc]])
                    P.op("act", lambda e, dc=dc, src=src: e.activation(out=a16f[:, dc, RG_HL:RG_HL + NTK], in_=src, func=AF.Identity,
                                                                       scale=a1[:, dc, j:j + 1], bias=msb[:, dc, j:j + 1]),
                         reads=[xt_res[k], msb_res], writes=[a16f_res[dc]])
                else:
                    P.dma("act", xt[k][:], xin[:, dc, :], writes=[xt_res[k]])
                    P.dma("act", pt[k][:], pein[:, dc, :], writes=[pt_res[k]])
                    P.op("dve", lambda e, k=k: e.tensor_tensor(out=xt[k][:], in0=xt[k][:], in1=pt[k][:], op=ALU.add),
                         reads=[pt_res[k]], writes=[xt_res[k]])
                    P.op("act", lambda e, dc=dc, k=k: e.activation(out=a16f[:, dc, :], in_=xt[k][:], func=AF.Identity,
                                                                   scale=a1[:, dc, j:j + 1], bias=msb[:, dc, j:j + 1]),
                         reads=[xt_res[k], msb_res], writes=[a16f_res[dc]])
                    P.op("dve", lambda e, dc=dc: e.tensor_scalar(out=a16f[:, dc, 0:RG_HL], in0=a16f[:, dc, 0:RG_HL], scalar1=edge_sb[:, 0:1],
                                                                 scalar2=None, op0=ALU.mult), reads=[edge_res], writes=[a16f_res[dc]])
                    P.op("dve", lambda e, dc=dc: e.tensor_scalar(out=a16f[:, dc, RG_HL + NTK:NW], in0=a16f[:, dc, RG_HL + NTK:NW], scalar1=edge_sb[:, 1:2],
                                                                 scalar2=None, op0=ALU.mult), reads=[edge_res], writes=[a16f_res[dc]])
            coltiles = [(c, min(c + 512, NW)) for c in range(0, NW, 512)]
            ctiles = [(c, min(c + 512, NTK)) for c in range(0, NTK, 512)]
            Rr = slice(0, RSUB)
            for n in range(16):
                for s in range(2):
                    sidx = 2 * n + s
                    wx, wxres = ws.get(plan[(pss, "wx", sidx)], lookahead=2)
                    for ci, (ca, cb) in enumerate(coltiles):
                        pp, ppr = ps[ci % 2], ps_res[ci % 2]
                        for kc in range(NDC):
                            P.op("pe", lambda e, kc=kc, pp=pp, ca=ca, cb=cb, wx=wx: e.matmul(
                                pp[Rr, 0:cb - ca], lhsT=wx[:, kc * RSUB:(kc + 1) * RSUB], rhs=a16f[:, kc, ca:cb],
                                start=(kc == 0), stop=(kc == NDC - 1)), reads=[wxres, a16f_res[kc]], writes=[ppr])
                        P.op("act", lambda e, pp=pp, ca=ca, cb=cb: e.activation(out=xbpre[Rr, ca:cb], in_=pp[Rr, 0:cb - ca], func=AF.Identity),
                             reads=[ppr], writes=[xbpre_res])
                    ws.release(plan[(pss, "wx", sidx)])
                    P.op("dve", lambda e, s=s, sidx=sidx: e.tensor_scalar(
                        out=xb[s][Rr, 0:NTK], in0=xbpre[Rr, 0:NTK], scalar1=rv_sb[Rr, 0, sidx:sidx + 1], scalar2=rv_sb[Rr, 4, sidx:sidx + 1],
                        op0=ALU.mult, op1=ALU.add), reads=[xbpre_res, rv_res], writes=[xb_res[s]])
                    for jj in range(1, 4):
                        P.op("dve", lambda e, s=s, sidx=sidx, jj=jj: e.scalar_tensor_tensor(
                            out=xb[s][Rr, 0:NTK], in0=xbpre[Rr, jj:jj + NTK], scalar=rv_sb[Rr, jj, sidx:sidx + 1], in1=xb[s][Rr, 0:NTK],
                            op0=ALU.mult, op1=ALU.add), reads=[xbpre_res, rv_res], writes=[xb_res[s]])
                    P.op("act", lambda e, s=s: e.activation(out=xb16[s][Rr, 0:NTK], in_=xb[s][Rr, 0:NTK], func=AF.Identity),
                         reads=[xb_res[s]], writes=[xb16_res[s]])
                for dd in range(2):
                    gw, gwres = ws.get(plan[(pss, "gw", dd, n)], lookahead=2)
                    for so in range(2):
                        sidx = 2 * n + so
                        for ci, (ca, cb) in enumerate(ctiles):
                            pr_, prr = ps[2 + ci % 2], ps_res[2 + ci % 2]
                            pi_, pir = ps[4 + ci % 2], ps_res[4 + ci % 2]
                            for si in range(2):
                                P.op("pe", lambda e, si=si, so=so, pr_=pr_, ca=ca, cb=cb, gw=gw: e.matmul(
                                    pr_[Rr, 0:cb - ca], lhsT=gw[Rr, (si * 2 + so) * RSUB:(si * 2 + so + 1) * RSUB], rhs=xb16[si][Rr, ca:cb],
                                    start=(si == 0), stop=(si == 1)), reads=[gwres, xb16_res[si]], writes=[prr])
                            for si in range(2):
                                P.op("pe", lambda e, si=si, so=so, pi_=pi_, ca=ca, cb=cb, gw=gw: e.matmul(
                                    pi_[Rr, 0:cb - ca], lhsT=gw[Rr, 352 + (si * 2 + so) * RSUB:352 + (si * 2 + so + 1) * RSUB], rhs=xb16[si][Rr, ca:cb],
                                    start=(si == 0), stop=(si == 1)), reads=[gwres, xb16_res[si]], writes=[pir])
                            P.op("act", lambda e, pr_=pr_, ca=ca, cb=cb, dd=dd, sidx=sidx: e.activation(
                                out=abuf[Rr, ca:cb], in_=pr_[Rr, 0:cb - ca], func=AF.Sigmoid, bias=rv_sb[Rr, 5 + dd, sidx:sidx + 1]),
                                reads=[prr, rv_res], writes=[ab_res])
                            P.op("act", lambda e, pi_=pi_, ca=ca, cb=cb, dd=dd, sidx=sidx: e.activation(
                                out=bbuf[Rr, ca:cb], in_=pi_[Rr, 0:cb - ca], func=AF.Sigmoid, bias=rv_sb[Rr, 7 + dd, sidx:sidx + 1]),
                                reads=[pir, rv_res], writes=[bb_res])
                        if phase == "A" and not ctx:
                            P.op("dve", lambda e: e.reduce_sum(out=rsum[Rr, :], in_=abuf[Rr, 0:NTK], axis=mybir.AxisListType.X),
                                 reads=[ab_res], writes=[rsum_res])
                            P.op("act", lambda e, dd=dd, sidx=sidx: e.activation(out=so_sb[Rr, 2 * dd, sidx:sidx + 1], in_=rsum[Rr, :], func=AF.Exp,
                                                                               scale=cp_sb[Rr, dd, sidx:sidx + 1]),
                                 reads=[rsum_res, rv_res], writes=[so_res])
                        P.op("act", lambda e, dd=dd, sidx=sidx: e.activation(out=abuf[Rr, 0:NTK], in_=abuf[Rr, 0:NTK], func=AF.Exp,
                                                                           scale=cp_sb[Rr, dd, sidx:sidx + 1]),
                             reads=[ab_res, rv_res], writes=[ab_res])
                        P.op("dve", lambda e: e.tensor_tensor(out=tbuf[Rr, 0:NTK], in0=abuf[Rr, 0:NTK], in1=abuf[Rr, 0:NTK], op=ALU.mult),
                             reads=[ab_res], writes=[tb_res])
                        P.op("dve", lambda e: e.tensor_scalar(out=tbuf[Rr, 0:NTK], in0=tbuf[Rr, 0:NTK], scalar1=-1.0, scalar2=1.0, op0=ALU.mult, op1=ALU.add),
                             reads=[tb_res], writes=[tb_res])
                        P.op("act", lambda e: e.activation(out=tbuf[Rr, 0:NTK], in_=tbuf[Rr, 0:NTK], func=AF.Sqrt), reads=[tb_res], writes=[tb_res])
                        P.op("dve", lambda e, so=so: e.tensor_tensor(out=bbuf[Rr, 0:NTK], in0=bbuf[Rr, 0:NTK], in1=xb[so][Rr, 0:NTK], op=ALU.mult),
                             reads=[bb_res, xb_res[so]], writes=[bb_res])
                        P.op("dve", lambda e: e.tensor_tensor(out=bbuf[Rr, 0:NTK], in0=bbuf[Rr, 0:NTK], in1=tbuf[Rr, 0:NTK], op=ALU.mult),
                             reads=[bb_res, tb_res], writes=[bb_res])
                        if phase == "B":
                            init = carry[Rr, dd, sidx:sidx + 1]
                            dst, dres = (ybuf[so], yb_res[so]) if dd == 0 else (tbuf, tb_res)
                        else:
                            init = 0.0
                            dst, dres = tbuf, tb_res
                        if dd == 0:
                            P.op("dve", lambda e, dst=dst, init=init: e.tensor_tensor_scan(
                                out=dst[Rr, 0:NTK], data0=abuf[Rr, 0:NTK], data1=bbuf[Rr, 0:NTK], initial=init, op0=ALU.mult, op1=ALU.add),
                                reads=[ab_res, bb_res] + ([carry_res] if phase == "B" else []), writes=[dres])
                        else:
                            P.op("dve", lambda e, dst=dst, init=init: e.tensor_tensor_scan(
                                out=dst[Rr, NTK - 1::-1] if False else dst[Rr, 0:NTK][:, ::-1], data0=abuf[Rr, 0:NTK][:, ::-1], data1=bbuf[Rr, 0:NTK][:, ::-1],
                                initial=init, op0=ALU.mult, op1=ALU.add),
                                reads=[ab_res, bb_res] + ([carry_res] if phase == "B" else []), writes=[dres])
                        if phase == "A":
                            col = NTK - 1 if dd == 0 else 0
                            row = (4 + dd) if ctx else (2 * dd + 1)
                            P.op("act", lambda e, col=col, row=row, sidx=sidx: e.activation(out=so_sb[Rr, row, sidx:sidx + 1], in_=tbuf[Rr, col:col + 1], func=AF.Identity),
                                 reads=[tb_res], writes=[so_res])
                        elif dd == 1:
                            P.op("dve", lambda e, so=so: e.tensor_tensor(out=ybuf[so][Rr, 0:NTK], in0=ybuf[so][Rr, 0:NTK], in1=tbuf[Rr, 0:NTK], op=ALU.add),
                                 reads=[tb_res], writes=[yb_res[so]])
                    ws.release(plan[(pss, "gw", dd, n)])
                if phase == "B":
                    for so in range(2):
                        sidx = 2 * n + so
                        wg, wgres = ws.get(plan[(pss, "wg", sidx)], lookahead=2)
                        pb, pbr = pr16[so], pr16_res[so]
                        for ci, (ca, cb) in enumerate(ctiles):
                            pg, pgr = ps[6 + ci % 2], ps_res[6 + ci % 2]
                            for kc in range(NDC):
                                P.op("pe", lambda e, kc=kc, pg=pg, ca=ca, cb=cb, wg=wg: e.matmul(
                                    pg[Rr, 0:cb - ca], lhsT=wg[:, kc * RSUB:(kc + 1) * RSUB], rhs=a16f[:, kc, RG_HL + ca:RG_HL + cb],
                                    start=(kc == 0), stop=(kc == NDC - 1)), reads=[wgres, a16f_res[kc]], writes=[pgr])
                            k = ci % 2
                            P.op("act", lambda e, pg=pg, k=k, ca=ca, cb=cb: e.activation(out=gl[k][Rr, 0:cb - ca], in_=pg[Rr, 0:cb - ca], func=AF.Gelu_apprx_tanh),
                                 reads=[pgr], writes=[gl_res[k]])
                            P.op("dve", lambda e, k=k, so=so, ca=ca, cb=cb, pb=pb: e.tensor_tensor(
                                out=pb[Rr, ca:cb], in0=ybuf[so][Rr, ca:cb], in1=gl[k][Rr, 0:cb - ca], op=ALU.mult),
                                reads=[yb_res[so], gl_res[k]], writes=[pbr])
                        ws.release(plan[(pss, "wg", sidx)])
                        P.dma("pool", prod[sidx], pb[Rr, :], reads=[pbr])

        for pss in passes:
            run_pass(pss)
        if phase == "A":
            P.dma("pool", sout, so_sb[:], reads=[so_res])
        P.finish("sp")
        nc._stats = dict(P.n_inst)
    return nc


def _pad128(a):
    shp = list(a.shape)
    shp[-2] = 128
    out = np.zeros(shp, np.float32)
    out[..., :a.shape[-2], :] = a
    return out


def rg_col(v):
    return np.asarray(v, np.float32).reshape(NSUB, RSUB).T


def run_rg_layer(inputs, x):
    for ph in ("A", "B"):
        if ("rg", ph) not in _NC_CACHE:
            _NC_CACHE[("rg", ph)] = build_rg(ph)
    if ("lin",) not in _NC_CACHE:
        _NC_CACHE[("lin",)] = build_layer("lin", 0, 0)
    pe = _pos_embed_table()
    modw = w_colblocks(np.asarray(inputs["mod_w"][0][:, 0:4096]), 32)
    modb_full = np.ascontiguousarray(np.asarray(inputs["mod_b"][0], np.float32).reshape(96, 128).T)
    rvt = np.zeros((128, 11, NSUB), np.float32)
    cw = np.asarray(inputs["rg_conv_w"][0], np.float32)
    for j in range(4):
        rvt[:RSUB, j] = rg_col(cw[j])
    rvt[:RSUB, 4] = rg_col(inputs["rg_conv_b"][0])
    for dd in range(2):
        rvt[:RSUB, 5 + dd] = rg_col(inputs["rg_br"][0, dd])
        rvt[:RSUB, 7 + dd] = rg_col(inputs["rg_bi"][0, dd])
        rvt[:RSUB, 9 + dd] = rg_col(inputs["rg_lam"][0, dd])
    rvt[RSUB:, 9:11] = 1.0

    def sub_cols(w):
        return np.ascontiguousarray(np.asarray(w, np.float32).reshape(NDC, 128, NSUB, RSUB).transpose(2, 1, 0, 3).reshape(NSUB, 128, NDC * RSUB))
    wxr = sub_cols(inputs["rg_w_x"][0])
    wgr = sub_cols(inputs["rg_w_gate"][0])
    gw = np.zeros((32, 128, 704), np.float32)
    for dd in range(2):
        for n in range(16):
            for k_, nm in enumerate(("rg_wr", "rg_wi")):
                blk = np.asarray(inputs[nm][0, dd, n], np.float32).reshape(2, RSUB, 2, RSUB).transpose(1, 0, 2, 3).reshape(RSUB, 352)
                gw[dd * 16 + n, :RSUB, k_ * 352:(k_ + 1) * 352] = blk
    base = []
    for core in range(NCORE):
        b, q = divmod(core, 4)
        ctxT = np.ascontiguousarray(np.asarray(inputs["ctx"][b], np.float32).reshape(CTXL, NDC, 128).transpose(2, 1, 0))
        base.append(dict(xin=to_xT(x[b], q, RG_HL, RG_HR), pein=to_xT(pe, q, RG_HL, RG_HR), ctxin=ctxT, modb=modb_full,
                         cT=cond_T(inputs, b), modw=modw, edge=edge_flags(q), rv=rvt, wxr=wxr, gwr=gw))
    rA = run_bass_kernel_spmd(_NC_CACHE[("rg", "A")], base, core_ids=list(range(NCORE)))
    souts = [rA.results[c]["sout"] for c in range(NCORE)]
    summ = np.ascontiguousarray(np.stack([s_[:, 0:4, :] for s_ in souts], axis=0))
    mapsB = []
    for core in range(NCORE):
        b, q = divmod(core, 4)
        mfb = np.zeros((128, 2, NCORE), np.float32)
        for r in range(NCORE):
            rb, rq = divmod(r, 4)
            if rb == b and rq < q:
                mfb[:, 0, r] = 1.0
            if rb == b and rq > q:
                mfb[:, 1, r] = 1.0
        m = dict(base[core])
        m.update(wgr=wgr, summ=summ, ctxs=np.ascontiguousarray(souts[core][:, 4:6, :]), mfb=mfb)
        mapsB.append(m)
    rB = run_bass_kernel_spmd(_NC_CACHE[("rg", "B")], mapsB, core_ids=list(range(NCORE)))
    cm = common_maps(inputs, 0, [])
    wor = np.ascontiguousarray(np.asarray(inputs["rg_w_out"][0], np.float32).reshape(22, 128, D))
    mapsC = []
    for core in range(NCORE):
        b, q = divmod(core, 4)
        prod = rB.results[core]["prod"]
        m = dict(cm)
        m.update(xin=to_xT(x[b], q, 0, 0), cT=cond_T(inputs, b), edge=edge_flags(q), pe=to_xT(pe, q, 0, 0),
                 prodin=np.ascontiguousarray(prod.reshape(22, 128, TOK)), wor=wor)
        mapsC.append(m)
    res = run_bass_kernel_spmd(_NC_CACHE[("lin",)], mapsC, core_ids=list(range(NCORE)))
    out = np.empty_like(x)
    for core in range(NCORE):
        b, q = divmod(core, 4)
        out[b, q * TOK:(q + 1) * TOK] = from_xT(res.results[core]["xout"])
    return out
```

```python
import math
from contextlib import ExitStack

import numpy as np
import concourse.bass as bass
import concourse.mybir as mybir
from concourse.bass_utils import run_bass_kernel_spmd

F32 = mybir.dt.float32
BF16 = mybir.dt.bfloat16
AF = mybir.ActivationFunctionType
ALU = mybir.AluOpType

D = 2048
NDC = 16
S = 8192
NCORE = 8
TOK = 2048
T = 512
NTT = TOK // T
DFF = 5632
NF = DFF // 128
GF = 4
DEPTH = 4
ALPHA = (2 * DEPTH) ** 0.25
LN_EPS = 1e-5
EPOCH = 30000


class Res:
    __slots__ = ("name", "last_write", "readers")

    def __init__(self, name=""):
        self.name = name
        self.last_write = None
        self.readers = []


class Prog:
    def __init__(self, nc, stack, n_dma_sems=8):
        self.nc = nc
        self.stack = stack
        self.eng = {"pe": nc.tensor, "act": nc.scalar, "dve": nc.vector, "pool": nc.gpsimd, "sp": nc.sync}
        self.sem = {}
        self.cnt = {}
        self.nsem = 0
        for e in ("pe", "act", "dve", "pool"):
            self._new_epoch(e)
        self.waited = {e: {} for e in self.eng}
        self.dma_sems = {}
        self.dma_rr = {}
        for q in ("sp", "pool", "act"):
            self.dma_sems[q] = [[self._alloc_sem(f"dma_{q}_{i}"), 0] for i in range(n_dma_sems)]
            self.dma_rr[q] = 0
        self.n_inst = {e: 0 for e in self.eng}

    def _alloc_sem(self, name):
        self.nsem += 1
        return self.stack.enter_context(self.nc.semaphore(f"{name}_{self.nsem}"))

    def _new_epoch(self, e):
        self.sem[e] = self._alloc_sem(f"eng_{e}")
        self.cnt[e] = 0

    def sbuf(self, name, shape, dtype):
        return self.stack.enter_context(self.nc.sbuf_tensor(name, list(shape), dtype))

    def psum(self, name, shape, dtype=F32):
        return self.stack.enter_context(self.nc.psum_tensor(name, list(shape), dtype))

    def _wait(self, e, tok):
        src, sem, val = tok
        key = id(sem)
        if self.waited[e].get(key, 0) >= val:
            return
        self.eng[e].wait_ge(sem, val)
        self.waited[e][key] = val

    def _deps(self, e, reads, writes, same_engine_ok=True):
        toks = []
        for r in reads:
            if r.last_write is not None:
                toks.append(r.last_write)
        for w in writes:
            if w.last_write is not None:
                toks.append(w.last_write)
            toks.extend(w.readers)
        for tok in toks:
            if same_engine_ok and tok[0] == e and e == "pe":
                continue
            self._wait(e, tok)

    def _commit(self, tok, reads, writes):
        for r in reads:
            r.readers.append(tok)
            if len(r.readers) > 48:
                latest = {}
                for t in r.readers:
                    k = (t[0], id(t[1]))
                    if k not in latest or latest[k][2] < t[2]:
                        latest[k] = t
                r.readers = list(latest.values())
        for w in writes:
            w.last_write = tok
            w.readers = []

    def op(self, e, fn, reads=(), writes=()):
        self._deps(e, reads, writes)
        inst = fn(self.eng[e])
        if self.cnt[e] >= EPOCH:
            self._new_epoch(e)
        self.cnt[e] += 1
        inst.then_inc(self.sem[e], 1)
        tok = (e, self.sem[e], self.cnt[e])
        self._commit(tok, reads, writes)
        self.n_inst[e] += 1
        return tok

    def dma(self, q, out, in_, reads=(), writes=(), **kw):
        self._deps(q, reads, writes, same_engine_ok=False)
        pool = self.dma_sems[q]
        slot = pool[self.dma_rr[q] % len(pool)]
        self.dma_rr[q] += 1
        sem, val = slot
        if val > 0:
            self._wait(q, ("dma", sem, val))
        inst = self.eng[q].dma_start(out=out, in_=in_, **kw)
        slot[1] = val + 16
        inst.then_inc(sem, 16)
        tok = ("dma", sem, val + 16)
        self._commit(tok, reads, writes)
        self.n_inst[q] += 1
        return tok

    def finish(self, e="sp"):
        for q, pool in self.dma_sems.items():
            for sem, val in pool:
                if val > 0:
                    self._wait(e, ("dma", sem, val))


class WStream:
    NSTG = 4
    NRING = 14

    def __init__(self, P, nstg=None, nring=None):
        self.P = P
        self.NSTG = nstg or WStream.NSTG
        self.NRING = nring or WStream.NRING
        self.stg = [P.sbuf(f"stg{i}", [128, 2048], F32) for i in range(self.NSTG)]
        self.stg_res = [Res(f"stg{i}") for i in range(self.NSTG)]
        self.ring = [P.sbuf(f"wring{i}", [128, 2048], BF16) for i in range(self.NRING)]
        self.ring_res = [Res(f"wring{i}") for i in range(self.NRING)]
        self.items = []
        self.issued = 0
        self.n_stg = 0
        self.n_ring = 0
        self.loc = {}
        self.stg_owner = [None] * self.NSTG
        self.ring_owner = [None] * self.NRING
        self.released = set()

    def add(self, src_ap, cast=True, width=2048):
        self.items.append((src_ap, cast, width))
        return len(self.items) - 1

    def issue_until(self, idx):
        P = self.P
        idx = min(idx, len(self.items) - 1)
        while self.issued <= idx:
            i = self.issued
            src, cast, wd = self.items[i]
            if cast:
                r = self.n_ring % self.NRING
                if self.ring_owner[r] is not None and self.ring_owner[r] not in self.released:
                    return
                self.n_ring += 1
                self.ring_owner[r] = i
                P.dma("pool", self.ring[r][:, 0:wd], src, writes=[self.ring_res[r]])
                self.loc[i] = (self.ring[r], self.ring_res[r])
            else:
                s = self.n_stg % self.NSTG
                if self.stg_owner[s] is not None and self.stg_owner[s] not in self.released:
                    return
                self.n_stg += 1
                self.stg_owner[s] = i
                P.dma("sp", self.stg[s][:, 0:wd], src, writes=[self.stg_res[s]])
                self.loc[i] = (self.stg[s], self.stg_res[s])
            self.issued += 1

    def get(self, idx, lookahead=6):
        self.issue_until(idx + lookahead)
        assert idx in self.loc, f"weight item {idx} could not be issued (ring full: missing release?)"
        return self.loc[idx]

    def release(self, idx):
        self.released.add(idx)


V_LNG0, V_LNB0, V_LNG1, V_LNB1, V_MX0, V_MX1, V_MX2, V_MX3 = range(8)
NV = 8


class LayerCtx:
    pass


def _bc(ap_col, n):
    return ap_col.to_broadcast([128, n])


def build_layer(kind, halo_l, halo_r, n_cond=2, extra=None):
    nc = bass.Bass("TRN2", target_bir_lowering=False)
    NT = halo_l + TOK + halo_r
    W = halo_l + T + halo_r
    dram = {}

    def din(name, shape, dt=F32):
        dram[name] = nc.dram_tensor(name, list(shape), dt, kind="ExternalInput").ap()
        return dram[name]

    xin = din("xin", [128, NDC, NT])
    vec = din("vec", [128, NV, NDC])
    modb = din("modb", [128, 96])
    cT = din("cT", [128, NDC, n_cond])
    modw = din("modw", [96, 128, NDC * 128])
    w1r = din("w1r", [NF, 128, 2048])
    w3r = din("w3r", [NF, 128, 2048])
    w2r = din("w2r", [NF, 128, 2048])
    edge = din("edge", [128, 2])
    if kind == "pool":
        poolw = din("poolw", [16, 128, 512])
        pinv = din("pinv", [4, TOK])
    NCV = 80 + 16 * 31
    if kind == "conf":
        cvw1 = din("cvw1", [32, 128, 2048])
        cvw2 = din("cvw2", [16, 128, 2048])
        cvv = din("cvv", [128, NCV])
    if kind == "lin":
        pe_in = din("pe", [128, NDC, TOK])
        prodin = din("prodin", [22, 128, TOK], BF16)
        wor = din("wor", [22, 128, 2048])
    if kind == "none":
        pe_in = din("pe", [128, NDC, TOK])
    if kind == "four":
        fin = din("fin", [128, 2, NDC, TOK], BF16)
        ccsc = din("ccsc", [4, 128, 1024])
        ftw = din("ftw", [16, 128, 2048])
    xout = nc.dram_tensor("xout", [128, NDC, TOK], F32, kind="ExternalOutput").ap()

    with ExitStack() as st:
        P = Prog(nc, st)
        ws = WStream(P)
        L = LayerCtx()
        xs = P.sbuf("xs", [128, NDC, W], F32)
        xs_res = [Res(f"xs{d}") for d in range(NDC)]
        a16 = P.sbuf("a16", [128, NDC, W], BF16)
        a16_res = [Res(f"a16_{d}") for d in range(NDC)]
        g16 = P.sbuf("g16", [128, 2, GF, T], BF16)
        g16_res = [[Res(f"g16_{i}_{j}") for j in range(GF)] for i in range(2)]
        stmp = [P.sbuf(f"stmp{i}", [128, T], F32) for i in range(2)]
        stmp_res = [Res(f"stmp{i}") for i in range(2)]
        sq = [P.sbuf(f"sq{i}", [128, T], F32) for i in range(2)]
        sq_res = [Res(f"sq{i}") for i in range(2)]
        lnt = P.sbuf("lnt", [128, 2, T], F32)
        lnt_res = Res("lnt")
        lnt2 = P.sbuf("lnt2", [128, T], F32)
        onesD = P.sbuf("onesD", [128, 128], F32)
        onesD_res = Res("onesD")
        vraw = P.sbuf("vraw", [128, NV, NDC], F32)
        vraw_res = Res("vraw")
        modb_sb = P.sbuf("modb_sb", [128, 96], F32)
        cs = P.sbuf("cs", [128, NDC, n_cond], F32)
        cs_res = Res("cs")
        msb = P.sbuf("msb", [128, 96, n_cond], F32)
        msb_res = Res("msb")
        dv = P.sbuf("dv", [128, 8, NDC], F32)
        dv_res = Res("dv")
        edge_sb = P.sbuf("edge_sb", [128, 2], F32)
        edge_res = Res("edge")
        ps = [P.psum(f"ps{i}", [128, T]) for i in range(8)]
        ps_res = [Res(f"ps{i}") for i in range(8)]

        mod_items = [ws.add(modw[oc], cast=False) for oc in range(96)]
        plan = []
        if kind == "pool":
            pw_items = [ws.add(poolw[i], width=512) for i in range(16)]
        if kind == "four":
            cc_items = [ws.add(ccsc[i], width=1024) for i in range(4)]
        for tt in range(NTT):
            d_ = {}
            if kind == "conf":
                for oc in range(NDC):
                    d_[("cv", oc)] = ws.add(cvw1[oc])
                    d_[("cg", oc)] = ws.add(cvw1[16 + oc])
                for oc in range(NDC):
                    d_[("c2", oc)] = ws.add(cvw2[oc])
            if kind == "four":
                for oc in range(NDC):
                    d_[("fw", oc)] = ws.add(ftw[oc])
            if kind == "lin":
                for kc in range(22):
                    d_[("wo", kc)] = ws.add(wor[kc])
            for f in range(NF):
                d_[("w1", f)] = ws.add(w1r[f])
                d_[("w3", f)] = ws.add(w3r[f])
                d_[("w2", f)] = ws.add(w2r[f])
            plan.append(d_)

        P.op("pool", lambda e: e.memset(onesD[:], 1.0 / D), writes=[onesD_res])
        P.dma("act", vraw[:], vec, writes=[vraw_res])
        P.dma("act", modb_sb[:], modb, writes=[vraw_res])
        P.dma("act", cs[:], cT, writes=[cs_res])
        P.dma("act", edge_sb[:], edge, writes=[edge_res])
        P.op("act", lambda e: e.activation(out=cs[:], in_=cs[:], func=AF.Silu), reads=[cs_res], writes=[cs_res])
        mps = ps[7]
        mps_res = ps_res[7]
        for oc in range(96):
            wt, wr = ws.get(mod_items[oc], lookahead=2)
            for kc in range(NDC):
                P.op("pe", lambda e, oc=oc, kc=kc, wt=wt: e.matmul(
                    mps[:, oc * n_cond:(oc + 1) * n_cond], lhsT=wt[:, kc * 128:(kc + 1) * 128], rhs=cs[:, kc, :],
                    start=(kc == 0), stop=(kc == NDC - 1)), reads=[wr, cs_res], writes=[mps_res])
            ws.release(mod_items[oc])
        for j in range(n_cond):
            P.op("dve", lambda e, j=j: e.tensor_tensor(
                out=msb[:, :, j], in0=mps[:, 0:96 * n_cond].rearrange("p (o j) -> p o j", j=n_cond)[:, :, j],
                in1=modb_sb[:], op=ALU.add), reads=[mps_res, vraw_res], writes=[msb_res])

        def mvec(k6, dc, j=0):
            return msb[:, k6 * 16 + dc, j:j + 1]

        def mrow(k6):
            return msb[:, k6 * 16:(k6 + 1) * 16, 0]
        P.op("dve", lambda e: e.tensor_scalar(out=dv[:, 0, :], in0=mrow(1), scalar1=1.0, scalar2=None, op0=ALU.add),
             reads=[msb_res], writes=[dv_res])
        if kind == "pool":
            P.op("dve", lambda e: e.tensor_tensor(out=dv[:, 1, :], in0=mrow(2), in1=vraw[:, V_MX1, :], op=ALU.mult),
                 reads=[msb_res, vraw_res], writes=[dv_res])
            P.op("dve", lambda e: e.tensor_tensor(out=dv[:, 2, :], in0=dv[:, 1, :], in1=vraw[:, V_MX0, :], op=ALU.mult),
                 reads=[vraw_res], writes=[dv_res])
        else:
            P.op("dve", lambda e: e.tensor_copy(out=dv[:, 1, :], in_=mrow(2)), reads=[msb_res], writes=[dv_res])
            P.op("dve", lambda e: e.tensor_tensor(out=dv[:, 2, :], in0=dv[:, 1, :], in1=vraw[:, V_MX0, :], op=ALU.mult),
                 reads=[vraw_res], writes=[dv_res])
        P.op("dve", lambda e: e.tensor_scalar(out=dv[:, 3, :], in0=vraw[:, V_LNG0, :], scalar1=ALPHA, scalar2=None, op0=ALU.mult),
             reads=[vraw_res], writes=[dv_res])
        P.op("dve", lambda e: e.tensor_scalar(out=dv[:, 4, :], in0=vraw[:, V_LNB0, :], scalar1=ALPHA, scalar2=None, op0=ALU.mult),
             reads=[vraw_res], writes=[dv_res])
        P.op("dve", lambda e: e.tensor_scalar(out=dv[:, 5, :], in0=mrow(4), scalar1=1.0, scalar2=1.0 / ALPHA, op0=ALU.add, op1=ALU.mult),
             reads=[msb_res], writes=[dv_res])
        VR = [vraw_res, dv_res, msb_res]

        def emit_ln(c0, gcol, bcol, buf=None, bres=None, func=AF.Identity, dst=None):
            buf = xs if buf is None else buf
            bres = xs_res if bres is None else bres
            pm, pq = ps[4], ps[5]
            pmr, pqr = ps_res[4], ps_res[5]
            for dc in range(NDC):
                k = dc % 2
                P.op("act", lambda e, dc=dc, k=k: e.activation(out=sq[k][:], in_=buf[:, dc, c0:c0 + T], func=AF.Square),
                     reads=[bres[dc]], writes=[sq_res[k]])
                P.op("pe", lambda e, dc=dc: e.matmul(pm[:], lhsT=onesD[:], rhs=buf[:, dc, c0:c0 + T],
                                                      start=(dc == 0), stop=(dc == NDC - 1)),
                     reads=[onesD_res, bres[dc]], writes=[pmr])
                P.op("pe", lambda e, dc=dc, k=k: e.matmul(pq[:], lhsT=onesD[:], rhs=sq[k][:],
                                                           start=(dc == 0), stop=(dc == NDC - 1)),
                     reads=[onesD_res, sq_res[k]], writes=[pqr])
            P.op("act", lambda e: e.activation(out=lnt[:, 0, :], in_=pm[:], func=AF.Square), reads=[pmr], writes=[lnt_res])
            P.op("dve", lambda e: e.tensor_tensor(out=lnt[:, 0, :], in0=pq[:], in1=lnt[:, 0, :], op=ALU.subtract),
                 reads=[pqr], writes=[lnt_res])
            P.op("dve", lambda e: e.tensor_scalar(out=lnt[:, 0, :], in0=lnt[:, 0, :], scalar1=LN_EPS, scalar2=None, op0=ALU.add),
                 writes=[lnt_res])
            P.op("act", lambda e: e.activation(out=lnt[:, 1, :], in_=lnt[:, 0, :], func=AF.Sqrt), reads=[lnt_res], writes=[lnt_res])
            P.op("dve", lambda e: e.reciprocal(out=lnt[:, 1, :], in_=lnt[:, 1, :]), reads=[lnt_res], writes=[lnt_res])
            for _it in range(2):
                P.op("dve", lambda e: e.tensor_tensor(out=lnt2[:], in0=lnt[:, 0, :], in1=lnt[:, 1, :], op=ALU.mult), writes=[lnt_res])
                P.op("dve", lambda e: e.tensor_tensor(out=lnt2[:], in0=lnt2[:], in1=lnt[:, 1, :], op=ALU.mult), writes=[lnt_res])
                P.op("dve", lambda e: e.tensor_scalar(out=lnt2[:], in0=lnt2[:], scalar1=-0.5, scalar2=1.5, op0=ALU.mult, op1=ALU.add), writes=[lnt_res])
                P.op("dve", lambda e: e.tensor_tensor(out=lnt[:, 1, :], in0=lnt[:, 1, :], in1=lnt2[:], op=ALU.mult), writes=[lnt_res])
            for dc in range(NDC):
                P.op("dve", lambda e, dc=dc: e.tensor_tensor(out=buf[:, dc, c0:c0 + T], in0=buf[:, dc, c0:c0 + T], in1=pm[:], op=ALU.subtract),
                     reads=[pmr], writes=[bres[dc]])
                P.op("dve", lambda e, dc=dc: e.tensor_tensor(out=buf[:, dc, c0:c0 + T], in0=buf[:, dc, c0:c0 + T], in1=lnt[:, 1, :], op=ALU.mult),
                     reads=[lnt_res], writes=[bres[dc]])
                if dst is None:
                    P.op("act", lambda e, dc=dc: e.activation(out=buf[:, dc, c0:c0 + T], in_=buf[:, dc, c0:c0 + T], func=func,
                                                              scale=gcol(dc), bias=bcol(dc)),
                         reads=VR + [bres[dc]], writes=[bres[dc]])
                else:
                    dap, dres = dst(dc)
                    P.op("act", lambda e, dc=dc, dap=dap: e.activation(out=dap, in_=buf[:, dc, c0:c0 + T], func=func,
                                                                       scale=gcol(dc), bias=bcol(dc)),
                         reads=VR + [bres[dc]], writes=[dres])

        def emit_ffn(tt, c0):
            pl = plan[tt]
            for dc in range(NDC):
                P.op("act", lambda e, dc=dc: e.activation(out=a16[:, dc, 0:T], in_=xs[:, dc, c0:c0 + T], func=AF.Identity,
                                                          scale=dv[:, 5, dc:dc + 1], bias=mvec(3, dc)),
                     reads=VR + [xs_res[dc]], writes=[a16_res[dc]])
            ngrp = NF // GF
            for grp in range(ngrp):
                gb = grp % 2
                for fi in range(GF):
                    f = grp * GF + fi
                    w1t, w1res = ws.get(pl[("w1", f)])
                    w3t, w3res = ws.get(pl[("w3", f)])
                    p1, p3 = ps[(f % 2) * 2], ps[(f % 2) * 2 + 1]
                    p1r, p3r = ps_res[(f % 2) * 2], ps_res[(f % 2) * 2 + 1]
                    for kc in range(NDC):
                        P.op("pe", lambda e, kc=kc, w1t=w1t, p1=p1: e.matmul(
                            p1[:], lhsT=w1t[:, kc * 128:(kc + 1) * 128], rhs=a16[:, kc, 0:T],
                            start=(kc == 0), stop=(kc == NDC - 1)), reads=[w1res, a16_res[kc]], writes=[p1r])
                    for kc in range(NDC):
                        P.op("pe", lambda e, kc=kc, w3t=w3t, p3=p3: e.matmul(
                            p3[:], lhsT=w3t[:, kc * 128:(kc + 1) * 128], rhs=a16[:, kc, 0:T],
                            start=(kc == 0), stop=(kc == NDC - 1)), reads=[w3res, a16_res[kc]], writes=[p3r])
                    ws.release(pl[("w1", f)])
                    ws.release(pl[("w3", f)])
                    k = f % 2
                    P.op("act", lambda e, k=k, p1=p1: e.activation(out=stmp[k][:], in_=p1[:], func=AF.Silu),
                         reads=[p1r], writes=[stmp_res[k]])
                    P.op("dve", lambda e, k=k, p3=p3, gb=gb, fi=fi: e.tensor_tensor(
                        out=g16[:, gb, fi, :], in0=p3[:], in1=stmp[k][:], op=ALU.mult),
                        reads=[p3r, stmp_res[k]], writes=[g16_res[gb][fi]])
                w2 = [ws.get(pl[("w2", grp * GF + fi)]) for fi in range(GF)]
                for dc in range(NDC):
                    po, por = ps[6 + dc % 2], ps_res[6 + dc % 2]
                    for fi in range(GF):
                        P.op("pe", lambda e, fi=fi, dc=dc, po=po: e.matmul(
                            po[:], lhsT=w2[fi][0][:, dc * 128:(dc + 1) * 128], rhs=g16[:, gb, fi, :],
                            start=(fi == 0), stop=(fi == GF - 1)), reads=[w2[fi][1], g16_res[gb][fi]], writes=[por])
                    P.op("dve", lambda e, dc=dc, po=po: e.scalar_tensor_tensor(
                        out=xs[:, dc, c0:c0 + T], in0=po[:], scalar=mvec(5, dc), in1=xs[:, dc, c0:c0 + T],
                        op0=ALU.mult, op1=ALU.add), reads=VR + [por], writes=[xs_res[dc]])
                for fi in range(GF):
                    ws.release(pl[("w2", grp * GF + fi)])

        if kind == "pool":
            pwb = P.sbuf("pwb", [128, 16, 512], BF16)
            pwb_res = Res("pwb")
            for i in range(16):
                wt, wr = ws.get(pw_items[i], lookahead=2)
                P.op("act", lambda e, i=i, wt=wt: e.activation(out=pwb[:, i, :], in_=wt[:, 0:512], func=AF.Identity), reads=[wr], writes=[pwb_res])
                ws.release(pw_items[i])
            hbuf = [P.sbuf(f"hbuf{i}", [128, W], F32) for i in range(2)]
            hbuf_res = [Res(f"hbuf{i}") for i in range(2)]
            sl = [P.sbuf(f"sl{i}", [128, W], F32) for i in range(4)]
            sl_res = [Res(f"sl{i}") for i in range(4)]
            inv_sb = P.sbuf("inv_sb", [128, 4, T], F32)
            inv_res = Res("inv")

        def emit_pool_mixer(tt):
            c0 = halo_l
            P.dma("act", inv_sb[:], pinv[:, tt * T:(tt + 1) * T].partition_broadcast(128), writes=[inv_res])
            for dc in range(NDC):
                g = dc // 4
                hb, hr = hbuf[dc % 2], hbuf_res[dc % 2]
                P.op("act", lambda e, dc=dc, hb=hb: e.activation(out=hb[:], in_=xs[:, dc, :], func=AF.Identity,
                                                               scale=dv[:, 0, dc:dc + 1], bias=mvec(0, dc)),
                     reads=VR + [xs_res[dc]], writes=[hr])
                if tt == 0:
                    P.op("dve", lambda e, hb=hb: e.tensor_scalar(out=hb[:, 0:halo_l], in0=hb[:, 0:halo_l], scalar1=edge_sb[:, 0:1],
                                                                 scalar2=None, op0=ALU.mult), reads=[edge_res], writes=[hr])
                if tt == NTT - 1:
                    P.op("dve", lambda e, hb=hb: e.tensor_scalar(out=hb[:, c0 + T:W], in0=hb[:, c0 + T:W], scalar1=edge_sb[:, 1:2],
                                                                 scalar2=None, op0=ALU.mult), reads=[edge_res], writes=[hr])
                cur, cur_r = hb, hr
                lo, hi = 0, W
                offs = [(1, 0), (1, 1), (2, 2), (4, 4)]
                for lev in range(g + 1):
                    a, b = offs[lev]
                    nlo, nhi = lo + a, hi - b
                    dst, dst_r = sl[lev], sl_res[lev]
                    P.op("pool", lambda e, cur=cur, dst=dst, a=a, b=b, nlo=nlo, nhi=nhi: e.tensor_tensor(
                        out=dst[:, nlo:nhi], in0=cur[:, nlo - a:nhi - a], in1=cur[:, nlo + b:nhi + b], op=ALU.add),
                        reads=[cur_r], writes=[dst_r])
                    cur, cur_r, lo, hi = dst, dst_r, nlo, nhi
                P.op("dve", lambda e, cur=cur, g=g: e.tensor_tensor(out=cur[:, c0:c0 + T], in0=cur[:, c0:c0 + T], in1=inv_sb[:, g, :], op=ALU.mult),
                     reads=[inv_res, cur_r], writes=[cur_r])
                P.op("dve", lambda e, cur=cur, hb=hb, dc=dc: e.tensor_tensor(out=a16[:, dc, 0:T], in0=cur[:, c0:c0 + T], in1=hb[:, c0:c0 + T], op=ALU.subtract),
                     reads=[cur_r, hr], writes=[a16_res[dc]])
            for oc in range(NDC):
                g = oc // 4
                po, por = ps[6 + oc % 2], ps_res[6 + oc % 2]
                for kc in range(4):
                    P.op("pe", lambda e, g=g, kc=kc, oc=oc, po=po: e.matmul(
                        po[:], lhsT=pwb[:, g * 4 + kc, (oc % 4) * 128:(oc % 4 + 1) * 128], rhs=a16[:, g * 4 + kc, 0:T],
                        start=(kc == 0), stop=(kc == 3)), reads=[pwb_res, a16_res[g * 4 + kc]], writes=[por])
                emit_residual(oc, po, por, c0)

        if kind == "conf":
            cvv_sb = P.sbuf("cvv_sb", [128, NCV], F32)
            P.dma("act", cvv_sb[:], cvv, writes=[vraw_res])
            ubuf = P.sbuf("ubuf", [128, NDC, W], F32)
            u_res = [Res(f"u{d}") for d in range(NDC)]
            cacc = [P.sbuf(f"cacc{i}", [128, T], F32) for i in range(2)]
            cacc_res = [Res(f"cacc{i}") for i in range(2)]
            HWD = W // 2
            sgt = [P.sbuf(f"sgt{i}", [128, HWD], F32) for i in range(2)]
            sgt_res = [Res(f"sgt{i}") for i in range(2)]

        def emit_conf_mixer(tt):
            pl = plan[tt]
            c0 = halo_l
            for dc in range(NDC):
                P.op("act", lambda e, dc=dc: e.activation(out=a16[:, dc, :], in_=xs[:, dc, :], func=AF.Identity,
                                                          scale=dv[:, 0, dc:dc + 1], bias=mvec(0, dc)),
                     reads=VR + [xs_res[dc]], writes=[a16_res[dc]])
            for oc in range(NDC):
                wv, wvr = ws.get(pl[("cv", oc)])
                wg, wgr = ws.get(pl[("cg", oc)])
                for half in range(2):
                    ca, cb = half * HWD, (half + 1) * HWD
                    pv, pvr = ps[half * 2], ps_res[half * 2]
                    pg, pgr = ps[half * 2 + 1], ps_res[half * 2 + 1]
                    for kc in range(NDC):
                        P.op("pe", lambda e, kc=kc, pv=pv, ca=ca, cb=cb: e.matmul(
                            pv[:, 0:HWD], lhsT=wv[:, kc * 128:(kc + 1) * 128], rhs=a16[:, kc, ca:cb],
                            start=(kc == 0), stop=(kc == NDC - 1)), reads=[wvr, a16_res[kc]], writes=[pvr])
                    for kc in range(NDC):
                        P.op("pe", lambda e, kc=kc, pg=pg, ca=ca, cb=cb: e.matmul(
                            pg[:, 0:HWD], lhsT=wg[:, kc * 128:(kc + 1) * 128], rhs=a16[:, kc, ca:cb],
                            start=(kc == 0), stop=(kc == NDC - 1)), reads=[wgr, a16_res[kc]], writes=[pgr])
                ws.release(pl[("cv", oc)])
                ws.release(pl[("cg", oc)])
                for half in range(2):
                    ca, cb = half * HWD, (half + 1) * HWD
                    pv, pvr = ps[half * 2], ps_res[half * 2]
                    pg, pgr = ps[half * 2 + 1], ps_res[half * 2 + 1]
                    P.op("act", lambda e, half=half, pg=pg: e.activation(out=sgt[half][:], in_=pg[:, 0:HWD], func=AF.Sigmoid,
                                                                         bias=cvv_sb[:, 16 + oc:17 + oc]),
                         reads=VR + [pgr], writes=[sgt_res[half]])
                    P.op("dve", lambda e, half=half, pv=pv, ca=ca, cb=cb: e.scalar_tensor_tensor(
                        out=ubuf[:, oc, ca:cb], in0=pv[:, 0:HWD], scalar=cvv_sb[:, oc:oc + 1], in1=sgt[half][:],
                        op0=ALU.add, op1=ALU.mult), reads=VR + [pvr, sgt_res[half]], writes=[u_res[oc]])
                if tt == 0:
                    P.op("dve", lambda e: e.tensor_scalar(out=ubuf[:, oc, 0:halo_l], in0=ubuf[:, oc, 0:halo_l], scalar1=edge_sb[:, 0:1],
                                                          scalar2=None, op0=ALU.mult), reads=[edge_res], writes=[u_res[oc]])
                if tt == NTT - 1:
                    P.op("dve", lambda e: e.tensor_scalar(out=ubuf[:, oc, c0 + T:W], in0=ubuf[:, oc, c0 + T:W], scalar1=edge_sb[:, 1:2],
                                                          scalar2=None, op0=ALU.mult), reads=[edge_res], writes=[u_res[oc]])
                ac, acr = cacc[oc % 2], cacc_res[oc % 2]
                dwc = 80 + oc * 31
                P.op("dve", lambda e, ac=ac: e.tensor_scalar(out=ac[:], in0=ubuf[:, oc, 0:T], scalar1=cvv_sb[:, dwc:dwc + 1],
                                                             scalar2=None, op0=ALU.mult), reads=VR + [u_res[oc]], writes=[acr])
                for j in range(1, 31):
                    P.op("dve", lambda e, ac=ac, j=j: e.scalar_tensor_tensor(
                        out=ac[:], in0=ubuf[:, oc, j:j + T], scalar=cvv_sb[:, dwc + j:dwc + j + 1], in1=ac[:],
                        op0=ALU.mult, op1=ALU.add), reads=[u_res[oc]], writes=[acr])
                P.op("act", lambda e, ac=ac: e.activation(out=ubuf[:, oc, 0:T], in_=ac[:], func=AF.Identity,
                                                          bias=cvv_sb[:, 32 + oc:33 + oc]),
                     reads=VR + [acr], writes=[u_res[oc]])
            emit_ln(0, lambda dc: cvv_sb[:, 48 + dc:49 + dc], lambda dc: cvv_sb[:, 64 + dc:65 + dc], buf=ubuf, bres=u_res,
                    func=AF.Silu, dst=lambda dc: (a16[:, dc, 0:T], a16_res[dc]))
            for oc in range(NDC):
                w2t, w2r_ = ws.get(pl[("c2", oc)])
                po, por = ps[6 + oc % 2], ps_res[6 + oc % 2]
                for kc in range(NDC):
                    P.op("pe", lambda e, kc=kc, po=po: e.matmul(
                        po[:], lhsT=w2t[:, kc * 128:(kc + 1) * 128], rhs=a16[:, kc, 0:T],
                        start=(kc == 0), stop=(kc == NDC - 1)), reads=[w2r_, a16_res[kc]], writes=[por])
                ws.release(pl[("c2", oc)])
                emit_residual(oc, po, por, c0)

        if kind == "four":
            ccb = P.sbuf("ccb", [128, 4, 1024], BF16)
            ccb_res = Res("ccb")
            for i in range(4):
                wt, wr = ws.get(cc_items[i], lookahead=2)
                P.op("act", lambda e, i=i, wt=wt: e.activation(out=ccb[:, i, :], in_=wt[:, 0:1024], func=AF.Identity), reads=[wr], writes=[ccb_res])
                ws.release(cc_items[i])
            fab = P.sbuf("fab", [128, 2, NDC, T], BF16)
            fab_res = [[Res(f"fab{i}_{d}") for d in range(NDC)] for i in range(2)]
            corr = P.sbuf("corr", [128, NDC, 2], F32)
            P.op("dve", lambda e: e.memset(corr[:], 0.0), writes=[dv_res])
            fl0 = P.sbuf("fl0", [128, 1], F32)
            P.op("dve", lambda e: e.tensor_scalar(out=fl0[:], in0=edge_sb[:, 0:1], scalar1=-float(S), scalar2=float(S), op0=ALU.mult, op1=ALU.add),
                 reads=[edge_res], writes=[dv_res])
            P.op("dve", lambda e: e.tensor_scalar(out=corr[:, :, 0], in0=mrow(0), scalar1=fl0[:, 0:1], scalar2=None, op0=ALU.mult),
                 reads=[msb_res], writes=[dv_res])

        def emit_four_mixer(tt):
            pl = plan[tt]
            c0 = halo_l
            for i in range(2):
                P.dma("act", fab[:, i], fin[:, i, :, tt * T:(tt + 1) * T], writes=fab_res[i])
            for i in range(2):
                for dc in range(NDC):
                    P.op("act", lambda e, i=i, dc=dc: e.activation(out=fab[:, i, dc, :], in_=fab[:, i, dc, :], func=AF.Identity,
                                                                   scale=dv[:, 0, dc:dc + 1]),
                         reads=VR + [fab_res[i][dc]], writes=[fab_res[i][dc]])
            if tt == 0:
                for dc in range(NDC):
                    P.op("dve", lambda e, dc=dc: e.tensor_tensor(out=fab[:, 0, dc, 0:2], in0=fab[:, 0, dc, 0:2], in1=corr[:, dc, :], op=ALU.add),
                         reads=VR + [fab_res[0][dc]], writes=[fab_res[0][dc]])
            for oc in range(NDC):
                g = oc // 4
                pj, pjr = ps[oc % 4], ps_res[oc % 4]
                n = 0
                for i in range(2):
                    for kc in range(4):
                        P.op("pe", lambda e, i=i, kc=kc, pj=pj, n=n: e.matmul(
                            pj[:], lhsT=ccb[:, kc, i * 512 + (oc % 4) * 128:i * 512 + (oc % 4 + 1) * 128], rhs=fab[:, i, 4 * g + kc, :],
                            start=(n == 0), stop=(n == 7)), reads=[ccb_res, fab_res[i][4 * g + kc]], writes=[pjr])
                        n += 1
                P.op("act", lambda e, pj=pj: e.activation(out=a16[:, oc, 0:T], in_=pj[:], func=AF.Identity), reads=[pjr], writes=[a16_res[oc]])
            for oc in range(NDC):
                w2t, w2r_ = ws.get(pl[("fw", oc)])
                po, por = ps[6 + oc % 2], ps_res[6 + oc % 2]
                for kc in range(NDC):
                    P.op("pe", lambda e, kc=kc, po=po: e.matmul(
                        po[:], lhsT=w2t[:, kc * 128:(kc + 1) * 128], rhs=a16[:, kc, 0:T],
                        start=(kc == 0), stop=(kc == NDC - 1)), reads=[w2r_, a16_res[kc]], writes=[por])
                ws.release(pl[("fw", oc)])
                emit_residual(oc, po, por, c0)

        if kind == "none":
            pebuf = P.sbuf("pebuf", [128, NDC, T], F32)
            pe_res = Res("pe")

        def emit_none_mixer(tt):
            P.dma("act", pebuf[:], pe_in[:, :, tt * T:(tt + 1) * T], writes=[pe_res])
            for dc in range(NDC):
                P.op("dve", lambda e, dc=dc: e.tensor_tensor(out=xs[:, dc, :], in0=xs[:, dc, :], in1=pebuf[:, dc, :], op=ALU.add),
                     reads=[pe_res], writes=[xs_res[dc]])
                P.op("act", lambda e, dc=dc: e.activation(out=xs[:, dc, :], in_=xs[:, dc, :], func=AF.Identity, scale=ALPHA),
                     reads=[xs_res[dc]], writes=[xs_res[dc]])

        if kind == "lin":
            prt = P.sbuf("prt", [128, 22, T], BF16)
            prt_res = Res("prt")
            petmp = [P.sbuf(f"petmp{i}", [128, T], F32) for i in range(2)]
            petmp_res = [Res(f"petmp{i}") for i in range(2)]

        def emit_lin_mixer(tt):
            pl = plan[tt]
            c0 = halo_l
            P.dma("act", prt[:], prodin[:, :, tt * T:(tt + 1) * T].rearrange("k p t -> p k t"), writes=[prt_res])
            for dc in range(NDC):
                k = dc % 2
                P.dma("act", petmp[k][:], pe_in[:, dc, tt * T:(tt + 1) * T], writes=[petmp_res[k]])
                P.op("dve", lambda e, dc=dc, k=k: e.tensor_tensor(out=xs[:, dc, :], in0=xs[:, dc, :], in1=petmp[k][:], op=ALU.add),
                     reads=[petmp_res[k]], writes=[xs_res[dc]])
            for half in range(2):
                wts = [ws.get(pl[("wo", half * 11 + i)]) for i in range(11)]
                for dc in range(NDC):
                    po, por = ps[6 + dc % 2], ps_res[6 + dc % 2]
                    for i in range(11):
                        P.op("pe", lambda e, i=i, dc=dc, po=po: e.matmul(
                            po[:], lhsT=wts[i][0][:, dc * 128:(dc + 1) * 128], rhs=prt[:, half * 11 + i, :],
                            start=(i == 0), stop=(i == 10)), reads=[wts[i][1], prt_res], writes=[por])
                    if half == 0:
                        emit_residual(dc, po, por, c0)
                    else:
                        P.op("dve", lambda e, dc=dc, po=po: e.scalar_tensor_tensor(
                            out=xs[:, dc, c0:c0 + T], in0=po[:], scalar=dv[:, 1, dc:dc + 1], in1=xs[:, dc, c0:c0 + T],
                            op0=ALU.mult, op1=ALU.add), reads=VR + [por], writes=[xs_res[dc]])
                for i in range(11):
                    ws.release(pl[("wo", half * 11 + i)])

        def emit_residual(dc, po, por, c0):
            P.op("act", lambda e: e.activation(out=xs[:, dc, c0:c0 + T], in_=xs[:, dc, c0:c0 + T], func=AF.Identity,
                                               scale=ALPHA, bias=dv[:, 2, dc:dc + 1]),
                 reads=VR + [xs_res[dc]], writes=[xs_res[dc]])
            P.op("dve", lambda e: e.scalar_tensor_tensor(out=xs[:, dc, c0:c0 + T], in0=po[:], scalar=dv[:, 1, dc:dc + 1],
                                                         in1=xs[:, dc, c0:c0 + T], op0=ALU.mult, op1=ALU.add),
                 reads=VR + [por], writes=[xs_res[dc]])

        for tt in range(NTT):
            c0 = halo_l
            P.dma("act", xs[:], xin[:, :, tt * T:tt * T + W], writes=xs_res)
            if kind == "pool":
                emit_pool_mixer(tt)
            elif kind == "conf":
                emit_conf_mixer(tt)
            elif kind == "four":
                emit_four_mixer(tt)
            elif kind == "none":
                emit_none_mixer(tt)
            elif kind == "lin":
                emit_lin_mixer(tt)
            emit_ln(c0, lambda dc: dv[:, 3, dc:dc + 1], lambda dc: dv[:, 4, dc:dc + 1])
            emit_ffn(tt, c0)

            def store(dc, tt=tt):
                pass
            emit_ln(c0, lambda dc: vraw[:, V_LNG1, dc:dc + 1], lambda dc: vraw[:, V_LNB1, dc:dc + 1])
            P.dma("pool", xout[:, :, tt * T:(tt + 1) * T], xs[:, :, c0:c0 + T], reads=xs_res)
        P.finish("sp")
        nc._stats = dict(P.n_inst)
    return nc


def col_table(v):
    return np.ascontiguousarray(np.asarray(v, np.float32).reshape(NDC, 128).T)


def to_xT(xb, q, halo_l, halo_r):
    t0 = q * TOK - halo_l
    t1 = (q + 1) * TOK + halo_r
    out = np.zeros((128, NDC, t1 - t0), np.float32)
    a, b = max(t0, 0), min(t1, S)
    blk = xb[a:b].reshape(b - a, NDC, 128).transpose(2, 1, 0)
    out[:, :, a - t0:b - t0] = blk
    return out


def from_xT(xt):
    return np.ascontiguousarray(xt.transpose(2, 1, 0).reshape(xt.shape[2], D))


def w_colblocks(w, nblk):
    K = w.shape[0] // 128
    return np.ascontiguousarray(w.reshape(K, 128, nblk, 128).transpose(2, 1, 0, 3).reshape(nblk, 128, K * 128))


def ffn_layouts(inputs, l):
    w1r = w_colblocks(np.asarray(inputs["ffn_w1"][l]), NF)
    w3r = w_colblocks(np.asarray(inputs["ffn_w3"][l]), NF)
    w2r = np.ascontiguousarray(np.asarray(inputs["ffn_w2"][l]).reshape(NF, 128, D))
    return w1r, w3r, w2r


def common_maps(inputs, l, mixvecs):
    vec = np.zeros((128, NV, NDC), np.float32)
    vec[:, V_LNG0] = col_table(inputs["ln_g"][l, 0])
    vec[:, V_LNB0] = col_table(inputs["ln_b"][l, 0])
    vec[:, V_LNG1] = col_table(inputs["ln_g"][l, 1])
    vec[:, V_LNB1] = col_table(inputs["ln_b"][l, 1])
    for i, v in enumerate(mixvecs):
        vec[:, V_MX0 + i] = col_table(v)
    modb = np.ascontiguousarray(np.asarray(inputs["mod_b"][l], np.float32).reshape(96, 128).T)
    modw = w_colblocks(np.asarray(inputs["mod_w"][l]), 96)
    w1r, w3r, w2r = ffn_layouts(inputs, l)
    return dict(vec=vec, modb=modb, modw=modw, w1r=w1r, w3r=w3r, w2r=w2r)


def cond_T(inputs, b):
    c = np.stack([np.asarray(inputs["c"][b], np.float32), np.asarray(inputs["c_ctx"], np.float32)], axis=-1)
    return np.ascontiguousarray(c.reshape(NDC, 128, 2).transpose(1, 0, 2))


def edge_flags(q):
    e = np.ones((128, 2), np.float32)
    if q == 0:
        e[:, 0] = 0.0
    if q == 3:
        e[:, 1] = 0.0
    return e


_NC_CACHE = {}


def run_pool_layer(inputs, l, x):
    HL = HR = 8
    key = ("pool",)
    if key not in _NC_CACHE:
        _NC_CACHE[key] = build_layer("pool", HL, HR)
    nc = _NC_CACHE[key]
    cm = common_maps(inputs, l, [inputs["pool_b"][0], inputs["pool_scale"][0]])
    pw = np.asarray(inputs["pool_w"][0], np.float32)
    poolw = np.ascontiguousarray(pw.reshape(4, 4, 128, 512).reshape(16, 128, 512))
    in_maps = []
    for core in range(NCORE):
        b, q = divmod(core, 4)
        t = np.arange(q * TOK, (q + 1) * TOK)
        pinv = np.zeros((4, TOK), np.float32)
        for g, win in enumerate((2, 4, 8, 16)):
            lo = np.clip(t - win // 2, 0, S)
            hi = np.clip(t - win // 2 + win, 0, S)
            pinv[g] = 1.0 / (hi - lo).astype(np.float32)
        m = dict(cm)
        m.update(xin=to_xT(x[b], q, HL, HR), cT=cond_T(inputs, b), edge=edge_flags(q), poolw=poolw, pinv=pinv)
        in_maps.append(m)
    res = run_bass_kernel_spmd(nc, in_maps, core_ids=list(range(NCORE)))
    out = np.empty_like(x)
    for core in range(NCORE):
        b, q = divmod(core, 4)
        out[b, q * TOK:(q + 1) * TOK] = from_xT(res.results[core]["xout"])
    return out


def run_conf_layer(inputs, l, x):
    HL = HR = 15
    key = ("conf",)
    if key not in _NC_CACHE:
        _NC_CACHE[key] = build_layer("conf", HL, HR)
    nc = _NC_CACHE[key]
    cm = common_maps(inputs, l, [inputs["cv_b2"][0]])
    cvw1 = w_colblocks(np.asarray(inputs["cv_w1"][0]), 32)
    cvw2 = w_colblocks(np.asarray(inputs["cv_w2"][0]), 16)
    b1 = np.asarray(inputs["cv_b1"][0], np.float32)
    cvv = np.concatenate([
        b1.reshape(32, 128).T,
        col_table(inputs["cv_dwb"][0]), col_table(inputs["cv_ln_g"][0]), col_table(inputs["cv_ln_b"][0]),
        np.asarray(inputs["cv_dw"][0], np.float32).reshape(31, NDC, 128).transpose(2, 1, 0).reshape(128, NDC * 31),
    ], axis=1).astype(np.float32)
    cvv = np.ascontiguousarray(cvv)
    in_maps = []
    for core in range(NCORE):
        b, q = divmod(core, 4)
        m = dict(cm)
        m.update(xin=to_xT(x[b], q, HL, HR), cT=cond_T(inputs, b), edge=edge_flags(q), cvw1=cvw1, cvw2=cvw2, cvv=cvv)
        in_maps.append(m)
    res = run_bass_kernel_spmd(nc, in_maps, core_ids=list(range(NCORE)))
    out = np.empty_like(x)
    for core in range(NCORE):
        b, q = divmod(core, 4)
        out[b, q * TOK:(q + 1) * TOK] = from_xT(res.results[core]["xout"])
    return out


def build_seqdft():
    nc = bass.Bass("TRN2", target_bir_lowering=False)
    xtok = nc.dram_tensor("xtok", [64, 128, D], F32, kind="ExternalInput").ap()
    tab = nc.dram_tensor("tab", [64, 128, 4096], BF16, kind="ExternalInput").ap()
    f12 = nc.dram_tensor("f12", [128, 2, NDC, TOK], BF16, kind="ExternalOutput").ap()
    NB = 6
    with ExitStack() as st:
        P = Prog(nc, st)
        ws = WStream(P)
        bring = [P.sbuf(f"bring{i}", [128, 2048], BF16) for i in range(NB)]
        bring_res = [Res(f"bring{i}") for i in range(NB)]
        obuf = [P.sbuf(f"obuf{i}", [128, 2, TOK], BF16) for i in range(2)]
        obuf_res = [Res(f"obuf{i}") for i in range(2)]
        ps = [P.psum(f"ps{i}", [128, 512]) for i in range(8)]
        ps_res = [Res(f"ps{i}") for i in range(8)]
        seq = [(cp, trig, sc) for cp in range(8) for trig in range(2) for sc in range(64)]
        a_items = [ws.add(xtok[sc][:, cp * 256:(cp + 1) * 256], width=256) for (cp, trig, sc) in seq]
        nb_issued = [0]

        def issue_b(upto):
            while nb_issued[0] <= min(upto, len(seq) - 1):
                i = nb_issued[0]
                cp, trig, sc = seq[i]
                P.dma("sp", bring[i % NB][:], tab[sc][:, trig * 2048:(trig + 1) * 2048], writes=[bring_res[i % NB]])
                nb_issued[0] += 1

        for i, (cp, trig, sc) in enumerate(seq):
            issue_b(i + 3)
            at, ar = ws.get(a_items[i], lookahead=4)
            bt, br = bring[i % NB], bring_res[i % NB]
            for c2 in range(2):
                for kt in range(4):
                    b_ = c2 * 4 + kt
                    P.op("pe", lambda e, c2=c2, kt=kt, b_=b_, at=at, bt=bt: e.matmul(
                        ps[b_][:], lhsT=at[:, c2 * 128:(c2 + 1) * 128], rhs=bt[:, kt * 512:(kt + 1) * 512],
                        start=(sc == 0), stop=(sc == 63)), reads=[ar, br], writes=[ps_res[b_]])
            ws.release(a_items[i])
            if sc == 63:
                ob, obr = obuf[(cp * 2 + trig) % 2], obuf_res[(cp * 2 + trig) % 2]
                for c2 in range(2):
                    for kt in range(4):
                        b_ = c2 * 4 + kt
                        eng = "act" if kt % 2 == 0 else "dve"
                        if eng == "act":
                            P.op("act", lambda e, c2=c2, kt=kt, b_=b_, ob=ob: e.activation(out=ob[:, c2, kt * 512:(kt + 1) * 512], in_=ps[b_][:], func=AF.Identity),
                                 reads=[ps_res[b_]], writes=[obr])
                        else:
                            P.op("dve", lambda e, c2=c2, kt=kt, b_=b_, ob=ob: e.tensor_copy(out=ob[:, c2, kt * 512:(kt + 1) * 512], in_=ps[b_][:]),
                                 reads=[ps_res[b_]], writes=[obr])
                P.dma("pool", f12[:, trig, cp * 2:cp * 2 + 2, :], ob[:], reads=[obr])
        P.finish("sp")
        nc._stats = dict(P.n_inst)
    return nc


def _bf16(a):
    import ml_dtypes
    return np.asarray(a, np.float32).astype(ml_dtypes.bfloat16)


def run_four_layer(inputs, l, x):
    if ("seqdft",) not in _NC_CACHE:
        _NC_CACHE[("seqdft",)] = build_seqdft()
    if ("four",) not in _NC_CACHE:
        _NC_CACHE[("four",)] = build_layer("four", 0, 0)
    s_idx = np.arange(S, dtype=np.int64)[:, None]
    in_maps = []
    for core in range(NCORE):
        b, q = divmod(core, 4)
        k_idx = np.arange(q * TOK, (q + 1) * TOK, dtype=np.int64)[None, :]
        ang = (2.0 * np.pi / S) * ((s_idx * k_idx) % S).astype(np.float64)
        tab = np.concatenate([np.cos(ang), np.sin(ang)], axis=1)
        in_maps.append(dict(xtok=np.ascontiguousarray(x[b].reshape(64, 128, D)), tab=_bf16(tab).reshape(64, 128, 4096)))
    r1 = run_bass_kernel_spmd(_NC_CACHE[("seqdft",)], in_maps, core_ids=list(range(NCORE)))
    cm = common_maps(inputs, l, [inputs["ft_b"][0]])
    c_idx = np.arange(512, dtype=np.int64)
    angc = (2.0 * np.pi / 512) * ((c_idx[:, None] * c_idx[None, :]) % 512).astype(np.float64)
    ccsc = (np.concatenate([np.cos(angc), -np.sin(angc)], axis=1) / 2048.0).astype(np.float32).reshape(4, 128, 1024)
    ftw = w_colblocks(np.asarray(inputs["ft_w"][0]), 16)
    in_maps = []
    for core in range(NCORE):
        b, q = divmod(core, 4)
        m = dict(cm)
        m.update(xin=to_xT(x[b], q, 0, 0), cT=cond_T(inputs, b), edge=edge_flags(q), fin=r1.results[core]["f12"], ccsc=ccsc, ftw=ftw)
        in_maps.append(m)
    res = run_bass_kernel_spmd(_NC_CACHE[("four",)], in_maps, core_ids=list(range(NCORE)))
    out = np.empty_like(x)
    for core in range(NCORE):
        b, q = divmod(core, 4)
        out[b, q * TOK:(q + 1) * TOK] = from_xT(res.results[core]["xout"])
    return out


def _pos_embed_table():
    rows, cols, dim = S // 64, 64, D
    quarter = dim // 4
    omega = (1.0 / (10000.0 ** (np.arange(quarter, dtype=np.float32) / np.float32(quarter)))).astype(np.float32)
    ar = np.arange(rows, dtype=np.float32)[:, None] * omega[None]
    ac = np.arange(cols, dtype=np.float32)[:, None] * omega[None]
    er = np.concatenate([np.sin(ar), np.cos(ar)], axis=-1)
    ec = np.concatenate([np.sin(ac), np.cos(ac)], axis=-1)
    pe = np.concatenate([np.broadcast_to(er[:, None, :], (rows, cols, dim // 2)),
                         np.broadcast_to(ec[None, :, :], (rows, cols, dim // 2))], axis=-1)
    return pe.reshape(rows * cols, dim).astype(np.float32)


def run_layer0_partial(inputs, x):
    key = ("none",)
    if key not in _NC_CACHE:
        _NC_CACHE[key] = build_layer("none", 0, 0)
    nc = _NC_CACHE[key]
    cm = common_maps(inputs, 0, [])
    pe = _pos_embed_table()
    in_maps = []
    for core in range(NCORE):
        b, q = divmod(core, 4)
        m = dict(cm)
        m.update(xin=to_xT(x[b], q, 0, 0), cT=cond_T(inputs, b), edge=edge_flags(q), pe=to_xT(pe, q, 0, 0))
        in_maps.append(m)
    res = run_bass_kernel_spmd(nc, in_maps, core_ids=list(range(NCORE)))
    out = np.empty_like(x)
    for core in range(NCORE):
        b, q = divmod(core, 4)
        out[b, q * TOK:(q + 1) * TOK] = from_xT(res.results[core]["xout"])
    return out


def kernel(**inputs):
    inputs = {k: np.asarray(v) for k, v in inputs.items()}
    x = np.ascontiguousarray(inputs["x"], dtype=np.float32)
    x = run_rg_layer(inputs, x)
    x = run_pool_layer(inputs, 1, x)
    x = run_conf_layer(inputs, 2, x)
    x = run_four_layer(inputs, 3, x)
    return x.astype(np.float32)


RSUB = 88
NSUB = 32
RG_HL, RG_HR = 1, 2
CTXL = 256


def build_rg(phase):
    nc = bass.Bass("TRN2", target_bir_lowering=False)
    NTW = RG_HL + TOK + RG_HR
    d = {}

    def din(name, shape, dt=F32):
        d[name] = nc.dram_tensor(name, list(shape), dt, kind="ExternalInput").ap()
        return d[name]

    xin = din("xin", [128, NDC, NTW])
    pein = din("pein", [128, NDC, NTW])
    ctxin = din("ctxin", [128, NDC, CTXL])
    modb = din("modb", [128, 96])
    cT = din("cT", [128, NDC, 2])
    modw = din("modw", [32, 128, NDC * 128])
    edge = din("edge", [128, 2])
    rv = din("rv", [128, 11, NSUB])
    wxr = din("wxr", [NSUB, 128, NDC * RSUB])
    gwr = din("gwr", [32, 128, 704])
    if phase == "B":
        wgr = din("wgr", [NSUB, 128, NDC * RSUB])
        summ = din("summ", [NCORE, 128, 4, NSUB])
        ctxs = din("ctxs", [128, 2, NSUB])
        mfb = din("mfb", [128, 2, NCORE])
        prod = nc.dram_tensor("prod", [NSUB, RSUB, TOK], BF16, kind="ExternalOutput").ap()
    else:
        sout = nc.dram_tensor("sout", [128, 6, NSUB], F32, kind="ExternalOutput").ap()

    with ExitStack() as st:
        P = Prog(nc, st)
        ws = WStream(P, nstg=3, nring=5)
        a16f = P.sbuf("a16f", [128, NDC, NTW], BF16)
        a16f_res = [Res(f"a16f{i}") for i in range(NDC)]
        xbpre = P.sbuf("xbpre", [128, NTW], F32)
        xbpre_res = Res("xbpre")
        xb = [P.sbuf(f"xb{i}", [128, TOK], F32) for i in range(2)]
        xb_res = [Res(f"xb{i}") for i in range(2)]
        xb16 = [P.sbuf(f"xb16_{i}", [128, TOK], BF16) for i in range(2)]
        xb16_res = [Res(f"xb16_{i}") for i in range(2)]
        abuf = P.sbuf("abuf", [128, NTW], F32)
        bbuf = P.sbuf("bbuf", [128, NTW], F32)
        tbuf = P.sbuf("tbuf", [128, NTW], F32)
        ab_res, bb_res, tb_res = Res("abuf"), Res("bbuf"), Res("tbuf")
        ybuf = [P.sbuf(f"ybuf{i}", [128, NTW], F32) for i in range(2)]
        yb_res = [Res(f"ybuf{i}") for i in range(2)]
        xt, xt_res = [abuf, bbuf], [ab_res, bb_res]
        pt, pt_res = [tbuf, ybuf[0]], [tb_res, yb_res[0]]
        rv_sb = P.sbuf("rv_sb", [128, 11, NSUB], F32)
        cp_sb = P.sbuf("cp_sb", [128, 2, NSUB], F32)
        rv_res = Res("rv")
        modb_sb = P.sbuf("modb_sb", [128, 96], F32)
        cs = P.sbuf("cs", [128, NDC, 2], F32)
        cs_res = Res("cs")
        msb = P.sbuf("msb", [128, 32, 2], F32)
        msb_res = Res("msb")
        a1 = P.sbuf("a1", [128, NDC, 2], F32)
        edge_sb = P.sbuf("edge_sb", [128, 2], F32)
        edge_res = Res("edge")
        rsum = P.sbuf("rsum", [128, 1], F32)
        rsum_res = Res("rsum")
        if phase == "A":
            so_sb = P.sbuf("so_sb", [128, 6, NSUB], F32)
            so_res = Res("so")
        else:
            summ_sb = P.sbuf("summ_sb", [128, NCORE, 4, NSUB], F32)
            carry = P.sbuf("carry", [128, 2, NSUB], F32)
            mfb_sb = P.sbuf("mfb_sb", [128, 2, NCORE], F32)
            ctmp = P.sbuf("ctmp", [128, NSUB], F32)
            carry_res = Res("carry")
            gl = [P.sbuf(f"gl{i}", [128, T], F32) for i in range(2)]
            gl_res = [Res(f"gl{i}") for i in range(2)]
            pr16 = [P.sbuf(f"pr16_{i}", [128, TOK], BF16) for i in range(2)]
            pr16_res = [Res(f"pr16_{i}") for i in range(2)]
        ps = [P.psum(f"ps{i}", [128, 512]) for i in range(8)]
        ps_res = [Res(f"ps{i}") for i in range(8)]

        mod_items = [ws.add(modw[oc], cast=False) for oc in range(32)]
        passes = ["ctx", "lat"] if phase == "A" else ["lat"]
        plan = {}
        for pss in passes:
            for n in range(16):
                for s in range(2):
                    plan[(pss, "wx", 2 * n + s)] = ws.add(wxr[2 * n + s], width=NDC * RSUB)
                for dd in range(2):
                    plan[(pss, "gw", dd, n)] = ws.add(gwr[dd * 16 + n], width=704)
                if phase == "B":
                    for s in range(2):
                        plan[(pss, "wg", 2 * n + s)] = ws.add(wgr[2 * n + s], width=NDC * RSUB)

        P.dma("act", rv_sb[:], rv, writes=[rv_res])
        P.dma("act", modb_sb[:], modb, writes=[rv_res])
        P.dma("act", cs[:], cT, writes=[cs_res])
        P.dma("act", edge_sb[:], edge, writes=[edge_res])
        P.op("act", lambda e: e.activation(out=cs[:], in_=cs[:], func=AF.Silu), reads=[cs_res], writes=[cs_res])
        P.op("act", lambda e: e.activation(out=cp_sb[:], in_=rv_sb[:, 9:11, :], func=AF.Sigmoid), reads=[rv_res], writes=[rv_res])
        P.op("act", lambda e: e.activation(out=cp_sb[:], in_=cp_sb[:], func=AF.Ln), reads=[rv_res], writes=[rv_res])
        P.op("dve", lambda e: e.tensor_scalar(out=cp_sb[:], in0=cp_sb[:], scalar1=8.0, scalar2=None, op0=ALU.mult), reads=[rv_res], writes=[rv_res])
        mps, mps_res = ps[7], ps_res[7]
        for oc in range(32):
            wt, wr = ws.get(mod_items[oc], lookahead=1)
            for kc in range(NDC):
                P.op("pe", lambda e, oc=oc, kc=kc, wt=wt: e.matmul(
                    mps[:, oc * 2:(oc + 1) * 2], lhsT=wt[:, kc * 128:(kc + 1) * 128], rhs=cs[:, kc, :],
                    start=(kc == 0), stop=(kc == NDC - 1)), reads=[wr, cs_res], writes=[mps_res])
            ws.release(mod_items[oc])
        for j in range(2):
            P.op("dve", lambda e, j=j: e.tensor_tensor(
                out=msb[:, :, j], in0=mps[:, 0:64].rearrange("p (o j) -> p o j", j=2)[:, :, j],
                in1=modb_sb[:, 0:32], op=ALU.add), reads=[mps_res, rv_res], writes=[msb_res])
        P.op("dve", lambda e: e.tensor_scalar(out=a1[:], in0=msb[:, 16:32, :], scalar1=1.0, scalar2=None, op0=ALU.add),
             reads=[msb_res], writes=[msb_res])
        if phase == "A":
            P.op("pool", lambda e: e.memset(so_sb[:], 0.0), writes=[so_res])
        else:
            P.dma("act", summ_sb[:], summ.rearrange("r p a s -> p r a s"), writes=[carry_res])
            P.dma("act", carry[:], ctxs, writes=[carry_res])
            P.dma("act", mfb_sb[:], mfb, writes=[carry_res])
            for dd in range(2):
                order = range(NCORE) if dd == 0 else range(NCORE - 1, -1, -1)
                for r in order:
                    A_r = summ_sb[:, r, 2 * dd, :]
                    B_r = summ_sb[:, r, 2 * dd + 1, :]
                    mk = mfb_sb[:, dd, r:r + 1]
                    P.op("dve", lambda e, A_r=A_r, mk=mk: e.tensor_scalar(out=ctmp[:], in0=A_r, scalar1=-1.0, scalar2=mk, op0=ALU.add, op1=ALU.mult),
                         reads=[carry_res], writes=[carry_res])
                    P.op("dve", lambda e: e.tensor_scalar(out=ctmp[:], in0=ctmp[:], scalar1=1.0, scalar2=None, op0=ALU.add),
                         reads=[carry_res], writes=[carry_res])
                    P.op("dve", lambda e, dd=dd: e.tensor_tensor(out=carry[:, dd, :], in0=carry[:, dd, :], in1=ctmp[:], op=ALU.mult),
                         reads=[carry_res], writes=[carry_res])
                    P.op("dve", lambda e, B_r=B_r, mk=mk: e.tensor_scalar(out=ctmp[:], in0=B_r, scalar1=mk, scalar2=None, op0=ALU.mult),
                         reads=[carry_res], writes=[carry_res])
                    P.op("dve", lambda e, dd=dd: e.tensor_tensor(out=carry[:, dd, :], in0=carry[:, dd, :], in1=ctmp[:], op=ALU.add),
                         reads=[carry_res], writes=[carry_res])

        def run_pass(pss):
            ctx = pss == "ctx"
            NTK = CTXL if ctx else TOK
            NW = RG_HL + NTK + RG_HR
            j = 1 if ctx else 0
            for dc in range(NDC):
                k = dc % 2
                if ctx:
                    P.dma("act", xt[k][:, RG_HL:RG_HL + NTK], ctxin[:, dc, :], writes=[xt_res[k]])
                    src = xt[k][:, RG_HL:RG_HL + NTK]
                    P.op("pool", lambda e, dc=dc: e.memset(a16f[:, dc, 0:NW], 0.0), writes=[a16f_res[dc]])
                    P.op("act", lambda e, dc=dc, src=src: e.activation(out=a16f[:, dc, RG_HL:RG_HL + NTK], in_=src, func=AF.Identity,
                                                                       scale=a1[:, dc, j:j + 1], bias=msb[:, dc, j:j + 1]),
                         reads=[xt_res[k], msb_res], writes=[a16f_res[dc]])
                else:
                    P.dma("act", xt[k][:], xin[:, dc, :], writes=[xt_res[k]])
                    P.dma("act", pt[k][:], pein[:, dc, :], writes=[pt_res[k]])
                    P.op("dve", lambda e, k=k: e.tensor_tensor(out=xt[k][:], in0=xt[k][:], in1=pt[k][:], op=ALU.add),
                         reads=[pt_res[k]], writes=[xt_res[k]])
                    P.op("act", lambda e, dc=dc, k=k: e.activation(out=a16f[:, dc, :], in_=xt[k][:], func=AF.Identity,
                                                                   scale=a1[:, dc, j:j + 1], bias=msb[:, dc, j:j + 1]),
                         reads=[xt_res[k], msb_res], writes=[a16f_res[dc]])
                    P.op("dve", lambda e, dc=dc: e.tensor_scalar(out=a16f[:, dc, 0:RG_HL], in0=a16f[:, dc, 0:RG_HL], scalar1=edge_sb[:, 0:1],
                                                                 scalar2=None, op0=ALU.mult), reads=[edge_res], writes=[a16f_res[dc]])
                    P.op("dve", lambda e, dc=dc: e.tensor_scalar(out=a16f[:, dc, RG_HL + NTK:NW], in0=a16f[:, dc, RG_HL + NTK:NW], scalar1=edge_sb[:, 1:2],
                                                                 scalar2=None, op0=ALU.mult), reads=[edge_res], writes=[a16f_res[dc]])
            coltiles = [(c, min(c + 512, NW)) for c in range(0, NW, 512)]
            ctiles = [(c, min(c + 512, NTK)) for c in range(0, NTK, 512)]
            Rr = slice(0, RSUB)
            for n in range(16):
                for s in range(2):
                    sidx = 2 * n + s
                    wx, wxres = ws.get(plan[(pss, "wx", sidx)], lookahead=2)
                    for ci, (ca, cb) in enumerate(coltiles):
                        pp, ppr = ps[ci % 2], ps_res[ci % 2]
                        for kc in range(NDC):
                            P.op("pe", lambda e, kc=kc, pp=pp, ca=ca, cb=cb, wx=wx: e.matmul(
                                pp[Rr, 0:cb - ca], lhsT=wx[:, kc * RSUB:(kc + 1) * RSUB], rhs=a16f[:, kc, ca:cb],
                                start=(kc == 0), stop=(kc == NDC - 1)), reads=[wxres, a16f_res[kc]], writes=[ppr])
                        P.op("act", lambda e, pp=pp, ca=ca, cb=cb: e.activation(out=xbpre[Rr, ca:cb], in_=pp[Rr, 0:cb - ca], func=AF.Identity),
                             reads=[ppr], writes=[xbpre_res])
                    ws.release(plan[(pss, "wx", sidx)])
                    P.op("dve", lambda e, s=s, sidx=sidx: e.tensor_scalar(
                        out=xb[s][Rr, 0:NTK], in0=xbpre[Rr, 0:NTK], scalar1=rv_sb[Rr, 0, sidx:sidx + 1], scalar2=rv_sb[Rr, 4, sidx:sidx + 1],
                        op0=ALU.mult, op1=ALU.add), reads=[xbpre_res, rv_res], writes=[xb_res[s]])
                    for jj in range(1, 4):
                        P.op("dve", lambda e, s=s, sidx=sidx, jj=jj: e.scalar_tensor_tensor(
                            out=xb[s][Rr, 0:NTK], in0=xbpre[Rr, jj:jj + NTK], scalar=rv_sb[Rr, jj, sidx:sidx + 1], in1=xb[s][Rr, 0:NTK],
                            op0=ALU.mult, op1=ALU.add), reads=[xbpre_res, rv_res], writes=[xb_res[s]])
                    P.op("act", lambda e, s=s: e.activation(out=xb16[s][Rr, 0:NTK], in_=xb[s][Rr, 0:NTK], func=AF.Identity),
                         reads=[xb_res[s]], writes=[xb16_res[s]])
                for dd in range(2):
                    gw, gwres = ws.get(plan[(pss, "gw", dd, n)], lookahead=2)
                    for so in range(2):
                        sidx = 2 * n + so
                        for ci, (ca, cb) in enumerate(ctiles):
                            pr_, prr = ps[2 + ci % 2], ps_res[2 + ci % 2]
                            pi_, pir = ps[4 + ci % 2], ps_res[4 + ci % 2]
                            for si in range(2):
                                P.op("pe", lambda e, si=si, so=so, pr_=pr_, ca=ca, cb=cb, gw=gw: e.matmul(
                                    pr_[Rr, 0:cb - ca], lhsT=gw[Rr, (si * 2 + so) * RSUB:(si * 2 + so + 1) * RSUB], rhs=xb16[si][Rr, ca:cb],
                                    start=(si == 0), stop=(si == 1)), reads=[gwres, xb16_res[si]], writes=[prr])
                            for si in range(2):
                                P.op("pe", lambda e, si=si, so=so, pi_=pi_, ca=ca, cb=cb, gw=gw: e.matmul(
                                    pi_[Rr, 0:cb - ca], lhsT=gw[Rr, 352 + (si * 2 + so) * RSUB:352 + (si * 2 + so + 1) * RSUB], rhs=xb16[si][Rr, ca:cb],
                                    start=(si == 0), stop=(si == 1)), reads=[gwres, xb16_res[si]], writes=[pir])
                            P.op("act", lambda e, pr_=pr_, ca=ca, cb=cb, dd=dd, sidx=sidx: e.activation(
                                out=abuf[Rr, ca:cb], in_=pr_[Rr, 0:cb - ca], func=AF.Sigmoid, bias=rv_sb[Rr, 5 + dd, sidx:sidx + 1]),
                                reads=[prr, rv_res], writes=[ab_res])
                            P.op("act", lambda e, pi_=pi_, ca=ca, cb=cb, dd=dd, sidx=sidx: e.activation(
                                out=bbuf[Rr, ca:cb], in_=pi_[Rr, 0:cb - ca], func=AF.Sigmoid, bias=rv_sb[Rr, 7 + dd, sidx:sidx + 1]),
                                reads=[pir, rv_res], writes=[bb_res])
                        if phase == "A" and not ctx:
                            P.op("dve", lambda e: e.reduce_sum(out=rsum[Rr, :], in_=abuf[Rr, 0:NTK], axis=mybir.AxisListType.X),
                                 reads=[ab_res], writes=[rsum_res])
                            P.op("act", lambda e, dd=dd, sidx=sidx: e.activation(out=so_sb[Rr, 2 * dd, sidx:sidx + 1], in_=rsum[Rr, :], func=AF.Exp,
                                                                               scale=cp_sb[Rr, dd, sidx:sidx + 1]),
                                 reads=[rsum_res, rv_res], writes=[so_res])
                        P.op("act", lambda e, dd=dd, sidx=sidx: e.activation(out=abuf[Rr, 0:NTK], in_=abuf[Rr, 0:NTK], func=AF.Exp,
                                                                           scale=cp_sb[Rr, dd, sidx:sidx + 1]),
                             reads=[ab_res, rv_res], writes=[ab_res])
                        P.op("dve", lambda e: e.tensor_tensor(out=tbuf[Rr, 0:NTK], in0=abuf[Rr, 0:NTK], in1=abuf[Rr, 0:NTK], op=ALU.mult),
                             reads=[ab_res], writes=[tb_res])
                        P.op("dve", lambda e: e.tensor_scalar(out=tbuf[Rr, 0:NTK], in0=tbuf[Rr, 0:NTK], scalar1=-1.0, scalar2=1.0, op0=ALU.mult, op1=ALU.add),
                             reads=[tb_res], writes=[tb_res])
                        P.op("act", lambda e: e.activation(out=tbuf[Rr, 0:NTK], in_=tbuf[Rr, 0:NTK], func=AF.Sqrt), reads=[tb_res], writes=[tb_res])
                        P.op("dve", lambda e, so=so: e.tensor_tensor(out=bbuf[Rr, 0:NTK], in0=bbuf[Rr, 0:NTK], in1=xb[so][Rr, 0:NTK], op=ALU.mult),
                             reads=[bb_res, xb_res[so]], writes=[bb_res])
                        P.op("dve", lambda e: e.tensor_tensor(out=bbuf[Rr, 0:NTK], in0=bbuf[Rr, 0:NTK], in1=tbuf[Rr, 0:NTK], op=ALU.mult),
                             reads=[bb_res, tb_res], writes=[bb_res])
                        if phase == "B":
                            init = carry[Rr, dd, sidx:sidx + 1]
                            dst, dres = (ybuf[so], yb_res[so]) if dd == 0 else (tbuf, tb_res)
                        else:
                            init = 0.0
                            dst, dres = tbuf, tb_res
                        if dd == 0:
                            P.op("dve", lambda e, dst=dst, init=init: e.tensor_tensor_scan(
                                out=dst[Rr, 0:NTK], data0=abuf[Rr, 0:NTK], data1=bbuf[Rr, 0:NTK], initial=init, op0=ALU.mult, op1=ALU.add),
                                reads=[ab_res, bb_res] + ([carry_res] if phase == "B" else []), writes=[dres])
                        else:
                            P.op("dve", lambda e, dst=dst, init=init: e.tensor_tensor_scan(
                                out=dst[Rr, NTK - 1::-1] if False else dst[Rr, 0:NTK][:, ::-1], data0=abuf[Rr, 0:NTK][:, ::-1], data1=bbuf[Rr, 0:NTK][:, ::-1],
                                initial=init, op0=ALU.mult, op1=ALU.add),
                                reads=[ab_res, bb_res] + ([carry_res] if phase == "B" else []), writes=[dres])
                        if phase == "A":
                            col = NTK - 1 if dd == 0 else 0
                            row = (4 + dd) if ctx else (2 * dd + 1)
                            P.op("act", lambda e, col=col, row=row, sidx=sidx: e.activation(out=so_sb[Rr, row, sidx:sidx + 1], in_=tbuf[Rr, col:col + 1], func=AF.Identity),
                                 reads=[tb_res], writes=[so_res])
                        elif dd == 1:
                            P.op("dve", lambda e, so=so: e.tensor_tensor(out=ybuf[so][Rr, 0:NTK], in0=ybuf[so][Rr, 0:NTK], in1=tbuf[Rr, 0:NTK], op=ALU.add),
                                 reads=[tb_res], writes=[yb_res[so]])
                    ws.release(plan[(pss, "gw", dd, n)])
                if phase == "B":
                    for so in range(2):
                        sidx = 2 * n + so
                        wg, wgres = ws.get(plan[(pss, "wg", sidx)], lookahead=2)
                        pb, pbr = pr16[so], pr16_res[so]
                        for ci, (ca, cb) in enumerate(ctiles):
                            pg, pgr = ps[6 + ci % 2], ps_res[6 + ci % 2]
                            for kc in range(NDC):
                                P.op("pe", lambda e, kc=kc, pg=pg, ca=ca, cb=cb, wg=wg: e.matmul(
                                    pg[Rr, 0:cb - ca], lhsT=wg[:, kc * RSUB:(kc + 1) * RSUB], rhs=a16f[:, kc, RG_HL + ca:RG_HL + cb],
                                    start=(kc == 0), stop=(kc == NDC - 1)), reads=[wgres, a16f_res[kc]], writes=[pgr])
                            k = ci % 2
                            P.op("act", lambda e, pg=pg, k=k, ca=ca, cb=cb: e.activation(out=gl[k][Rr, 0:cb - ca], in_=pg[Rr, 0:cb - ca], func=AF.Gelu_apprx_tanh),
                                 reads=[pgr], writes=[gl_res[k]])
                            P.op("dve", lambda e, k=k, so=so, ca=ca, cb=cb, pb=pb: e.tensor_tensor(
                                out=pb[Rr, ca:cb], in0=ybuf[so][Rr, ca:cb], in1=gl[k][Rr, 0:cb - ca], op=ALU.mult),
                                reads=[yb_res[so], gl_res[k]], writes=[pbr])
                        ws.release(plan[(pss, "wg", sidx)])
                        P.dma("pool", prod[sidx], pb[Rr, :], reads=[pbr])

        for pss in passes:
            run_pass(pss)
        if phase == "A":
            P.dma("pool", sout, so_sb[:], reads=[so_res])
        P.finish("sp")
        nc._stats = dict(P.n_inst)
    return nc


def _pad128(a):
    shp = list(a.shape)
    shp[-2] = 128
    out = np.zeros(shp, np.float32)
    out[..., :a.shape[-2], :] = a
    return out


def rg_col(v):
    return np.asarray(v, np.float32).reshape(NSUB, RSUB).T


def run_rg_layer(inputs, x):
    for ph in ("A", "B"):
        if ("rg", ph) not in _NC_CACHE:
            _NC_CACHE[("rg", ph)] = build_rg(ph)
    if ("lin",) not in _NC_CACHE:
        _NC_CACHE[("lin",)] = build_layer("lin", 0, 0)
    pe = _pos_embed_table()
    modw = w_colblocks(np.asarray(inputs["mod_w"][0][:, 0:4096]), 32)
    modb_full = np.ascontiguousarray(np.asarray(inputs["mod_b"][0], np.float32).reshape(96, 128).T)
    rvt = np.zeros((128, 11, NSUB), np.float32)
    cw = np.asarray(inputs["rg_conv_w"][0], np.float32)
    for j in range(4):
        rvt[:RSUB, j] = rg_col(cw[j])
    rvt[:RSUB, 4] = rg_col(inputs["rg_conv_b"][0])
    for dd in range(2):
        rvt[:RSUB, 5 + dd] = rg_col(inputs["rg_br"][0, dd])
        rvt[:RSUB, 7 + dd] = rg_col(inputs["rg_bi"][0, dd])
        rvt[:RSUB, 9 + dd] = rg_col(inputs["rg_lam"][0, dd])
    rvt[RSUB:, 9:11] = 1.0

    def sub_cols(w):
        return np.ascontiguousarray(np.asarray(w, np.float32).reshape(NDC, 128, NSUB, RSUB).transpose(2, 1, 0, 3).reshape(NSUB, 128, NDC * RSUB))
    wxr = sub_cols(inputs["rg_w_x"][0])
    wgr = sub_cols(inputs["rg_w_gate"][0])
    gw = np.zeros((32, 128, 704), np.float32)
    for dd in range(2):
        for n in range(16):
            for k_, nm in enumerate(("rg_wr", "rg_wi")):
                blk = np.asarray(inputs[nm][0, dd, n], np.float32).reshape(2, RSUB, 2, RSUB).transpose(1, 0, 2, 3).reshape(RSUB, 352)
                gw[dd * 16 + n, :RSUB, k_ * 352:(k_ + 1) * 352] = blk
    base = []
    for core in range(NCORE):
        b, q = divmod(core, 4)
        ctxT = np.ascontiguousarray(np.asarray(inputs["ctx"][b], np.float32).reshape(CTXL, NDC, 128).transpose(2, 1, 0))
        base.append(dict(xin=to_xT(x[b], q, RG_HL, RG_HR), pein=to_xT(pe, q, RG_HL, RG_HR), ctxin=ctxT, modb=modb_full,
                         cT=cond_T(inputs, b), modw=modw, edge=edge_flags(q), rv=rvt, wxr=wxr, gwr=gw))
    rA = run_bass_kernel_spmd(_NC_CACHE[("rg", "A")], base, core_ids=list(range(NCORE)))
    souts = [rA.results[c]["sout"] for c in range(NCORE)]
    summ = np.ascontiguousarray(np.stack([s_[:, 0:4, :] for s_ in souts], axis=0))
    mapsB = []
    for core in range(NCORE):
        b, q = divmod(core, 4)
        mfb = np.zeros((128, 2, NCORE), np.float32)
        for r in range(NCORE):
            rb, rq = divmod(r, 4)
            if rb == b and rq < q:
                mfb[:, 0, r] = 1.0
            if rb == b and rq > q:
                mfb[:, 1, r] = 1.0
        m = dict(base[core])
        m.update(wgr=wgr, summ=summ, ctxs=np.ascontiguousarray(souts[core][:, 4:6, :]), mfb=mfb)
        mapsB.append(m)
    rB = run_bass_kernel_spmd(_NC_CACHE[("rg", "B")], mapsB, core_ids=list(range(NCORE)))
    cm = common_maps(inputs, 0, [])
    wor = np.ascontiguousarray(np.asarray(inputs["rg_w_out"][0], np.float32).reshape(22, 128, D))
    mapsC = []
    for core in range(NCORE):
        b, q = divmod(core, 4)
        prod = rB.results[core]["prod"]
        m = dict(cm)
        m.update(xin=to_xT(x[b], q, 0, 0), cT=cond_T(inputs, b), edge=edge_flags(q), pe=to_xT(pe, q, 0, 0),
                 prodin=np.ascontiguousarray(prod.reshape(22, 128, TOK)), wor=wor)
        mapsC.append(m)
    res = run_bass_kernel_spmd(_NC_CACHE[("lin",)], mapsC, core_ids=list(range(NCORE)))
    out = np.empty_like(x)
    for core in range(NCORE):
        b, q = divmod(core, 4)
        out[b, q * TOK:(q + 1) * TOK] = from_xT(res.results[core]["xout"])
    return out
```

```python
import math
from contextlib import ExitStack

import numpy as np
import concourse.bass as bass
import concourse.mybir as mybir
from concourse.bass_utils import run_bass_kernel_spmd

F32 = mybir.dt.float32
BF16 = mybir.dt.bfloat16
AF = mybir.ActivationFunctionType
ALU = mybir.AluOpType

D = 2048
NDC = 16
S = 8192
NCORE = 8
TOK = 2048
T = 512
NTT = TOK // T
DFF = 5632
NF = DFF // 128
GF = 4
DEPTH = 4
ALPHA = (2 * DEPTH) ** 0.25
LN_EPS = 1e-5
EPOCH = 30000


class Res:
    __slots__ = ("name", "last_write", "readers")

    def __init__(self, name=""):
        self.name = name
        self.last_write = None
        self.readers = []


class Prog:
    def __init__(self, nc, stack, n_dma_sems=8):
        self.nc = nc
        self.stack = stack
        self.eng = {"pe": nc.tensor, "act": nc.scalar, "dve": nc.vector, "pool": nc.gpsimd, "sp": nc.sync}
        self.sem = {}
        self.cnt = {}
        self.nsem = 0
        for e in ("pe", "act", "dve", "pool"):
            self._new_epoch(e)
        self.waited = {e: {} for e in self.eng}
        self.dma_sems = {}
        self.dma_rr = {}
        for q in ("sp", "pool", "act"):
            self.dma_sems[q] = [[self._alloc_sem(f"dma_{q}_{i}"), 0] for i in range(n_dma_sems)]
            self.dma_rr[q] = 0
        self.n_inst = {e: 0 for e in self.eng}

    def _alloc_sem(self, name):
        self.nsem += 1
        return self.stack.enter_context(self.nc.semaphore(f"{name}_{self.nsem}"))

    def _new_epoch(self, e):
        self.sem[e] = self._alloc_sem(f"eng_{e}")
        self.cnt[e] = 0

    def sbuf(self, name, shape, dtype):
        return self.stack.enter_context(self.nc.sbuf_tensor(name, list(shape), dtype))

    def psum(self, name, shape, dtype=F32):
        return self.stack.enter_context(self.nc.psum_tensor(name, list(shape), dtype))

    def _wait(self, e, tok):
        src, sem, val = tok
        key = id(sem)
        if self.waited[e].get(key, 0) >= val:
            return
        self.eng[e].wait_ge(sem, val)
        self.waited[e][key] = val

    def _deps(self, e, reads, writes, same_engine_ok=True):
        toks = []
        for r in reads:
            if r.last_write is not None:
                toks.append(r.last_write)
        for w in writes:
            if w.last_write is not None:
                toks.append(w.last_write)
            toks.extend(w.readers)
        for tok in toks:
            if same_engine_ok and tok[0] == e and e == "pe":
                continue
            self._wait(e, tok)

    def _commit(self, tok, reads, writes):
        for r in reads:
            r.readers.append(tok)
            if len(r.readers) > 48:
                latest = {}
                for t in r.readers:
                    k = (t[0], id(t[1]))
                    if k not in latest or latest[k][2] < t[2]:
                        latest[k] = t
                r.readers = list(latest.values())
        for w in writes:
            w.last_write = tok
            w.readers = []

    def op(self, e, fn, reads=(), writes=()):
        self._deps(e, reads, writes)
        inst = fn(self.eng[e])
        if self.cnt[e] >= EPOCH:
            self._new_epoch(e)
        self.cnt[e] += 1
        inst.then_inc(self.sem[e], 1)
        tok = (e, self.sem[e], self.cnt[e])
        self._commit(tok, reads, writes)
        self.n_inst[e] += 1
        return tok

    def dma(self, q, out, in_, reads=(), writes=(), **kw):
        self._deps(q, reads, writes, same_engine_ok=False)
        pool = self.dma_sems[q]
        slot = pool[self.dma_rr[q] % len(pool)]
        self.dma_rr[q] += 1
        sem, val = slot
        if val > 0:
            self._wait(q, ("dma", sem, val))
        inst = self.eng[q].dma_start(out=out, in_=in_, **kw)
        slot[1] = val + 16
        inst.then_inc(sem, 16)
        tok = ("dma", sem, val + 16)
        self._commit(tok, reads, writes)
        self.n_inst[q] += 1
        return tok

    def finish(self, e="sp"):
        for q, pool in self.dma_sems.items():
            for sem, val in pool:
                if val > 0:
                    self._wait(e, ("dma", sem, val))


class WStream:
    NSTG = 4
    NRING = 14

    def __init__(self, P, nstg=None, nring=None):
        self.P = P
        self.NSTG = nstg or WStream.NSTG
        self.NRING = nring or WStream.NRING
        self.stg = [P.sbuf(f"stg{i}", [128, 2048], F32) for i in range(self.NSTG)]
        self.stg_res = [Res(f"stg{i}") for i in range(self.NSTG)]
        self.ring = [P.sbuf(f"wring{i}", [128, 2048], BF16) for i in range(self.NRING)]
        self.ring_res = [Res(f"wring{i}") for i in range(self.NRING)]
        self.items = []
        self.issued = 0
        self.n_stg = 0
        self.n_ring = 0
        self.loc = {}
        self.stg_owner = [None] * self.NSTG
        self.ring_owner = [None] * self.NRING
        self.released = set()

    def add(self, src_ap, cast=True, width=2048):
        self.items.append((src_ap, cast, width))
        return len(self.items) - 1

    def issue_until(self, idx):
        P = self.P
        idx = min(idx, len(self.items) - 1)
        while self.issued <= idx:
            i = self.issued
            src, cast, wd = self.items[i]
            if cast:
                r = self.n_ring % self.NRING
                if self.ring_owner[r] is not None and self.ring_owner[r] not in self.released:
                    return
                self.n_ring += 1
                self.ring_owner[r] = i
                P.dma("pool", self.ring[r][:, 0:wd], src, writes=[self.ring_res[r]])
                self.loc[i] = (self.ring[r], self.ring_res[r])
            else:
                s = self.n_stg % self.NSTG
                if self.stg_owner[s] is not None and self.stg_owner[s] not in self.released:
                    return
                self.n_stg += 1
                self.stg_owner[s] = i
                P.dma("sp", self.stg[s][:, 0:wd], src, writes=[self.stg_res[s]])
                self.loc[i] = (self.stg[s], self.stg_res[s])
            self.issued += 1

    def get(self, idx, lookahead=6):
        self.issue_until(idx + lookahead)
        assert idx in self.loc, f"weight item {idx} could not be issued (ring full: missing release?)"
        return self.loc[idx]

    def release(self, idx):
        self.released.add(idx)


V_LNG0, V_LNB0, V_LNG1, V_LNB1, V_MX0, V_MX1, V_MX2, V_MX3 = range(8)
NV = 8


class LayerCtx:
    pass


def _bc(ap_col, n):
    return ap_col.to_broadcast([128, n])


def build_layer(kind, halo_l, halo_r, n_cond=2, extra=None):
    nc = bass.Bass("TRN2", target_bir_lowering=False)
    NT = halo_l + TOK + halo_r
    W = halo_l + T + halo_r
    dram = {}

    def din(name, shape, dt=F32):
        dram[name] = nc.dram_tensor(name, list(shape), dt, kind="ExternalInput").ap()
        return dram[name]

    xin = din("xin", [128, NDC, NT])
    vec = din("vec", [128, NV, NDC])
    modb = din("modb", [128, 96])
    cT = din("cT", [128, NDC, n_cond])
    modw = din("modw", [96, 128, NDC * 128])
    w1r = din("w1r", [NF, 128, 2048])
    w3r = din("w3r", [NF, 128, 2048])
    w2r = din("w2r", [NF, 128, 2048])
    edge = din("edge", [128, 2])
    if kind == "pool":
        poolw = din("poolw", [16, 128, 512])
        pinv = din("pinv", [4, TOK])
    NCV = 80 + 16 * 31
    if kind == "conf":
        cvw1 = din("cvw1", [32, 128, 2048])
        cvw2 = din("cvw2", [16, 128, 2048])
        cvv = din("cvv", [128, NCV])
        ident_in = din("ident", [128, 128])
    if kind == "lin":
        pe_in = din("pe", [128, NDC, TOK])
        prodin = din("prodin", [22, 128, TOK], BF16)
        wor = din("wor", [22, 128, 2048])
    if kind == "none":
        pe_in = din("pe", [128, NDC, TOK])
    if kind == "four":
        fin = din("fin", [128, 2, NDC, TOK], BF16)
        ccsc = din("ccsc", [4, 128, 1024])
        ftw = din("ftw", [16, 128, 2048])
    xout = nc.dram_tensor("xout", [128, NDC, TOK], F32, kind="ExternalOutput").ap()

    with ExitStack() as st:
        P = Prog(nc, st)
        ws = WStream(P, nstg=2)
        L = LayerCtx()
        xs = P.sbuf("xs", [128, NDC, W], F32)
        xs_res = [Res(f"xs{d}") for d in range(NDC)]
        a16 = P.sbuf("a16", [128, NDC, W], BF16)
        a16_res = [Res(f"a16_{d}") for d in range(NDC)]
        g16 = P.sbuf("g16", [128, 2, GF, T], BF16)
        g16_res = [[Res(f"g16_{i}_{j}") for j in range(GF)] for i in range(2)]
        stmp = [P.sbuf(f"stmp{i}", [128, T], F32) for i in range(2)]
        stmp_res = [Res(f"stmp{i}") for i in range(2)]
        sq = [P.sbuf(f"sq{i}", [128, T], F32) for i in range(2)]
        sq_res = [Res(f"sq{i}") for i in range(2)]
        lnt = P.sbuf("lnt", [128, 2, T], F32)
        lnt_res = Res("lnt")
        lnt2 = P.sbuf("lnt2", [128, T], F32)
        onesD = P.sbuf("onesD", [128, 128], F32)
        onesD_res = Res("onesD")
        vraw = P.sbuf("vraw", [128, NV, NDC], F32)
        vraw_res = Res("vraw")
        modb_sb = P.sbuf("modb_sb", [128, 96], F32)
        cs = P.sbuf("cs", [128, NDC, n_cond], F32)
        cs_res = Res("cs")
        msb = P.sbuf("msb", [128, 96, n_cond], F32)
        msb_res = Res("msb")
        dv = P.sbuf("dv", [128, 8, NDC], F32)
        dv_res = Res("dv")
        edge_sb = P.sbuf("edge_sb", [128, 2], F32)
        edge_res = Res("edge")
        ps = [P.psum(f"ps{i}", [128, T]) for i in range(8)]
        ps_res = [Res(f"ps{i}") for i in range(8)]

        mod_items = [ws.add(modw[oc], cast=False) for oc in range(96)]
        plan = []
        if kind == "pool":
            pw_items = [ws.add(poolw[i], width=512) for i in range(16)]
        if kind == "four":
            cc_items = [ws.add(ccsc[i], width=1024) for i in range(4)]
        for tt in range(NTT):
            d_ = {}
            if kind == "conf":
                for oc in range(NDC):
                    d_[("cv", oc)] = ws.add(cvw1[oc])
                    d_[("cg", oc)] = ws.add(cvw1[16 + oc])
                for oc in range(NDC):
                    d_[("c2", oc)] = ws.add(cvw2[oc])
            if kind == "four":
                for oc in range(NDC):
                    d_[("fw", oc)] = ws.add(ftw[oc])
            if kind == "lin":
                for kc in range(22):
                    d_[("wo", kc)] = ws.add(wor[kc])
            for f in range(NF):
                d_[("w1", f)] = ws.add(w1r[f])
                d_[("w3", f)] = ws.add(w3r[f])
                d_[("w2", f)] = ws.add(w2r[f])
            plan.append(d_)

        P.op("pool", lambda e: e.memset(onesD[:], 1.0 / D), writes=[onesD_res])
        P.dma("act", vraw[:], vec, writes=[vraw_res])
        P.dma("act", modb_sb[:], modb, writes=[vraw_res])
        P.dma("act", cs[:], cT, writes=[cs_res])
        P.dma("act", edge_sb[:], edge, writes=[edge_res])
        P.op("act", lambda e: e.activation(out=cs[:], in_=cs[:], func=AF.Silu), reads=[cs_res], writes=[cs_res])
        mps = ps[7]
        mps_res = ps_res[7]
        for oc in range(96):
            wt, wr = ws.get(mod_items[oc], lookahead=2)
            for kc in range(NDC):
                P.op("pe", lambda e, oc=oc, kc=kc, wt=wt: e.matmul(
                    mps[:, oc * n_cond:(oc + 1) * n_cond], lhsT=wt[:, kc * 128:(kc + 1) * 128], rhs=cs[:, kc, :],
                    start=(kc == 0), stop=(kc == NDC - 1)), reads=[wr, cs_res], writes=[mps_res])
            ws.release(mod_items[oc])
        for j in range(n_cond):
            P.op("dve", lambda e, j=j: e.tensor_tensor(
                out=msb[:, :, j], in0=mps[:, 0:96 * n_cond].rearrange("p (o j) -> p o j", j=n_cond)[:, :, j],
                in1=modb_sb[:], op=ALU.add), reads=[mps_res, vraw_res], writes=[msb_res])

        def mvec(k6, dc, j=0):
            return msb[:, k6 * 16 + dc, j:j + 1]

        def mrow(k6):
            return msb[:, k6 * 16:(k6 + 1) * 16, 0]
        P.op("dve", lambda e: e.tensor_scalar(out=dv[:, 0, :], in0=mrow(1), scalar1=1.0, scalar2=None, op0=ALU.add),
             reads=[msb_res], writes=[dv_res])
        if kind == "pool":
            P.op("dve", lambda e: e.tensor_tensor(out=dv[:, 1, :], in0=mrow(2), in1=vraw[:, V_MX1, :], op=ALU.mult),
                 reads=[msb_res, vraw_res], writes=[dv_res])
            P.op("dve", lambda e: e.tensor_tensor(out=dv[:, 2, :], in0=dv[:, 1, :], in1=vraw[:, V_MX0, :], op=ALU.mult),
                 reads=[vraw_res], writes=[dv_res])
        else:
            P.op("dve", lambda e: e.tensor_copy(out=dv[:, 1, :], in_=mrow(2)), reads=[msb_res], writes=[dv_res])
            P.op("dve", lambda e: e.tensor_tensor(out=dv[:, 2, :], in0=dv[:, 1, :], in1=vraw[:, V_MX0, :], op=ALU.mult),
                 reads=[vraw_res], writes=[dv_res])
        P.op("dve", lambda e: e.tensor_scalar(out=dv[:, 3, :], in0=vraw[:, V_LNG0, :], scalar1=ALPHA, scalar2=None, op0=ALU.mult),
             reads=[vraw_res], writes=[dv_res])
        P.op("dve", lambda e: e.tensor_scalar(out=dv[:, 4, :], in0=vraw[:, V_LNB0, :], scalar1=ALPHA, scalar2=None, op0=ALU.mult),
             reads=[vraw_res], writes=[dv_res])
        P.op("dve", lambda e: e.tensor_scalar(out=dv[:, 5, :], in0=mrow(4), scalar1=1.0, scalar2=1.0 / ALPHA, op0=ALU.add, op1=ALU.mult),
             reads=[msb_res], writes=[dv_res])
        VR = [vraw_res, dv_res, msb_res]

        def emit_ln(c0, gcol, bcol, buf=None, bres=None, func=AF.Identity, dst=None):
            buf = xs if buf is None else buf
            bres = xs_res if bres is None else bres
            pm, pq = ps[4], ps[5]
            pmr, pqr = ps_res[4], ps_res[5]
            for dc in range(NDC):
                k = dc % 2
                P.op("act", lambda e, dc=dc, k=k: e.activation(out=sq[k][:], in_=buf[:, dc, c0:c0 + T], func=AF.Square),
                     reads=[bres[dc]], writes=[sq_res[k]])
                P.op("pe", lambda e, dc=dc: e.matmul(pm[:], lhsT=onesD[:], rhs=buf[:, dc, c0:c0 + T],
                                                      start=(dc == 0), stop=(dc == NDC - 1)),
                     reads=[onesD_res, bres[dc]], writes=[pmr])
                P.op("pe", lambda e, dc=dc, k=k: e.matmul(pq[:], lhsT=onesD[:], rhs=sq[k][:],
                                                           start=(dc == 0), stop=(dc == NDC - 1)),
                     reads=[onesD_res, sq_res[k]], writes=[pqr])
            P.op("act", lambda e: e.activation(out=lnt[:, 0, :], in_=pm[:], func=AF.Square), reads=[pmr], writes=[lnt_res])
            P.op("dve", lambda e: e.tensor_tensor(out=lnt[:, 0, :], in0=pq[:], in1=lnt[:, 0, :], op=ALU.subtract),
                 reads=[pqr], writes=[lnt_res])
            P.op("dve", lambda e: e.tensor_scalar(out=lnt[:, 0, :], in0=lnt[:, 0, :], scalar1=LN_EPS, scalar2=None, op0=ALU.add),
                 writes=[lnt_res])
            P.op("act", lambda e: e.activation(out=lnt[:, 1, :], in_=lnt[:, 0, :], func=AF.Sqrt), reads=[lnt_res], writes=[lnt_res])
            P.op("dve", lambda e: e.reciprocal(out=lnt[:, 1, :], in_=lnt[:, 1, :]), reads=[lnt_res], writes=[lnt_res])
            for _it in range(2):
                P.op("dve", lambda e: e.tensor_tensor(out=lnt2[:], in0=lnt[:, 0, :], in1=lnt[:, 1, :], op=ALU.mult), writes=[lnt_res])
                P.op("dve", lambda e: e.tensor_tensor(out=lnt2[:], in0=lnt2[:], in1=lnt[:, 1, :], op=ALU.mult), writes=[lnt_res])
                P.op("dve", lambda e: e.tensor_scalar(out=lnt2[:], in0=lnt2[:], scalar1=-0.5, scalar2=1.5, op0=ALU.mult, op1=ALU.add), writes=[lnt_res])
                P.op("dve", lambda e: e.tensor_tensor(out=lnt[:, 1, :], in0=lnt[:, 1, :], in1=lnt2[:], op=ALU.mult), writes=[lnt_res])
            for dc in range(NDC):
                P.op("dve", lambda e, dc=dc: e.tensor_tensor(out=buf[:, dc, c0:c0 + T], in0=buf[:, dc, c0:c0 + T], in1=pm[:], op=ALU.subtract),
                     reads=[pmr], writes=[bres[dc]])
                P.op("dve", lambda e, dc=dc: e.tensor_tensor(out=buf[:, dc, c0:c0 + T], in0=buf[:, dc, c0:c0 + T], in1=lnt[:, 1, :], op=ALU.mult),
                     reads=[lnt_res], writes=[bres[dc]])
                if dst is None:
                    P.op("act", lambda e, dc=dc: e.activation(out=buf[:, dc, c0:c0 + T], in_=buf[:, dc, c0:c0 + T], func=func,
                                                              scale=gcol(dc), bias=bcol(dc)),
                         reads=VR + [bres[dc]], writes=[bres[dc]])
                else:
                    dap, dres = dst(dc)
                    P.op("act", lambda e, dc=dc, dap=dap: e.activation(out=dap, in_=buf[:, dc, c0:c0 + T], func=func,
                                                                       scale=gcol(dc), bias=bcol(dc)),
                         reads=VR + [bres[dc]], writes=[dres])

        def emit_ffn(tt, c0):
            pl = plan[tt]
            for dc in range(NDC):
                P.op("act", lambda e, dc=dc: e.activation(out=a16[:, dc, 0:T], in_=xs[:, dc, c0:c0 + T], func=AF.Identity,
                                                          scale=dv[:, 5, dc:dc + 1], bias=mvec(3, dc)),
                     reads=VR + [xs_res[dc]], writes=[a16_res[dc]])
            ngrp = NF // GF
            for grp in range(ngrp):
                gb = grp % 2
                for fi in range(GF):
                    f = grp * GF + fi
                    w1t, w1res = ws.get(pl[("w1", f)])
                    w3t, w3res = ws.get(pl[("w3", f)])
                    p1, p3 = ps[(f % 2) * 2], ps[(f % 2) * 2 + 1]
                    p1r, p3r = ps_res[(f % 2) * 2], ps_res[(f % 2) * 2 + 1]
                    for kc in range(NDC):
                        P.op("pe", lambda e, kc=kc, w1t=w1t, p1=p1: e.matmul(
                            p1[:], lhsT=w1t[:, kc * 128:(kc + 1) * 128], rhs=a16[:, kc, 0:T],
                            start=(kc == 0), stop=(kc == NDC - 1)), reads=[w1res, a16_res[kc]], writes=[p1r])
                    for kc in range(NDC):
                        P.op("pe", lambda e, kc=kc, w3t=w3t, p3=p3: e.matmul(
                            p3[:], lhsT=w3t[:, kc * 128:(kc + 1) * 128], rhs=a16[:, kc, 0:T],
                            start=(kc == 0), stop=(kc == NDC - 1)), reads=[w3res, a16_res[kc]], writes=[p3r])
                    ws.release(pl[("w1", f)])
                    ws.release(pl[("w3", f)])
                    k = f % 2
                    P.op("act", lambda e, k=k, p1=p1: e.activation(out=stmp[k][:], in_=p1[:], func=AF.Silu),
                         reads=[p1r], writes=[stmp_res[k]])
                    P.op("dve", lambda e, k=k, p3=p3, gb=gb, fi=fi: e.tensor_tensor(
                        out=g16[:, gb, fi, :], in0=p3[:], in1=stmp[k][:], op=ALU.mult),
                        reads=[p3r, stmp_res[k]], writes=[g16_res[gb][fi]])
                w2 = [ws.get(pl[("w2", grp * GF + fi)]) for fi in range(GF)]
                for dc in range(NDC):
                    po, por = ps[6 + dc % 2], ps_res[6 + dc % 2]
                    for fi in range(GF):
                        P.op("pe", lambda e, fi=fi, dc=dc, po=po: e.matmul(
                            po[:], lhsT=w2[fi][0][:, dc * 128:(dc + 1) * 128], rhs=g16[:, gb, fi, :],
                            start=(fi == 0), stop=(fi == GF - 1)), reads=[w2[fi][1], g16_res[gb][fi]], writes=[por])
                    P.op("dve", lambda e, dc=dc, po=po: e.scalar_tensor_tensor(
                        out=xs[:, dc, c0:c0 + T], in0=po[:], scalar=mvec(5, dc), in1=xs[:, dc, c0:c0 + T],
                        op0=ALU.mult, op1=ALU.add), reads=VR + [por], writes=[xs_res[dc]])
                for fi in range(GF):
                    ws.release(pl[("w2", grp * GF + fi)])

        if kind == "pool":
            pwb = P.sbuf("pwb", [128, 16, 512], BF16)
            pwb_res = Res("pwb")
            for i in range(16):
                wt, wr = ws.get(pw_items[i], lookahead=2)
                P.op("act", lambda e, i=i, wt=wt: e.activation(out=pwb[:, i, :], in_=wt[:, 0:512], func=AF.Identity), reads=[wr], writes=[pwb_res])
                ws.release(pw_items[i])
            hbuf = [P.sbuf(f"hbuf{i}", [128, W], F32) for i in range(2)]
            hbuf_res = [Res(f"hbuf{i}") for i in range(2)]
            sl = [P.sbuf(f"sl{i}", [128, W], F32) for i in range(4)]
            sl_res = [Res(f"sl{i}") for i in range(4)]
            inv_sb = P.sbuf("inv_sb", [128, 4, T], F32)
            inv_res = Res("inv")

        def emit_pool_mixer(tt):
            c0 = halo_l
            P.dma("act", inv_sb[:], pinv[:, tt * T:(tt + 1) * T].partition_broadcast(128), writes=[inv_res])
            for dc in range(NDC):
                g = dc // 4
                hb, hr = hbuf[dc % 2], hbuf_res[dc % 2]
                P.op("act", lambda e, dc=dc, hb=hb: e.activation(out=hb[:], in_=xs[:, dc, :], func=AF.Identity,
                                                               scale=dv[:, 0, dc:dc + 1], bias=mvec(0, dc)),
                     reads=VR + [xs_res[dc]], writes=[hr])
                if tt == 0:
                    P.op("dve", lambda e, hb=hb: e.tensor_scalar(out=hb[:, 0:halo_l], in0=hb[:, 0:halo_l], scalar1=edge_sb[:, 0:1],
                                                                 scalar2=None, op0=ALU.mult), reads=[edge_res], writes=[hr])
                if tt == NTT - 1:
                    P.op("dve", lambda e, hb=hb: e.tensor_scalar(out=hb[:, c0 + T:W], in0=hb[:, c0 + T:W], scalar1=edge_sb[:, 1:2],
                                                                 scalar2=None, op0=ALU.mult), reads=[edge_res], writes=[hr])
                cur, cur_r = hb, hr
                lo, hi = 0, W
                offs = [(1, 0), (1, 1), (2, 2), (4, 4)]
                for lev in range(g + 1):
                    a, b = offs[lev]
                    nlo, nhi = lo + a, hi - b
                    dst, dst_r = sl[lev], sl_res[lev]
                    P.op("pool", lambda e, cur=cur, dst=dst, a=a, b=b, nlo=nlo, nhi=nhi: e.tensor_tensor(
                        out=dst[:, nlo:nhi], in0=cur[:, nlo - a:nhi - a], in1=cur[:, nlo + b:nhi + b], op=ALU.add),
                        reads=[cur_r], writes=[dst_r])
                    cur, cur_r, lo, hi = dst, dst_r, nlo, nhi
                P.op("dve", lambda e, cur=cur, g=g: e.tensor_tensor(out=cur[:, c0:c0 + T], in0=cur[:, c0:c0 + T], in1=inv_sb[:, g, :], op=ALU.mult),
                     reads=[inv_res, cur_r], writes=[cur_r])
                P.op("dve", lambda e, cur=cur, hb=hb, dc=dc: e.tensor_tensor(out=a16[:, dc, 0:T], in0=cur[:, c0:c0 + T], in1=hb[:, c0:c0 + T], op=ALU.subtract),
                     reads=[cur_r, hr], writes=[a16_res[dc]])
            for oc in range(NDC):
                g = oc // 4
                po, por = ps[6 + oc % 2], ps_res[6 + oc % 2]
                for kc in range(4):
                    P.op("pe", lambda e, g=g, kc=kc, oc=oc, po=po: e.matmul(
                        po[:], lhsT=pwb[:, g * 4 + kc, (oc % 4) * 128:(oc % 4 + 1) * 128], rhs=a16[:, g * 4 + kc, 0:T],
                        start=(kc == 0), stop=(kc == 3)), reads=[pwb_res, a16_res[g * 4 + kc]], writes=[por])
                emit_residual(oc, po, por, c0)

        if kind == "conf":
            cvv_sb = P.sbuf("cvv_sb", [128, NCV], F32)
            P.dma("act", cvv_sb[:], cvv, writes=[vraw_res])
            ubuf = P.sbuf("ubuf", [128, NDC, T], F32)
            u_res = [Res(f"u{d}") for d in range(NDC)]
            u16 = [P.sbuf(f"u16_{i}", [128, W], BF16) for i in range(2)]
            u16_res = [Res(f"u16_{i}") for i in range(2)]
            dg = [P.sbuf(f"dg{i}", [128, 31, 128], BF16) for i in range(2)]
            dg_res = [Res(f"dg{i}") for i in range(2)]
            ident_sb = P.sbuf("ident_sb", [128, 128], F32)
            P.dma("act", ident_sb[:], ident_in, writes=[vraw_res])
            HWD = W // 2
            sgt = [P.sbuf(f"sgt{i}", [128, HWD], F32) for i in range(2)]
            sgt_res = [Res(f"sgt{i}") for i in range(2)]

        def emit_conf_mixer(tt):
            pl = plan[tt]
            c0 = halo_l
            for dc in range(NDC):
                P.op("act", lambda e, dc=dc: e.activation(out=a16[:, dc, :], in_=xs[:, dc, :], func=AF.Identity,
                                                          scale=dv[:, 0, dc:dc + 1], bias=mvec(0, dc)),
                     reads=VR + [xs_res[dc]], writes=[a16_res[dc]])
            for oc in range(NDC):
                wv, wvr = ws.get(pl[("cv", oc)])
                wg, wgr = ws.get(pl[("cg", oc)])
                for half in range(2):
                    ca, cb = half * HWD, (half + 1) * HWD
                    pv, pvr = ps[half * 2], ps_res[half * 2]
                    pg, pgr = ps[half * 2 + 1], ps_res[half * 2 + 1]
                    for kc in range(NDC):
                        P.op("pe", lambda e, kc=kc, pv=pv, ca=ca, cb=cb: e.matmul(
                            pv[:, 0:HWD], lhsT=wv[:, kc * 128:(kc + 1) * 128], rhs=a16[:, kc, ca:cb],
                            start=(kc == 0), stop=(kc == NDC - 1)), reads=[wvr, a16_res[kc]], writes=[pvr])
                    for kc in range(NDC):
                        P.op("pe", lambda e, kc=kc, pg=pg, ca=ca, cb=cb: e.matmul(
                            pg[:, 0:HWD], lhsT=wg[:, kc * 128:(kc + 1) * 128], rhs=a16[:, kc, ca:cb],
                            start=(kc == 0), stop=(kc == NDC - 1)), reads=[wgr, a16_res[kc]], writes=[pgr])
                ws.release(pl[("cv", oc)])
                ws.release(pl[("cg", oc)])
                for half in range(2):
                    ca, cb = half * HWD, (half + 1) * HWD
                    pv, pvr = ps[half * 2], ps_res[half * 2]
                    pg, pgr = ps[half * 2 + 1], ps_res[half * 2 + 1]
                    P.op("act", lambda e, half=half, pg=pg: e.activation(out=sgt[half][:], in_=pg[:, 0:HWD], func=AF.Sigmoid,
                                                                         bias=cvv_sb[:, 16 + oc:17 + oc]),
                         reads=VR + [pgr], writes=[sgt_res[half]])
                    P.op("dve", lambda e, half=half, pv=pv, ca=ca, cb=cb: e.scalar_tensor_tensor(
                        out=u16[oc % 2][:, ca:cb], in0=pv[:, 0:HWD], scalar=cvv_sb[:, oc:oc + 1], in1=sgt[half][:],
                        op0=ALU.add, op1=ALU.mult), reads=VR + [pvr, sgt_res[half]], writes=[u16_res[oc % 2]])
                uu, uur = u16[oc % 2], u16_res[oc % 2]
                if tt == 0:
                    P.op("dve", lambda e: e.tensor_scalar(out=uu[:, 0:halo_l], in0=uu[:, 0:halo_l], scalar1=edge_sb[:, 0:1],
                                                          scalar2=None, op0=ALU.mult), reads=[edge_res], writes=[uur])
                if tt == NTT - 1:
                    P.op("dve", lambda e: e.tensor_scalar(out=uu[:, c0 + T:W], in0=uu[:, c0 + T:W], scalar1=edge_sb[:, 1:2],
                                                          scalar2=None, op0=ALU.mult), reads=[edge_res], writes=[uur])
                dgo, dgr = dg[oc % 2], dg_res[oc % 2]
                dwc = 80 + oc * 31
                P.op("pool", lambda e: e.tensor_tensor(
                    out=dgo[:], in0=ident_sb[:].unsqueeze(1).to_broadcast([128, 31, 128]),
                    in1=cvv_sb[:, dwc:dwc + 31].unsqueeze(2).to_broadcast([128, 31, 128]), op=ALU.mult),
                    reads=VR, writes=[dgr])
                pc, pcr = ps[4 + oc % 2], ps_res[4 + oc % 2]
                for j in range(31):
                    P.op("pe", lambda e, j=j: e.matmul(pc[:], lhsT=dgo[:, j, :], rhs=uu[:, j:j + T], start=(j == 0), stop=(j == 30)),
                         reads=[dgr, uur], writes=[pcr])
                P.op("act", lambda e: e.activation(out=ubuf[:, oc, :], in_=pc[:], func=AF.Identity, bias=cvv_sb[:, 32 + oc:33 + oc]),
                     reads=VR + [pcr], writes=[u_res[oc]])
            emit_ln(0, lambda dc: cvv_sb[:, 48 + dc:49 + dc], lambda dc: cvv_sb[:, 64 + dc:65 + dc], buf=ubuf, bres=u_res,
                    func=AF.Silu, dst=lambda dc: (a16[:, dc, 0:T], a16_res[dc]))
            for oc in range(NDC):
                w2t, w2r_ = ws.get(pl[("c2", oc)])
                po, por = ps[6 + oc % 2], ps_res[6 + oc % 2]
                for kc in range(NDC):
                    P.op("pe", lambda e, kc=kc, po=po: e.matmul(
                        po[:], lhsT=w2t[:, kc * 128:(kc + 1) * 128], rhs=a16[:, kc, 0:T],
                        start=(kc == 0), stop=(kc == NDC - 1)), reads=[w2r_, a16_res[kc]], writes=[por])
                ws.release(pl[("c2", oc)])
                emit_residual(oc, po, por, c0)

        if kind == "four":
            ccb = P.sbuf("ccb", [128, 4, 1024], BF16)
            ccb_res = Res("ccb")
            for i in range(4):
                wt, wr = ws.get(cc_items[i], lookahead=2)
                P.op("act", lambda e, i=i, wt=wt: e.activation(out=ccb[:, i, :], in_=wt[:, 0:1024], func=AF.Identity), reads=[wr], writes=[ccb_res])
                ws.release(cc_items[i])
            fab = P.sbuf("fab", [128, 2, NDC, T], BF16)
            fab_res = [[Res(f"fab{i}_{d}") for d in range(NDC)] for i in range(2)]
            corr = P.sbuf("corr", [128, NDC, 2], F32)
            P.op("dve", lambda e: e.memset(corr[:], 0.0), writes=[dv_res])
            fl0 = P.sbuf("fl0", [128, 1], F32)
            P.op("dve", lambda e: e.tensor_scalar(out=fl0[:], in0=edge_sb[:, 0:1], scalar1=-float(S), scalar2=float(S), op0=ALU.mult, op1=ALU.add),
                 reads=[edge_res], writes=[dv_res])
            P.op("dve", lambda e: e.tensor_scalar(out=corr[:, :, 0], in0=mrow(0), scalar1=fl0[:, 0:1], scalar2=None, op0=ALU.mult),
                 reads=[msb_res], writes=[dv_res])

        def emit_four_mixer(tt):
            pl = plan[tt]
            c0 = halo_l
            for i in range(2):
                P.dma("act", fab[:, i], fin[:, i, :, tt * T:(tt + 1) * T], writes=fab_res[i])
            for i in range(2):
                for dc in range(NDC):
                    P.op("act", lambda e, i=i, dc=dc: e.activation(out=fab[:, i, dc, :], in_=fab[:, i, dc, :], func=AF.Identity,
                                                                   scale=dv[:, 0, dc:dc + 1]),
                         reads=VR + [fab_res[i][dc]], writes=[fab_res[i][dc]])
            if tt == 0:
                for dc in range(NDC):
                    P.op("dve", lambda e, dc=dc: e.tensor_tensor(out=fab[:, 0, dc, 0:2], in0=fab[:, 0, dc, 0:2], in1=corr[:, dc, :], op=ALU.add),
                         reads=VR + [fab_res[0][dc]], writes=[fab_res[0][dc]])
            for oc in range(NDC):
                g = oc // 4
                pj, pjr = ps[oc % 4], ps_res[oc % 4]
                n = 0
                for i in range(2):
                    for kc in range(4):
                        P.op("pe", lambda e, i=i, kc=kc, pj=pj, n=n: e.matmul(
                            pj[:], lhsT=ccb[:, kc, i * 512 + (oc % 4) * 128:i * 512 + (oc % 4 + 1) * 128], rhs=fab[:, i, 4 * g + kc, :],
                            start=(n == 0), stop=(n == 7)), reads=[ccb_res, fab_res[i][4 * g + kc]], writes=[pjr])
                        n += 1
                P.op("act", lambda e, pj=pj: e.activation(out=a16[:, oc, 0:T], in_=pj[:], func=AF.Identity), reads=[pjr], writes=[a16_res[oc]])
            for oc in range(NDC):
                w2t, w2r_ = ws.get(pl[("fw", oc)])
                po, por = ps[6 + oc % 2], ps_res[6 + oc % 2]
                for kc in range(NDC):
                    P.op("pe", lambda e, kc=kc, po=po: e.matmul(
                        po[:], lhsT=w2t[:, kc * 128:(kc + 1) * 128], rhs=a16[:, kc, 0:T],
                        start=(kc == 0), stop=(kc == NDC - 1)), reads=[w2r_, a16_res[kc]], writes=[por])
                ws.release(pl[("fw", oc)])
                emit_residual(oc, po, por, c0)

        if kind == "none":
            pebuf = P.sbuf("pebuf", [128, NDC, T], F32)
            pe_res = Res("pe")

        def emit_none_mixer(tt):
            P.dma("act", pebuf[:], pe_in[:, :, tt * T:(tt + 1) * T], writes=[pe_res])
            for dc in range(NDC):
                P.op("dve", lambda e, dc=dc: e.tensor_tensor(out=xs[:, dc, :], in0=xs[:, dc, :], in1=pebuf[:, dc, :], op=ALU.add),
                     reads=[pe_res], writes=[xs_res[dc]])
                P.op("act", lambda e, dc=dc: e.activation(out=xs[:, dc, :], in_=xs[:, dc, :], func=AF.Identity, scale=ALPHA),
                     reads=[xs_res[dc]], writes=[xs_res[dc]])

        if kind == "lin":
            prt = P.sbuf("prt", [128, 22, T], BF16)
            prt_res = Res("prt")
            petmp = [P.sbuf(f"petmp{i}", [128, T], F32) for i in range(2)]
            petmp_res = [Res(f"petmp{i}") for i in range(2)]

        def emit_lin_mixer(tt):
            pl = plan[tt]
            c0 = halo_l
            P.dma("act", prt[:], prodin[:, :, tt * T:(tt + 1) * T].rearrange("k p t -> p k t"), writes=[prt_res])
            for dc in range(NDC):
                k = dc % 2
                P.dma("act", petmp[k][:], pe_in[:, dc, tt * T:(tt + 1) * T], writes=[petmp_res[k]])
                P.op("dve", lambda e, dc=dc, k=k: e.tensor_tensor(out=xs[:, dc, :], in0=xs[:, dc, :], in1=petmp[k][:], op=ALU.add),
                     reads=[petmp_res[k]], writes=[xs_res[dc]])
            for half in range(2):
                wts = [ws.get(pl[("wo", half * 11 + i)]) for i in range(11)]
                for dc in range(NDC):
                    po, por = ps[6 + dc % 2], ps_res[6 + dc % 2]
                    for i in range(11):
                        P.op("pe", lambda e, i=i, dc=dc, po=po: e.matmul(
                            po[:], lhsT=wts[i][0][:, dc * 128:(dc + 1) * 128], rhs=prt[:, half * 11 + i, :],
                            start=(i == 0), stop=(i == 10)), reads=[wts[i][1], prt_res], writes=[por])
                    if half == 0:
                        emit_residual(dc, po, por, c0)
                    else:
                        P.op("dve", lambda e, dc=dc, po=po: e.scalar_tensor_tensor(
                            out=xs[:, dc, c0:c0 + T], in0=po[:], scalar=dv[:, 1, dc:dc + 1], in1=xs[:, dc, c0:c0 + T],
                            op0=ALU.mult, op1=ALU.add), reads=VR + [por], writes=[xs_res[dc]])
                for i in range(11):
                    ws.release(pl[("wo", half * 11 + i)])

        def emit_residual(dc, po, por, c0):
            P.op("act", lambda e: e.activation(out=xs[:, dc, c0:c0 + T], in_=xs[:, dc, c0:c0 + T], func=AF.Identity,
                                               scale=ALPHA, bias=dv[:, 2, dc:dc + 1]),
                 reads=VR + [xs_res[dc]], writes=[xs_res[dc]])
            P.op("dve", lambda e: e.scalar_tensor_tensor(out=xs[:, dc, c0:c0 + T], in0=po[:], scalar=dv[:, 1, dc:dc + 1],
                                                         in1=xs[:, dc, c0:c0 + T], op0=ALU.mult, op1=ALU.add),
                 reads=VR + [por], writes=[xs_res[dc]])

        for tt in range(NTT):
            c0 = halo_l
            P.dma("act", xs[:], xin[:, :, tt * T:tt * T + W], writes=xs_res)
            if kind == "pool":
                emit_pool_mixer(tt)
            elif kind == "conf":
                emit_conf_mixer(tt)
            elif kind == "four":
                emit_four_mixer(tt)
            elif kind == "none":
                emit_none_mixer(tt)
            elif kind == "lin":
                emit_lin_mixer(tt)
            emit_ln(c0, lambda dc: dv[:, 3, dc:dc + 1], lambda dc: dv[:, 4, dc:dc + 1])
            emit_ffn(tt, c0)

            def store(dc, tt=tt):
                pass
            emit_ln(c0, lambda dc: vraw[:, V_LNG1, dc:dc + 1], lambda dc: vraw[:, V_LNB1, dc:dc + 1])
            P.dma("pool", xout[:, :, tt * T:(tt + 1) * T], xs[:, :, c0:c0 + T], reads=xs_res)
        P.finish("sp")
        nc._stats = dict(P.n_inst)
    return nc


def col_table(v):
    return np.ascontiguousarray(np.asarray(v, np.float32).reshape(NDC, 128).T)


def to_xT(xb, q, halo_l, halo_r):
    t0 = q * TOK - halo_l
    t1 = (q + 1) * TOK + halo_r
    out = np.zeros((128, NDC, t1 - t0), np.float32)
    a, b = max(t0, 0), min(t1, S)
    blk = xb[a:b].reshape(b - a, NDC, 128).transpose(2, 1, 0)
    out[:, :, a - t0:b - t0] = blk
    return out


def from_xT(xt):
    return np.ascontiguousarray(xt.transpose(2, 1, 0).reshape(xt.shape[2], D))


def w_colblocks(w, nblk):
    K = w.shape[0] // 128
    return np.ascontiguousarray(w.reshape(K, 128, nblk, 128).transpose(2, 1, 0, 3).reshape(nblk, 128, K * 128))


def ffn_layouts(inputs, l):
    w1r = w_colblocks(np.asarray(inputs["ffn_w1"][l]), NF)
    w3r = w_colblocks(np.asarray(inputs["ffn_w3"][l]), NF)
    w2r = np.ascontiguousarray(np.asarray(inputs["ffn_w2"][l]).reshape(NF, 128, D))
    return w1r, w3r, w2r


def common_maps(inputs, l, mixvecs):
    vec = np.zeros((128, NV, NDC), np.float32)
    vec[:, V_LNG0] = col_table(inputs["ln_g"][l, 0])
    vec[:, V_LNB0] = col_table(inputs["ln_b"][l, 0])
    vec[:, V_LNG1] = col_table(inputs["ln_g"][l, 1])
    vec[:, V_LNB1] = col_table(inputs["ln_b"][l, 1])
    for i, v in enumerate(mixvecs):
        vec[:, V_MX0 + i] = col_table(v)
    modb = np.ascontiguousarray(np.asarray(inputs["mod_b"][l], np.float32).reshape(96, 128).T)
    modw = w_colblocks(np.asarray(inputs["mod_w"][l]), 96)
    w1r, w3r, w2r = ffn_layouts(inputs, l)
    return dict(vec=vec, modb=modb, modw=modw, w1r=w1r, w3r=w3r, w2r=w2r)


def cond_T(inputs, b):
    c = np.stack([np.asarray(inputs["c"][b], np.float32), np.asarray(inputs["c_ctx"], np.float32)], axis=-1)
    return np.ascontiguousarray(c.reshape(NDC, 128, 2).transpose(1, 0, 2))


def edge_flags(q):
    e = np.ones((128, 2), np.float32)
    if q == 0:
        e[:, 0] = 0.0
    if q == 3:
        e[:, 1] = 0.0
    return e


_NC_CACHE = {}


def run_pool_layer(inputs, l, x):
    HL = HR = 8
    key = ("pool",)
    if key not in _NC_CACHE:
        _NC_CACHE[key] = build_layer("pool", HL, HR)
    nc = _NC_CACHE[key]
    cm = common_maps(inputs, l, [inputs["pool_b"][0], inputs["pool_scale"][0]])
    pw = np.asarray(inputs["pool_w"][0], np.float32)
    poolw = np.ascontiguousarray(pw.reshape(4, 4, 128, 512).reshape(16, 128, 512))
    in_maps = []
    for core in range(NCORE):
        b, q = divmod(core, 4)
        t = np.arange(q * TOK, (q + 1) * TOK)
        pinv = np.zeros((4, TOK), np.float32)
        for g, win in enumerate((2, 4, 8, 16)):
            lo = np.clip(t - win // 2, 0, S)
            hi = np.clip(t - win // 2 + win, 0, S)
            pinv[g] = 1.0 / (hi - lo).astype(np.float32)
        m = dict(cm)
        m.update(xin=to_xT(x[b], q, HL, HR), cT=cond_T(inputs, b), edge=edge_flags(q), poolw=poolw, pinv=pinv)
        in_maps.append(m)
    res = run_bass_kernel_spmd(nc, in_maps, core_ids=list(range(NCORE)))
    out = np.empty_like(x)
    for core in range(NCORE):
        b, q = divmod(core, 4)
        out[b, q * TOK:(q + 1) * TOK] = from_xT(res.results[core]["xout"])
    return out


def run_conf_layer(inputs, l, x):
    HL = HR = 15
    key = ("conf",)
    if key not in _NC_CACHE:
        _NC_CACHE[key] = build_layer("conf", HL, HR)
    nc = _NC_CACHE[key]
    cm = common_maps(inputs, l, [inputs["cv_b2"][0]])
    cvw1 = w_colblocks(np.asarray(inputs["cv_w1"][0]), 32)
    cvw2 = w_colblocks(np.asarray(inputs["cv_w2"][0]), 16)
    b1 = np.asarray(inputs["cv_b1"][0], np.float32)
    cvv = np.concatenate([
        b1.reshape(32, 128).T,
        col_table(inputs["cv_dwb"][0]), col_table(inputs["cv_ln_g"][0]), col_table(inputs["cv_ln_b"][0]),
        np.asarray(inputs["cv_dw"][0], np.float32).reshape(31, NDC, 128).transpose(2, 1, 0).reshape(128, NDC * 31),
    ], axis=1).astype(np.float32)
    cvv = np.ascontiguousarray(cvv)
    in_maps = []
    for core in range(NCORE):
        b, q = divmod(core, 4)
        m = dict(cm)
        m.update(xin=to_xT(x[b], q, HL, HR), cT=cond_T(inputs, b), edge=edge_flags(q), cvw1=cvw1, cvw2=cvw2, cvv=cvv,
                 ident=np.eye(128, dtype=np.float32))
        in_maps.append(m)
    res = run_bass_kernel_spmd(nc, in_maps, core_ids=list(range(NCORE)))
    out = np.empty_like(x)
    for core in range(NCORE):
        b, q = divmod(core, 4)
        out[b, q * TOK:(q + 1) * TOK] = from_xT(res.results[core]["xout"])
    return out


def build_seqdft():
    nc = bass.Bass("TRN2", target_bir_lowering=False)
    xtok = nc.dram_tensor("xtok", [64, 128, D], F32, kind="ExternalInput").ap()
    tab = nc.dram_tensor("tab", [64, 128, 4096], BF16, kind="ExternalInput").ap()
    f12 = nc.dram_tensor("f12", [128, 2, NDC, TOK], BF16, kind="ExternalOutput").ap()
    NB = 6
    with ExitStack() as st:
        P = Prog(nc, st)
        ws = WStream(P)
        bring = [P.sbuf(f"bring{i}", [128, 2048], BF16) for i in range(NB)]
        bring_res = [Res(f"bring{i}") for i in range(NB)]
        obuf = [P.sbuf(f"obuf{i}", [128, 2, TOK], BF16) for i in range(2)]
        obuf_res = [Res(f"obuf{i}") for i in range(2)]
        ps = [P.psum(f"ps{i}", [128, 512]) for i in range(8)]
        ps_res = [Res(f"ps{i}") for i in range(8)]
        seq = [(cp, trig, sc) for cp in range(8) for trig in range(2) for sc in range(64)]
        a_items = [ws.add(xtok[sc][:, cp * 256:(cp + 1) * 256], width=256) for (cp, trig, sc) in seq]
        nb_issued = [0]

        def issue_b(upto):
            while nb_issued[0] <= min(upto, len(seq) - 1):
                i = nb_issued[0]
                cp, trig, sc = seq[i]
                P.dma("sp", bring[i % NB][:], tab[sc][:, trig * 2048:(trig + 1) * 2048], writes=[bring_res[i % NB]])
                nb_issued[0] += 1

        for i, (cp, trig, sc) in enumerate(seq):
            issue_b(i + 3)
            at, ar = ws.get(a_items[i], lookahead=4)
            bt, br = bring[i % NB], bring_res[i % NB]
            for c2 in range(2):
                for kt in range(4):
                    b_ = c2 * 4 + kt
                    P.op("pe", lambda e, c2=c2, kt=kt, b_=b_, at=at, bt=bt: e.matmul(
                        ps[b_][:], lhsT=at[:, c2 * 128:(c2 + 1) * 128], rhs=bt[:, kt * 512:(kt + 1) * 512],
                        start=(sc == 0), stop=(sc == 63)), reads=[ar, br], writes=[ps_res[b_]])
            ws.release(a_items[i])
            if sc == 63:
                ob, obr = obuf[(cp * 2 + trig) % 2], obuf_res[(cp * 2 + trig) % 2]
                for c2 in range(2):
                    for kt in range(4):
                        b_ = c2 * 4 + kt
                        eng = "act" if kt % 2 == 0 else "dve"
                        if eng == "act":
                            P.op("act", lambda e, c2=c2, kt=kt, b_=b_, ob=ob: e.activation(out=ob[:, c2, kt * 512:(kt + 1) * 512], in_=ps[b_][:], func=AF.Identity),
                                 reads=[ps_res[b_]], writes=[obr])
                        else:
                            P.op("dve", lambda e, c2=c2, kt=kt, b_=b_, ob=ob: e.tensor_copy(out=ob[:, c2, kt * 512:(kt + 1) * 512], in_=ps[b_][:]),
                                 reads=[ps_res[b_]], writes=[obr])
                P.dma("pool", f12[:, trig, cp * 2:cp * 2 + 2, :], ob[:], reads=[obr])
        P.finish("sp")
        nc._stats = dict(P.n_inst)
    return nc


def _bf16(a):
    import ml_dtypes
    return np.asarray(a, np.float32).astype(ml_dtypes.bfloat16)


def run_four_layer(inputs, l, x):
    if ("seqdft",) not in _NC_CACHE:
        _NC_CACHE[("seqdft",)] = build_seqdft()
    if ("four",) not in _NC_CACHE:
        _NC_CACHE[("four",)] = build_layer("four", 0, 0)
    s_idx = np.arange(S, dtype=np.int64)[:, None]
    in_maps = []
    for core in range(NCORE):
        b, q = divmod(core, 4)
        k_idx = np.arange(q * TOK, (q + 1) * TOK, dtype=np.int64)[None, :]
        ang = (2.0 * np.pi / S) * ((s_idx * k_idx) % S).astype(np.float64)
        tab = np.concatenate([np.cos(ang), np.sin(ang)], axis=1)
        in_maps.append(dict(xtok=np.ascontiguousarray(x[b].reshape(64, 128, D)), tab=_bf16(tab).reshape(64, 128, 4096)))
    r1 = run_bass_kernel_spmd(_NC_CACHE[("seqdft",)], in_maps, core_ids=list(range(NCORE)))
    cm = common_maps(inputs, l, [inputs["ft_b"][0]])
    c_idx = np.arange(512, dtype=np.int64)
    angc = (2.0 * np.pi / 512) * ((c_idx[:, None] * c_idx[None, :]) % 512).astype(np.float64)
    ccsc = (np.concatenate([np.cos(angc), -np.sin(angc)], axis=1) / 2048.0).astype(np.float32).reshape(4, 128, 1024)
    ftw = w_colblocks(np.asarray(inputs["ft_w"][0]), 16)
    in_maps = []
    for core in range(NCORE):
        b, q = divmod(core, 4)
        m = dict(cm)
        m.update(xin=to_xT(x[b], q, 0, 0), cT=cond_T(inputs, b), edge=edge_flags(q), fin=r1.results[core]["f12"], ccsc=ccsc, ftw=ftw)
        in_maps.append(m)
    res = run_bass_kernel_spmd(_NC_CACHE[("four",)], in_maps, core_ids=list(range(NCORE)))
    out = np.empty_like(x)
    for core in range(NCORE):
        b, q = divmod(core, 4)
        out[b, q * TOK:(q + 1) * TOK] = from_xT(res.results[core]["xout"])
    return out


def _pos_embed_table():
    rows, cols, dim = S // 64, 64, D
    quarter = dim // 4
    omega = (1.0 / (10000.0 ** (np.arange(quarter, dtype=np.float32) / np.float32(quarter)))).astype(np.float32)
    ar = np.arange(rows, dtype=np.float32)[:, None] * omega[None]
    ac = np.arange(cols, dtype=np.float32)[:, None] * omega[None]
    er = np.concatenate([np.sin(ar), np.cos(ar)], axis=-1)
    ec = np.concatenate([np.sin(ac), np.cos(ac)], axis=-1)
    pe = np.concatenate([np.broadcast_to(er[:, None, :], (rows, cols, dim // 2)),
                         np.broadcast_to(ec[None, :, :], (rows, cols, dim // 2))], axis=-1)
    return pe.reshape(rows * cols, dim).astype(np.float32)


def run_layer0_partial(inputs, x):
    key = ("none",)
    if key not in _NC_CACHE:
        _NC_CACHE[key] = build_layer("none", 0, 0)
    nc = _NC_CACHE[key]
    cm = common_maps(inputs, 0, [])
    pe = _pos_embed_table()
    in_maps = []
    for core in range(NCORE):
        b, q = divmod(core, 4)
        m = dict(cm)
        m.update(xin=to_xT(x[b], q, 0, 0), cT=cond_T(inputs, b), edge=edge_flags(q), pe=to_xT(pe, q, 0, 0))
        in_maps.append(m)
    res = run_bass_kernel_spmd(nc, in_maps, core_ids=list(range(NCORE)))
    out = np.empty_like(x)
    for core in range(NCORE):
        b, q = divmod(core, 4)
        out[b, q * TOK:(q + 1) * TOK] = from_xT(res.results[core]["xout"])
    return out


def kernel(**inputs):
    inputs = {k: np.asarray(v) for k, v in inputs.items()}
    x = np.ascontiguousarray(inputs["x"], dtype=np.float32)
    x = run_rg_layer(inputs, x)
    x = run_pool_layer(inputs, 1, x)
    x = run_conf_layer(inputs, 2, x)
    x = run_four_layer(inputs, 3, x)
    return x.astype(np.float32)


RSUB = 88
NSUB = 32
RG_HL, RG_HR = 1, 2
CTXL = 256


def build_rg(phase):
    nc = bass.Bass("TRN2", target_bir_lowering=False)
    NTW = RG_HL + TOK + RG_HR
    d = {}

    def din(name, shape, dt=F32):
        d[name] = nc.dram_tensor(name, list(shape), dt, kind="ExternalInput").ap()
        return d[name]

    xin = din("xin", [128, NDC, NTW])
    pein = din("pein", [128, NDC, NTW])
    ctxin = din("ctxin", [128, NDC, CTXL])
    modb = din("modb", [128, 96])
    cT = din("cT", [128, NDC, 2])
    modw = din("modw", [32, 128, NDC * 128])
    edge = din("edge", [128, 2])
    rv = din("rv", [128, 11, NSUB])
    wxr = din("wxr", [NSUB, 128, NDC * RSUB])
    gwr = din("gwr", [32, 128, 704])
    if phase == "B":
        wgr = din("wgr", [NSUB, 128, NDC * RSUB])
        summ = din("summ", [NCORE, 128, 4, NSUB])
        ctxs = din("ctxs", [128, 2, NSUB])
        mfb = din("mfb", [128, 2, NCORE])
        prod = nc.dram_tensor("prod", [NSUB, RSUB, TOK], BF16, kind="ExternalOutput").ap()
    else:
        sout = nc.dram_tensor("sout", [128, 6, NSUB], F32, kind="ExternalOutput").ap()

    with ExitStack() as st:
        P = Prog(nc, st)
        ws = WStream(P, nstg=2, nring=4)
        a16f = P.sbuf("a16f", [128, NDC, NTW], BF16)
        a16f_res = [Res(f"a16f{i}") for i in range(NDC)]
        xbpre = P.sbuf("xbpre", [128, NTW], F32)
        xbpre_res = Res("xbpre")
        xb = [P.sbuf(f"xb{i}", [128, TOK], F32) for i in range(2)]
        xb_res = [Res(f"xb{i}") for i in range(2)]
        xb16 = [P.sbuf(f"xb16_{i}", [128, TOK], BF16) for i in range(2)]
        xb16_res = [Res(f"xb16_{i}") for i in range(2)]
        abuf = P.sbuf("abuf", [128, NTW], F32)
        bbuf = P.sbuf("bbuf", [128, NTW], F32)
        tbuf = P.sbuf("tbuf", [128, NTW], F32)
        ab_res, bb_res, tb_res = Res("abuf"), Res("bbuf"), Res("tbuf")
        ybuf = [P.sbuf(f"ybuf{i}", [128, NTW], F32) for i in range(2)]
        yb_res = [Res(f"ybuf{i}") for i in range(2)]
        xt, xt_res = [abuf, bbuf], [ab_res, bb_res]
        abuf2 = P.sbuf("abuf2", [128, TOK], F32)
        bbuf2 = P.sbuf("bbuf2", [128, TOK], F32)
        abufs, ab_ress = [abuf, abuf2], [ab_res, Res("abuf2")]
        bbufs, bb_ress = [bbuf, bbuf2], [bb_res, Res("bbuf2")]
        itc = [0]
        one_c = P.sbuf("one_c", [128, 1], F32)
        P.op("pool", lambda e: e.memset(one_c[:], 1.0), writes=[Res("one_c")])
        pt, pt_res = [tbuf, ybuf[0]], [tb_res, yb_res[0]]
        rv_sb = P.sbuf("rv_sb", [128, 11, NSUB], F32)
        cp_sb = P.sbuf("cp_sb", [128, 2, NSUB], F32)
        rv_res = Res("rv")
        modb_sb = P.sbuf("modb_sb", [128, 96], F32)
        cs = P.sbuf("cs", [128, NDC, 2], F32)
        cs_res = Res("cs")
        msb = P.sbuf("msb", [128, 32, 2], F32)
        msb_res = Res("msb")
        a1 = P.sbuf("a1", [128, NDC, 2], F32)
        edge_sb = P.sbuf("edge_sb", [128, 2], F32)
        edge_res = Res("edge")
        rsum = P.sbuf("rsum", [128, 1], F32)
        rsum_res = Res("rsum")
        if phase == "A":
            so_sb = P.sbuf("so_sb", [128, 6, NSUB], F32)
            so_res = Res("so")
        else:
            summ_sb = P.sbuf("summ_sb", [128, NCORE, 4, NSUB], F32)
            carry = P.sbuf("carry", [128, 2, NSUB], F32)
            mfb_sb = P.sbuf("mfb_sb", [128, 2, NCORE], F32)
            ctmp = P.sbuf("ctmp", [128, NSUB], F32)
            carry_res = Res("carry")
            gl = [P.sbuf(f"gl{i}", [128, T], F32) for i in range(2)]
            gl_res = [Res(f"gl{i}") for i in range(2)]
            pr16 = [P.sbuf(f"pr16_{i}", [128, TOK], BF16) for i in range(2)]
            pr16_res = [Res(f"pr16_{i}") for i in range(2)]
        ps = [P.psum(f"ps{i}", [128, 512]) for i in range(8)]
        ps_res = [Res(f"ps{i}") for i in range(8)]

        mod_items = [ws.add(modw[oc], cast=False) for oc in range(32)]
        passes = ["ctx", "lat"] if phase == "A" else ["lat"]
        plan = {}
        for pss in passes:
            for n in range(16):
                for s in range(2):
                    plan[(pss, "wx", 2 * n + s)] = ws.add(wxr[2 * n + s], width=NDC * RSUB)
                for dd in range(2):
                    plan[(pss, "gw", dd, n)] = ws.add(gwr[dd * 16 + n], width=704)
                if phase == "B":
                    for s in range(2):
                        plan[(pss, "wg", 2 * n + s)] = ws.add(wgr[2 * n + s], width=NDC * RSUB)

        P.dma("act", rv_sb[:], rv, writes=[rv_res])
        P.dma("act", modb_sb[:], modb, writes=[rv_res])
        P.dma("act", cs[:], cT, writes=[cs_res])
        P.dma("act", edge_sb[:], edge, writes=[edge_res])
        P.op("act", lambda e: e.activation(out=cs[:], in_=cs[:], func=AF.Silu), reads=[cs_res], writes=[cs_res])
        P.op("act", lambda e: e.activation(out=cp_sb[:], in_=rv_sb[:, 9:11, :], func=AF.Sigmoid), reads=[rv_res], writes=[rv_res])
        P.op("act", lambda e: e.activation(out=cp_sb[:], in_=cp_sb[:], func=AF.Ln), reads=[rv_res], writes=[rv_res])
        P.op("dve", lambda e: e.tensor_scalar(out=cp_sb[:], in0=cp_sb[:], scalar1=8.0, scalar2=None, op0=ALU.mult), reads=[rv_res], writes=[rv_res])
        mps, mps_res = ps[7], ps_res[7]
        for oc in range(32):
            wt, wr = ws.get(mod_items[oc], lookahead=1)
            for kc in range(NDC):
                P.op("pe", lambda e, oc=oc, kc=kc, wt=wt: e.matmul(
                    mps[:, oc * 2:(oc + 1) * 2], lhsT=wt[:, kc * 128:(kc + 1) * 128], rhs=cs[:, kc, :],
                    start=(kc == 0), stop=(kc == NDC - 1)), reads=[wr, cs_res], writes=[mps_res])
            ws.release(mod_items[oc])
        for j in range(2):
            P.op("dve", lambda e, j=j: e.tensor_tensor(
                out=msb[:, :, j], in0=mps[:, 0:64].rearrange("p (o j) -> p o j", j=2)[:, :, j],
                in1=modb_sb[:, 0:32], op=ALU.add), reads=[mps_res, rv_res], writes=[msb_res])
        P.op("dve", lambda e: e.tensor_scalar(out=a1[:], in0=msb[:, 16:32, :], scalar1=1.0, scalar2=None, op0=ALU.add),
             reads=[msb_res], writes=[msb_res])
        if phase == "A":
            P.op("pool", lambda e: e.memset(so_sb[:], 0.0), writes=[so_res])
        else:
            P.dma("act", summ_sb[:], summ.rearrange("r p a s -> p r a s"), writes=[carry_res])
            P.dma("act", carry[:], ctxs, writes=[carry_res])
            P.dma("act", mfb_sb[:], mfb, writes=[carry_res])
            for dd in range(2):
                order = range(NCORE) if dd == 0 else range(NCORE - 1, -1, -1)
                for r in order:
                    A_r = summ_sb[:, r, 2 * dd, :]
                    B_r = summ_sb[:, r, 2 * dd + 1, :]
                    mk = mfb_sb[:, dd, r:r + 1]
                    P.op("dve", lambda e, A_r=A_r, mk=mk: e.tensor_scalar(out=ctmp[:], in0=A_r, scalar1=-1.0, scalar2=mk, op0=ALU.add, op1=ALU.mult),
                         reads=[carry_res], writes=[carry_res])
                    P.op("dve", lambda e: e.tensor_scalar(out=ctmp[:], in0=ctmp[:], scalar1=1.0, scalar2=None, op0=ALU.add),
                         reads=[carry_res], writes=[carry_res])
                    P.op("dve", lambda e, dd=dd: e.tensor_tensor(out=carry[:, dd, :], in0=carry[:, dd, :], in1=ctmp[:], op=ALU.mult),
                         reads=[carry_res], writes=[carry_res])
                    P.op("dve", lambda e, B_r=B_r, mk=mk: e.tensor_scalar(out=ctmp[:], in0=B_r, scalar1=mk, scalar2=None, op0=ALU.mult),
                         reads=[carry_res], writes=[carry_res])
                    P.op("dve", lambda e, dd=dd: e.tensor_tensor(out=carry[:, dd, :], in0=carry[:, dd, :], in1=ctmp[:], op=ALU.add),
                         reads=[carry_res], writes=[carry_res])

        def run_pass(pss):
            ctx = pss == "ctx"
            NTK = CTXL if ctx else TOK
            NW = RG_HL + NTK + RG_HR
            j = 1 if ctx else 0
            for dc in range(NDC):
                k = dc % 2
                if ctx:
                    P.dma("act", xt[k][:, RG_HL:RG_HL + NTK], ctxin[:, dc, :], writes=[xt_res[k]])
                    src = xt[k][:, RG_HL:RG_HL + NTK]
                    P.op("pool", lambda e, dc=dc: e.memset(a16f[:, dc, 0:NW], 0.0), writes=[a16f_res[dc]])
                    P.op("act", lambda e, dc=dc, src=src: e.activation(out=a16f[:, dc, RG_HL:RG_HL + NTK], in_=src, func=AF.Identity,
                                                                       scale=a1[:, dc, j:j + 1], bias=msb[:, dc, j:j + 1]),
                         reads=[xt_res[k], msb_res], writes=[a16f_res[dc]])
                else:
                    P.dma("act", xt[k][:], xin[:, dc, :], writes=[xt_res[k]])
                    P.dma("act", pt[k][:], pein[:, dc, :], writes=[pt_res[k]])
                    P.op("dve", lambda e, k=k: e.tensor_tensor(out=xt[k][:], in0=xt[k][:], in1=pt[k][:], op=ALU.add),
                         reads=[pt_res[k]], writes=[xt_res[k]])
                    P.op("act", lambda e, dc=dc, k=k: e.activation(out=a16f[:, dc, :], in_=xt[k][:], func=AF.Identity,
                                                                   scale=a1[:, dc, j:j + 1], bias=msb[:, dc, j:j + 1]),
                         reads=[xt_res[k], msb_res], writes=[a16f_res[dc]])
                    P.op("dve", lambda e, dc=dc: e.tensor_scalar(out=a16f[:, dc, 0:RG_HL], in0=a16f[:, dc, 0:RG_HL], scalar1=edge_sb[:, 0:1],
                                                                 scalar2=None, op0=ALU.mult), reads=[edge_res], writes=[a16f_res[dc]])
                    P.op("dve", lambda e, dc=dc: e.tensor_scalar(out=a16f[:, dc, RG_HL + NTK:NW], in0=a16f[:, dc, RG_HL + NTK:NW], scalar1=edge_sb[:, 1:2],
                                                                 scalar2=None, op0=ALU.mult), reads=[edge_res], writes=[a16f_res[dc]])
            coltiles = [(c, min(c + 512, NW)) for c in range(0, NW, 512)]
            ctiles = [(c, min(c + 512, NTK)) for c in range(0, NTK, 512)]
            Rr = slice(0, RSUB)
            for n in range(16):
                for s in range(2):
                    sidx = 2 * n + s
                    wx, wxres = ws.get(plan[(pss, "wx", sidx)], lookahead=2)
                    for ci, (ca, cb) in enumerate(coltiles):
                        pp, ppr = ps[ci % 2], ps_res[ci % 2]
                        for kc in range(NDC):
                            P.op("pe", lambda e, kc=kc, pp=pp, ca=ca, cb=cb, wx=wx: e.matmul(
                                pp[Rr, 0:cb - ca], lhsT=wx[:, kc * RSUB:(kc + 1) * RSUB], rhs=a16f[:, kc, ca:cb],
                                start=(kc == 0), stop=(kc == NDC - 1)), reads=[wxres, a16f_res[kc]], writes=[ppr])
                        P.op("act", lambda e, pp=pp, ca=ca, cb=cb: e.activation(out=xbpre[Rr, ca:cb], in_=pp[Rr, 0:cb - ca], func=AF.Identity),
                             reads=[ppr], writes=[xbpre_res])
                    ws.release(plan[(pss, "wx", sidx)])
                    P.op("dve", lambda e, s=s, sidx=sidx: e.tensor_scalar(
                        out=xb[s][Rr, 0:NTK], in0=xbpre[Rr, 0:NTK], scalar1=rv_sb[Rr, 0, sidx:sidx + 1], scalar2=rv_sb[Rr, 4, sidx:sidx + 1],
                        op0=ALU.mult, op1=ALU.add), reads=[xbpre_res, rv_res], writes=[xb_res[s]])
                    for jj in range(1, 4):
                        P.op("dve", lambda e, s=s, sidx=sidx, jj=jj: e.scalar_tensor_tensor(
                            out=xb[s][Rr, 0:NTK], in0=xbpre[Rr, jj:jj + NTK], scalar=rv_sb[Rr, jj, sidx:sidx + 1], in1=xb[s][Rr, 0:NTK],
                            op0=ALU.mult, op1=ALU.add), reads=[xbpre_res, rv_res], writes=[xb_res[s]])
                    P.op("act", lambda e, s=s: e.activation(out=xb16[s][Rr, 0:NTK], in_=xb[s][Rr, 0:NTK], func=AF.Identity),
                         reads=[xb_res[s]], writes=[xb16_res[s]])
                for dd in range(2):
                    gw, gwres = ws.get(plan[(pss, "gw", dd, n)], lookahead=2)
                    for so in range(2):
                        sidx = 2 * n + so
                        kk_ = itc[0] % 2
                        itc[0] += 1
                        A_, A_r = abufs[kk_], ab_ress[kk_]
                        B_, B_r = bbufs[kk_], bb_ress[kk_]
                        for ci, (ca, cb) in enumerate(ctiles):
                            pr_, prr = ps[2 + ci % 2], ps_res[2 + ci % 2]
                            pi_, pir = ps[4 + ci % 2], ps_res[4 + ci % 2]
                            for si in range(2):
                                P.op("pe", lambda e, si=si, so=so, pr_=pr_, ca=ca, cb=cb, gw=gw: e.matmul(
                                    pr_[Rr, 0:cb - ca], lhsT=gw[Rr, (si * 2 + so) * RSUB:(si * 2 + so + 1) * RSUB], rhs=xb16[si][Rr, ca:cb],
                                    start=(si == 0), stop=(si == 1)), reads=[gwres, xb16_res[si]], writes=[prr])
                            for si in range(2):
                                P.op("pe", lambda e, si=si, so=so, pi_=pi_, ca=ca, cb=cb, gw=gw: e.matmul(
                                    pi_[Rr, 0:cb - ca], lhsT=gw[Rr, 352 + (si * 2 + so) * RSUB:352 + (si * 2 + so + 1) * RSUB], rhs=xb16[si][Rr, ca:cb],
                                    start=(si == 0), stop=(si == 1)), reads=[gwres, xb16_res[si]], writes=[pir])
                            P.op("act", lambda e, pr_=pr_, ca=ca, cb=cb, dd=dd, sidx=sidx: e.activation(
                                out=A_[Rr, ca:cb], in_=pr_[Rr, 0:cb - ca], func=AF.Sigmoid, bias=rv_sb[Rr, 5 + dd, sidx:sidx + 1]),
                                reads=[prr, rv_res], writes=[A_r])
                            P.op("act", lambda e, pi_=pi_, ca=ca, cb=cb, dd=dd, sidx=sidx: e.activation(
                                out=B_[Rr, ca:cb], in_=pi_[Rr, 0:cb - ca], func=AF.Sigmoid, bias=rv_sb[Rr, 7 + dd, sidx:sidx + 1]),
                                reads=[pir, rv_res], writes=[B_r])
                        if phase == "A" and not ctx:
                            P.op("dve", lambda e: e.reduce_sum(out=rsum[Rr, :], in_=A_[Rr, 0:NTK], axis=mybir.AxisListType.X),
                                 reads=[A_r], writes=[rsum_res])
                            P.op("act", lambda e, dd=dd, sidx=sidx: e.activation(out=so_sb[Rr, 2 * dd, sidx:sidx + 1], in_=rsum[Rr, :], func=AF.Exp,
                                                                               scale=cp_sb[Rr, dd, sidx:sidx + 1]),
                                 reads=[rsum_res, rv_res], writes=[so_res])
                        P.op("act", lambda e, dd=dd, sidx=sidx: e.activation(out=A_[Rr, 0:NTK], in_=A_[Rr, 0:NTK], func=AF.Exp,
                                                                           scale=cp_sb[Rr, dd, sidx:sidx + 1]),
                             reads=[A_r, rv_res], writes=[A_r])
                        P.op("act", lambda e: e.activation(out=tbuf[Rr, 0:NTK], in_=A_[Rr, 0:NTK], func=AF.Square), reads=[A_r], writes=[tb_res])
                        P.op("act", lambda e: e.activation(out=tbuf[Rr, 0:NTK], in_=tbuf[Rr, 0:NTK], func=AF.Sqrt, scale=-1.0, bias=one_c[Rr, 0:1]),
                             reads=[tb_res], writes=[tb_res])
                        P.op("dve", lambda e, so=so: e.tensor_tensor(out=B_[Rr, 0:NTK], in0=B_[Rr, 0:NTK], in1=xb[so][Rr, 0:NTK], op=ALU.mult),
                             reads=[B_r, xb_res[so]], writes=[B_r])
                        P.op("dve", lambda e: e.tensor_tensor(out=B_[Rr, 0:NTK], in0=B_[Rr, 0:NTK], in1=tbuf[Rr, 0:NTK], op=ALU.mult),
                             reads=[B_r, tb_res], writes=[B_r])
                        if phase == "B":
                            init = carry[Rr, dd, sidx:sidx + 1]
                            dst, dres = (ybuf[so], yb_res[so]) if dd == 0 else (tbuf, tb_res)
                        else:
                            init = 0.0
                            dst, dres = tbuf, tb_res
                        if dd == 0:
                            P.op("dve", lambda e, dst=dst, init=init: e.tensor_tensor_scan(
                                out=dst[Rr, 0:NTK], data0=A_[Rr, 0:NTK], data1=B_[Rr, 0:NTK], initial=init, op0=ALU.mult, op1=ALU.add),
                                reads=[A_r, B_r] + ([carry_res] if phase == "B" else []), writes=[dres])
                        else:
                            P.op("dve", lambda e, dst=dst, init=init: e.tensor_tensor_scan(
                                out=dst[Rr, NTK - 1::-1] if False else dst[Rr, 0:NTK][:, ::-1], data0=A_[Rr, 0:NTK][:, ::-1], data1=B_[Rr, 0:NTK][:, ::-1],
                                initial=init, op0=ALU.mult, op1=ALU.add),
                                reads=[A_r, B_r] + ([carry_res] if phase == "B" else []), writes=[dres])
                        if phase == "A":
                            col = NTK - 1 if dd == 0 else 0
                            row = (4 + dd) if ctx else (2 * dd + 1)
                            P.op("act", lambda e, col=col, row=row, sidx=sidx: e.activation(out=so_sb[Rr, row, sidx:sidx + 1], in_=tbuf[Rr, col:col + 1], func=AF.Identity),
                                 reads=[tb_res], writes=[so_res])
                        elif dd == 1:
                            P.op("dve", lambda e, so=so: e.tensor_tensor(out=ybuf[so][Rr, 0:NTK], in0=ybuf[so][Rr, 0:NTK], in1=tbuf[Rr, 0:NTK], op=ALU.add),
                                 reads=[tb_res], writes=[yb_res[so]])
                    ws.release(plan[(pss, "gw", dd, n)])
                if phase == "B":
                    for so in range(2):
                        sidx = 2 * n + so
                        wg, wgres = ws.get(plan[(pss, "wg", sidx)], lookahead=2)
                        pb, pbr = pr16[so], pr16_res[so]
                        for ci, (ca, cb) in enumerate(ctiles):
                            pg, pgr = ps[6 + ci % 2], ps_res[6 + ci % 2]
                            for kc in range(NDC):
                                P.op("pe", lambda e, kc=kc, pg=pg, ca=ca, cb=cb, wg=wg: e.matmul(
                                    pg[Rr, 0:cb - ca], lhsT=wg[:, kc * RSUB:(kc + 1) * RSUB], rhs=a16f[:, kc, RG_HL + ca:RG_HL + cb],
                                    start=(kc == 0), stop=(kc == NDC - 1)), reads=[wgres, a16f_res[kc]], writes=[pgr])
                            k = ci % 2
                            P.op("act", lambda e, pg=pg, k=k, ca=ca, cb=cb: e.activation(out=gl[k][Rr, 0:cb - ca], in_=pg[Rr, 0:cb - ca], func=AF.Gelu_apprx_tanh),
                                 reads=[pgr], writes=[gl_res[k]])
                            P.op("dve", lambda e, k=k, so=so, ca=ca, cb=cb, pb=pb: e.tensor_tensor(
                                out=pb[Rr, ca:cb], in0=ybuf[so][Rr, ca:cb], in1=gl[k][Rr, 0:cb - ca], op=ALU.mult),
                                reads=[yb_res[so], gl_res[k]], writes=[pbr])
                        ws.release(plan[(pss, "wg", sidx)])
                        P.dma("pool", prod[sidx], pb[Rr, :], reads=[pbr])

        for pss in passes:
            run_pass(pss)
        if phase == "A":
            P.dma("pool", sout, so_sb[:], reads=[so_res])
        P.finish("sp")
        nc._stats = dict(P.n_inst)
    return nc


def _pad128(a):
    shp = list(a.shape)
    shp[-2] = 128
    out = np.zeros(shp, np.float32)
    out[..., :a.shape[-2], :] = a
    return out


def rg_col(v):
    return np.asarray(v, np.float32).reshape(NSUB, RSUB).T


def run_rg_layer(inputs, x):
    for ph in ("A", "B"):
        if ("rg", ph) not in _NC_CACHE:
            _NC_CACHE[("rg", ph)] = build_rg(ph)
    if ("lin",) not in _NC_CACHE:
        _NC_CACHE[("lin",)] = build_layer("lin", 0, 0)
    pe = _pos_embed_table()
    modw = w_colblocks(np.asarray(inputs["mod_w"][0][:, 0:4096]), 32)
    modb_full = np.ascontiguousarray(np.asarray(inputs["mod_b"][0], np.float32).reshape(96, 128).T)
    rvt = np.zeros((128, 11, NSUB), np.float32)
    cw = np.asarray(inputs["rg_conv_w"][0], np.float32)
    for j in range(4):
        rvt[:RSUB, j] = rg_col(cw[j])
    rvt[:RSUB, 4] = rg_col(inputs["rg_conv_b"][0])
    for dd in range(2):
        rvt[:RSUB, 5 + dd] = rg_col(inputs["rg_br"][0, dd])
        rvt[:RSUB, 7 + dd] = rg_col(inputs["rg_bi"][0, dd])
        rvt[:RSUB, 9 + dd] = rg_col(inputs["rg_lam"][0, dd])
    rvt[RSUB:, 9:11] = 1.0

    def sub_cols(w):
        return np.ascontiguousarray(np.asarray(w, np.float32).reshape(NDC, 128, NSUB, RSUB).transpose(2, 1, 0, 3).reshape(NSUB, 128, NDC * RSUB))
    wxr = sub_cols(inputs["rg_w_x"][0])
    wgr = sub_cols(inputs["rg_w_gate"][0])
    gw = np.zeros((32, 128, 704), np.float32)
    for dd in range(2):
        for n in range(16):
            for k_, nm in enumerate(("rg_wr", "rg_wi")):
                blk = np.asarray(inputs[nm][0, dd, n], np.float32).reshape(2, RSUB, 2, RSUB).transpose(1, 0, 2, 3).reshape(RSUB, 352)
                gw[dd * 16 + n, :RSUB, k_ * 352:(k_ + 1) * 352] = blk
    base = []
    for core in range(NCORE):
        b, q = divmod(core, 4)
        ctxT = np.ascontiguousarray(np.asarray(inputs["ctx"][b], np.float32).reshape(CTXL, NDC, 128).transpose(2, 1, 0))
        base.append(dict(xin=to_xT(x[b], q, RG_HL, RG_HR), pein=to_xT(pe, q, RG_HL, RG_HR), ctxin=ctxT, modb=modb_full,
                         cT=cond_T(inputs, b), modw=modw, edge=edge_flags(q), rv=rvt, wxr=wxr, gwr=gw))
    rA = run_bass_kernel_spmd(_NC_CACHE[("rg", "A")], base, core_ids=list(range(NCORE)))
    souts = [rA.results[c]["sout"] for c in range(NCORE)]
    summ = np.ascontiguousarray(np.stack([s_[:, 0:4, :] for s_ in souts], axis=0))
    mapsB = []
    for core in range(NCORE):
        b, q = divmod(core, 4)
        mfb = np.zeros((128, 2, NCORE), np.float32)
        for r in range(NCORE):
            rb, rq = divmod(r, 4)
            if rb == b and rq < q:
                mfb[:, 0, r] = 1.0
            if rb == b and rq > q:
                mfb[:, 1, r] = 1.0
        m = dict(base[core])
        m.update(wgr=wgr, summ=summ, ctxs=np.ascontiguousarray(souts[core][:, 4:6, :]), mfb=mfb)
        mapsB.append(m)
    rB = run_bass_kernel_spmd(_NC_CACHE[("rg", "B")], mapsB, core_ids=list(range(NCORE)))
    cm = common_maps(inputs, 0, [])
    wor = np.ascontiguousarray(np.asarray(inputs["rg_w_out"][0], np.float32).reshape(22, 128, D))
    mapsC = []
    for core in range(NCORE):
        b, q = divmod(core, 4)
        prod = rB.results[core]["prod"]
        m = dict(cm)
        m.update(xin=to_xT(x[b], q, 0, 0), cT=cond_T(inputs, b), edge=edge_flags(q), pe=to_xT(pe, q, 0, 0),
                 prodin=np.ascontiguousarray(prod.reshape(22, 128, TOK)), wor=wor)
        mapsC.append(m)
    res = run_bass_kernel_spmd(_NC_CACHE[("lin",)], mapsC, core_ids=list(range(NCORE)))
    out = np.empty_like(x)
    for core in range(NCORE):
        b, q = divmod(core, 4)
        out[b, q * TOK:(q + 1) * TOK] = from_xT(res.results[core]["xout"])
    return out
```

```python
import math
from contextlib import ExitStack

import numpy as np
import concourse.bass as bass
import concourse.mybir as mybir
from concourse.bass_utils import run_bass_kernel_spmd

F32 = mybir.dt.float32
BF16 = mybir.dt.bfloat16
AF = mybir.ActivationFunctionType
ALU = mybir.AluOpType

D = 2048
NDC = 16
S = 8192
NCORE = 8
TOK = 2048
T = 512
NTT = TOK // T
DFF = 5632
NF = DFF // 128
GF = 4
DEPTH = 4
ALPHA = (2 * DEPTH) ** 0.25
LN_EPS = 1e-5
EPOCH = 30000


class Res:
    __slots__ = ("name", "last_write", "readers")

    def __init__(self, name=""):
        self.name = name
        self.last_write = None
        self.readers = []


class Prog:
    def __init__(self, nc, stack, n_dma_sems=8):
        self.nc = nc
        self.stack = stack
        self.eng = {"pe": nc.tensor, "act": nc.scalar, "dve": nc.vector, "pool": nc.gpsimd, "sp": nc.sync}
        self.sem = {}
        self.cnt = {}
        self.nsem = 0
        for e in ("pe", "act", "dve", "pool"):
            self._new_epoch(e)
        self.waited = {e: {} for e in self.eng}
        self.dma_sems = {}
        self.dma_rr = {}
        for q in ("sp", "pool", "act"):
            self.dma_sems[q] = [[self._alloc_sem(f"dma_{q}_{i}"), 0] for i in range(n_dma_sems)]
            self.dma_rr[q] = 0
        self.n_inst = {e: 0 for e in self.eng}

    def _alloc_sem(self, name):
        self.nsem += 1
        return self.stack.enter_context(self.nc.semaphore(f"{name}_{self.nsem}"))

    def _new_epoch(self, e):
        self.sem[e] = self._alloc_sem(f"eng_{e}")
        self.cnt[e] = 0

    def sbuf(self, name, shape, dtype):
        return self.stack.enter_context(self.nc.sbuf_tensor(name, list(shape), dtype))

    def psum(self, name, shape, dtype=F32):
        return self.stack.enter_context(self.nc.psum_tensor(name, list(shape), dtype))

    def _wait(self, e, tok):
        src, sem, val = tok
        key = id(sem)
        if self.waited[e].get(key, 0) >= val:
            return
        self.eng[e].wait_ge(sem, val)
        self.waited[e][key] = val

    def _deps(self, e, reads, writes, same_engine_ok=True):
        toks = []
        for r in reads:
            if r.last_write is not None:
                toks.append(r.last_write)
        for w in writes:
            if w.last_write is not None:
                toks.append(w.last_write)
            toks.extend(w.readers)
        for tok in toks:
            if same_engine_ok and tok[0] == e and e == "pe":
                continue
            self._wait(e, tok)

    def _commit(self, tok, reads, writes):
        for r in reads:
            r.readers.append(tok)
            if len(r.readers) > 48:
                latest = {}
                for t in r.readers:
                    k = (t[0], id(t[1]))
                    if k not in latest or latest[k][2] < t[2]:
                        latest[k] = t
                r.readers = list(latest.values())
        for w in writes:
            w.last_write = tok
            w.readers = []

    def op(self, e, fn, reads=(), writes=()):
        self._deps(e, reads, writes)
        inst = fn(self.eng[e])
        if self.cnt[e] >= EPOCH:
            self._new_epoch(e)
        self.cnt[e] += 1
        inst.then_inc(self.sem[e], 1)
        tok = (e, self.sem[e], self.cnt[e])
        self._commit(tok, reads, writes)
        self.n_inst[e] += 1
        return tok

    def dma(self, q, out, in_, reads=(), writes=(), **kw):
        self._deps(q, reads, writes, same_engine_ok=False)
        pool = self.dma_sems[q]
        slot = pool[self.dma_rr[q] % len(pool)]
        self.dma_rr[q] += 1
        sem, val = slot
        if val > 0:
            self._wait(q, ("dma", sem, val))
        inst = self.eng[q].dma_start(out=out, in_=in_, **kw)
        slot[1] = val + 16
        inst.then_inc(sem, 16)
        tok = ("dma", sem, val + 16)
        self._commit(tok, reads, writes)
        self.n_inst[q] += 1
        return tok

    def finish(self, e="sp"):
        for q, pool in self.dma_sems.items():
            for sem, val in pool:
                if val > 0:
                    self._wait(e, ("dma", sem, val))


class WStream:
    NSTG = 4
    NRING = 14

    def __init__(self, P, nstg=None, nring=None):
        self.P = P
        self.NSTG = nstg or WStream.NSTG
        self.NRING = nring or WStream.NRING
        self.stg = [P.sbuf(f"stg{i}", [128, 2048], F32) for i in range(self.NSTG)]
        self.stg_res = [Res(f"stg{i}") for i in range(self.NSTG)]
        self.ring = [P.sbuf(f"wring{i}", [128, 2048], BF16) for i in range(self.NRING)]
        self.ring_res = [Res(f"wring{i}") for i in range(self.NRING)]
        self.items = []
        self.issued = 0
        self.n_stg = 0
        self.n_ring = 0
        self.loc = {}
        self.stg_owner = [None] * self.NSTG
        self.ring_owner = [None] * self.NRING
        self.released = set()

    def add(self, src_ap, cast=True, width=2048):
        self.items.append((src_ap, cast, width))
        return len(self.items) - 1

    def issue_until(self, idx):
        P = self.P
        idx = min(idx, len(self.items) - 1)
        while self.issued <= idx:
            i = self.issued
            src, cast, wd = self.items[i]
            if cast:
                r = self.n_ring % self.NRING
                if self.ring_owner[r] is not None and self.ring_owner[r] not in self.released:
                    return
                self.n_ring += 1
                self.ring_owner[r] = i
                P.dma("pool", self.ring[r][:, 0:wd], src, writes=[self.ring_res[r]])
                self.loc[i] = (self.ring[r], self.ring_res[r])
            else:
                s = self.n_stg % self.NSTG
                if self.stg_owner[s] is not None and self.stg_owner[s] not in self.released:
                    return
                self.n_stg += 1
                self.stg_owner[s] = i
                P.dma("sp", self.stg[s][:, 0:wd], src, writes=[self.stg_res[s]])
                self.loc[i] = (self.stg[s], self.stg_res[s])
            self.issued += 1

    def get(self, idx, lookahead=6):
        self.issue_until(idx + lookahead)
        assert idx in self.loc, f"weight item {idx} could not be issued (ring full: missing release?)"
        return self.loc[idx]

    def release(self, idx):
        self.released.add(idx)


V_LNG0, V_LNB0, V_LNG1, V_LNB1, V_MX0, V_MX1, V_MX2, V_MX3 = range(8)
NV = 8


class LayerCtx:
    pass


def _bc(ap_col, n):
    return ap_col.to_broadcast([128, n])


def build_layer(kind, halo_l, halo_r, n_cond=2, extra=None):
    nc = bass.Bass("TRN2", target_bir_lowering=False)
    NT = halo_l + TOK + halo_r
    W = halo_l + T + halo_r
    dram = {}

    def din(name, shape, dt=F32):
        dram[name] = nc.dram_tensor(name, list(shape), dt, kind="ExternalInput").ap()
        return dram[name]

    xin = din("xin", [128, NDC, NT])
    vec = din("vec", [128, NV, NDC])
    min_ = din("min", [128, 96, n_cond])
    w1r = din("w1r", [NF, 128, 2048])
    w3r = din("w3r", [NF, 128, 2048])
    w2r = din("w2r", [NF, 128, 2048])
    edge = din("edge", [128, 2])
    if kind == "pool":
        poolw = din("poolw", [16, 128, 512])
        pinv = din("pinv", [4, TOK])
    NCV = 80 + 16 * 31
    if kind == "conf":
        cvw1 = din("cvw1", [32, 128, 2048])
        cvw2 = din("cvw2", [16, 128, 2048])
        cvv = din("cvv", [128, NCV])
        ident_in = din("ident", [128, 128])
    if kind == "lin":
        pe_in = din("pe", [128, NDC, TOK])
        prodin = din("prodin", [22, 128, TOK], BF16)
        wor = din("wor", [22, 128, 2048])
    if kind == "none":
        pe_in = din("pe", [128, NDC, TOK])
    if kind == "four":
        fin = din("fin", [128, 2, NDC, TOK], BF16)
        ccsc = din("ccsc", [4, 128, 1024])
        ftw = din("ftw", [16, 128, 2048])
    xout = nc.dram_tensor("xout", [128, NDC, TOK], F32, kind="ExternalOutput").ap()

    with ExitStack() as st:
        P = Prog(nc, st)
        ws = WStream(P, nstg=2)
        L = LayerCtx()
        xs = P.sbuf("xs", [128, NDC, W], F32)
        xs_res = [Res(f"xs{d}") for d in range(NDC)]
        a16 = P.sbuf("a16", [128, NDC, W], BF16)
        a16_res = [Res(f"a16_{d}") for d in range(NDC)]
        g16 = P.sbuf("g16", [128, 2, GF, T], BF16)
        g16_res = [[Res(f"g16_{i}_{j}") for j in range(GF)] for i in range(2)]
        stmp = [P.sbuf(f"stmp{i}", [128, T], F32) for i in range(2)]
        stmp_res = [Res(f"stmp{i}") for i in range(2)]
        sq = [P.sbuf(f"sq{i}", [128, T], F32) for i in range(2)]
        sq_res = [Res(f"sq{i}") for i in range(2)]
        lnt = P.sbuf("lnt", [128, 2, T], F32)
        lnt_res = Res("lnt")
        lnt2 = P.sbuf("lnt2", [128, T], F32)
        onesD = P.sbuf("onesD", [128, 128], F32)
        onesD_res = Res("onesD")
        vraw = P.sbuf("vraw", [128, NV, NDC], F32)
        vraw_res = Res("vraw")
        modb_sb = P.sbuf("modb_sb", [128, 96], F32)
        cs = P.sbuf("cs", [128, NDC, n_cond], F32)
        cs_res = Res("cs")
        msb = P.sbuf("msb", [128, 96, n_cond], F32)
        msb_res = Res("msb")
        dv = P.sbuf("dv", [128, 8, NDC], F32)
        dv_res = Res("dv")
        edge_sb = P.sbuf("edge_sb", [128, 2], F32)
        edge_res = Res("edge")
        ps = [P.psum(f"ps{i}", [128, T]) for i in range(8)]
        ps_res = [Res(f"ps{i}") for i in range(8)]

        plan = []
        if kind == "pool":
            pw_items = [ws.add(poolw[i], width=512) for i in range(16)]
        if kind == "four":
            cc_items = [ws.add(ccsc[i], width=1024) for i in range(4)]
        for tt in range(NTT):
            d_ = {}
            if kind == "conf":
                for oc in range(NDC):
                    d_[("cv", oc)] = ws.add(cvw1[oc])
                    d_[("cg", oc)] = ws.add(cvw1[16 + oc])
                for oc in range(NDC):
                    d_[("c2", oc)] = ws.add(cvw2[oc])
            if kind == "four":
                for oc in range(NDC):
                    d_[("fw", oc)] = ws.add(ftw[oc])
            if kind == "lin":
                for kc in range(22):
                    d_[("wo", kc)] = ws.add(wor[kc])
            for f in range(NF):
                d_[("w1", f)] = ws.add(w1r[f])
                d_[("w3", f)] = ws.add(w3r[f])
                d_[("w2", f)] = ws.add(w2r[f])
            plan.append(d_)

        P.op("pool", lambda e: e.memset(onesD[:], 1.0 / D), writes=[onesD_res])
        P.dma("act", vraw[:], vec, writes=[vraw_res])
        P.dma("act", edge_sb[:], edge, writes=[edge_res])
        P.dma("act", msb[:], min_, writes=[msb_res])

        def mvec(k6, dc, j=0):
            return msb[:, k6 * 16 + dc, j:j + 1]

        def mrow(k6):
            return msb[:, k6 * 16:(k6 + 1) * 16, 0]
        P.op("dve", lambda e: e.tensor_scalar(out=dv[:, 0, :], in0=mrow(1), scalar1=1.0, scalar2=None, op0=ALU.add),
             reads=[msb_res], writes=[dv_res])
        if kind == "pool":
            P.op("dve", lambda e: e.tensor_tensor(out=dv[:, 1, :], in0=mrow(2), in1=vraw[:, V_MX1, :], op=ALU.mult),
                 reads=[msb_res, vraw_res], writes=[dv_res])
            P.op("dve", lambda e: e.tensor_tensor(out=dv[:, 2, :], in0=dv[:, 1, :], in1=vraw[:, V_MX0, :], op=ALU.mult),
                 reads=[vraw_res], writes=[dv_res])
        else:
            P.op("dve", lambda e: e.tensor_copy(out=dv[:, 1, :], in_=mrow(2)), reads=[msb_res], writes=[dv_res])
            P.op("dve", lambda e: e.tensor_tensor(out=dv[:, 2, :], in0=dv[:, 1, :], in1=vraw[:, V_MX0, :], op=ALU.mult),
                 reads=[vraw_res], writes=[dv_res])
        P.op("dve", lambda e: e.tensor_scalar(out=dv[:, 3, :], in0=vraw[:, V_LNG0, :], scalar1=ALPHA, scalar2=None, op0=ALU.mult),
             reads=[vraw_res], writes=[dv_res])
        P.op("dve", lambda e: e.tensor_scalar(out=dv[:, 4, :], in0=vraw[:, V_LNB0, :], scalar1=ALPHA, scalar2=None, op0=ALU.mult),
             reads=[vraw_res], writes=[dv_res])
        P.op("dve", lambda e: e.tensor_scalar(out=dv[:, 5, :], in0=mrow(4), scalar1=1.0, scalar2=1.0 / ALPHA, op0=ALU.add, op1=ALU.mult),
             reads=[msb_res], writes=[dv_res])
        VR = [vraw_res, dv_res, msb_res]

        def emit_ln(c0, gcol, bcol, buf=None, bres=None, func=AF.Identity, dst=None):
            buf = xs if buf is None else buf
            bres = xs_res if bres is None else bres
            pm, pq = ps[4], ps[5]
            pmr, pqr = ps_res[4], ps_res[5]
            for dc in range(NDC):
                k = dc % 2
                P.op("act", lambda e, dc=dc, k=k: e.activation(out=sq[k][:], in_=buf[:, dc, c0:c0 + T], func=AF.Square),
                     reads=[bres[dc]], writes=[sq_res[k]])
                P.op("pe", lambda e, dc=dc: e.matmul(pm[:], lhsT=onesD[:], rhs=buf[:, dc, c0:c0 + T],
                                                      start=(dc == 0), stop=(dc == NDC - 1)),
                     reads=[onesD_res, bres[dc]], writes=[pmr])
                P.op("pe", lambda e, dc=dc, k=k: e.matmul(pq[:], lhsT=onesD[:], rhs=sq[k][:],
                                                           start=(dc == 0), stop=(dc == NDC - 1)),
                     reads=[onesD_res, sq_res[k]], writes=[pqr])
            P.op("act", lambda e: e.activation(out=lnt[:, 0, :], in_=pm[:], func=AF.Square), reads=[pmr], writes=[lnt_res])
            P.op("dve", lambda e: e.tensor_tensor(out=lnt[:, 0, :], in0=pq[:], in1=lnt[:, 0, :], op=ALU.subtract),
                 reads=[pqr], writes=[lnt_res])
            P.op("dve", lambda e: e.tensor_scalar(out=lnt[:, 0, :], in0=lnt[:, 0, :], scalar1=LN_EPS, scalar2=None, op0=ALU.add),
                 writes=[lnt_res])
            P.op("act", lambda e: e.activation(out=lnt[:, 1, :], in_=lnt[:, 0, :], func=AF.Sqrt), reads=[lnt_res], writes=[lnt_res])
            P.op("dve", lambda e: e.reciprocal(out=lnt[:, 1, :], in_=lnt[:, 1, :]), reads=[lnt_res], writes=[lnt_res])
            for _it in range(2):
                P.op("dve", lambda e: e.tensor_tensor(out=lnt2[:], in0=lnt[:, 0, :], in1=lnt[:, 1, :], op=ALU.mult), writes=[lnt_res])
                P.op("dve", lambda e: e.tensor_tensor(out=lnt2[:], in0=lnt2[:], in1=lnt[:, 1, :], op=ALU.mult), writes=[lnt_res])
                P.op("dve", lambda e: e.tensor_scalar(out=lnt2[:], in0=lnt2[:], scalar1=-0.5, scalar2=1.5, op0=ALU.mult, op1=ALU.add), writes=[lnt_res])
                P.op("dve", lambda e: e.tensor_tensor(out=lnt[:, 1, :], in0=lnt[:, 1, :], in1=lnt2[:], op=ALU.mult), writes=[lnt_res])
            for dc in range(NDC):
                P.op("dve", lambda e, dc=dc: e.tensor_tensor(out=buf[:, dc, c0:c0 + T], in0=buf[:, dc, c0:c0 + T], in1=pm[:], op=ALU.subtract),
                     reads=[pmr], writes=[bres[dc]])
                P.op("dve", lambda e, dc=dc: e.tensor_tensor(out=buf[:, dc, c0:c0 + T], in0=buf[:, dc, c0:c0 + T], in1=lnt[:, 1, :], op=ALU.mult),
                     reads=[lnt_res], writes=[bres[dc]])
                if dst is None:
                    P.op("act", lambda e, dc=dc: e.activation(out=buf[:, dc, c0:c0 + T], in_=buf[:, dc, c0:c0 + T], func=func,
                                                              scale=gcol(dc), bias=bcol(dc)),
                         reads=VR + [bres[dc]], writes=[bres[dc]])
                else:
                    dap, dres = dst(dc)
                    P.op("act", lambda e, dc=dc, dap=dap: e.activation(out=dap, in_=buf[:, dc, c0:c0 + T], func=func,
                                                                       scale=gcol(dc), bias=bcol(dc)),
                         reads=VR + [bres[dc]], writes=[dres])

        def emit_ffn(tt, c0):
            pl = plan[tt]
            for dc in range(NDC):
                P.op("act", lambda e, dc=dc: e.activation(out=a16[:, dc, 0:T], in_=xs[:, dc, c0:c0 + T], func=AF.Identity,
                                                          scale=dv[:, 5, dc:dc + 1], bias=mvec(3, dc)),
                     reads=VR + [xs_res[dc]], writes=[a16_res[dc]])
            ngrp = NF // GF
            for grp in range(ngrp):
                gb = grp % 2
                for fi in range(GF):
                    f = grp * GF + fi
                    w1t, w1res = ws.get(pl[("w1", f)])
                    w3t, w3res = ws.get(pl[("w3", f)])
                    p1, p3 = ps[(f % 2) * 2], ps[(f % 2) * 2 + 1]
                    p1r, p3r = ps_res[(f % 2) * 2], ps_res[(f % 2) * 2 + 1]
                    for kc in range(NDC):
                        P.op("pe", lambda e, kc=kc, w1t=w1t, p1=p1: e.matmul(
                            p1[:], lhsT=w1t[:, kc * 128:(kc + 1) * 128], rhs=a16[:, kc, 0:T],
                            start=(kc == 0), stop=(kc == NDC - 1)), reads=[w1res, a16_res[kc]], writes=[p1r])
                    for kc in range(NDC):
                        P.op("pe", lambda e, kc=kc, w3t=w3t, p3=p3: e.matmul(
                            p3[:], lhsT=w3t[:, kc * 128:(kc + 1) * 128], rhs=a16[:, kc, 0:T],
                            start=(kc == 0), stop=(kc == NDC - 1)), reads=[w3res, a16_res[kc]], writes=[p3r])
                    ws.release(pl[("w1", f)])
                    ws.release(pl[("w3", f)])
                    k = f % 2
                    P.op("act", lambda e, k=k, p1=p1: e.activation(out=stmp[k][:], in_=p1[:], func=AF.Silu),
                         reads=[p1r], writes=[stmp_res[k]])
                    P.op("dve", lambda e, k=k, p3=p3, gb=gb, fi=fi: e.tensor_tensor(
                        out=g16[:, gb, fi, :], in0=p3[:], in1=stmp[k][:], op=ALU.mult),
                        reads=[p3r, stmp_res[k]], writes=[g16_res[gb][fi]])
                w2 = [ws.get(pl[("w2", grp * GF + fi)]) for fi in range(GF)]
                for dc in range(NDC):
                    po, por = ps[6 + dc % 2], ps_res[6 + dc % 2]
                    for fi in range(GF):
                        P.op("pe", lambda e, fi=fi, dc=dc, po=po: e.matmul(
                            po[:], lhsT=w2[fi][0][:, dc * 128:(dc + 1) * 128], rhs=g16[:, gb, fi, :],
                            start=(fi == 0), stop=(fi == GF - 1)), reads=[w2[fi][1], g16_res[gb][fi]], writes=[por])
                    P.op("dve", lambda e, dc=dc, po=po: e.scalar_tensor_tensor(
                        out=xs[:, dc, c0:c0 + T], in0=po[:], scalar=mvec(5, dc), in1=xs[:, dc, c0:c0 + T],
                        op0=ALU.mult, op1=ALU.add), reads=VR + [por], writes=[xs_res[dc]])
                for fi in range(GF):
                    ws.release(pl[("w2", grp * GF + fi)])

        if kind == "pool":
            pwb = P.sbuf("pwb", [128, 16, 512], BF16)
            pwb_res = Res("pwb")
            for i in range(16):
                wt, wr = ws.get(pw_items[i], lookahead=2)
                P.op("act", lambda e, i=i, wt=wt: e.activation(out=pwb[:, i, :], in_=wt[:, 0:512], func=AF.Identity), reads=[wr], writes=[pwb_res])
                ws.release(pw_items[i])
            hbuf = [P.sbuf(f"hbuf{i}", [128, W], F32) for i in range(2)]
            hbuf_res = [Res(f"hbuf{i}") for i in range(2)]
            sl = [P.sbuf(f"sl{i}", [128, W], F32) for i in range(4)]
            sl_res = [Res(f"sl{i}") for i in range(4)]
            inv_sb = P.sbuf("inv_sb", [128, 4, T], F32)
            inv_res = Res("inv")

        def emit_pool_mixer(tt):
            c0 = halo_l
            P.dma("act", inv_sb[:], pinv[:, tt * T:(tt + 1) * T].partition_broadcast(128), writes=[inv_res])
            for dc in range(NDC):
                g = dc // 4
                hb, hr = hbuf[dc % 2], hbuf_res[dc % 2]
                P.op("act", lambda e, dc=dc, hb=hb: e.activation(out=hb[:], in_=xs[:, dc, :], func=AF.Identity,
                                                               scale=dv[:, 0, dc:dc + 1], bias=mvec(0, dc)),
                     reads=VR + [xs_res[dc]], writes=[hr])
                if tt == 0:
                    P.op("dve", lambda e, hb=hb: e.tensor_scalar(out=hb[:, 0:halo_l], in0=hb[:, 0:halo_l], scalar1=edge_sb[:, 0:1],
                                                                 scalar2=None, op0=ALU.mult), reads=[edge_res], writes=[hr])
                if tt == NTT - 1:
                    P.op("dve", lambda e, hb=hb: e.tensor_scalar(out=hb[:, c0 + T:W], in0=hb[:, c0 + T:W], scalar1=edge_sb[:, 1:2],
                                                                 scalar2=None, op0=ALU.mult), reads=[edge_res], writes=[hr])
                cur, cur_r = hb, hr
                lo, hi = 0, W
                offs = [(1, 0), (1, 1), (2, 2), (4, 4)]
                for lev in range(g + 1):
                    a, b = offs[lev]
                    nlo, nhi = lo + a, hi - b
                    dst, dst_r = sl[lev], sl_res[lev]
                    P.op("pool", lambda e, cur=cur, dst=dst, a=a, b=b, nlo=nlo, nhi=nhi: e.tensor_tensor(
                        out=dst[:, nlo:nhi], in0=cur[:, nlo - a:nhi - a], in1=cur[:, nlo + b:nhi + b], op=ALU.add),
                        reads=[cur_r], writes=[dst_r])
                    cur, cur_r, lo, hi = dst, dst_r, nlo, nhi
                P.op("dve", lambda e, cur=cur, g=g: e.tensor_tensor(out=cur[:, c0:c0 + T], in0=cur[:, c0:c0 + T], in1=inv_sb[:, g, :], op=ALU.mult),
                     reads=[inv_res, cur_r], writes=[cur_r])
                P.op("dve", lambda e, cur=cur, hb=hb, dc=dc: e.tensor_tensor(out=a16[:, dc, 0:T], in0=cur[:, c0:c0 + T], in1=hb[:, c0:c0 + T], op=ALU.subtract),
                     reads=[cur_r, hr], writes=[a16_res[dc]])
            for oc in range(NDC):
                g = oc // 4
                po, por = ps[6 + oc % 2], ps_res[6 + oc % 2]
                for kc in range(4):
                    P.op("pe", lambda e, g=g, kc=kc, oc=oc, po=po: e.matmul(
                        po[:], lhsT=pwb[:, g * 4 + kc, (oc % 4) * 128:(oc % 4 + 1) * 128], rhs=a16[:, g * 4 + kc, 0:T],
                        start=(kc == 0), stop=(kc == 3)), reads=[pwb_res, a16_res[g * 4 + kc]], writes=[por])
                emit_residual(oc, po, por, c0)

        if kind == "conf":
            cvv_sb = P.sbuf("cvv_sb", [128, NCV], F32)
            P.dma("act", cvv_sb[:], cvv, writes=[vraw_res])
            ubuf = P.sbuf("ubuf", [128, NDC, T], F32)
            u_res = [Res(f"u{d}") for d in range(NDC)]
            u16 = [P.sbuf(f"u16_{i}", [128, W], BF16) for i in range(2)]
            u16_res = [Res(f"u16_{i}") for i in range(2)]
            dg = [P.sbuf(f"dg{i}", [128, 31, 128], BF16) for i in range(2)]
            dg_res = [Res(f"dg{i}") for i in range(2)]
            ident_sb = P.sbuf("ident_sb", [128, 128], F32)
            P.dma("act", ident_sb[:], ident_in, writes=[vraw_res])
            HWD = W // 2
            sgt = [P.sbuf(f"sgt{i}", [128, HWD], F32) for i in range(2)]
            sgt_res = [Res(f"sgt{i}") for i in range(2)]

        def emit_conf_mixer(tt):
            pl = plan[tt]
            c0 = halo_l
            for dc in range(NDC):
                P.op("act", lambda e, dc=dc: e.activation(out=a16[:, dc, :], in_=xs[:, dc, :], func=AF.Identity,
                                                          scale=dv[:, 0, dc:dc + 1], bias=mvec(0, dc)),
                     reads=VR + [xs_res[dc]], writes=[a16_res[dc]])
            for oc in range(NDC):
                wv, wvr = ws.get(pl[("cv", oc)])
                wg, wgr = ws.get(pl[("cg", oc)])
                for half in range(2):
                    ca, cb = half * HWD, (half + 1) * HWD
                    pv, pvr = ps[half * 2], ps_res[half * 2]
                    pg, pgr = ps[half * 2 + 1], ps_res[half * 2 + 1]
                    for kc in range(NDC):
                        P.op("pe", lambda e, kc=kc, pv=pv, ca=ca, cb=cb: e.matmul(
                            pv[:, 0:HWD], lhsT=wv[:, kc * 128:(kc + 1) * 128], rhs=a16[:, kc, ca:cb],
                            start=(kc == 0), stop=(kc == NDC - 1)), reads=[wvr, a16_res[kc]], writes=[pvr])
                    for kc in range(NDC):
                        P.op("pe", lambda e, kc=kc, pg=pg, ca=ca, cb=cb: e.matmul(
                            pg[:, 0:HWD], lhsT=wg[:, kc * 128:(kc + 1) * 128], rhs=a16[:, kc, ca:cb],
                            start=(kc == 0), stop=(kc == NDC - 1)), reads=[wgr, a16_res[kc]], writes=[pgr])
                ws.release(pl[("cv", oc)])
                ws.release(pl[("cg", oc)])
                for half in range(2):
                    ca, cb = half * HWD, (half + 1) * HWD
                    pv, pvr = ps[half * 2], ps_res[half * 2]
                    pg, pgr = ps[half * 2 + 1], ps_res[half * 2 + 1]
                    P.op("act", lambda e, half=half, pg=pg: e.activation(out=sgt[half][:], in_=pg[:, 0:HWD], func=AF.Sigmoid,
                                                                         bias=cvv_sb[:, 16 + oc:17 + oc]),
                         reads=VR + [pgr], writes=[sgt_res[half]])
                    P.op("dve", lambda e, half=half, pv=pv, ca=ca, cb=cb: e.scalar_tensor_tensor(
                        out=u16[oc % 2][:, ca:cb], in0=pv[:, 0:HWD], scalar=cvv_sb[:, oc:oc + 1], in1=sgt[half][:],
                        op0=ALU.add, op1=ALU.mult), reads=VR + [pvr, sgt_res[half]], writes=[u16_res[oc % 2]])
                uu, uur = u16[oc % 2], u16_res[oc % 2]
                if tt == 0:
                    P.op("dve", lambda e: e.tensor_scalar(out=uu[:, 0:halo_l], in0=uu[:, 0:halo_l], scalar1=edge_sb[:, 0:1],
                                                          scalar2=None, op0=ALU.mult), reads=[edge_res], writes=[uur])
                if tt == NTT - 1:
                    P.op("dve", lambda e: e.tensor_scalar(out=uu[:, c0 + T:W], in0=uu[:, c0 + T:W], scalar1=edge_sb[:, 1:2],
                                                          scalar2=None, op0=ALU.mult), reads=[edge_res], writes=[uur])
                dgo, dgr = dg[oc % 2], dg_res[oc % 2]
                dwc = 80 + oc * 31
                P.op("pool", lambda e: e.tensor_tensor(
                    out=dgo[:], in0=ident_sb[:].unsqueeze(1).to_broadcast([128, 31, 128]),
                    in1=cvv_sb[:, dwc:dwc + 31].unsqueeze(2).to_broadcast([128, 31, 128]), op=ALU.mult),
                    reads=VR, writes=[dgr])
                pc, pcr = ps[4 + oc % 2], ps_res[4 + oc % 2]
                for j in range(31):
                    P.op("pe", lambda e, j=j: e.matmul(pc[:], lhsT=dgo[:, j, :], rhs=uu[:, j:j + T], start=(j == 0), stop=(j == 30)),
                         reads=[dgr, uur], writes=[pcr])
                P.op("act", lambda e: e.activation(out=ubuf[:, oc, :], in_=pc[:], func=AF.Identity, bias=cvv_sb[:, 32 + oc:33 + oc]),
                     reads=VR + [pcr], writes=[u_res[oc]])
            emit_ln(0, lambda dc: cvv_sb[:, 48 + dc:49 + dc], lambda dc: cvv_sb[:, 64 + dc:65 + dc], buf=ubuf, bres=u_res,
                    func=AF.Silu, dst=lambda dc: (a16[:, dc, 0:T], a16_res[dc]))
            for oc in range(NDC):
                w2t, w2r_ = ws.get(pl[("c2", oc)])
                po, por = ps[6 + oc % 2], ps_res[6 + oc % 2]
                for kc in range(NDC):
                    P.op("pe", lambda e, kc=kc, po=po: e.matmul(
                        po[:], lhsT=w2t[:, kc * 128:(kc + 1) * 128], rhs=a16[:, kc, 0:T],
                        start=(kc == 0), stop=(kc == NDC - 1)), reads=[w2r_, a16_res[kc]], writes=[por])
                ws.release(pl[("c2", oc)])
                emit_residual(oc, po, por, c0)

        if kind == "four":
            ccb = P.sbuf("ccb", [128, 4, 1024], BF16)
            ccb_res = Res("ccb")
            for i in range(4):
                wt, wr = ws.get(cc_items[i], lookahead=2)
                P.op("act", lambda e, i=i, wt=wt: e.activation(out=ccb[:, i, :], in_=wt[:, 0:1024], func=AF.Identity), reads=[wr], writes=[ccb_res])
                ws.release(cc_items[i])
            fab = P.sbuf("fab", [128, 2, NDC, T], BF16)
            fab_res = [[Res(f"fab{i}_{d}") for d in range(NDC)] for i in range(2)]
            corr = P.sbuf("corr", [128, NDC, 2], F32)
            P.op("dve", lambda e: e.memset(corr[:], 0.0), writes=[dv_res])
            fl0 = P.sbuf("fl0", [128, 1], F32)
            P.op("dve", lambda e: e.tensor_scalar(out=fl0[:], in0=edge_sb[:, 0:1], scalar1=-float(S), scalar2=float(S), op0=ALU.mult, op1=ALU.add),
                 reads=[edge_res], writes=[dv_res])
            P.op("dve", lambda e: e.tensor_scalar(out=corr[:, :, 0], in0=mrow(0), scalar1=fl0[:, 0:1], scalar2=None, op0=ALU.mult),
                 reads=[msb_res], writes=[dv_res])

        def emit_four_mixer(tt):
            pl = plan[tt]
            c0 = halo_l
            for i in range(2):
                P.dma("act", fab[:, i], fin[:, i, :, tt * T:(tt + 1) * T], writes=fab_res[i])
            for i in range(2):
                for dc in range(NDC):
                    P.op("act", lambda e, i=i, dc=dc: e.activation(out=fab[:, i, dc, :], in_=fab[:, i, dc, :], func=AF.Identity,
                                                                   scale=dv[:, 0, dc:dc + 1]),
                         reads=VR + [fab_res[i][dc]], writes=[fab_res[i][dc]])
            if tt == 0:
                for dc in range(NDC):
                    P.op("dve", lambda e, dc=dc: e.tensor_tensor(out=fab[:, 0, dc, 0:2], in0=fab[:, 0, dc, 0:2], in1=corr[:, dc, :], op=ALU.add),
                         reads=VR + [fab_res[0][dc]], writes=[fab_res[0][dc]])
            for oc in range(NDC):
                g = oc // 4
                pj, pjr = ps[oc % 4], ps_res[oc % 4]
                n = 0
                for i in range(2):
                    for kc in range(4):
                        P.op("pe", lambda e, i=i, kc=kc, pj=pj, n=n: e.matmul(
                            pj[:], lhsT=ccb[:, kc, i * 512 + (oc % 4) * 128:i * 512 + (oc % 4 + 1) * 128], rhs=fab[:, i, 4 * g + kc, :],
                            start=(n == 0), stop=(n == 7)), reads=[ccb_res, fab_res[i][4 * g + kc]], writes=[pjr])
                        n += 1
                P.op("act", lambda e, pj=pj: e.activation(out=a16[:, oc, 0:T], in_=pj[:], func=AF.Identity), reads=[pjr], writes=[a16_res[oc]])
            for oc in range(NDC):
                w2t, w2r_ = ws.get(pl[("fw", oc)])
                po, por = ps[6 + oc % 2], ps_res[6 + oc % 2]
                for kc in range(NDC):
                    P.op("pe", lambda e, kc=kc, po=po: e.matmul(
                        po[:], lhsT=w2t[:, kc * 128:(kc + 1) * 128], rhs=a16[:, kc, 0:T],
                        start=(kc == 0), stop=(kc == NDC - 1)), reads=[w2r_, a16_res[kc]], writes=[por])
                ws.release(pl[("fw", oc)])
                emit_residual(oc, po, por, c0)

        if kind == "none":
            pebuf = P.sbuf("pebuf", [128, NDC, T], F32)
            pe_res = Res("pe")

        def emit_none_mixer(tt):
            P.dma("act", pebuf[:], pe_in[:, :, tt * T:(tt + 1) * T], writes=[pe_res])
            for dc in range(NDC):
                P.op("dve", lambda e, dc=dc: e.tensor_tensor(out=xs[:, dc, :], in0=xs[:, dc, :], in1=pebuf[:, dc, :], op=ALU.add),
                     reads=[pe_res], writes=[xs_res[dc]])
                P.op("act", lambda e, dc=dc: e.activation(out=xs[:, dc, :], in_=xs[:, dc, :], func=AF.Identity, scale=ALPHA),
                     reads=[xs_res[dc]], writes=[xs_res[dc]])

        if kind == "lin":
            prt = P.sbuf("prt", [128, 22, T], BF16)
            prt_res = Res("prt")
            petmp = [P.sbuf(f"petmp{i}", [128, T], F32) for i in range(2)]
            petmp_res = [Res(f"petmp{i}") for i in range(2)]

        def emit_lin_mixer(tt):
            pl = plan[tt]
            c0 = halo_l
            P.dma("act", prt[:], prodin[:, :, tt * T:(tt + 1) * T].rearrange("k p t -> p k t"), writes=[prt_res])
            for dc in range(NDC):
                k = dc % 2
                P.dma("act", petmp[k][:], pe_in[:, dc, tt * T:(tt + 1) * T], writes=[petmp_res[k]])
                P.op("dve", lambda e, dc=dc, k=k: e.tensor_tensor(out=xs[:, dc, :], in0=xs[:, dc, :], in1=petmp[k][:], op=ALU.add),
                     reads=[petmp_res[k]], writes=[xs_res[dc]])
            for half in range(2):
                wts = [ws.get(pl[("wo", half * 11 + i)]) for i in range(11)]
                for dc in range(NDC):
                    po, por = ps[6 + dc % 2], ps_res[6 + dc % 2]
                    for i in range(11):
                        P.op("pe", lambda e, i=i, dc=dc, po=po: e.matmul(
                            po[:], lhsT=wts[i][0][:, dc * 128:(dc + 1) * 128], rhs=prt[:, half * 11 + i, :],
                            start=(i == 0), stop=(i == 10)), reads=[wts[i][1], prt_res], writes=[por])
                    if half == 0:
                        emit_residual(dc, po, por, c0)
                    else:
                        P.op("dve", lambda e, dc=dc, po=po: e.scalar_tensor_tensor(
                            out=xs[:, dc, c0:c0 + T], in0=po[:], scalar=dv[:, 1, dc:dc + 1], in1=xs[:, dc, c0:c0 + T],
                            op0=ALU.mult, op1=ALU.add), reads=VR + [por], writes=[xs_res[dc]])
                for i in range(11):
                    ws.release(pl[("wo", half * 11 + i)])

        def emit_residual(dc, po, por, c0):
            P.op("act", lambda e: e.activation(out=xs[:, dc, c0:c0 + T], in_=xs[:, dc, c0:c0 + T], func=AF.Identity,
                                               scale=ALPHA, bias=dv[:, 2, dc:dc + 1]),
                 reads=VR + [xs_res[dc]], writes=[xs_res[dc]])
            P.op("dve", lambda e: e.scalar_tensor_tensor(out=xs[:, dc, c0:c0 + T], in0=po[:], scalar=dv[:, 1, dc:dc + 1],
                                                         in1=xs[:, dc, c0:c0 + T], op0=ALU.mult, op1=ALU.add),
                 reads=VR + [por], writes=[xs_res[dc]])

        for tt in range(NTT):
            c0 = halo_l
            P.dma("act", xs[:], xin[:, :, tt * T:tt * T + W], writes=xs_res)
            if kind == "pool":
                emit_pool_mixer(tt)
            elif kind == "conf":
                emit_conf_mixer(tt)
            elif kind == "four":
                emit_four_mixer(tt)
            elif kind == "none":
                emit_none_mixer(tt)
            elif kind == "lin":
                emit_lin_mixer(tt)
            emit_ln(c0, lambda dc: dv[:, 3, dc:dc + 1], lambda dc: dv[:, 4, dc:dc + 1])
            emit_ffn(tt, c0)

            def store(dc, tt=tt):
                pass
            emit_ln(c0, lambda dc: vraw[:, V_LNG1, dc:dc + 1], lambda dc: vraw[:, V_LNB1, dc:dc + 1])
            P.dma("pool", xout[:, :, tt * T:(tt + 1) * T], xs[:, :, c0:c0 + T], reads=xs_res)
        P.finish("sp")
        nc._stats = dict(P.n_inst)
    return nc


def col_table(v):
    return np.ascontiguousarray(np.asarray(v, np.float32).reshape(NDC, 128).T)


def to_xT(xb, q, halo_l, halo_r):
    t0 = q * TOK - halo_l
    t1 = (q + 1) * TOK + halo_r
    out = np.zeros((128, NDC, t1 - t0), np.float32)
    a, b = max(t0, 0), min(t1, S)
    blk = xb[a:b].reshape(b - a, NDC, 128).transpose(2, 1, 0)
    out[:, :, a - t0:b - t0] = blk
    return out


def from_xT(xt):
    return np.ascontiguousarray(xt.transpose(2, 1, 0).reshape(xt.shape[2], D))


def w_colblocks(w, nblk):
    K = w.shape[0] // 128
    return np.ascontiguousarray(w.reshape(K, 128, nblk, 128).transpose(2, 1, 0, 3).reshape(nblk, 128, K * 128))


def ffn_layouts(inputs, l):
    w1r = w_colblocks(np.asarray(inputs["ffn_w1"][l]), NF)
    w3r = w_colblocks(np.asarray(inputs["ffn_w3"][l]), NF)
    w2r = np.ascontiguousarray(np.asarray(inputs["ffn_w2"][l]).reshape(NF, 128, D))
    return w1r, w3r, w2r


def common_maps(inputs, l, mixvecs):
    vec = np.zeros((128, NV, NDC), np.float32)
    vec[:, V_LNG0] = col_table(inputs["ln_g"][l, 0])
    vec[:, V_LNB0] = col_table(inputs["ln_b"][l, 0])
    vec[:, V_LNG1] = col_table(inputs["ln_g"][l, 1])
    vec[:, V_LNB1] = col_table(inputs["ln_b"][l, 1])
    for i, v in enumerate(mixvecs):
        vec[:, V_MX0 + i] = col_table(v)
    w1r, w3r, w2r = ffn_layouts(inputs, l)
    return dict(vec=vec, w1r=w1r, w3r=w3r, w2r=w2r)


_MOD = {}


def mod_in(l, b, ncol=96):
    M = _MOD[l]
    return np.ascontiguousarray(np.stack([M[:, :ncol, b], M[:, :ncol, 2]], axis=-1))


def cond_T(inputs, b):
    c = np.stack([np.asarray(inputs["c"][b], np.float32), np.asarray(inputs["c_ctx"], np.float32)], axis=-1)
    return np.ascontiguousarray(c.reshape(NDC, 128, 2).transpose(1, 0, 2))


def edge_flags(q):
    e = np.ones((128, 2), np.float32)
    if q == 0:
        e[:, 0] = 0.0
    if q == 3:
        e[:, 1] = 0.0
    return e


_NC_CACHE = {}


def run_pool_layer(inputs, l, x):
    HL = HR = 8
    key = ("pool",)
    if key not in _NC_CACHE:
        _NC_CACHE[key] = build_layer("pool", HL, HR)
    nc = _NC_CACHE[key]
    cm = common_maps(inputs, l, [inputs["pool_b"][0], inputs["pool_scale"][0]])
    pw = np.asarray(inputs["pool_w"][0], np.float32)
    poolw = np.ascontiguousarray(pw.reshape(4, 4, 128, 512).reshape(16, 128, 512))
    in_maps = []
    for core in range(NCORE):
        b, q = divmod(core, 4)
        t = np.arange(q * TOK, (q + 1) * TOK)
        pinv = np.zeros((4, TOK), np.float32)
        for g, win in enumerate((2, 4, 8, 16)):
            lo = np.clip(t - win // 2, 0, S)
            hi = np.clip(t - win // 2 + win, 0, S)
            pinv[g] = 1.0 / (hi - lo).astype(np.float32)
        m = dict(cm)
        m.update(xin=to_xT(x[b], q, HL, HR), min=mod_in(l, b), edge=edge_flags(q), poolw=poolw, pinv=pinv)
        in_maps.append(m)
    res = run_bass_kernel_spmd(nc, in_maps, core_ids=list(range(NCORE)))
    out = np.empty_like(x)
    for core in range(NCORE):
        b, q = divmod(core, 4)
        out[b, q * TOK:(q + 1) * TOK] = from_xT(res.results[core]["xout"])
    return out


def run_conf_layer(inputs, l, x):
    HL = HR = 15
    key = ("conf",)
    if key not in _NC_CACHE:
        _NC_CACHE[key] = build_layer("conf", HL, HR)
    nc = _NC_CACHE[key]
    cm = common_maps(inputs, l, [inputs["cv_b2"][0]])
    cvw1 = w_colblocks(np.asarray(inputs["cv_w1"][0]), 32)
    cvw2 = w_colblocks(np.asarray(inputs["cv_w2"][0]), 16)
    b1 = np.asarray(inputs["cv_b1"][0], np.float32)
    cvv = np.concatenate([
        b1.reshape(32, 128).T,
        col_table(inputs["cv_dwb"][0]), col_table(inputs["cv_ln_g"][0]), col_table(inputs["cv_ln_b"][0]),
        np.asarray(inputs["cv_dw"][0], np.float32).reshape(31, NDC, 128).transpose(2, 1, 0).reshape(128, NDC * 31),
    ], axis=1).astype(np.float32)
    cvv = np.ascontiguousarray(cvv)
    in_maps = []
    for core in range(NCORE):
        b, q = divmod(core, 4)
        m = dict(cm)
        m.update(xin=to_xT(x[b], q, HL, HR), min=mod_in(l, b), edge=edge_flags(q), cvw1=cvw1, cvw2=cvw2, cvv=cvv,
                 ident=np.eye(128, dtype=np.float32))
        in_maps.append(m)
    res = run_bass_kernel_spmd(nc, in_maps, core_ids=list(range(NCORE)))
    out = np.empty_like(x)
    for core in range(NCORE):
        b, q = divmod(core, 4)
        out[b, q * TOK:(q + 1) * TOK] = from_xT(res.results[core]["xout"])
    return out


def build_seqdft():
    nc = bass.Bass("TRN2", target_bir_lowering=False)
    xtok = nc.dram_tensor("xtok", [64, 128, D], F32, kind="ExternalInput").ap()
    tab = nc.dram_tensor("tab", [64, 128, 4096], BF16, kind="ExternalInput").ap()
    f12 = nc.dram_tensor("f12", [128, 2, NDC, TOK], BF16, kind="ExternalOutput").ap()
    NB = 6
    with ExitStack() as st:
        P = Prog(nc, st)
        ws = WStream(P)
        bring = [P.sbuf(f"bring{i}", [128, 2048], BF16) for i in range(NB)]
        bring_res = [Res(f"bring{i}") for i in range(NB)]
        obuf = [P.sbuf(f"obuf{i}", [128, 2, TOK], BF16) for i in range(2)]
        obuf_res = [Res(f"obuf{i}") for i in range(2)]
        ps = [P.psum(f"ps{i}", [128, 512]) for i in range(8)]
        ps_res = [Res(f"ps{i}") for i in range(8)]
        seq = [(cp, trig, sc) for cp in range(8) for trig in range(2) for sc in range(64)]
        a_items = [ws.add(xtok[sc][:, cp * 256:(cp + 1) * 256], width=256) for (cp, trig, sc) in seq]
        nb_issued = [0]

        def issue_b(upto):
            while nb_issued[0] <= min(upto, len(seq) - 1):
                i = nb_issued[0]
                cp, trig, sc = seq[i]
                P.dma("sp", bring[i % NB][:], tab[sc][:, trig * 2048:(trig + 1) * 2048], writes=[bring_res[i % NB]])
                nb_issued[0] += 1

        for i, (cp, trig, sc) in enumerate(seq):
            issue_b(i + 3)
            at, ar = ws.get(a_items[i], lookahead=4)
            bt, br = bring[i % NB], bring_res[i % NB]
            for c2 in range(2):
                for kt in range(4):
                    b_ = c2 * 4 + kt
                    P.op("pe", lambda e, c2=c2, kt=kt, b_=b_, at=at, bt=bt: e.matmul(
                        ps[b_][:], lhsT=at[:, c2 * 128:(c2 + 1) * 128], rhs=bt[:, kt * 512:(kt + 1) * 512],
                        start=(sc == 0), stop=(sc == 63)), reads=[ar, br], writes=[ps_res[b_]])
            ws.release(a_items[i])
            if sc == 63:
                ob, obr = obuf[(cp * 2 + trig) % 2], obuf_res[(cp * 2 + trig) % 2]
                for c2 in range(2):
                    for kt in range(4):
                        b_ = c2 * 4 + kt
                        eng = "act" if kt % 2 == 0 else "dve"
                        if eng == "act":
                            P.op("act", lambda e, c2=c2, kt=kt, b_=b_, ob=ob: e.activation(out=ob[:, c2, kt * 512:(kt + 1) * 512], in_=ps[b_][:], func=AF.Identity),
                                 reads=[ps_res[b_]], writes=[obr])
                        else:
                            P.op("dve", lambda e, c2=c2, kt=kt, b_=b_, ob=ob: e.tensor_copy(out=ob[:, c2, kt * 512:(kt + 1) * 512], in_=ps[b_][:]),
                                 reads=[ps_res[b_]], writes=[obr])
                P.dma("pool", f12[:, trig, cp * 2:cp * 2 + 2, :], ob[:], reads=[obr])
        P.finish("sp")
        nc._stats = dict(P.n_inst)
    return nc


def _bf16(a):
    import ml_dtypes
    return np.asarray(a, np.float32).astype(ml_dtypes.bfloat16)


def run_four_layer(inputs, l, x):
    if ("seqdft",) not in _NC_CACHE:
        _NC_CACHE[("seqdft",)] = build_seqdft()
    if ("four",) not in _NC_CACHE:
        _NC_CACHE[("four",)] = build_layer("four", 0, 0)
    s_idx = np.arange(S, dtype=np.int64)[:, None]
    in_maps = []
    for core in range(NCORE):
        b, q = divmod(core, 4)
        k_idx = np.arange(q * TOK, (q + 1) * TOK, dtype=np.int64)[None, :]
        ang = (2.0 * np.pi / S) * ((s_idx * k_idx) % S).astype(np.float64)
        tab = np.concatenate([np.cos(ang), np.sin(ang)], axis=1)
        in_maps.append(dict(xtok=np.ascontiguousarray(x[b].reshape(64, 128, D)), tab=_bf16(tab).reshape(64, 128, 4096)))
    r1 = run_bass_kernel_spmd(_NC_CACHE[("seqdft",)], in_maps, core_ids=list(range(NCORE)))
    cm = common_maps(inputs, l, [inputs["ft_b"][0]])
    c_idx = np.arange(512, dtype=np.int64)
    angc = (2.0 * np.pi / 512) * ((c_idx[:, None] * c_idx[None, :]) % 512).astype(np.float64)
    ccsc = (np.concatenate([np.cos(angc), -np.sin(angc)], axis=1) / 2048.0).astype(np.float32).reshape(4, 128, 1024)
    ftw = w_colblocks(np.asarray(inputs["ft_w"][0]), 16)
    in_maps = []
    for core in range(NCORE):
        b, q = divmod(core, 4)
        m = dict(cm)
        m.update(xin=to_xT(x[b], q, 0, 0), min=mod_in(l, b), edge=edge_flags(q), fin=r1.results[core]["f12"], ccsc=ccsc, ftw=ftw)
        in_maps.append(m)
    res = run_bass_kernel_spmd(_NC_CACHE[("four",)], in_maps, core_ids=list(range(NCORE)))
    out = np.empty_like(x)
    for core in range(NCORE):
        b, q = divmod(core, 4)
        out[b, q * TOK:(q + 1) * TOK] = from_xT(res.results[core]["xout"])
    return out


def _pos_embed_table():
    rows, cols, dim = S // 64, 64, D
    quarter = dim // 4
    omega = (1.0 / (10000.0 ** (np.arange(quarter, dtype=np.float32) / np.float32(quarter)))).astype(np.float32)
    ar = np.arange(rows, dtype=np.float32)[:, None] * omega[None]
    ac = np.arange(cols, dtype=np.float32)[:, None] * omega[None]
    er = np.concatenate([np.sin(ar), np.cos(ar)], axis=-1)
    ec = np.concatenate([np.sin(ac), np.cos(ac)], axis=-1)
    pe = np.concatenate([np.broadcast_to(er[:, None, :], (rows, cols, dim // 2)),
                         np.broadcast_to(ec[None, :, :], (rows, cols, dim // 2))], axis=-1)
    return pe.reshape(rows * cols, dim).astype(np.float32)


def run_layer0_partial(inputs, x):
    key = ("none",)
    if key not in _NC_CACHE:
        _NC_CACHE[key] = build_layer("none", 0, 0)
    nc = _NC_CACHE[key]
    cm = common_maps(inputs, 0, [])
    pe = _pos_embed_table()
    in_maps = []
    for core in range(NCORE):
        b, q = divmod(core, 4)
        m = dict(cm)
        m.update(xin=to_xT(x[b], q, 0, 0), cT=cond_T(inputs, b), edge=edge_flags(q), pe=to_xT(pe, q, 0, 0))
        in_maps.append(m)
    res = run_bass_kernel_spmd(nc, in_maps, core_ids=list(range(NCORE)))
    out = np.empty_like(x)
    for core in range(NCORE):
        b, q = divmod(core, 4)
        out[b, q * TOK:(q + 1) * TOK] = from_xT(res.results[core]["xout"])
    return out


def kernel(**inputs):
    inputs = {k: np.asarray(v) for k, v in inputs.items()}
    x = np.ascontiguousarray(inputs["x"], dtype=np.float32)
    x = run_rg_layer(inputs, x)
    x = run_pool_layer(inputs, 1, x)
    x = run_conf_layer(inputs, 2, x)
    x = run_four_layer(inputs, 3, x)
    return x.astype(np.float32)


RSUB = 88
NSUB = 32
RG_HL, RG_HR = 1, 2
CTXL = 256


def build_rg(phase):
    nc = bass.Bass("TRN2", target_bir_lowering=False)
    NTW = RG_HL + TOK + RG_HR
    d = {}

    def din(name, shape, dt=F32):
        d[name] = nc.dram_tensor(name, list(shape), dt, kind="ExternalInput").ap()
        return d[name]

    xin = din("xin", [128, NDC, NTW])
    pein = din("pein", [128, NDC, NTW])
    ctxin = din("ctxin", [128, NDC, CTXL])
    if phase == "A":
        modb = din("modb", [128, 96])
        cT = din("cT", [128, NDC, 2])
        modw = din("modw", [32, 128, NDC * 128])
        modw_sh = din("modw_sh", [48, 128, NDC * 128])
        modb_sh = din("modb_sh", [128, 48])
        cT3 = din("cT3", [128, NDC, 3])
        mout = nc.dram_tensor("mout", [128, 48, 3], F32, kind="ExternalOutput").ap()
    else:
        min_ = din("min", [128, 32, 2])
    edge = din("edge", [128, 2])
    rv = din("rv", [128, 11, NSUB])
    wxr = din("wxr", [NSUB, 128, NDC * RSUB])
    gwr = din("gwr", [32, 128, 704])
    if phase == "B":
        wgr = din("wgr", [NSUB, 128, NDC * RSUB])
        summ = din("summ", [NCORE, 128, 4, NSUB])
        ctxs = din("ctxs", [128, 2, NSUB])
        mfb = din("mfb", [128, 2, NCORE])
        prod = nc.dram_tensor("prod", [NSUB, RSUB, TOK], BF16, kind="ExternalOutput").ap()
    else:
        sout = nc.dram_tensor("sout", [128, 6, NSUB], F32, kind="ExternalOutput").ap()

    with ExitStack() as st:
        P = Prog(nc, st)
        ws = WStream(P, nstg=2, nring=4)
        a16f = P.sbuf("a16f", [128, NDC, NTW], BF16)
        a16f_res = [Res(f"a16f{i}") for i in range(NDC)]
        xbpre = P.sbuf("xbpre", [128, NTW], F32)
        xbpre_res = Res("xbpre")
        xb = [P.sbuf(f"xb{i}", [128, TOK], F32) for i in range(2)]
        xb_res = [Res(f"xb{i}") for i in range(2)]
        xb16 = [P.sbuf(f"xb16_{i}", [128, TOK], BF16) for i in range(2)]
        xb16_res = [Res(f"xb16_{i}") for i in range(2)]
        abuf = P.sbuf("abuf", [128, NTW], F32)
        bbuf = P.sbuf("bbuf", [128, NTW], F32)
        tbuf = P.sbuf("tbuf", [128, NTW], F32)
        ab_res, bb_res, tb_res = Res("abuf"), Res("bbuf"), Res("tbuf")
        ybuf = [P.sbuf(f"ybuf{i}", [128, NTW], F32) for i in range(2)]
        yb_res = [Res(f"ybuf{i}") for i in range(2)]
        xt, xt_res = [abuf, bbuf], [ab_res, bb_res]
        abuf2 = P.sbuf("abuf2", [128, TOK], F32)
        bbuf2 = P.sbuf("bbuf2", [128, TOK], F32)
        abufs, ab_ress = [abuf, abuf2], [ab_res, Res("abuf2")]
        bbufs, bb_ress = [bbuf, bbuf2], [bb_res, Res("bbuf2")]
        itc = [0]
        one_c = P.sbuf("one_c", [128, 1], F32)
        P.op("pool", lambda e: e.memset(one_c[:], 1.0), writes=[Res("one_c")])
        pt, pt_res = [tbuf, ybuf[0]], [tb_res, yb_res[0]]
        rv_sb = P.sbuf("rv_sb", [128, 11, NSUB], F32)
        cp_sb = P.sbuf("cp_sb", [128, 2, NSUB], F32)
        rv_res = Res("rv")
        modb_sb = P.sbuf("modb_sb", [128, 96], F32)
        cs = P.sbuf("cs", [128, NDC, 2], F32)
        cs_res = Res("cs")
        msb = P.sbuf("msb", [128, 32, 2], F32)
        msb_res = Res("msb")
        a1 = P.sbuf("a1", [128, NDC, 2], F32)
        edge_sb = P.sbuf("edge_sb", [128, 2], F32)
        edge_res = Res("edge")
        rsum = P.sbuf("rsum", [128, 1], F32)
        rsum_res = Res("rsum")
        if phase == "A":
            so_sb = P.sbuf("so_sb", [128, 6, NSUB], F32)
            so_res = Res("so")
        else:
            summ_sb = P.sbuf("summ_sb", [128, NCORE, 4, NSUB], F32)
            carry = P.sbuf("carry", [128, 2, NSUB], F32)
            mfb_sb = P.sbuf("mfb_sb", [128, 2, NCORE], F32)
            ctmp = P.sbuf("ctmp", [128, NSUB], F32)
            carry_res = Res("carry")
            gl = [P.sbuf(f"gl{i}", [128, T], F32) for i in range(2)]
            gl_res = [Res(f"gl{i}") for i in range(2)]
            pr16 = [P.sbuf(f"pr16_{i}", [128, TOK], BF16) for i in range(2)]
            pr16_res = [Res(f"pr16_{i}") for i in range(2)]
        ps = [P.psum(f"ps{i}", [128, 512]) for i in range(8)]
        ps_res = [Res(f"ps{i}") for i in range(8)]

        mod_items = [ws.add(modw[oc], cast=False) for oc in range(32)] if phase == "A" else []
        passes = ["ctx", "lat"] if phase == "A" else ["lat"]
        plan = {}
        for pss in passes:
            for n in range(16):
                for s in range(2):
                    plan[(pss, "wx", 2 * n + s)] = ws.add(wxr[2 * n + s], width=NDC * RSUB)
                for dd in range(2):
                    plan[(pss, "gw", dd, n)] = ws.add(gwr[dd * 16 + n], width=704)
                if phase == "B":
                    for s in range(2):
                        plan[(pss, "wg", 2 * n + s)] = ws.add(wgr[2 * n + s], width=NDC * RSUB)

        P.dma("act", rv_sb[:], rv, writes=[rv_res])
        P.dma("act", edge_sb[:], edge, writes=[edge_res])
        P.op("act", lambda e: e.activation(out=cp_sb[:], in_=rv_sb[:, 9:11, :], func=AF.Sigmoid), reads=[rv_res], writes=[rv_res])
        P.op("act", lambda e: e.activation(out=cp_sb[:], in_=cp_sb[:], func=AF.Ln), reads=[rv_res], writes=[rv_res])
        P.op("dve", lambda e: e.tensor_scalar(out=cp_sb[:], in0=cp_sb[:], scalar1=8.0, scalar2=None, op0=ALU.mult), reads=[rv_res], writes=[rv_res])
        mps, mps_res = ps[7], ps_res[7]
        if phase == "A":
            P.dma("act", modb_sb[:], modb, writes=[rv_res])
            P.dma("act", cs[:], cT, writes=[cs_res])
            P.op("act", lambda e: e.activation(out=cs[:], in_=cs[:], func=AF.Silu), reads=[cs_res], writes=[cs_res])
            for oc in range(32):
                wt, wr = ws.get(mod_items[oc], lookahead=1)
                for kc in range(NDC):
                    P.op("pe", lambda e, oc=oc, kc=kc, wt=wt: e.matmul(
                        mps[:, oc * 2:(oc + 1) * 2], lhsT=wt[:, kc * 128:(kc + 1) * 128], rhs=cs[:, kc, :],
                        start=(kc == 0), stop=(kc == NDC - 1)), reads=[wr, cs_res], writes=[mps_res])
                ws.release(mod_items[oc])
            for j in range(2):
                P.op("dve", lambda e, j=j: e.tensor_tensor(
                    out=msb[:, :, j], in0=mps[:, 0:64].rearrange("p (o j) -> p o j", j=2)[:, :, j],
                    in1=modb_sb[:, 0:32], op=ALU.add), reads=[mps_res, rv_res], writes=[msb_res])
        else:
            P.dma("act", msb[:], min_, writes=[msb_res])
        P.op("dve", lambda e: e.tensor_scalar(out=a1[:], in0=msb[:, 16:32, :], scalar1=1.0, scalar2=None, op0=ALU.add),
             reads=[msb_res], writes=[msb_res])
        if phase == "A":
            P.op("pool", lambda e: e.memset(so_sb[:], 0.0), writes=[so_res])
        else:
            P.dma("act", summ_sb[:], summ.rearrange("r p a s -> p r a s"), writes=[carry_res])
            P.dma("act", carry[:], ctxs, writes=[carry_res])
            P.dma("act", mfb_sb[:], mfb, writes=[carry_res])
            for dd in range(2):
                order = range(NCORE) if dd == 0 else range(NCORE - 1, -1, -1)
                for r in order:
                    A_r = summ_sb[:, r, 2 * dd, :]
                    B_r = summ_sb[:, r, 2 * dd + 1, :]
                    mk = mfb_sb[:, dd, r:r + 1]
                    P.op("dve", lambda e, A_r=A_r, mk=mk: e.tensor_scalar(out=ctmp[:], in0=A_r, scalar1=-1.0, scalar2=mk, op0=ALU.add, op1=ALU.mult),
                         reads=[carry_res], writes=[carry_res])
                    P.op("dve", lambda e: e.tensor_scalar(out=ctmp[:], in0=ctmp[:], scalar1=1.0, scalar2=None, op0=ALU.add),
                         reads=[carry_res], writes=[carry_res])
                    P.op("dve", lambda e, dd=dd: e.tensor_tensor(out=carry[:, dd, :], in0=carry[:, dd, :], in1=ctmp[:], op=ALU.mult),
                         reads=[carry_res], writes=[carry_res])
                    P.op("dve", lambda e, B_r=B_r, mk=mk: e.tensor_scalar(out=ctmp[:], in0=B_r, scalar1=mk, scalar2=None, op0=ALU.mult),
                         reads=[carry_res], writes=[carry_res])
                    P.op("dve", lambda e, dd=dd: e.tensor_tensor(out=carry[:, dd, :], in0=carry[:, dd, :], in1=ctmp[:], op=ALU.add),
                         reads=[carry_res], writes=[carry_res])

        def run_pass(pss):
            ctx = pss == "ctx"
            NTK = CTXL if ctx else TOK
            NW = RG_HL + NTK + RG_HR
            j = 1 if ctx else 0
            for dc in range(NDC):
                k = dc % 2
                if ctx:
                    P.dma("act", xt[k][:, RG_HL:RG_HL + NTK], ctxin[:, dc, :], writes=[xt_res[k]])
                    src = xt[k][:, RG_HL:RG_HL + NTK]
                    P.op("pool", lambda e, dc=dc: e.memset(a16f[:, dc, 0:NW], 0.0), writes=[a16f_res[dc]])
                    P.op("act", lambda e, dc=dc, src=src: e.activation(out=a16f[:, dc, RG_HL:RG_HL + NTK], in_=src, func=AF.Identity,
                                                                       scale=a1[:, dc, j:j + 1], bias=msb[:, dc, j:j + 1]),
                         reads=[xt_res[k], msb_res], writes=[a16f_res[dc]])
                else:
                    P.dma("act", xt[k][:], xin[:, dc, :], writes=[xt_res[k]])
                    P.dma("act", pt[k][:], pein[:, dc, :], writes=[pt_res[k]])
                    P.op("dve", lambda e, k=k: e.tensor_tensor(out=xt[k][:], in0=xt[k][:], in1=pt[k][:], op=ALU.add),
                         reads=[pt_res[k]], writes=[xt_res[k]])
                    P.op("act", lambda e, dc=dc, k=k: e.activation(out=a16f[:, dc, :], in_=xt[k][:], func=AF.Identity,
                                                                   scale=a1[:, dc, j:j + 1], bias=msb[:, dc, j:j + 1]),
                         reads=[xt_res[k], msb_res], writes=[a16f_res[dc]])
                    P.op("dve", lambda e, dc=dc: e.tensor_scalar(out=a16f[:, dc, 0:RG_HL], in0=a16f[:, dc, 0:RG_HL], scalar1=edge_sb[:, 0:1],
                                                                 scalar2=None, op0=ALU.mult), reads=[edge_res], writes=[a16f_res[dc]])
                    P.op("dve", lambda e, dc=dc: e.tensor_scalar(out=a16f[:, dc, RG_HL + NTK:NW], in0=a16f[:, dc, RG_HL + NTK:NW], scalar1=edge_sb[:, 1:2],
                                                                 scalar2=None, op0=ALU.mult), reads=[edge_res], writes=[a16f_res[dc]])
            coltiles = [(c, min(c + 512, NW)) for c in range(0, NW, 512)]
            ctiles = [(c, min(c + 512, NTK)) for c in range(0, NTK, 512)]
            Rr = slice(0, RSUB)
            for n in range(16):
                for s in range(2):
                    sidx = 2 * n + s
                    wx, wxres = ws.get(plan[(pss, "wx", sidx)], lookahead=2)
                    for ci, (ca, cb) in enumerate(coltiles):
                        pp, ppr = ps[ci % 2], ps_res[ci % 2]
                        for kc in range(NDC):
                            P.op("pe", lambda e, kc=kc, pp=pp, ca=ca, cb=cb, wx=wx: e.matmul(
                                pp[Rr, 0:cb - ca], lhsT=wx[:, kc * RSUB:(kc + 1) * RSUB], rhs=a16f[:, kc, ca:cb],
                                start=(kc == 0), stop=(kc == NDC - 1)), reads=[wxres, a16f_res[kc]], writes=[ppr])
                        P.op("act", lambda e, pp=pp, ca=ca, cb=cb: e.activation(out=xbpre[Rr, ca:cb], in_=pp[Rr, 0:cb - ca], func=AF.Identity),
                             reads=[ppr], writes=[xbpre_res])
                    ws.release(plan[(pss, "wx", sidx)])
                    P.op("dve", lambda e, s=s, sidx=sidx: e.tensor_scalar(
                        out=xb[s][Rr, 0:NTK], in0=xbpre[Rr, 0:NTK], scalar1=rv_sb[Rr, 0, sidx:sidx + 1], scalar2=rv_sb[Rr, 4, sidx:sidx + 1],
                        op0=ALU.mult, op1=ALU.add), reads=[xbpre_res, rv_res], writes=[xb_res[s]])
                    for jj in range(1, 4):
                        P.op("dve", lambda e, s=s, sidx=sidx, jj=jj: e.scalar_tensor_tensor(
                            out=xb[s][Rr, 0:NTK], in0=xbpre[Rr, jj:jj + NTK], scalar=rv_sb[Rr, jj, sidx:sidx + 1], in1=xb[s][Rr, 0:NTK],
                            op0=ALU.mult, op1=ALU.add), reads=[xbpre_res, rv_res], writes=[xb_res[s]])
                    P.op("act", lambda e, s=s: e.activation(out=xb16[s][Rr, 0:NTK], in_=xb[s][Rr, 0:NTK], func=AF.Identity),
                         reads=[xb_res[s]], writes=[xb16_res[s]])
                for dd in range(2):
                    gw, gwres = ws.get(plan[(pss, "gw", dd, n)], lookahead=2)
                    for so in range(2):
                        sidx = 2 * n + so
                        kk_ = itc[0] % 2
                        itc[0] += 1
                        A_, A_r = abufs[kk_], ab_ress[kk_]
                        B_, B_r = bbufs[kk_], bb_ress[kk_]
                        for ci, (ca, cb) in enumerate(ctiles):
                            pr_, prr = ps[2 + ci % 2], ps_res[2 + ci % 2]
                            pi_, pir = ps[4 + ci % 2], ps_res[4 + ci % 2]
                            for si in range(2):
                                P.op("pe", lambda e, si=si, so=so, pr_=pr_, ca=ca, cb=cb, gw=gw: e.matmul(
                                    pr_[Rr, 0:cb - ca], lhsT=gw[Rr, (si * 2 + so) * RSUB:(si * 2 + so + 1) * RSUB], rhs=xb16[si][Rr, ca:cb],
                                    start=(si == 0), stop=(si == 1)), reads=[gwres, xb16_res[si]], writes=[prr])
                            for si in range(2):
                                P.op("pe", lambda e, si=si, so=so, pi_=pi_, ca=ca, cb=cb, gw=gw: e.matmul(
                                    pi_[Rr, 0:cb - ca], lhsT=gw[Rr, 352 + (si * 2 + so) * RSUB:352 + (si * 2 + so + 1) * RSUB], rhs=xb16[si][Rr, ca:cb],
                                    start=(si == 0), stop=(si == 1)), reads=[gwres, xb16_res[si]], writes=[pir])
                            P.op("act", lambda e, pr_=pr_, ca=ca, cb=cb, dd=dd, sidx=sidx: e.activation(
                                out=A_[Rr, ca:cb], in_=pr_[Rr, 0:cb - ca], func=AF.Sigmoid, bias=rv_sb[Rr, 5 + dd, sidx:sidx + 1]),
                                reads=[prr, rv_res], writes=[A_r])
                            P.op("act", lambda e, pi_=pi_, ca=ca, cb=cb, dd=dd, sidx=sidx: e.activation(
                                out=B_[Rr, ca:cb], in_=pi_[Rr, 0:cb - ca], func=AF.Sigmoid, bias=rv_sb[Rr, 7 + dd, sidx:sidx + 1]),
                                reads=[pir, rv_res], writes=[B_r])
                        if phase == "A" and not ctx:
                            P.op("dve", lambda e: e.reduce_sum(out=rsum[Rr, :], in_=A_[Rr, 0:NTK], axis=mybir.AxisListType.X),
                                 reads=[A_r], writes=[rsum_res])
                            P.op("act", lambda e, dd=dd, sidx=sidx: e.activation(out=so_sb[Rr, 2 * dd, sidx:sidx + 1], in_=rsum[Rr, :], func=AF.Exp,
                                                                               scale=cp_sb[Rr, dd, sidx:sidx + 1]),
                                 reads=[rsum_res, rv_res], writes=[so_res])
                        P.op("act", lambda e, dd=dd, sidx=sidx: e.activation(out=A_[Rr, 0:NTK], in_=A_[Rr, 0:NTK], func=AF.Exp,
                                                                           scale=cp_sb[Rr, dd, sidx:sidx + 1]),
                             reads=[A_r, rv_res], writes=[A_r])
                        P.op("act", lambda e: e.activation(out=tbuf[Rr, 0:NTK], in_=A_[Rr, 0:NTK], func=AF.Square), reads=[A_r], writes=[tb_res])
                        P.op("act", lambda e: e.activation(out=tbuf[Rr, 0:NTK], in_=tbuf[Rr, 0:NTK], func=AF.Sqrt, scale=-1.0, bias=one_c[Rr, 0:1]),
                             reads=[tb_res], writes=[tb_res])
                        P.op("dve", lambda e, so=so: e.tensor_tensor(out=B_[Rr, 0:NTK], in0=B_[Rr, 0:NTK], in1=xb[so][Rr, 0:NTK], op=ALU.mult),
                             reads=[B_r, xb_res[so]], writes=[B_r])
                        P.op("dve", lambda e: e.tensor_tensor(out=B_[Rr, 0:NTK], in0=B_[Rr, 0:NTK], in1=tbuf[Rr, 0:NTK], op=ALU.mult),
                             reads=[B_r, tb_res], writes=[B_r])
                        if phase == "B":
                            init = carry[Rr, dd, sidx:sidx + 1]
                            dst, dres = (ybuf[so], yb_res[so]) if dd == 0 else (tbuf, tb_res)
                        else:
                            init = 0.0
                            dst, dres = tbuf, tb_res
                        if dd == 0:
                            P.op("dve", lambda e, dst=dst, init=init: e.tensor_tensor_scan(
                                out=dst[Rr, 0:NTK], data0=A_[Rr, 0:NTK], data1=B_[Rr, 0:NTK], initial=init, op0=ALU.mult, op1=ALU.add),
                                reads=[A_r, B_r] + ([carry_res] if phase == "B" else []), writes=[dres])
                        else:
                            P.op("dve", lambda e, dst=dst, init=init: e.tensor_tensor_scan(
                                out=dst[Rr, NTK - 1::-1] if False else dst[Rr, 0:NTK][:, ::-1], data0=A_[Rr, 0:NTK][:, ::-1], data1=B_[Rr, 0:NTK][:, ::-1],
                                initial=init, op0=ALU.mult, op1=ALU.add),
                                reads=[A_r, B_r] + ([carry_res] if phase == "B" else []), writes=[dres])
                        if phase == "A":
                            col = NTK - 1 if dd == 0 else 0
                            row = (4 + dd) if ctx else (2 * dd + 1)
                            P.op("act", lambda e, col=col, row=row, sidx=sidx: e.activation(out=so_sb[Rr, row, sidx:sidx + 1], in_=tbuf[Rr, col:col + 1], func=AF.Identity),
                                 reads=[tb_res], writes=[so_res])
                        elif dd == 1:
                            P.op("dve", lambda e, so=so: e.tensor_tensor(out=ybuf[so][Rr, 0:NTK], in0=ybuf[so][Rr, 0:NTK], in1=tbuf[Rr, 0:NTK], op=ALU.add),
                                 reads=[tb_res], writes=[yb_res[so]])
                    ws.release(plan[(pss, "gw", dd, n)])
                if phase == "B":
                    for so in range(2):
                        sidx = 2 * n + so
                        wg, wgres = ws.get(plan[(pss, "wg", sidx)], lookahead=2)
                        pb, pbr = pr16[so], pr16_res[so]
                        for ci, (ca, cb) in enumerate(ctiles):
                            pg, pgr = ps[6 + ci % 2], ps_res[6 + ci % 2]
                            for kc in range(NDC):
                                P.op("pe", lambda e, kc=kc, pg=pg, ca=ca, cb=cb, wg=wg: e.matmul(
                                    pg[Rr, 0:cb - ca], lhsT=wg[:, kc * RSUB:(kc + 1) * RSUB], rhs=a16f[:, kc, RG_HL + ca:RG_HL + cb],
                                    start=(kc == 0), stop=(kc == NDC - 1)), reads=[wgres, a16f_res[kc]], writes=[pgr])
                            k = ci % 2
                            P.op("act", lambda e, pg=pg, k=k, ca=ca, cb=cb: e.activation(out=gl[k][Rr, 0:cb - ca], in_=pg[Rr, 0:cb - ca], func=AF.Gelu_apprx_tanh),
                                 reads=[pgr], writes=[gl_res[k]])
                            P.op("dve", lambda e, k=k, so=so, ca=ca, cb=cb, pb=pb: e.tensor_tensor(
                                out=pb[Rr, ca:cb], in0=ybuf[so][Rr, ca:cb], in1=gl[k][Rr, 0:cb - ca], op=ALU.mult),
                                reads=[yb_res[so], gl_res[k]], writes=[pbr])
                        ws.release(plan[(pss, "wg", sidx)])
                        P.dma("pool", prod[sidx], pb[Rr, :], reads=[pbr])

        for pss in passes:
            run_pass(pss)
        if phase == "A":
            P.dma("pool", sout, so_sb[:], reads=[so_res])
            cs3 = P.sbuf("cs3", [128, NDC, 3], F32)
            cs3_res = Res("cs3")
            mbs = P.sbuf("mbs", [128, 48], F32)
            mo_sb = P.sbuf("mo_sb", [128, 48, 3], F32)
            mo_res = Res("mo")
            P.dma("act", cs3[:], cT3, writes=[cs3_res])
            P.dma("act", mbs[:], modb_sh, writes=[cs3_res])
            P.op("act", lambda e: e.activation(out=cs3[:], in_=cs3[:], func=AF.Silu), reads=[cs3_res], writes=[cs3_res])
            sh_items = [ws.add(modw_sh[i], cast=False) for i in range(48)]
            mp2, mp2r = ps[6], ps_res[6]
            for i in range(48):
                wt, wr = ws.get(sh_items[i], lookahead=1)
                for kc in range(NDC):
                    P.op("pe", lambda e, i=i, kc=kc, wt=wt: e.matmul(
                        mp2[:, i * 3:(i + 1) * 3], lhsT=wt[:, kc * 128:(kc + 1) * 128], rhs=cs3[:, kc, :],
                        start=(kc == 0), stop=(kc == NDC - 1)), reads=[wr, cs3_res], writes=[mp2r])
                ws.release(sh_items[i])
            for j in range(3):
                P.op("dve", lambda e, j=j: e.tensor_tensor(
                    out=mo_sb[:, :, j], in0=mp2[:, 0:144].rearrange("p (o j) -> p o j", j=3)[:, :, j],
                    in1=mbs[:], op=ALU.add), reads=[mp2r, cs3_res], writes=[mo_res])
            P.dma("pool", mout, mo_sb[:], reads=[mo_res])
        P.finish("sp")
        nc._stats = dict(P.n_inst)
    return nc


def _pad128(a):
    shp = list(a.shape)
    shp[-2] = 128
    out = np.zeros(shp, np.float32)
    out[..., :a.shape[-2], :] = a
    return out


def rg_col(v):
    return np.asarray(v, np.float32).reshape(NSUB, RSUB).T


def run_rg_layer(inputs, x):
    for ph in ("A", "B"):
        if ("rg", ph) not in _NC_CACHE:
            _NC_CACHE[("rg", ph)] = build_rg(ph)
    if ("lin",) not in _NC_CACHE:
        _NC_CACHE[("lin",)] = build_layer("lin", 0, 0)
    pe = _pos_embed_table()
    modw = w_colblocks(np.asarray(inputs["mod_w"][0][:, 0:4096]), 32)
    modb_full = np.ascontiguousarray(np.asarray(inputs["mod_b"][0], np.float32).reshape(96, 128).T)
    rvt = np.zeros((128, 11, NSUB), np.float32)
    cw = np.asarray(inputs["rg_conv_w"][0], np.float32)
    for j in range(4):
        rvt[:RSUB, j] = rg_col(cw[j])
    rvt[:RSUB, 4] = rg_col(inputs["rg_conv_b"][0])
    for dd in range(2):
        rvt[:RSUB, 5 + dd] = rg_col(inputs["rg_br"][0, dd])
        rvt[:RSUB, 7 + dd] = rg_col(inputs["rg_bi"][0, dd])
        rvt[:RSUB, 9 + dd] = rg_col(inputs["rg_lam"][0, dd])
    rvt[RSUB:, 9:11] = 1.0

    def sub_cols(w):
        return np.ascontiguousarray(np.asarray(w, np.float32).reshape(NDC, 128, NSUB, RSUB).transpose(2, 1, 0, 3).reshape(NSUB, 128, NDC * RSUB))
    wxr = sub_cols(inputs["rg_w_x"][0])
    wgr = sub_cols(inputs["rg_w_gate"][0])
    gw = np.zeros((32, 128, 704), np.float32)
    for dd in range(2):
        for n in range(16):
            for k_, nm in enumerate(("rg_wr", "rg_wi")):
                blk = np.asarray(inputs[nm][0, dd, n], np.float32).reshape(2, RSUB, 2, RSUB).transpose(1, 0, 2, 3).reshape(RSUB, 352)
                gw[dd * 16 + n, :RSUB, k_ * 352:(k_ + 1) * 352] = blk
    mwl = [w_colblocks(np.asarray(inputs["mod_w"][l]), 96) for l in range(DEPTH)]
    mbl = [np.asarray(inputs["mod_b"][l], np.float32).reshape(96, 128).T for l in range(DEPTH)]
    cT3 = np.stack([np.asarray(inputs["c"][0], np.float32), np.asarray(inputs["c"][1], np.float32),
                    np.asarray(inputs["c_ctx"], np.float32)], axis=-1)
    cT3 = np.ascontiguousarray(cT3.reshape(NDC, 128, 3).transpose(1, 0, 2))
    base = []
    for core in range(NCORE):
        b, q = divmod(core, 4)
        ctxT = np.ascontiguousarray(np.asarray(inputs["ctx"][b], np.float32).reshape(CTXL, NDC, 128).transpose(2, 1, 0))
        base.append(dict(xin=to_xT(x[b], q, RG_HL, RG_HR), pein=to_xT(pe, q, RG_HL, RG_HR), ctxin=ctxT,
                         edge=edge_flags(q), rv=rvt, wxr=wxr, gwr=gw))
    mapsA = []
    for core in range(NCORE):
        b, q = divmod(core, 4)
        m = dict(base[core])
        sh_w = np.ascontiguousarray(np.concatenate([mwl[l][core * 12:(core + 1) * 12] for l in range(DEPTH)], axis=0))
        sh_b = np.ascontiguousarray(np.concatenate([mbl[l][:, core * 12:(core + 1) * 12] for l in range(DEPTH)], axis=1))
        m.update(modb=modb_full, cT=cond_T(inputs, b), modw=modw, modw_sh=sh_w, modb_sh=sh_b, cT3=cT3)
        mapsA.append(m)
    rA = run_bass_kernel_spmd(_NC_CACHE[("rg", "A")], mapsA, core_ids=list(range(NCORE)))
    for l in range(DEPTH):
        M = np.zeros((128, 96, 3), np.float32)
        for core in range(NCORE):
            M[:, core * 12:(core + 1) * 12, :] = rA.results[core]["mout"][:, l * 12:(l + 1) * 12, :]
        _MOD[l] = M
    souts = [rA.results[c]["sout"] for c in range(NCORE)]
    summ = np.ascontiguousarray(np.stack([s_[:, 0:4, :] for s_ in souts], axis=0))
    mapsB = []
    for core in range(NCORE):
        b, q = divmod(core, 4)
        mfb = np.zeros((128, 2, NCORE), np.float32)
        for r in range(NCORE):
            rb, rq = divmod(r, 4)
            if rb == b and rq < q:
                mfb[:, 0, r] = 1.0
            if rb == b and rq > q:
                mfb[:, 1, r] = 1.0
        m = dict(base[core])
        m.update(wgr=wgr, summ=summ, ctxs=np.ascontiguousarray(souts[core][:, 4:6, :]), mfb=mfb, min=mod_in(0, b, 32))
        mapsB.append(m)
    rB = run_bass_kernel_spmd(_NC_CACHE[("rg", "B")], mapsB, core_ids=list(range(NCORE)))
    cm = common_maps(inputs, 0, [])
    wor = np.ascontiguousarray(np.asarray(inputs["rg_w_out"][0], np.float32).reshape(22, 128, D))
    mapsC = []
    for core in range(NCORE):
        b, q = divmod(core, 4)
        prod = rB.results[core]["prod"]
        m = dict(cm)
        m.update(xin=to_xT(x[b], q, 0, 0), min=mod_in(0, b), edge=edge_flags(q), pe=to_xT(pe, q, 0, 0),
                 prodin=np.ascontiguousarray(prod.reshape(22, 128, TOK)), wor=wor)
        mapsC.append(m)
    res = run_bass_kernel_spmd(_NC_CACHE[("lin",)], mapsC, core_ids=list(range(NCORE)))
    out = np.empty_like(x)
    for core in range(NCORE):
        b, q = divmod(core, 4)
        out[b, q * TOK:(q + 1) * TOK] = from_xT(res.results[core]["xout"])
    return out
```

```python
import math
from contextlib import ExitStack

import numpy as np
import concourse.bass as bass
import concourse.mybir as mybir
from concourse.bass_utils import run_bass_kernel_spmd

F32 = mybir.dt.float32
BF16 = mybir.dt.bfloat16
AF = mybir.ActivationFunctionType
ALU = mybir.AluOpType

D = 2048
NDC = 16
S = 8192
NCORE = 8
TOK = 2048
T = 512
NTT = TOK // T
DFF = 5632
NF = DFF // 128
GF = 4
DEPTH = 4
ALPHA = (2 * DEPTH) ** 0.25
LN_EPS = 1e-5
EPOCH = 30000


class Res:
    __slots__ = ("name", "last_write", "readers")

    def __init__(self, name=""):
        self.name = name
        self.last_write = None
        self.readers = []


class Prog:
    def __init__(self, nc, stack, n_dma_sems=8):
        self.nc = nc
        self.stack = stack
        self.eng = {"pe": nc.tensor, "act": nc.scalar, "dve": nc.vector, "pool": nc.gpsimd, "sp": nc.sync}
        self.sem = {}
        self.cnt = {}
        self.nsem = 0
        for e in ("pe", "act", "dve", "pool"):
            self._new_epoch(e)
        self.waited = {e: {} for e in self.eng}
        self.dma_sems = {}
        self.dma_rr = {}
        for q in ("sp", "pool", "act"):
            self.dma_sems[q] = [[self._alloc_sem(f"dma_{q}_{i}"), 0] for i in range(n_dma_sems)]
            self.dma_rr[q] = 0
        self.n_inst = {e: 0 for e in self.eng}

    def _alloc_sem(self, name):
        self.nsem += 1
        return self.stack.enter_context(self.nc.semaphore(f"{name}_{self.nsem}"))

    def _new_epoch(self, e):
        self.sem[e] = self._alloc_sem(f"eng_{e}")
        self.cnt[e] = 0

    def sbuf(self, name, shape, dtype):
        return self.stack.enter_context(self.nc.sbuf_tensor(name, list(shape), dtype))

    def psum(self, name, shape, dtype=F32):
        return self.stack.enter_context(self.nc.psum_tensor(name, list(shape), dtype))

    def _wait(self, e, tok):
        src, sem, val = tok
        key = id(sem)
        if self.waited[e].get(key, 0) >= val:
            return
        self.eng[e].wait_ge(sem, val)
        self.waited[e][key] = val

    def _deps(self, e, reads, writes, same_engine_ok=True):
        toks = []
        for r in reads:
            if r.last_write is not None:
                toks.append(r.last_write)
        for w in writes:
            if w.last_write is not None:
                toks.append(w.last_write)
            toks.extend(w.readers)
        for tok in toks:
            if same_engine_ok and tok[0] == e and e == "pe":
                continue
            self._wait(e, tok)

    def _commit(self, tok, reads, writes):
        for r in reads:
            r.readers.append(tok)
            if len(r.readers) > 48:
                latest = {}
                for t in r.readers:
                    k = (t[0], id(t[1]))
                    if k not in latest or latest[k][2] < t[2]:
                        latest[k] = t
                r.readers = list(latest.values())
        for w in writes:
            w.last_write = tok
            w.readers = []

    def op(self, e, fn, reads=(), writes=()):
        self._deps(e, reads, writes)
        inst = fn(self.eng[e])
        if self.cnt[e] >= EPOCH:
            self._new_epoch(e)
        self.cnt[e] += 1
        inst.then_inc(self.sem[e], 1)
        tok = (e, self.sem[e], self.cnt[e])
        self._commit(tok, reads, writes)
        self.n_inst[e] += 1
        return tok

    def dma(self, q, out, in_, reads=(), writes=(), **kw):
        self._deps(q, reads, writes, same_engine_ok=False)
        pool = self.dma_sems[q]
        slot = pool[self.dma_rr[q] % len(pool)]
        self.dma_rr[q] += 1
        sem, val = slot
        if val > 0:
            self._wait(q, ("dma", sem, val))
        inst = self.eng[q].dma_start(out=out, in_=in_, **kw)
        slot[1] = val + 16
        inst.then_inc(sem, 16)
        tok = ("dma", sem, val + 16)
        self._commit(tok, reads, writes)
        self.n_inst[q] += 1
        return tok

    def finish(self, e="sp"):
        for q, pool in self.dma_sems.items():
            for sem, val in pool:
                if val > 0:
                    self._wait(e, ("dma", sem, val))


class WStream:
    NSTG = 4
    NRING = 14

    def __init__(self, P, nstg=None, nring=None):
        self.P = P
        self.NSTG = nstg or WStream.NSTG
        self.NRING = nring or WStream.NRING
        self.stg = [P.sbuf(f"stg{i}", [128, 2048], F32) for i in range(self.NSTG)]
        self.stg_res = [Res(f"stg{i}") for i in range(self.NSTG)]
        self.ring = [P.sbuf(f"wring{i}", [128, 2048], BF16) for i in range(self.NRING)]
        self.ring_res = [Res(f"wring{i}") for i in range(self.NRING)]
        self.items = []
        self.issued = 0
        self.n_stg = 0
        self.n_ring = 0
        self.loc = {}
        self.stg_owner = [None] * self.NSTG
        self.ring_owner = [None] * self.NRING
        self.released = set()

    def add(self, src_ap, cast=True, width=2048):
        self.items.append((src_ap, cast, width))
        return len(self.items) - 1

    def issue_until(self, idx):
        P = self.P
        idx = min(idx, len(self.items) - 1)
        while self.issued <= idx:
            i = self.issued
            src, cast, wd = self.items[i]
            if cast:
                r = self.n_ring % self.NRING
                if self.ring_owner[r] is not None and self.ring_owner[r] not in self.released:
                    return
                self.n_ring += 1
                self.ring_owner[r] = i
                P.dma("pool", self.ring[r][:, 0:wd], src, writes=[self.ring_res[r]])
                self.loc[i] = (self.ring[r], self.ring_res[r])
            else:
                s = self.n_stg % self.NSTG
                if self.stg_owner[s] is not None and self.stg_owner[s] not in self.released:
                    return
                self.n_stg += 1
                self.stg_owner[s] = i
                P.dma("sp", self.stg[s][:, 0:wd], src, writes=[self.stg_res[s]])
                self.loc[i] = (self.stg[s], self.stg_res[s])
            self.issued += 1

    def get(self, idx, lookahead=6):
        self.issue_until(idx + lookahead)
        assert idx in self.loc, f"weight item {idx} could not be issued (ring full: missing release?)"
        return self.loc[idx]

    def release(self, idx):
        self.released.add(idx)


V_LNG0, V_LNB0, V_LNG1, V_LNB1, V_MX0, V_MX1, V_MX2, V_MX3 = range(8)
NV = 8


class LayerCtx:
    pass


def _bc(ap_col, n):
    return ap_col.to_broadcast([128, n])


def build_layer(kind, halo_l, halo_r, n_cond=2, extra=None):
    nc = bass.Bass("TRN2", target_bir_lowering=False)
    NT = halo_l + TOK + halo_r
    W = halo_l + T + halo_r
    dram = {}

    def din(name, shape, dt=F32):
        dram[name] = nc.dram_tensor(name, list(shape), dt, kind="ExternalInput").ap()
        return dram[name]

    xin = din("xin", [128, NDC, NT])
    vec = din("vec", [128, NV, NDC])
    min_ = din("min", [128, 96, n_cond])
    w1r = din("w1r", [NF, 128, 2048])
    w3r = din("w3r", [NF, 128, 2048])
    w2r = din("w2r", [NF, 128, 2048])
    edge = din("edge", [128, 2])
    if kind == "pool":
        poolw = din("poolw", [16, 128, 512])
        pinv = din("pinv", [4, TOK])
    NCV = 80 + 16 * 31
    if kind == "conf":
        cvw1 = din("cvw1", [32, 128, 2048])
        cvw2 = din("cvw2", [16, 128, 2048])
        cvv = din("cvv", [128, NCV])
        ident_in = din("ident", [128, 128])
    if kind == "lin":
        pe_in = din("pe", [128, NDC, TOK])
        prodin = din("prodin", [22, 128, TOK], BF16)
        wor = din("wor", [22, 128, 2048])
    if kind == "none":
        pe_in = din("pe", [128, NDC, TOK])
    if kind == "four":
        fin = din("fin", [128, 2, NDC, TOK], BF16)
        ccsc = din("ccsc", [4, 128, 1024])
        ftw = din("ftw", [16, 128, 2048])
    xout = nc.dram_tensor("xout", [128, NDC, TOK], F32, kind="ExternalOutput").ap()

    with ExitStack() as st:
        P = Prog(nc, st)
        ws = WStream(P, nstg=2)
        L = LayerCtx()
        xs = P.sbuf("xs", [128, NDC, W], F32)
        xs_res = [Res(f"xs{d}") for d in range(NDC)]
        a16 = P.sbuf("a16", [128, NDC, W], BF16)
        a16_res = [Res(f"a16_{d}") for d in range(NDC)]
        g16 = P.sbuf("g16", [128, 2, GF, T], BF16)
        g16_res = [[Res(f"g16_{i}_{j}") for j in range(GF)] for i in range(2)]
        stmp = [P.sbuf(f"stmp{i}", [128, T], F32) for i in range(2)]
        stmp_res = [Res(f"stmp{i}") for i in range(2)]
        sq = [P.sbuf(f"sq{i}", [128, T], F32) for i in range(2)]
        sq_res = [Res(f"sq{i}") for i in range(2)]
        lnt = P.sbuf("lnt", [128, 2, T], F32)
        lnt_res = Res("lnt")
        lnt2 = P.sbuf("lnt2", [128, T], F32)
        zacc = P.sbuf("zacc", [128, T], F32)
        qacc = P.sbuf("qacc", [128, T], F32)
        zacc_res, qacc_res = Res("zacc"), Res("qacc")
        onesD = P.sbuf("onesD", [128, 128], F32)
        onesD_res = Res("onesD")
        vraw = P.sbuf("vraw", [128, NV, NDC], F32)
        vraw_res = Res("vraw")
        modb_sb = P.sbuf("modb_sb", [128, 96], F32)
        cs = P.sbuf("cs", [128, NDC, n_cond], F32)
        cs_res = Res("cs")
        msb = P.sbuf("msb", [128, 96, n_cond], F32)
        msb_res = Res("msb")
        dv = P.sbuf("dv", [128, 8, NDC], F32)
        dv_res = Res("dv")
        edge_sb = P.sbuf("edge_sb", [128, 2], F32)
        edge_res = Res("edge")
        ps = [P.psum(f"ps{i}", [128, T]) for i in range(8)]
        ps_res = [Res(f"ps{i}") for i in range(8)]

        plan = []
        if kind == "pool":
            pw_items = [ws.add(poolw[i], width=512) for i in range(16)]
        if kind == "four":
            cc_items = [ws.add(ccsc[i], width=1024) for i in range(4)]
        for tt in range(NTT):
            d_ = {}
            if kind == "conf":
                for oc in range(NDC):
                    d_[("cv", oc)] = ws.add(cvw1[oc])
                    d_[("cg", oc)] = ws.add(cvw1[16 + oc])
                for oc in range(NDC):
                    d_[("c2", oc)] = ws.add(cvw2[oc])
            if kind == "four":
                for oc in range(NDC):
                    d_[("fw", oc)] = ws.add(ftw[oc])
            if kind == "lin":
                for kc in range(22):
                    d_[("wo", kc)] = ws.add(wor[kc])
            for f in range(NF):
                d_[("w1", f)] = ws.add(w1r[f])
                d_[("w3", f)] = ws.add(w3r[f])
                d_[("w2", f)] = ws.add(w2r[f])
            plan.append(d_)

        P.op("pool", lambda e: e.memset(onesD[:], 1.0 / D), writes=[onesD_res])
        P.dma("act", vraw[:], vec, writes=[vraw_res])
        P.dma("act", edge_sb[:], edge, writes=[edge_res])
        P.dma("act", msb[:], min_, writes=[msb_res])

        def mvec(k6, dc, j=0):
            return msb[:, k6 * 16 + dc, j:j + 1]

        def mrow(k6):
            return msb[:, k6 * 16:(k6 + 1) * 16, 0]
        P.op("dve", lambda e: e.tensor_scalar(out=dv[:, 0, :], in0=mrow(1), scalar1=1.0, scalar2=None, op0=ALU.add),
             reads=[msb_res], writes=[dv_res])
        if kind == "pool":
            P.op("dve", lambda e: e.tensor_tensor(out=dv[:, 1, :], in0=mrow(2), in1=vraw[:, V_MX1, :], op=ALU.mult),
                 reads=[msb_res, vraw_res], writes=[dv_res])
            P.op("dve", lambda e: e.tensor_tensor(out=dv[:, 2, :], in0=dv[:, 1, :], in1=vraw[:, V_MX0, :], op=ALU.mult),
                 reads=[vraw_res], writes=[dv_res])
        else:
            P.op("dve", lambda e: e.tensor_copy(out=dv[:, 1, :], in_=mrow(2)), reads=[msb_res], writes=[dv_res])
            P.op("dve", lambda e: e.tensor_tensor(out=dv[:, 2, :], in0=dv[:, 1, :], in1=vraw[:, V_MX0, :], op=ALU.mult),
                 reads=[vraw_res], writes=[dv_res])
        P.op("dve", lambda e: e.tensor_scalar(out=dv[:, 3, :], in0=vraw[:, V_LNG0, :], scalar1=ALPHA, scalar2=None, op0=ALU.mult),
             reads=[vraw_res], writes=[dv_res])
        P.op("dve", lambda e: e.tensor_scalar(out=dv[:, 4, :], in0=vraw[:, V_LNB0, :], scalar1=ALPHA, scalar2=None, op0=ALU.mult),
             reads=[vraw_res], writes=[dv_res])
        P.op("dve", lambda e: e.tensor_scalar(out=dv[:, 5, :], in0=mrow(4), scalar1=1.0, scalar2=1.0 / ALPHA, op0=ALU.add, op1=ALU.mult),
             reads=[msb_res], writes=[dv_res])
        VR = [vraw_res, dv_res, msb_res]

        def emit_ln(c0, gcol, bcol, buf=None, bres=None, func=AF.Identity, dst=None):
            buf = xs if buf is None else buf
            bres = xs_res if bres is None else bres
            pm, pq = ps[4], ps[5]
            pmr, pqr = ps_res[4], ps_res[5]
            P.op("pool", lambda e: e.tensor_tensor(out=zacc[:], in0=buf[:, 0, c0:c0 + T], in1=buf[:, 1, c0:c0 + T], op=ALU.add),
                 reads=[bres[0], bres[1]], writes=[zacc_res])
            for dc in range(2, NDC):
                P.op("pool", lambda e, dc=dc: e.tensor_tensor(out=zacc[:], in0=zacc[:], in1=buf[:, dc, c0:c0 + T], op=ALU.add),
                     reads=[bres[dc]], writes=[zacc_res])
            P.op("act", lambda e: e.activation(out=qacc[:], in_=buf[:, 0, c0:c0 + T], func=AF.Square), reads=[bres[0]], writes=[qacc_res])
            for dc in range(1, NDC):
                k = dc % 2
                P.op("act", lambda e, dc=dc, k=k: e.activation(out=sq[k][:], in_=buf[:, dc, c0:c0 + T], func=AF.Square),
                     reads=[bres[dc]], writes=[sq_res[k]])
                P.op("dve", lambda e, k=k: e.tensor_tensor(out=qacc[:], in0=qacc[:], in1=sq[k][:], op=ALU.add),
                     reads=[sq_res[k]], writes=[qacc_res])
            P.op("pe", lambda e: e.matmul(pm[:], lhsT=onesD[:], rhs=zacc[:], start=True, stop=True),
                 reads=[onesD_res, zacc_res], writes=[pmr])
            P.op("pe", lambda e: e.matmul(pq[:], lhsT=onesD[:], rhs=qacc[:], start=True, stop=True),
                 reads=[onesD_res, qacc_res], writes=[pqr])
            P.op("act", lambda e: e.activation(out=lnt[:, 0, :], in_=pm[:], func=AF.Square), reads=[pmr], writes=[lnt_res])
            P.op("dve", lambda e: e.tensor_tensor(out=lnt[:, 0, :], in0=pq[:], in1=lnt[:, 0, :], op=ALU.subtract),
                 reads=[pqr], writes=[lnt_res])
            P.op("dve", lambda e: e.tensor_scalar(out=lnt[:, 0, :], in0=lnt[:, 0, :], scalar1=LN_EPS, scalar2=None, op0=ALU.add),
                 writes=[lnt_res])
            P.op("act", lambda e: e.activation(out=lnt[:, 1, :], in_=lnt[:, 0, :], func=AF.Sqrt), reads=[lnt_res], writes=[lnt_res])
            P.op("dve", lambda e: e.reciprocal(out=lnt[:, 1, :], in_=lnt[:, 1, :]), reads=[lnt_res], writes=[lnt_res])
            for _it in range(2):
                P.op("dve", lambda e: e.tensor_tensor(out=lnt2[:], in0=lnt[:, 0, :], in1=lnt[:, 1, :], op=ALU.mult), writes=[lnt_res])
                P.op("dve", lambda e: e.tensor_tensor(out=lnt2[:], in0=lnt2[:], in1=lnt[:, 1, :], op=ALU.mult), writes=[lnt_res])
                P.op("dve", lambda e: e.tensor_scalar(out=lnt2[:], in0=lnt2[:], scalar1=-0.5, scalar2=1.5, op0=ALU.mult, op1=ALU.add), writes=[lnt_res])
                P.op("dve", lambda e: e.tensor_tensor(out=lnt[:, 1, :], in0=lnt[:, 1, :], in1=lnt2[:], op=ALU.mult), writes=[lnt_res])
            for dc in range(NDC):
                P.op("dve", lambda e, dc=dc: e.tensor_tensor(out=buf[:, dc, c0:c0 + T], in0=buf[:, dc, c0:c0 + T], in1=pm[:], op=ALU.subtract),
                     reads=[pmr], writes=[bres[dc]])
                P.op("dve", lambda e, dc=dc: e.tensor_tensor(out=buf[:, dc, c0:c0 + T], in0=buf[:, dc, c0:c0 + T], in1=lnt[:, 1, :], op=ALU.mult),
                     reads=[lnt_res], writes=[bres[dc]])
                if dst is None:
                    P.op("act", lambda e, dc=dc: e.activation(out=buf[:, dc, c0:c0 + T], in_=buf[:, dc, c0:c0 + T], func=func,
                                                              scale=gcol(dc), bias=bcol(dc)),
                         reads=VR + [bres[dc]], writes=[bres[dc]])
                else:
                    dap, dres = dst(dc)
                    P.op("act", lambda e, dc=dc, dap=dap: e.activation(out=dap, in_=buf[:, dc, c0:c0 + T], func=func,
                                                                       scale=gcol(dc), bias=bcol(dc)),
                         reads=VR + [bres[dc]], writes=[dres])

        def emit_ffn(tt, c0):
            pl = plan[tt]
            for dc in range(NDC):
                P.op("act", lambda e, dc=dc: e.activation(out=a16[:, dc, 0:T], in_=xs[:, dc, c0:c0 + T], func=AF.Identity,
                                                          scale=dv[:, 5, dc:dc + 1], bias=mvec(3, dc)),
                     reads=VR + [xs_res[dc]], writes=[a16_res[dc]])
            ngrp = NF // GF
            for grp in range(ngrp):
                gb = grp % 2
                for fi in range(GF):
                    f = grp * GF + fi
                    w1t, w1res = ws.get(pl[("w1", f)])
                    w3t, w3res = ws.get(pl[("w3", f)])
                    p1, p3 = ps[(f % 2) * 2], ps[(f % 2) * 2 + 1]
                    p1r, p3r = ps_res[(f % 2) * 2], ps_res[(f % 2) * 2 + 1]
                    for kc in range(NDC):
                        P.op("pe", lambda e, kc=kc, w1t=w1t, p1=p1: e.matmul(
                            p1[:], lhsT=w1t[:, kc * 128:(kc + 1) * 128], rhs=a16[:, kc, 0:T],
                            start=(kc == 0), stop=(kc == NDC - 1)), reads=[w1res, a16_res[kc]], writes=[p1r])
                    for kc in range(NDC):
                        P.op("pe", lambda e, kc=kc, w3t=w3t, p3=p3: e.matmul(
                            p3[:], lhsT=w3t[:, kc * 128:(kc + 1) * 128], rhs=a16[:, kc, 0:T],
                            start=(kc == 0), stop=(kc == NDC - 1)), reads=[w3res, a16_res[kc]], writes=[p3r])
                    ws.release(pl[("w1", f)])
                    ws.release(pl[("w3", f)])
                    k = f % 2
                    P.op("act", lambda e, k=k, p1=p1: e.activation(out=stmp[k][:], in_=p1[:], func=AF.Silu),
                         reads=[p1r], writes=[stmp_res[k]])
                    P.op("dve", lambda e, k=k, p3=p3, gb=gb, fi=fi: e.tensor_tensor(
                        out=g16[:, gb, fi, :], in0=p3[:], in1=stmp[k][:], op=ALU.mult),
                        reads=[p3r, stmp_res[k]], writes=[g16_res[gb][fi]])
                w2 = [ws.get(pl[("w2", grp * GF + fi)]) for fi in range(GF)]
                for dc in range(NDC):
                    po, por = ps[6 + dc % 2], ps_res[6 + dc % 2]
                    for fi in range(GF):
                        P.op("pe", lambda e, fi=fi, dc=dc, po=po: e.matmul(
                            po[:], lhsT=w2[fi][0][:, dc * 128:(dc + 1) * 128], rhs=g16[:, gb, fi, :],
                            start=(fi == 0), stop=(fi == GF - 1)), reads=[w2[fi][1], g16_res[gb][fi]], writes=[por])
                    P.op("dve", lambda e, dc=dc, po=po: e.scalar_tensor_tensor(
                        out=xs[:, dc, c0:c0 + T], in0=po[:], scalar=mvec(5, dc), in1=xs[:, dc, c0:c0 + T],
                        op0=ALU.mult, op1=ALU.add), reads=VR + [por], writes=[xs_res[dc]])
                for fi in range(GF):
                    ws.release(pl[("w2", grp * GF + fi)])

        if kind == "pool":
            pwb = P.sbuf("pwb", [128, 16, 512], BF16)
            pwb_res = Res("pwb")
            for i in range(16):
                wt, wr = ws.get(pw_items[i], lookahead=2)
                P.op("act", lambda e, i=i, wt=wt: e.activation(out=pwb[:, i, :], in_=wt[:, 0:512], func=AF.Identity), reads=[wr], writes=[pwb_res])
                ws.release(pw_items[i])
            hbuf = [P.sbuf(f"hbuf{i}", [128, W], F32) for i in range(2)]
            hbuf_res = [Res(f"hbuf{i}") for i in range(2)]
            sl = [P.sbuf(f"sl{i}", [128, W], F32) for i in range(4)]
            sl_res = [Res(f"sl{i}") for i in range(4)]
            inv_sb = P.sbuf("inv_sb", [128, 4, T], F32)
            inv_res = Res("inv")

        def emit_pool_mixer(tt):
            c0 = halo_l
            P.dma("act", inv_sb[:], pinv[:, tt * T:(tt + 1) * T].partition_broadcast(128), writes=[inv_res])
            for dc in range(NDC):
                g = dc // 4
                hb, hr = hbuf[dc % 2], hbuf_res[dc % 2]
                P.op("act", lambda e, dc=dc, hb=hb: e.activation(out=hb[:], in_=xs[:, dc, :], func=AF.Identity,
                                                               scale=dv[:, 0, dc:dc + 1], bias=mvec(0, dc)),
                     reads=VR + [xs_res[dc]], writes=[hr])
                if tt == 0:
                    P.op("dve", lambda e, hb=hb: e.tensor_scalar(out=hb[:, 0:halo_l], in0=hb[:, 0:halo_l], scalar1=edge_sb[:, 0:1],
                                                                 scalar2=None, op0=ALU.mult), reads=[edge_res], writes=[hr])
                if tt == NTT - 1:
                    P.op("dve", lambda e, hb=hb: e.tensor_scalar(out=hb[:, c0 + T:W], in0=hb[:, c0 + T:W], scalar1=edge_sb[:, 1:2],
                                                                 scalar2=None, op0=ALU.mult), reads=[edge_res], writes=[hr])
                cur, cur_r = hb, hr
                lo, hi = 0, W
                offs = [(1, 0), (1, 1), (2, 2), (4, 4)]
                for lev in range(g + 1):
                    a, b = offs[lev]
                    nlo, nhi = lo + a, hi - b
                    dst, dst_r = sl[lev], sl_res[lev]
                    P.op("pool", lambda e, cur=cur, dst=dst, a=a, b=b, nlo=nlo, nhi=nhi: e.tensor_tensor(
                        out=dst[:, nlo:nhi], in0=cur[:, nlo - a:nhi - a], in1=cur[:, nlo + b:nhi + b], op=ALU.add),
                        reads=[cur_r], writes=[dst_r])
                    cur, cur_r, lo, hi = dst, dst_r, nlo, nhi
                P.op("dve", lambda e, cur=cur, g=g: e.tensor_tensor(out=cur[:, c0:c0 + T], in0=cur[:, c0:c0 + T], in1=inv_sb[:, g, :], op=ALU.mult),
                     reads=[inv_res, cur_r], writes=[cur_r])
                P.op("dve", lambda e, cur=cur, hb=hb, dc=dc: e.tensor_tensor(out=a16[:, dc, 0:T], in0=cur[:, c0:c0 + T], in1=hb[:, c0:c0 + T], op=ALU.subtract),
                     reads=[cur_r, hr], writes=[a16_res[dc]])
            for oc in range(NDC):
                g = oc // 4
                po, por = ps[6 + oc % 2], ps_res[6 + oc % 2]
                for kc in range(4):
                    P.op("pe", lambda e, g=g, kc=kc, oc=oc, po=po: e.matmul(
                        po[:], lhsT=pwb[:, g * 4 + kc, (oc % 4) * 128:(oc % 4 + 1) * 128], rhs=a16[:, g * 4 + kc, 0:T],
                        start=(kc == 0), stop=(kc == 3)), reads=[pwb_res, a16_res[g * 4 + kc]], writes=[por])
                emit_residual(oc, po, por, c0)

        if kind == "conf":
            cvv_sb = P.sbuf("cvv_sb", [128, NCV], F32)
            P.dma("act", cvv_sb[:], cvv, writes=[vraw_res])
            ubuf = P.sbuf("ubuf", [128, NDC, T], F32)
            u_res = [Res(f"u{d}") for d in range(NDC)]
            u16 = [P.sbuf(f"u16_{i}", [128, W], BF16) for i in range(2)]
            u16_res = [Res(f"u16_{i}") for i in range(2)]
            dg = [P.sbuf(f"dg{i}", [128, 31, 128], BF16) for i in range(2)]
            dg_res = [Res(f"dg{i}") for i in range(2)]
            ident_sb = P.sbuf("ident_sb", [128, 128], F32)
            P.dma("act", ident_sb[:], ident_in, writes=[vraw_res])
            HWD = W // 2
            sgt = [P.sbuf(f"sgt{i}", [128, HWD], F32) for i in range(2)]
            sgt_res = [Res(f"sgt{i}") for i in range(2)]

        def emit_conf_mixer(tt):
            pl = plan[tt]
            c0 = halo_l
            for dc in range(NDC):
                P.op("act", lambda e, dc=dc: e.activation(out=a16[:, dc, :], in_=xs[:, dc, :], func=AF.Identity,
                                                          scale=dv[:, 0, dc:dc + 1], bias=mvec(0, dc)),
                     reads=VR + [xs_res[dc]], writes=[a16_res[dc]])
            for oc in range(NDC):
                wv, wvr = ws.get(pl[("cv", oc)])
                wg, wgr = ws.get(pl[("cg", oc)])
                for half in range(2):
                    ca, cb = half * HWD, (half + 1) * HWD
                    pv, pvr = ps[half * 2], ps_res[half * 2]
                    pg, pgr = ps[half * 2 + 1], ps_res[half * 2 + 1]
                    for kc in range(NDC):
                        P.op("pe", lambda e, kc=kc, pv=pv, ca=ca, cb=cb: e.matmul(
                            pv[:, 0:HWD], lhsT=wv[:, kc * 128:(kc + 1) * 128], rhs=a16[:, kc, ca:cb],
                            start=(kc == 0), stop=(kc == NDC - 1)), reads=[wvr, a16_res[kc]], writes=[pvr])
                    for kc in range(NDC):
                        P.op("pe", lambda e, kc=kc, pg=pg, ca=ca, cb=cb: e.matmul(
                            pg[:, 0:HWD], lhsT=wg[:, kc * 128:(kc + 1) * 128], rhs=a16[:, kc, ca:cb],
                            start=(kc == 0), stop=(kc == NDC - 1)), reads=[wgr, a16_res[kc]], writes=[pgr])
                ws.release(pl[("cv", oc)])
                ws.release(pl[("cg", oc)])
                for half in range(2):
                    ca, cb = half * HWD, (half + 1) * HWD
                    pv, pvr = ps[half * 2], ps_res[half * 2]
                    pg, pgr = ps[half * 2 + 1], ps_res[half * 2 + 1]
                    P.op("act", lambda e, half=half, pg=pg: e.activation(out=sgt[half][:], in_=pg[:, 0:HWD], func=AF.Sigmoid,
                                                                         bias=cvv_sb[:, 16 + oc:17 + oc]),
                         reads=VR + [pgr], writes=[sgt_res[half]])
                    P.op("dve", lambda e, half=half, pv=pv, ca=ca, cb=cb: e.scalar_tensor_tensor(
                        out=u16[oc % 2][:, ca:cb], in0=pv[:, 0:HWD], scalar=cvv_sb[:, oc:oc + 1], in1=sgt[half][:],
                        op0=ALU.add, op1=ALU.mult), reads=VR + [pvr, sgt_res[half]], writes=[u16_res[oc % 2]])
                uu, uur = u16[oc % 2], u16_res[oc % 2]
                if tt == 0:
                    P.op("dve", lambda e: e.tensor_scalar(out=uu[:, 0:halo_l], in0=uu[:, 0:halo_l], scalar1=edge_sb[:, 0:1],
                                                          scalar2=None, op0=ALU.mult), reads=[edge_res], writes=[uur])
                if tt == NTT - 1:
                    P.op("dve", lambda e: e.tensor_scalar(out=uu[:, c0 + T:W], in0=uu[:, c0 + T:W], scalar1=edge_sb[:, 1:2],
                                                          scalar2=None, op0=ALU.mult), reads=[edge_res], writes=[uur])
                dgo, dgr = dg[oc % 2], dg_res[oc % 2]
                dwc = 80 + oc * 31
                P.op("pool", lambda e: e.tensor_tensor(
                    out=dgo[:], in0=ident_sb[:].unsqueeze(1).to_broadcast([128, 31, 128]),
                    in1=cvv_sb[:, dwc:dwc + 31].unsqueeze(2).to_broadcast([128, 31, 128]), op=ALU.mult),
                    reads=VR, writes=[dgr])
                pc, pcr = ps[4 + oc % 2], ps_res[4 + oc % 2]
                for j in range(31):
                    P.op("pe", lambda e, j=j: e.matmul(pc[:], lhsT=dgo[:, j, :], rhs=uu[:, j:j + T], start=(j == 0), stop=(j == 30)),
                         reads=[dgr, uur], writes=[pcr])
                P.op("act", lambda e: e.activation(out=ubuf[:, oc, :], in_=pc[:], func=AF.Identity, bias=cvv_sb[:, 32 + oc:33 + oc]),
                     reads=VR + [pcr], writes=[u_res[oc]])
            emit_ln(0, lambda dc: cvv_sb[:, 48 + dc:49 + dc], lambda dc: cvv_sb[:, 64 + dc:65 + dc], buf=ubuf, bres=u_res,
                    func=AF.Silu, dst=lambda dc: (a16[:, dc, 0:T], a16_res[dc]))
            for oc in range(NDC):
                w2t, w2r_ = ws.get(pl[("c2", oc)])
                po, por = ps[6 + oc % 2], ps_res[6 + oc % 2]
                for kc in range(NDC):
                    P.op("pe", lambda e, kc=kc, po=po: e.matmul(
                        po[:], lhsT=w2t[:, kc * 128:(kc + 1) * 128], rhs=a16[:, kc, 0:T],
                        start=(kc == 0), stop=(kc == NDC - 1)), reads=[w2r_, a16_res[kc]], writes=[por])
                ws.release(pl[("c2", oc)])
                emit_residual(oc, po, por, c0)

        if kind == "four":
            ccb = P.sbuf("ccb", [128, 4, 1024], BF16)
            ccb_res = Res("ccb")
            for i in range(4):
                wt, wr = ws.get(cc_items[i], lookahead=2)
                P.op("act", lambda e, i=i, wt=wt: e.activation(out=ccb[:, i, :], in_=wt[:, 0:1024], func=AF.Identity), reads=[wr], writes=[ccb_res])
                ws.release(cc_items[i])
            fab = P.sbuf("fab", [128, 2, NDC, T], BF16)
            fab_res = [[Res(f"fab{i}_{d}") for d in range(NDC)] for i in range(2)]
            corr = P.sbuf("corr", [128, NDC, 2], F32)
            P.op("dve", lambda e: e.memset(corr[:], 0.0), writes=[dv_res])
            fl0 = P.sbuf("fl0", [128, 1], F32)
            P.op("dve", lambda e: e.tensor_scalar(out=fl0[:], in0=edge_sb[:, 0:1], scalar1=-float(S), scalar2=float(S), op0=ALU.mult, op1=ALU.add),
                 reads=[edge_res], writes=[dv_res])
            P.op("dve", lambda e: e.tensor_scalar(out=corr[:, :, 0], in0=mrow(0), scalar1=fl0[:, 0:1], scalar2=None, op0=ALU.mult),
                 reads=[msb_res], writes=[dv_res])

        def emit_four_mixer(tt):
            pl = plan[tt]
            c0 = halo_l
            for i in range(2):
                P.dma("act", fab[:, i], fin[:, i, :, tt * T:(tt + 1) * T], writes=fab_res[i])
            for i in range(2):
                for dc in range(NDC):
                    P.op("act", lambda e, i=i, dc=dc: e.activation(out=fab[:, i, dc, :], in_=fab[:, i, dc, :], func=AF.Identity,
                                                                   scale=dv[:, 0, dc:dc + 1]),
                         reads=VR + [fab_res[i][dc]], writes=[fab_res[i][dc]])
            if tt == 0:
                for dc in range(NDC):
                    P.op("dve", lambda e, dc=dc: e.tensor_tensor(out=fab[:, 0, dc, 0:2], in0=fab[:, 0, dc, 0:2], in1=corr[:, dc, :], op=ALU.add),
                         reads=VR + [fab_res[0][dc]], writes=[fab_res[0][dc]])
            for oc in range(NDC):
                g = oc // 4
                pj, pjr = ps[oc % 4], ps_res[oc % 4]
                n = 0
                for i in range(2):
                    for kc in range(4):
                        P.op("pe", lambda e, i=i, kc=kc, pj=pj, n=n: e.matmul(
                            pj[:], lhsT=ccb[:, kc, i * 512 + (oc % 4) * 128:i * 512 + (oc % 4 + 1) * 128], rhs=fab[:, i, 4 * g + kc, :],
                            start=(n == 0), stop=(n == 7)), reads=[ccb_res, fab_res[i][4 * g + kc]], writes=[pjr])
                        n += 1
                P.op("act", lambda e, pj=pj: e.activation(out=a16[:, oc, 0:T], in_=pj[:], func=AF.Identity), reads=[pjr], writes=[a16_res[oc]])
            for oc in range(NDC):
                w2t, w2r_ = ws.get(pl[("fw", oc)])
                po, por = ps[6 + oc % 2], ps_res[6 + oc % 2]
                for kc in range(NDC):
                    P.op("pe", lambda e, kc=kc, po=po: e.matmul(
                        po[:], lhsT=w2t[:, kc * 128:(kc + 1) * 128], rhs=a16[:, kc, 0:T],
                        start=(kc == 0), stop=(kc == NDC - 1)), reads=[w2r_, a16_res[kc]], writes=[por])
                ws.release(pl[("fw", oc)])
                emit_residual(oc, po, por, c0)

        if kind == "none":
            pebuf = P.sbuf("pebuf", [128, NDC, T], F32)
            pe_res = Res("pe")

        def emit_none_mixer(tt):
            P.dma("act", pebuf[:], pe_in[:, :, tt * T:(tt + 1) * T], writes=[pe_res])
            for dc in range(NDC):
                P.op("dve", lambda e, dc=dc: e.tensor_tensor(out=xs[:, dc, :], in0=xs[:, dc, :], in1=pebuf[:, dc, :], op=ALU.add),
                     reads=[pe_res], writes=[xs_res[dc]])
                P.op("act", lambda e, dc=dc: e.activation(out=xs[:, dc, :], in_=xs[:, dc, :], func=AF.Identity, scale=ALPHA),
                     reads=[xs_res[dc]], writes=[xs_res[dc]])

        if kind == "lin":
            prt = P.sbuf("prt", [128, 22, T], BF16)
            prt_res = Res("prt")
            petmp = [P.sbuf(f"petmp{i}", [128, T], F32) for i in range(2)]
            petmp_res = [Res(f"petmp{i}") for i in range(2)]

        def emit_lin_mixer(tt):
            pl = plan[tt]
            c0 = halo_l
            P.dma("act", prt[:], prodin[:, :, tt * T:(tt + 1) * T].rearrange("k p t -> p k t"), writes=[prt_res])
            for dc in range(NDC):
                k = dc % 2
                P.dma("act", petmp[k][:], pe_in[:, dc, tt * T:(tt + 1) * T], writes=[petmp_res[k]])
                P.op("dve", lambda e, dc=dc, k=k: e.tensor_tensor(out=xs[:, dc, :], in0=xs[:, dc, :], in1=petmp[k][:], op=ALU.add),
                     reads=[petmp_res[k]], writes=[xs_res[dc]])
            for half in range(2):
                wts = [ws.get(pl[("wo", half * 11 + i)]) for i in range(11)]
                for dc in range(NDC):
                    po, por = ps[6 + dc % 2], ps_res[6 + dc % 2]
                    for i in range(11):
                        P.op("pe", lambda e, i=i, dc=dc, po=po: e.matmul(
                            po[:], lhsT=wts[i][0][:, dc * 128:(dc + 1) * 128], rhs=prt[:, half * 11 + i, :],
                            start=(i == 0), stop=(i == 10)), reads=[wts[i][1], prt_res], writes=[por])
                    if half == 0:
                        emit_residual(dc, po, por, c0)
                    else:
                        P.op("dve", lambda e, dc=dc, po=po: e.scalar_tensor_tensor(
                            out=xs[:, dc, c0:c0 + T], in0=po[:], scalar=dv[:, 1, dc:dc + 1], in1=xs[:, dc, c0:c0 + T],
                            op0=ALU.mult, op1=ALU.add), reads=VR + [por], writes=[xs_res[dc]])
                for i in range(11):
                    ws.release(pl[("wo", half * 11 + i)])

        def emit_residual(dc, po, por, c0):
            P.op("act", lambda e: e.activation(out=xs[:, dc, c0:c0 + T], in_=xs[:, dc, c0:c0 + T], func=AF.Identity,
                                               scale=ALPHA, bias=dv[:, 2, dc:dc + 1]),
                 reads=VR + [xs_res[dc]], writes=[xs_res[dc]])
            P.op("dve", lambda e: e.scalar_tensor_tensor(out=xs[:, dc, c0:c0 + T], in0=po[:], scalar=dv[:, 1, dc:dc + 1],
                                                         in1=xs[:, dc, c0:c0 + T], op0=ALU.mult, op1=ALU.add),
                 reads=VR + [por], writes=[xs_res[dc]])

        for tt in range(NTT):
            c0 = halo_l
            P.dma("act", xs[:], xin[:, :, tt * T:tt * T + W], writes=xs_res)
            if kind == "pool":
                emit_pool_mixer(tt)
            elif kind == "conf":
                emit_conf_mixer(tt)
            elif kind == "four":
                emit_four_mixer(tt)
            elif kind == "none":
                emit_none_mixer(tt)
            elif kind == "lin":
                emit_lin_mixer(tt)
            emit_ln(c0, lambda dc: dv[:, 3, dc:dc + 1], lambda dc: dv[:, 4, dc:dc + 1])
            emit_ffn(tt, c0)

            def store(dc, tt=tt):
                pass
            emit_ln(c0, lambda dc: vraw[:, V_LNG1, dc:dc + 1], lambda dc: vraw[:, V_LNB1, dc:dc + 1])
            P.dma("pool", xout[:, :, tt * T:(tt + 1) * T], xs[:, :, c0:c0 + T], reads=xs_res)
        P.finish("sp")
        nc._stats = dict(P.n_inst)
    return nc


def col_table(v):
    return np.ascontiguousarray(np.asarray(v, np.float32).reshape(NDC, 128).T)


def to_xT(xb, q, halo_l, halo_r):
    t0 = q * TOK - halo_l
    t1 = (q + 1) * TOK + halo_r
    out = np.zeros((128, NDC, t1 - t0), np.float32)
    a, b = max(t0, 0), min(t1, S)
    blk = xb[a:b].reshape(b - a, NDC, 128).transpose(2, 1, 0)
    out[:, :, a - t0:b - t0] = blk
    return out


def from_xT(xt):
    return np.ascontiguousarray(xt.transpose(2, 1, 0).reshape(xt.shape[2], D))


def w_colblocks(w, nblk):
    K = w.shape[0] // 128
    return np.ascontiguousarray(w.reshape(K, 128, nblk, 128).transpose(2, 1, 0, 3).reshape(nblk, 128, K * 128))


def ffn_layouts(inputs, l):
    w1r = w_colblocks(np.asarray(inputs["ffn_w1"][l]), NF)
    w3r = w_colblocks(np.asarray(inputs["ffn_w3"][l]), NF)
    w2r = np.ascontiguousarray(np.asarray(inputs["ffn_w2"][l]).reshape(NF, 128, D))
    return w1r, w3r, w2r


def common_maps(inputs, l, mixvecs):
    vec = np.zeros((128, NV, NDC), np.float32)
    vec[:, V_LNG0] = col_table(inputs["ln_g"][l, 0])
    vec[:, V_LNB0] = col_table(inputs["ln_b"][l, 0])
    vec[:, V_LNG1] = col_table(inputs["ln_g"][l, 1])
    vec[:, V_LNB1] = col_table(inputs["ln_b"][l, 1])
    for i, v in enumerate(mixvecs):
        vec[:, V_MX0 + i] = col_table(v)
    w1r, w3r, w2r = ffn_layouts(inputs, l)
    return dict(vec=vec, w1r=w1r, w3r=w3r, w2r=w2r)


_MOD = {}


def mod_in(l, b, ncol=96):
    M = _MOD[l]
    return np.ascontiguousarray(np.stack([M[:, :ncol, b], M[:, :ncol, 2]], axis=-1))


def cond_T(inputs, b):
    c = np.stack([np.asarray(inputs["c"][b], np.float32), np.asarray(inputs["c_ctx"], np.float32)], axis=-1)
    return np.ascontiguousarray(c.reshape(NDC, 128, 2).transpose(1, 0, 2))


def edge_flags(q):
    e = np.ones((128, 2), np.float32)
    if q == 0:
        e[:, 0] = 0.0
    if q == 3:
        e[:, 1] = 0.0
    return e


_NC_CACHE = {}


def run_pool_layer(inputs, l, x):
    HL = HR = 8
    key = ("pool",)
    if key not in _NC_CACHE:
        _NC_CACHE[key] = build_layer("pool", HL, HR)
    nc = _NC_CACHE[key]
    cm = common_maps(inputs, l, [inputs["pool_b"][0], inputs["pool_scale"][0]])
    pw = np.asarray(inputs["pool_w"][0], np.float32)
    poolw = np.ascontiguousarray(pw.reshape(4, 4, 128, 512).reshape(16, 128, 512))
    in_maps = []
    for core in range(NCORE):
        b, q = divmod(core, 4)
        t = np.arange(q * TOK, (q + 1) * TOK)
        pinv = np.zeros((4, TOK), np.float32)
        for g, win in enumerate((2, 4, 8, 16)):
            lo = np.clip(t - win // 2, 0, S)
            hi = np.clip(t - win // 2 + win, 0, S)
            pinv[g] = 1.0 / (hi - lo).astype(np.float32)
        m = dict(cm)
        m.update(xin=to_xT(x[b], q, HL, HR), min=mod_in(l, b), edge=edge_flags(q), poolw=poolw, pinv=pinv)
        in_maps.append(m)
    res = run_bass_kernel_spmd(nc, in_maps, core_ids=list(range(NCORE)))
    out = np.empty_like(x)
    for core in range(NCORE):
        b, q = divmod(core, 4)
        out[b, q * TOK:(q + 1) * TOK] = from_xT(res.results[core]["xout"])
    return out


def run_conf_layer(inputs, l, x):
    HL = HR = 15
    key = ("conf",)
    if key not in _NC_CACHE:
        _NC_CACHE[key] = build_layer("conf", HL, HR)
    nc = _NC_CACHE[key]
    cm = common_maps(inputs, l, [inputs["cv_b2"][0]])
    cvw1 = w_colblocks(np.asarray(inputs["cv_w1"][0]), 32)
    cvw2 = w_colblocks(np.asarray(inputs["cv_w2"][0]), 16)
    b1 = np.asarray(inputs["cv_b1"][0], np.float32)
    cvv = np.concatenate([
        b1.reshape(32, 128).T,
        col_table(inputs["cv_dwb"][0]), col_table(inputs["cv_ln_g"][0]), col_table(inputs["cv_ln_b"][0]),
        np.asarray(inputs["cv_dw"][0], np.float32).reshape(31, NDC, 128).transpose(2, 1, 0).reshape(128, NDC * 31),
    ], axis=1).astype(np.float32)
    cvv = np.ascontiguousarray(cvv)
    in_maps = []
    for core in range(NCORE):
        b, q = divmod(core, 4)
        m = dict(cm)
        m.update(xin=to_xT(x[b], q, HL, HR), min=mod_in(l, b), edge=edge_flags(q), cvw1=cvw1, cvw2=cvw2, cvv=cvv,
                 ident=np.eye(128, dtype=np.float32))
        in_maps.append(m)
    res = run_bass_kernel_spmd(nc, in_maps, core_ids=list(range(NCORE)))
    out = np.empty_like(x)
    for core in range(NCORE):
        b, q = divmod(core, 4)
        out[b, q * TOK:(q + 1) * TOK] = from_xT(res.results[core]["xout"])
    return out


def build_seqdft():
    nc = bass.Bass("TRN2", target_bir_lowering=False)
    xtok = nc.dram_tensor("xtok", [64, 128, D], F32, kind="ExternalInput").ap()
    tab = nc.dram_tensor("tab", [64, 128, 4096], BF16, kind="ExternalInput").ap()
    f12 = nc.dram_tensor("f12", [128, 2, NDC, TOK], BF16, kind="ExternalOutput").ap()
    NB = 6
    with ExitStack() as st:
        P = Prog(nc, st)
        ws = WStream(P)
        bring = [P.sbuf(f"bring{i}", [128, 2048], BF16) for i in range(NB)]
        bring_res = [Res(f"bring{i}") for i in range(NB)]
        obuf = [P.sbuf(f"obuf{i}", [128, 2, TOK], BF16) for i in range(2)]
        obuf_res = [Res(f"obuf{i}") for i in range(2)]
        ps = [P.psum(f"ps{i}", [128, 512]) for i in range(8)]
        ps_res = [Res(f"ps{i}") for i in range(8)]
        seq = [(cp, trig, sc) for cp in range(8) for trig in range(2) for sc in range(64)]
        a_items = [ws.add(xtok[sc][:, cp * 256:(cp + 1) * 256], width=256) for (cp, trig, sc) in seq]
        nb_issued = [0]

        def issue_b(upto):
            while nb_issued[0] <= min(upto, len(seq) - 1):
                i = nb_issued[0]
                cp, trig, sc = seq[i]
                P.dma("sp", bring[i % NB][:], tab[sc][:, trig * 2048:(trig + 1) * 2048], writes=[bring_res[i % NB]])
                nb_issued[0] += 1

        for i, (cp, trig, sc) in enumerate(seq):
            issue_b(i + 3)
            at, ar = ws.get(a_items[i], lookahead=4)
            bt, br = bring[i % NB], bring_res[i % NB]
            for c2 in range(2):
                for kt in range(4):
                    b_ = c2 * 4 + kt
                    P.op("pe", lambda e, c2=c2, kt=kt, b_=b_, at=at, bt=bt: e.matmul(
                        ps[b_][:], lhsT=at[:, c2 * 128:(c2 + 1) * 128], rhs=bt[:, kt * 512:(kt + 1) * 512],
                        start=(sc == 0), stop=(sc == 63)), reads=[ar, br], writes=[ps_res[b_]])
            ws.release(a_items[i])
            if sc == 63:
                ob, obr = obuf[(cp * 2 + trig) % 2], obuf_res[(cp * 2 + trig) % 2]
                for c2 in range(2):
                    for kt in range(4):
                        b_ = c2 * 4 + kt
                        eng = "act" if kt % 2 == 0 else "dve"
                        if eng == "act":
                            P.op("act", lambda e, c2=c2, kt=kt, b_=b_, ob=ob: e.activation(out=ob[:, c2, kt * 512:(kt + 1) * 512], in_=ps[b_][:], func=AF.Identity),
                                 reads=[ps_res[b_]], writes=[obr])
                        else:
                            P.op("dve", lambda e, c2=c2, kt=kt, b_=b_, ob=ob: e.tensor_copy(out=ob[:, c2, kt * 512:(kt + 1) * 512], in_=ps[b_][:]),
                                 reads=[ps_res[b_]], writes=[obr])
                P.dma("pool", f12[:, trig, cp * 2:cp * 2 + 2, :], ob[:], reads=[obr])
        P.finish("sp")
        nc._stats = dict(P.n_inst)
    return nc


def _bf16(a):
    import ml_dtypes
    return np.asarray(a, np.float32).astype(ml_dtypes.bfloat16)


def run_four_layer(inputs, l, x):
    if ("seqdft",) not in _NC_CACHE:
        _NC_CACHE[("seqdft",)] = build_seqdft()
    if ("four",) not in _NC_CACHE:
        _NC_CACHE[("four",)] = build_layer("four", 0, 0)
    s_idx = np.arange(S, dtype=np.int64)[:, None]
    in_maps = []
    for core in range(NCORE):
        b, q = divmod(core, 4)
        k_idx = np.arange(q * TOK, (q + 1) * TOK, dtype=np.int64)[None, :]
        ang = (2.0 * np.pi / S) * ((s_idx * k_idx) % S).astype(np.float64)
        tab = np.concatenate([np.cos(ang), np.sin(ang)], axis=1)
        in_maps.append(dict(xtok=np.ascontiguousarray(x[b].reshape(64, 128, D)), tab=_bf16(tab).reshape(64, 128, 4096)))
    r1 = run_bass_kernel_spmd(_NC_CACHE[("seqdft",)], in_maps, core_ids=list(range(NCORE)))
    cm = common_maps(inputs, l, [inputs["ft_b"][0]])
    c_idx = np.arange(512, dtype=np.int64)
    angc = (2.0 * np.pi / 512) * ((c_idx[:, None] * c_idx[None, :]) % 512).astype(np.float64)
    ccsc = (np.concatenate([np.cos(angc), -np.sin(angc)], axis=1) / 2048.0).astype(np.float32).reshape(4, 128, 1024)
    ftw = w_colblocks(np.asarray(inputs["ft_w"][0]), 16)
    in_maps = []
    for core in range(NCORE):
        b, q = divmod(core, 4)
        m = dict(cm)
        m.update(xin=to_xT(x[b], q, 0, 0), min=mod_in(l, b), edge=edge_flags(q), fin=r1.results[core]["f12"], ccsc=ccsc, ftw=ftw)
        in_maps.append(m)
    res = run_bass_kernel_spmd(_NC_CACHE[("four",)], in_maps, core_ids=list(range(NCORE)))
    out = np.empty_like(x)
    for core in range(NCORE):
        b, q = divmod(core, 4)
        out[b, q * TOK:(q + 1) * TOK] = from_xT(res.results[core]["xout"])
    return out


def _pos_embed_table():
    rows, cols, dim = S // 64, 64, D
    quarter = dim // 4
    omega = (1.0 / (10000.0 ** (np.arange(quarter, dtype=np.float32) / np.float32(quarter)))).astype(np.float32)
    ar = np.arange(rows, dtype=np.float32)[:, None] * omega[None]
    ac = np.arange(cols, dtype=np.float32)[:, None] * omega[None]
    er = np.concatenate([np.sin(ar), np.cos(ar)], axis=-1)
    ec = np.concatenate([np.sin(ac), np.cos(ac)], axis=-1)
    pe = np.concatenate([np.broadcast_to(er[:, None, :], (rows, cols, dim // 2)),
                         np.broadcast_to(ec[None, :, :], (rows, cols, dim // 2))], axis=-1)
    return pe.reshape(rows * cols, dim).astype(np.float32)


def run_layer0_partial(inputs, x):
    key = ("none",)
    if key not in _NC_CACHE:
        _NC_CACHE[key] = build_layer("none", 0, 0)
    nc = _NC_CACHE[key]
    cm = common_maps(inputs, 0, [])
    pe = _pos_embed_table()
    in_maps = []
    for core in range(NCORE):
        b, q = divmod(core, 4)
        m = dict(cm)
        m.update(xin=to_xT(x[b], q, 0, 0), cT=cond_T(inputs, b), edge=edge_flags(q), pe=to_xT(pe, q, 0, 0))
        in_maps.append(m)
    res = run_bass_kernel_spmd(nc, in_maps, core_ids=list(range(NCORE)))
    out = np.empty_like(x)
    for core in range(NCORE):
        b, q = divmod(core, 4)
        out[b, q * TOK:(q + 1) * TOK] = from_xT(res.results[core]["xout"])
    return out


def kernel(**inputs):
    inputs = {k: np.asarray(v) for k, v in inputs.items()}
    x = np.ascontiguousarray(inputs["x"], dtype=np.float32)
    x = run_rg_layer(inputs, x)
    x = run_pool_layer(inputs, 1, x)
    x = run_conf_layer(inputs, 2, x)
    x = run_four_layer(inputs, 3, x)
    return x.astype(np.float32)


RSUB = 88
NSUB = 32
RG_HL, RG_HR = 1, 2
CTXL = 256


def build_rg(phase):
    nc = bass.Bass("TRN2", target_bir_lowering=False)
    NTW = RG_HL + TOK + RG_HR
    d = {}

    def din(name, shape, dt=F32):
        d[name] = nc.dram_tensor(name, list(shape), dt, kind="ExternalInput").ap()
        return d[name]

    xin = din("xin", [128, NDC, NTW])
    pein = din("pein", [128, NDC, NTW])
    ctxin = din("ctxin", [128, NDC, CTXL])
    if phase == "A":
        modb = din("modb", [128, 96])
        cT = din("cT", [128, NDC, 2])
        modw = din("modw", [32, 128, NDC * 128])
        modw_sh = din("modw_sh", [48, 128, NDC * 128])
        modb_sh = din("modb_sh", [128, 48])
        cT3 = din("cT3", [128, NDC, 3])
        mout = nc.dram_tensor("mout", [128, 48, 3], F32, kind="ExternalOutput").ap()
    else:
        min_ = din("min", [128, 32, 2])
    edge = din("edge", [128, 2])
    rv = din("rv", [128, 11, NSUB])
    wxr = din("wxr", [NSUB, 128, NDC * RSUB])
    gwr = din("gwr", [32, 128, 704])
    if phase == "B":
        wgr = din("wgr", [NSUB, 128, NDC * RSUB])
        summ = din("summ", [NCORE, 128, 4, NSUB])
        ctxs = din("ctxs", [128, 2, NSUB])
        mfb = din("mfb", [128, 2, NCORE])
        prod = nc.dram_tensor("prod", [NSUB, RSUB, TOK], BF16, kind="ExternalOutput").ap()
    else:
        sout = nc.dram_tensor("sout", [128, 6, NSUB], F32, kind="ExternalOutput").ap()

    with ExitStack() as st:
        P = Prog(nc, st)
        ws = WStream(P, nstg=2, nring=4)
        a16f = P.sbuf("a16f", [128, NDC, NTW], BF16)
        a16f_res = [Res(f"a16f{i}") for i in range(NDC)]
        xbpre = P.sbuf("xbpre", [128, NTW], F32)
        xbpre_res = Res("xbpre")
        xb = [P.sbuf(f"xb{i}", [128, TOK], F32) for i in range(2)]
        xb_res = [Res(f"xb{i}") for i in range(2)]
        xb16 = [P.sbuf(f"xb16_{i}", [128, TOK], BF16) for i in range(2)]
        xb16_res = [Res(f"xb16_{i}") for i in range(2)]
        abuf = P.sbuf("abuf", [128, NTW], F32)
        bbuf = P.sbuf("bbuf", [128, NTW], F32)
        tbuf = P.sbuf("tbuf", [128, NTW], F32)
        ab_res, bb_res, tb_res = Res("abuf"), Res("bbuf"), Res("tbuf")
        ybuf = [P.sbuf(f"ybuf{i}", [128, NTW], F32) for i in range(2)]
        yb_res = [Res(f"ybuf{i}") for i in range(2)]
        xt, xt_res = [abuf, bbuf], [ab_res, bb_res]
        abuf2 = P.sbuf("abuf2", [128, TOK], F32)
        bbuf2 = P.sbuf("bbuf2", [128, TOK], F32)
        abufs, ab_ress = [abuf, abuf2], [ab_res, Res("abuf2")]
        bbufs, bb_ress = [bbuf, bbuf2], [bb_res, Res("bbuf2")]
        itc = [0]
        one_c = P.sbuf("one_c", [128, 1], F32)
        P.op("pool", lambda e: e.memset(one_c[:], 1.0), writes=[Res("one_c")])
        pt, pt_res = [tbuf, ybuf[0]], [tb_res, yb_res[0]]
        rv_sb = P.sbuf("rv_sb", [128, 11, NSUB], F32)
        cp_sb = P.sbuf("cp_sb", [128, 2, NSUB], F32)
        rv_res = Res("rv")
        modb_sb = P.sbuf("modb_sb", [128, 96], F32)
        cs = P.sbuf("cs", [128, NDC, 2], F32)
        cs_res = Res("cs")
        msb = P.sbuf("msb", [128, 32, 2], F32)
        msb_res = Res("msb")
        a1 = P.sbuf("a1", [128, NDC, 2], F32)
        edge_sb = P.sbuf("edge_sb", [128, 2], F32)
        edge_res = Res("edge")
        rsum = P.sbuf("rsum", [128, 1], F32)
        rsum_res = Res("rsum")
        if phase == "A":
            so_sb = P.sbuf("so_sb", [128, 6, NSUB], F32)
            so_res = Res("so")
        else:
            summ_sb = P.sbuf("summ_sb", [128, NCORE, 4, NSUB], F32)
            carry = P.sbuf("carry", [128, 2, NSUB], F32)
            mfb_sb = P.sbuf("mfb_sb", [128, 2, NCORE], F32)
            ctmp = P.sbuf("ctmp", [128, NSUB], F32)
            carry_res = Res("carry")
            gl = [P.sbuf(f"gl{i}", [128, T], F32) for i in range(2)]
            gl_res = [Res(f"gl{i}") for i in range(2)]
            pr16 = [P.sbuf(f"pr16_{i}", [128, TOK], BF16) for i in range(2)]
            pr16_res = [Res(f"pr16_{i}") for i in range(2)]
        ps = [P.psum(f"ps{i}", [128, 512]) for i in range(8)]
        ps_res = [Res(f"ps{i}") for i in range(8)]

        mod_items = [ws.add(modw[oc], cast=False) for oc in range(32)] if phase == "A" else []
        passes = ["ctx", "lat"] if phase == "A" else ["lat"]
        plan = {}
        for pss in passes:
            for n in range(16):
                for s in range(2):
                    plan[(pss, "wx", 2 * n + s)] = ws.add(wxr[2 * n + s], width=NDC * RSUB)
                for dd in range(2):
                    plan[(pss, "gw", dd, n)] = ws.add(gwr[dd * 16 + n], width=704)
                if phase == "B":
                    for s in range(2):
                        plan[(pss, "wg", 2 * n + s)] = ws.add(wgr[2 * n + s], width=NDC * RSUB)

        P.dma("act", rv_sb[:], rv, writes=[rv_res])
        P.dma("act", edge_sb[:], edge, writes=[edge_res])
        P.op("act", lambda e: e.activation(out=cp_sb[:], in_=rv_sb[:, 9:11, :], func=AF.Sigmoid), reads=[rv_res], writes=[rv_res])
        P.op("act", lambda e: e.activation(out=cp_sb[:], in_=cp_sb[:], func=AF.Ln), reads=[rv_res], writes=[rv_res])
        P.op("dve", lambda e: e.tensor_scalar(out=cp_sb[:], in0=cp_sb[:], scalar1=8.0, scalar2=None, op0=ALU.mult), reads=[rv_res], writes=[rv_res])
        mps, mps_res = ps[7], ps_res[7]
        if phase == "A":
            P.dma("act", modb_sb[:], modb, writes=[rv_res])
            P.dma("act", cs[:], cT, writes=[cs_res])
            P.op("act", lambda e: e.activation(out=cs[:], in_=cs[:], func=AF.Silu), reads=[cs_res], writes=[cs_res])
            for oc in range(32):
                wt, wr = ws.get(mod_items[oc], lookahead=1)
                for kc in range(NDC):
                    P.op("pe", lambda e, oc=oc, kc=kc, wt=wt: e.matmul(
                        mps[:, oc * 2:(oc + 1) * 2], lhsT=wt[:, kc * 128:(kc + 1) * 128], rhs=cs[:, kc, :],
                        start=(kc == 0), stop=(kc == NDC - 1)), reads=[wr, cs_res], writes=[mps_res])
                ws.release(mod_items[oc])
            for j in range(2):
                P.op("dve", lambda e, j=j: e.tensor_tensor(
                    out=msb[:, :, j], in0=mps[:, 0:64].rearrange("p (o j) -> p o j", j=2)[:, :, j],
                    in1=modb_sb[:, 0:32], op=ALU.add), reads=[mps_res, rv_res], writes=[msb_res])
        else:
            P.dma("act", msb[:], min_, writes=[msb_res])
        P.op("dve", lambda e: e.tensor_scalar(out=a1[:], in0=msb[:, 16:32, :], scalar1=1.0, scalar2=None, op0=ALU.add),
             reads=[msb_res], writes=[msb_res])
        if phase == "A":
            P.op("pool", lambda e: e.memset(so_sb[:], 0.0), writes=[so_res])
        else:
            P.dma("act", summ_sb[:], summ.rearrange("r p a s -> p r a s"), writes=[carry_res])
            P.dma("act", carry[:], ctxs, writes=[carry_res])
            P.dma("act", mfb_sb[:], mfb, writes=[carry_res])
            for dd in range(2):
                order = range(NCORE) if dd == 0 else range(NCORE - 1, -1, -1)
                for r in order:
                    A_r = summ_sb[:, r, 2 * dd, :]
                    B_r = summ_sb[:, r, 2 * dd + 1, :]
                    mk = mfb_sb[:, dd, r:r + 1]
                    P.op("dve", lambda e, A_r=A_r, mk=mk: e.tensor_scalar(out=ctmp[:], in0=A_r, scalar1=-1.0, scalar2=mk, op0=ALU.add, op1=ALU.mult),
                         reads=[carry_res], writes=[carry_res])
                    P.op("dve", lambda e: e.tensor_scalar(out=ctmp[:], in0=ctmp[:], scalar1=1.0, scalar2=None, op0=ALU.add),
                         reads=[carry_res], writes=[carry_res])
                    P.op("dve", lambda e, dd=dd: e.tensor_tensor(out=carry[:, dd, :], in0=carry[:, dd, :], in1=ctmp[:], op=ALU.mult),
                         reads=[carry_res], writes=[carry_res])
                    P.op("dve", lambda e, B_r=B_r, mk=mk: e.tensor_scalar(out=ctmp[:], in0=B_r, scalar1=mk, scalar2=None, op0=ALU.mult),
                         reads=[carry_res], writes=[carry_res])
                    P.op("dve", lambda e, dd=dd: e.tensor_tensor(out=carry[:, dd, :], in0=carry[:, dd, :], in1=ctmp[:], op=ALU.add),
                         reads=[carry_res], writes=[carry_res])

        def run_pass(pss):
            ctx = pss == "ctx"
            NTK = CTXL if ctx else TOK
            NW = RG_HL + NTK + RG_HR
            j = 1 if ctx else 0
            for dc in range(NDC):
                k = dc % 2
                if ctx:
                    P.dma("act", xt[k][:, RG_HL:RG_HL + NTK], ctxin[:, dc, :], writes=[xt_res[k]])
                    src = xt[k][:, RG_HL:RG_HL + NTK]
                    P.op("pool", lambda e, dc=dc: e.memset(a16f[:, dc, 0:NW], 0.0), writes=[a16f_res[dc]])
                    P.op("act", lambda e, dc=dc, src=src: e.activation(out=a16f[:, dc, RG_HL:RG_HL + NTK], in_=src, func=AF.Identity,
                                                                       scale=a1[:, dc, j:j + 1], bias=msb[:, dc, j:j + 1]),
                         reads=[xt_res[k], msb_res], writes=[a16f_res[dc]])
                else:
                    P.dma("act", xt[k][:], xin[:, dc, :], writes=[xt_res[k]])
                    P.dma("act", pt[k][:], pein[:, dc, :], writes=[pt_res[k]])
                    P.op("dve", lambda e, k=k: e.tensor_tensor(out=xt[k][:], in0=xt[k][:], in1=pt[k][:], op=ALU.add),
                         reads=[pt_res[k]], writes=[xt_res[k]])
                    P.op("act", lambda e, dc=dc, k=k: e.activation(out=a16f[:, dc, :], in_=xt[k][:], func=AF.Identity,
                                                                   scale=a1[:, dc, j:j + 1], bias=msb[:, dc, j:j + 1]),
                         reads=[xt_res[k], msb_res], writes=[a16f_res[dc]])
                    P.op("dve", lambda e, dc=dc: e.tensor_scalar(out=a16f[:, dc, 0:RG_HL], in0=a16f[:, dc, 0:RG_HL], scalar1=edge_sb[:, 0:1],
                                                                 scalar2=None, op0=ALU.mult), reads=[edge_res], writes=[a16f_res[dc]])
                    P.op("dve", lambda e, dc=dc: e.tensor_scalar(out=a16f[:, dc, RG_HL + NTK:NW], in0=a16f[:, dc, RG_HL + NTK:NW], scalar1=edge_sb[:, 1:2],
                                                                 scalar2=None, op0=ALU.mult), reads=[edge_res], writes=[a16f_res[dc]])
            coltiles = [(c, min(c + 512, NW)) for c in range(0, NW, 512)]
            ctiles = [(c, min(c + 512, NTK)) for c in range(0, NTK, 512)]
            Rr = slice(0, RSUB)
            for n in range(16):
                for s in range(2):
                    sidx = 2 * n + s
                    wx, wxres = ws.get(plan[(pss, "wx", sidx)], lookahead=2)
                    for ci, (ca, cb) in enumerate(coltiles):
                        pp, ppr = ps[ci % 2], ps_res[ci % 2]
                        for kc in range(NDC):
                            P.op("pe", lambda e, kc=kc, pp=pp, ca=ca, cb=cb, wx=wx: e.matmul(
                                pp[Rr, 0:cb - ca], lhsT=wx[:, kc * RSUB:(kc + 1) * RSUB], rhs=a16f[:, kc, ca:cb],
                                start=(kc == 0), stop=(kc == NDC - 1)), reads=[wxres, a16f_res[kc]], writes=[ppr])
                        P.op("act", lambda e, pp=pp, ca=ca, cb=cb: e.activation(out=xbpre[Rr, ca:cb], in_=pp[Rr, 0:cb - ca], func=AF.Identity),
                             reads=[ppr], writes=[xbpre_res])
                    ws.release(plan[(pss, "wx", sidx)])
                    P.op("dve", lambda e, s=s, sidx=sidx: e.tensor_scalar(
                        out=xb[s][Rr, 0:NTK], in0=xbpre[Rr, 0:NTK], scalar1=rv_sb[Rr, 0, sidx:sidx + 1], scalar2=rv_sb[Rr, 4, sidx:sidx + 1],
                        op0=ALU.mult, op1=ALU.add), reads=[xbpre_res, rv_res], writes=[xb_res[s]])
                    for jj in range(1, 4):
                        P.op("dve", lambda e, s=s, sidx=sidx, jj=jj: e.scalar_tensor_tensor(
                            out=xb[s][Rr, 0:NTK], in0=xbpre[Rr, jj:jj + NTK], scalar=rv_sb[Rr, jj, sidx:sidx + 1], in1=xb[s][Rr, 0:NTK],
                            op0=ALU.mult, op1=ALU.add), reads=[xbpre_res, rv_res], writes=[xb_res[s]])
                    P.op("act", lambda e, s=s: e.activation(out=xb16[s][Rr, 0:NTK], in_=xb[s][Rr, 0:NTK], func=AF.Identity),
                         reads=[xb_res[s]], writes=[xb16_res[s]])
                for dd in range(2):
                    gw, gwres = ws.get(plan[(pss, "gw", dd, n)], lookahead=2)
                    for so in range(2):
                        sidx = 2 * n + so
                        kk_ = itc[0] % 2
                        itc[0] += 1
                        A_, A_r = abufs[kk_], ab_ress[kk_]
                        B_, B_r = bbufs[kk_], bb_ress[kk_]
                        for ci, (ca, cb) in enumerate(ctiles):
                            pr_, prr = ps[2 + ci % 2], ps_res[2 + ci % 2]
                            pi_, pir = ps[4 + ci % 2], ps_res[4 + ci % 2]
                            for si in range(2):
                                P.op("pe", lambda e, si=si, so=so, pr_=pr_, ca=ca, cb=cb, gw=gw: e.matmul(
                                    pr_[Rr, 0:cb - ca], lhsT=gw[Rr, (si * 2 + so) * RSUB:(si * 2 + so + 1) * RSUB], rhs=xb16[si][Rr, ca:cb],
                                    start=(si == 0), stop=(si == 1)), reads=[gwres, xb16_res[si]], writes=[prr])
                            for si in range(2):
                                P.op("pe", lambda e, si=si, so=so, pi_=pi_, ca=ca, cb=cb, gw=gw: e.matmul(
                                    pi_[Rr, 0:cb - ca], lhsT=gw[Rr, 352 + (si * 2 + so) * RSUB:352 + (si * 2 + so + 1) * RSUB], rhs=xb16[si][Rr, ca:cb],
                                    start=(si == 0), stop=(si == 1)), reads=[gwres, xb16_res[si]], writes=[pir])
                            P.op("act", lambda e, pr_=pr_, ca=ca, cb=cb, dd=dd, sidx=sidx: e.activation(
                                out=A_[Rr, ca:cb], in_=pr_[Rr, 0:cb - ca], func=AF.Sigmoid, bias=rv_sb[Rr, 5 + dd, sidx:sidx + 1]),
                                reads=[prr, rv_res], writes=[A_r])
                            P.op("act", lambda e, pi_=pi_, ca=ca, cb=cb, dd=dd, sidx=sidx: e.activation(
                                out=B_[Rr, ca:cb], in_=pi_[Rr, 0:cb - ca], func=AF.Sigmoid, bias=rv_sb[Rr, 7 + dd, sidx:sidx + 1]),
                                reads=[pir, rv_res], writes=[B_r])
                        if phase == "A" and not ctx:
                            P.op("dve", lambda e: e.reduce_sum(out=rsum[Rr, :], in_=A_[Rr, 0:NTK], axis=mybir.AxisListType.X),
                                 reads=[A_r], writes=[rsum_res])
                            P.op("act", lambda e, dd=dd, sidx=sidx: e.activation(out=so_sb[Rr, 2 * dd, sidx:sidx + 1], in_=rsum[Rr, :], func=AF.Exp,
                                                                               scale=cp_sb[Rr, dd, sidx:sidx + 1]),
                                 reads=[rsum_res, rv_res], writes=[so_res])
                        P.op("act", lambda e, dd=dd, sidx=sidx: e.activation(out=A_[Rr, 0:NTK], in_=A_[Rr, 0:NTK], func=AF.Exp,
                                                                           scale=cp_sb[Rr, dd, sidx:sidx + 1]),
                             reads=[A_r, rv_res], writes=[A_r])
                        P.op("act", lambda e: e.activation(out=tbuf[Rr, 0:NTK], in_=A_[Rr, 0:NTK], func=AF.Square), reads=[A_r], writes=[tb_res])
                        P.op("act", lambda e: e.activation(out=tbuf[Rr, 0:NTK], in_=tbuf[Rr, 0:NTK], func=AF.Sqrt, scale=-1.0, bias=one_c[Rr, 0:1]),
                             reads=[tb_res], writes=[tb_res])
                        P.op("pool", lambda e, so=so: e.tensor_tensor(out=B_[Rr, 0:NTK], in0=B_[Rr, 0:NTK], in1=xb[so][Rr, 0:NTK], op=ALU.mult),
                             reads=[B_r, xb_res[so]], writes=[B_r])
                        P.op("dve", lambda e: e.tensor_tensor(out=B_[Rr, 0:NTK], in0=B_[Rr, 0:NTK], in1=tbuf[Rr, 0:NTK], op=ALU.mult),
                             reads=[B_r, tb_res], writes=[B_r])
                        if phase == "B":
                            init = carry[Rr, dd, sidx:sidx + 1]
                            dst, dres = (ybuf[so], yb_res[so]) if dd == 0 else (tbuf, tb_res)
                        else:
                            init = 0.0
                            dst, dres = tbuf, tb_res
                        if dd == 0:
                            P.op("dve", lambda e, dst=dst, init=init: e.tensor_tensor_scan(
                                out=dst[Rr, 0:NTK], data0=A_[Rr, 0:NTK], data1=B_[Rr, 0:NTK], initial=init, op0=ALU.mult, op1=ALU.add),
                                reads=[A_r, B_r] + ([carry_res] if phase == "B" else []), writes=[dres])
                        else:
                            P.op("dve", lambda e, dst=dst, init=init: e.tensor_tensor_scan(
                                out=dst[Rr, NTK - 1::-1] if False else dst[Rr, 0:NTK][:, ::-1], data0=A_[Rr, 0:NTK][:, ::-1], data1=B_[Rr, 0:NTK][:, ::-1],
                                initial=init, op0=ALU.mult, op1=ALU.add),
                                reads=[A_r, B_r] + ([carry_res] if phase == "B" else []), writes=[dres])
                        if phase == "A":
                            col = NTK - 1 if dd == 0 else 0
                            row = (4 + dd) if ctx else (2 * dd + 1)
                            P.op("act", lambda e, col=col, row=row, sidx=sidx: e.activation(out=so_sb[Rr, row, sidx:sidx + 1], in_=tbuf[Rr, col:col + 1], func=AF.Identity),
                                 reads=[tb_res], writes=[so_res])
                        elif dd == 1:
                            P.op("dve", lambda e, so=so: e.tensor_tensor(out=ybuf[so][Rr, 0:NTK], in0=ybuf[so][Rr, 0:NTK], in1=tbuf[Rr, 0:NTK], op=ALU.add),
                                 reads=[tb_res], writes=[yb_res[so]])
                    ws.release(plan[(pss, "gw", dd, n)])
                if phase == "B":
                    for so in range(2):
                        sidx = 2 * n + so
                        wg, wgres = ws.get(plan[(pss, "wg", sidx)], lookahead=2)
                        pb, pbr = pr16[so], pr16_res[so]
                        for ci, (ca, cb) in enumerate(ctiles):
                            pg, pgr = ps[6 + ci % 2], ps_res[6 + ci % 2]
                            for kc in range(NDC):
                                P.op("pe", lambda e, kc=kc, pg=pg, ca=ca, cb=cb, wg=wg: e.matmul(
                                    pg[Rr, 0:cb - ca], lhsT=wg[:, kc * RSUB:(kc + 1) * RSUB], rhs=a16f[:, kc, RG_HL + ca:RG_HL + cb],
                                    start=(kc == 0), stop=(kc == NDC - 1)), reads=[wgres, a16f_res[kc]], writes=[pgr])
                            k = ci % 2
                            P.op("act", lambda e, pg=pg, k=k, ca=ca, cb=cb: e.activation(out=gl[k][Rr, 0:cb - ca], in_=pg[Rr, 0:cb - ca], func=AF.Gelu_apprx_tanh),
                                 reads=[pgr], writes=[gl_res[k]])
                            P.op("dve", lambda e, k=k, so=so, ca=ca, cb=cb, pb=pb: e.tensor_tensor(
                                out=pb[Rr, ca:cb], in0=ybuf[so][Rr, ca:cb], in1=gl[k][Rr, 0:cb - ca], op=ALU.mult),
                                reads=[yb_res[so], gl_res[k]], writes=[pbr])
                        ws.release(plan[(pss, "wg", sidx)])
                        P.dma("pool", prod[sidx], pb[Rr, :], reads=[pbr])

        for pss in passes:
            run_pass(pss)
        if phase == "A":
            P.dma("pool", sout, so_sb[:], reads=[so_res])
            cs3 = P.sbuf("cs3", [128, NDC, 3], F32)
            cs3_res = Res("cs3")
            mbs = P.sbuf("mbs", [128, 48], F32)
            mo_sb = P.sbuf("mo_sb", [128, 48, 3], F32)
            mo_res = Res("mo")
            P.dma("act", cs3[:], cT3, writes=[cs3_res])
            P.dma("act", mbs[:], modb_sh, writes=[cs3_res])
            P.op("act", lambda e: e.activation(out=cs3[:], in_=cs3[:], func=AF.Silu), reads=[cs3_res], writes=[cs3_res])
            sh_items = [ws.add(modw_sh[i], cast=False) for i in range(48)]
            mp2, mp2r = ps[6], ps_res[6]
            for i in range(48):
                wt, wr = ws.get(sh_items[i], lookahead=1)
                for kc in range(NDC):
                    P.op("pe", lambda e, i=i, kc=kc, wt=wt: e.matmul(
                        mp2[:, i * 3:(i + 1) * 3], lhsT=wt[:, kc * 128:(kc + 1) * 128], rhs=cs3[:, kc, :],
                        start=(kc == 0), stop=(kc == NDC - 1)), reads=[wr, cs3_res], writes=[mp2r])
                ws.release(sh_items[i])
            for j in range(3):
                P.op("dve", lambda e, j=j: e.tensor_tensor(
                    out=mo_sb[:, :, j], in0=mp2[:, 0:144].rearrange("p (o j) -> p o j", j=3)[:, :, j],
                    in1=mbs[:], op=ALU.add), reads=[mp2r, cs3_res], writes=[mo_res])
            P.dma("pool", mout, mo_sb[:], reads=[mo_res])
        P.finish("sp")
        nc._stats = dict(P.n_inst)
    return nc


def _pad128(a):
    shp = list(a.shape)
    shp[-2] = 128
    out = np.zeros(shp, np.float32)
    out[..., :a.shape[-2], :] = a
    return out


def rg_col(v):
    return np.asarray(v, np.float32).reshape(NSUB, RSUB).T


def run_rg_layer(inputs, x):
    for ph in ("A", "B"):
        if ("rg", ph) not in _NC_CACHE:
            _NC_CACHE[("rg", ph)] = build_rg(ph)
    if ("lin",) not in _NC_CACHE:
        _NC_CACHE[("lin",)] = build_layer("lin", 0, 0)
    pe = _pos_embed_table()
    modw = w_colblocks(np.asarray(inputs["mod_w"][0][:, 0:4096]), 32)
    modb_full = np.ascontiguousarray(np.asarray(inputs["mod_b"][0], np.float32).reshape(96, 128).T)
    rvt = np.zeros((128, 11, NSUB), np.float32)
    cw = np.asarray(inputs["rg_conv_w"][0], np.float32)
    for j in range(4):
        rvt[:RSUB, j] = rg_col(cw[j])
    rvt[:RSUB, 4] = rg_col(inputs["rg_conv_b"][0])
    for dd in range(2):
        rvt[:RSUB, 5 + dd] = rg_col(inputs["rg_br"][0, dd])
        rvt[:RSUB, 7 + dd] = rg_col(inputs["rg_bi"][0, dd])
        rvt[:RSUB, 9 + dd] = rg_col(inputs["rg_lam"][0, dd])
    rvt[RSUB:, 9:11] = 1.0

    def sub_cols(w):
        return np.ascontiguousarray(np.asarray(w, np.float32).reshape(NDC, 128, NSUB, RSUB).transpose(2, 1, 0, 3).reshape(NSUB, 128, NDC * RSUB))
    wxr = sub_cols(inputs["rg_w_x"][0])
    wgr = sub_cols(inputs["rg_w_gate"][0])
    gw = np.zeros((32, 128, 704), np.float32)
    for dd in range(2):
        for n in range(16):
            for k_, nm in enumerate(("rg_wr", "rg_wi")):
                blk = np.asarray(inputs[nm][0, dd, n], np.float32).reshape(2, RSUB, 2, RSUB).transpose(1, 0, 2, 3).reshape(RSUB, 352)
                gw[dd * 16 + n, :RSUB, k_ * 352:(k_ + 1) * 352] = blk
    mwl = [w_colblocks(np.asarray(inputs["mod_w"][l]), 96) for l in range(DEPTH)]
    mbl = [np.asarray(inputs["mod_b"][l], np.float32).reshape(96, 128).T for l in range(DEPTH)]
    cT3 = np.stack([np.asarray(inputs["c"][0], np.float32), np.asarray(inputs["c"][1], np.float32),
                    np.asarray(inputs["c_ctx"], np.float32)], axis=-1)
    cT3 = np.ascontiguousarray(cT3.reshape(NDC, 128, 3).transpose(1, 0, 2))
    base = []
    for core in range(NCORE):
        b, q = divmod(core, 4)
        ctxT = np.ascontiguousarray(np.asarray(inputs["ctx"][b], np.float32).reshape(CTXL, NDC, 128).transpose(2, 1, 0))
        base.append(dict(xin=to_xT(x[b], q, RG_HL, RG_HR), pein=to_xT(pe, q, RG_HL, RG_HR), ctxin=ctxT,
                         edge=edge_flags(q), rv=rvt, wxr=wxr, gwr=gw))
    mapsA = []
    for core in range(NCORE):
        b, q = divmod(core, 4)
        m = dict(base[core])
        sh_w = np.ascontiguousarray(np.concatenate([mwl[l][core * 12:(core + 1) * 12] for l in range(DEPTH)], axis=0))
        sh_b = np.ascontiguousarray(np.concatenate([mbl[l][:, core * 12:(core + 1) * 12] for l in range(DEPTH)], axis=1))
        m.update(modb=modb_full, cT=cond_T(inputs, b), modw=modw, modw_sh=sh_w, modb_sh=sh_b, cT3=cT3)
        mapsA.append(m)
    rA = run_bass_kernel_spmd(_NC_CACHE[("rg", "A")], mapsA, core_ids=list(range(NCORE)))
    for l in range(DEPTH):
        M = np.zeros((128, 96, 3), np.float32)
        for core in range(NCORE):
            M[:, core * 12:(core + 1) * 12, :] = rA.results[core]["mout"][:, l * 12:(l + 1) * 12, :]
        _MOD[l] = M
    souts = [rA.results[c]["sout"] for c in range(NCORE)]
    summ = np.ascontiguousarray(np.stack([s_[:, 0:4, :] for s_ in souts], axis=0))
    mapsB = []
    for core in range(NCORE):
        b, q = divmod(core, 4)
        mfb = np.zeros((128, 2, NCORE), np.float32)
        for r in range(NCORE):
            rb, rq = divmod(r, 4)
            if rb == b and rq < q:
                mfb[:, 0, r] = 1.0
            if rb == b and rq > q:
                mfb[:, 1, r] = 1.0
        m = dict(base[core])
        m.update(wgr=wgr, summ=summ, ctxs=np.ascontiguousarray(souts[core][:, 4:6, :]), mfb=mfb, min=mod_in(0, b, 32))
        mapsB.append(m)
    rB = run_bass_kernel_spmd(_NC_CACHE[("rg", "B")], mapsB, core_ids=list(range(NCORE)))
    cm = common_maps(inputs, 0, [])
    wor = np.ascontiguousarray(np.asarray(inputs["rg_w_out"][0], np.float32).reshape(22, 128, D))
    mapsC = []
    for core in range(NCORE):
        b, q = divmod(core, 4)
        prod = rB.results[core]["prod"]
        m = dict(cm)
        m.update(xin=to_xT(x[b], q, 0, 0), min=mod_in(0, b), edge=edge_flags(q), pe=to_xT(pe, q, 0, 0),
                 prodin=np.ascontiguousarray(prod.reshape(22, 128, TOK)), wor=wor)
        mapsC.append(m)
    res = run_bass_kernel_spmd(_NC_CACHE[("lin",)], mapsC, core_ids=list(range(NCORE)))
    out = np.empty_like(x)
    for core in range(NCORE):
        b, q = divmod(core, 4)
        out[b, q * TOK:(q + 1) * TOK] = from_xT(res.results[core]["xout"])
    return out
```
